# Optimizing a Trainium2 kernel written in Bass

```python
import jax, jax.numpy as jnp
from jax import lax
import numpy as np

D_MODEL = 1024
BATCH = 1
SEQ = 16384
DEPTH = 2

N_HEADS = 16
HEAD_DIM = D_MODEL // N_HEADS
HD = N_HEADS * HEAD_DIM
D_FF = ((8 * D_MODEL + 3 * 256 - 1) // (3 * 256)) * 256
PLE_DIM = 256
BLOCK_Q = 128
N_A = DEPTH // 2
N_B = DEPTH - N_A
EPS = 1e-6

kernel_name = 'yoco_stickbreak_fox_hybrid'


def rms_norm(x, g):
    xf = x.astype(jnp.float32)
    y = xf * lax.rsqrt(jnp.mean(xf * xf, axis=-1, keepdims=True) + EPS)
    return (y * g.astype(jnp.float32)).astype(x.dtype)


def split_heads(t):
    b, s, _ = t.shape
    return t.reshape(b, s, N_HEADS, HEAD_DIM).transpose(0, 2, 1, 3)


def merge_heads(t):
    b, h, s, d = t.shape
    return t.transpose(0, 2, 1, 3).reshape(b, s, h * d)


def to_blocks(t):
    b, h, s = t.shape[:3]
    t = t.reshape((b, h, s // BLOCK_Q, BLOCK_Q) + t.shape[3:])
    return jnp.moveaxis(t, 2, 0)


def from_blocks(t):
    t = jnp.moveaxis(t, 0, 2)
    b, h, nb, bq, d = t.shape
    return t.reshape(b, h, nb * bq, d)


def stick_breaking_attention(q, k, v):
    s_len = q.shape[2]
    nb = s_len // BLOCK_Q
    kpos = jnp.arange(s_len)
    scale = HEAD_DIM ** -0.5

    def block(args):
        qb, i = args
        tpos = i * BLOCK_Q + jnp.arange(BLOCK_Q)
        z = jnp.einsum('bhqd,bhkd->bhqk', qb, k).astype(jnp.float32) * scale
        strict = kpos[None, :] < tpos[:, None]
        log_one_minus = jnp.where(strict, -jax.nn.softplus(z), 0.0)
        later = lax.cumsum(log_one_minus, axis=3, reverse=True) - log_one_minus
        w = jnp.where(strict, jnp.exp(jax.nn.log_sigmoid(z) + later), 0.0)
        return jnp.einsum('bhqk,bhkd->bhqd', w.astype(v.dtype), v)

    out = lax.map(block, (to_blocks(q), jnp.arange(nb)))
    return from_blocks(out)


def forgetting_attention(q, k, v, f_cum):
    s_len = q.shape[2]
    nb = s_len // BLOCK_Q
    kpos = jnp.arange(s_len)
    scale = HEAD_DIM ** -0.5

    def block(args):
        qb, fb, i = args
        tpos = i * BLOCK_Q + jnp.arange(BLOCK_Q)
        logits = (jnp.einsum('bhqd,bhkd->bhqk', qb, k).astype(jnp.float32) * scale
                  + fb[..., :, None] - f_cum[:, :, None, :])
        causal = kpos[None, :] <= tpos[:, None]
        logits = jnp.where(causal, logits, -jnp.inf)
        probs = jax.nn.softmax(logits, axis=-1)
        return jnp.einsum('bhqk,bhkd->bhqd', probs.astype(v.dtype), v)

    out = lax.map(block, (to_blocks(q), to_blocks(f_cum), jnp.arange(nb)))
    return from_blocks(out)


def swiglu(h, w_gu, w_d):
    g, u = jnp.split(h @ w_gu, 2, axis=-1)
    return (jax.nn.silu(g) * u) @ w_d


def setup_inputs(seed: int = 0) -> dict:
    key = jax.random.key(seed)
    ks = jax.random.split(key, 20)
    f32 = jnp.float32

    def w(k, shape, fan_in, extra=1.0):
        return jax.random.normal(k, shape, f32) * (fan_in ** -0.5) * extra

    def gain(k, shape):
        return 1.0 + 0.05 * jax.random.normal(k, shape, f32)

    res_scale = (2.0 * DEPTH) ** -0.5
    return {
        'x': jax.random.normal(ks[0], (BATCH, SEQ, D_MODEL), f32),
        'p': jax.random.normal(ks[1], (DEPTH, BATCH, SEQ, PLE_DIM), f32),
        'attn_norm_g': gain(ks[2], (DEPTH, D_MODEL)),
        'sb_w_qkv': w(ks[3], (N_A, D_MODEL, 3 * HD), D_MODEL),
        'sb_w_o': w(ks[4], (N_A, HD, D_MODEL), HD, res_scale),
        'shared_norm_g': gain(ks[5], (D_MODEL,)),
        'shared_w_kvf': w(ks[6], (D_MODEL, 2 * HD + N_HEADS), D_MODEL),
        'shared_b_f': 0.1 * jax.random.normal(ks[7], (N_HEADS,), f32),
        'shared_k_norm_g': gain(ks[8], (HEAD_DIM,)),
        'fox_w_q': w(ks[9], (N_B, D_MODEL, HD), D_MODEL),
        'fox_q_norm_g': gain(ks[10], (N_B, HEAD_DIM)),
        'fox_w_o': w(ks[11], (N_B, HD, D_MODEL), HD, res_scale),
        'ffn_norm_g': gain(ks[12], (DEPTH, D_MODEL)),
        'ffn_w_gu': w(ks[13], (DEPTH, D_MODEL, 2 * D_FF), D_MODEL),
        'ffn_w_d': w(ks[14], (DEPTH, D_FF, D_MODEL), D_FF, res_scale),
        'ple_norm_g': gain(ks[15], (DEPTH, D_MODEL)),
        'ple_w_gate': w(ks[16], (DEPTH, D_MODEL, D_MODEL), D_MODEL),
        'ple_w_proj': w(ks[17], (DEPTH, PLE_DIM, D_MODEL), PLE_DIM, res_scale),
    }


def reference(x, p, attn_norm_g, sb_w_qkv, sb_w_o, shared_norm_g, shared_w_kvf, shared_b_f,
              shared_k_norm_g, fox_w_q, fox_q_norm_g, fox_w_o, ffn_norm_g, ffn_w_gu, ffn_w_d,
              ple_norm_g, ple_w_gate, ple_w_proj):
    h = x
    k_sh = v_sh = f_cum = None
    for i in range(DEPTH):
        hn = rms_norm(h, attn_norm_g[i])
        if i < N_A:
            q, k, v = jnp.split(hn @ sb_w_qkv[i], 3, axis=-1)
            mix = stick_breaking_attention(split_heads(q), split_heads(k), split_heads(v))
            h = h + merge_heads(mix) @ sb_w_o[i]
        else:
            if i == N_A:
                hs = rms_norm(h, shared_norm_g)
                kvf = hs @ shared_w_kvf
                k_sh = rms_norm(split_heads(kvf[..., :HD]), shared_k_norm_g)
                v_sh = split_heads(kvf[..., HD:2 * HD])
                f_logit = kvf[..., 2 * HD:].astype(jnp.float32) + shared_b_f.astype(jnp.float32)
                f_cum = lax.cumsum(jax.nn.log_sigmoid(f_logit), axis=1).transpose(0, 2, 1)
            j = i - N_A
            q = rms_norm(split_heads(hn @ fox_w_q[j]), fox_q_norm_g[j])
            mix = forgetting_attention(q, k_sh, v_sh, f_cum)
            h = h + merge_heads(mix) @ fox_w_o[j]
        h = h + swiglu(rms_norm(h, ffn_norm_g[i]), ffn_w_gu[i], ffn_w_d[i])
        gate = jax.nn.sigmoid(rms_norm(h, ple_norm_g[i]) @ ple_w_gate[i])
        h = h + gate * (p[i] @ ple_w_proj[i])
    return h
```

```python
import numpy as np
import ml_dtypes
from contextlib import ExitStack
import concourse.bass as bass
import concourse.mybir as mybir
from concourse.bass_utils import run_bass_kernel_spmd

F32 = mybir.dt.float32
BF16 = mybir.dt.bfloat16
AF = mybir.ActivationFunctionType
ALU = mybir.AluOpType
NPBF = ml_dtypes.bfloat16

NC_ = 8
D = 1024
S_ALL = 16384
TL = 2048
NH = 16
DH = 64
DFF = 2816
PLE = 256
EPS = 1e-6
EPOCH = 4096


class Buf:
    __slots__ = ("name", "w", "r", "wm")

    def __init__(self, name, multi=False):
        self.name = name
        self.w = None
        self.r = []
        self.wm = {} if multi else None


class Tile:
    __slots__ = ("t", "b")

    def __init__(self, t, name):
        self.t = t
        self.b = Buf(name)


class Sched:
    ENG = ("pe", "act", "dve", "pool", "sp")

    def __init__(self, nc, es):
        self.nc = nc
        self.es = es
        self.lists = {e: [] for e in self.ENG}
        self.sems = {}
        self.cnt = {}
        self.alias = {}
        self.waited = {e: {} for e in self.ENG}
        self.ecount = {e: 0 for e in self.ENG}
        self.nsem = 0

    def _mksem(self, key):
        self.nsem += 1
        self.sems[key] = self.es.enter_context(self.nc.semaphore("s%d" % self.nsem))
        self.cnt[key] = 0

    def _deps(self, eng, reads, writes):
        deps = {}

        def add(tok):
            if tok is None:
                return
            k, v = tok
            if deps.get(k, 0) < v:
                deps[k] = v

        for b in reads:
            add(b.w)
            if b.wm is not None:
                for t in b.wm.items():
                    add(t)
        for b in writes:
            add(b.w)
            for t in b.r:
                add(t)
        out = []
        w = self.waited[eng]
        for k, v in deps.items():
            if eng == "pe" and k.startswith("E_pe#"):
                continue
            if w.get(k, 0) >= v:
                continue
            w[k] = v
            out.append((k, v))
        return out

    @staticmethod
    def _commit(tok, reads, writes):
        for b in writes:
            if b.wm is not None:
                if b.wm.get(tok[0], 0) < tok[1]:
                    b.wm[tok[0]] = tok[1]
                continue
            b.w = tok
            b.r = []
        for b in reads:
            b.r.append(tok)

    def op(self, eng, fn, reads=(), writes=()):
        deps = self._deps(eng, reads, writes)
        ep = self.ecount[eng] // EPOCH
        key = "E_%s#%d" % (eng, ep)
        if key not in self.sems:
            self._mksem(key)
        self.ecount[eng] += 1
        self.cnt[key] += 1
        tok = (key, self.cnt[key])
        sems = self.sems

        def thunk(e, deps=deps, fn=fn, key=key):
            for k, v in deps:
                e.wait_ge(sems[k], v)
            fn(e).then_inc(sems[key], 1)

        self.lists[eng].append(thunk)
        self._commit(tok, reads, writes)
        return tok

    def dma(self, eng, semname, fn, reads=(), writes=()):
        deps = self._deps(eng, reads, writes)
        key = self.alias.get(semname)
        if key is None or self.cnt[key] >= 16 * 240:
            n = 0 if key is None else int(key.split("#")[1]) + 1
            key = "D_%s#%d" % (semname, n)
            self.alias[semname] = key
            self._mksem(key)
        self.cnt[key] += 16
        tok = (key, self.cnt[key])
        sems = self.sems

        def thunk(e, deps=deps, fn=fn, key=key):
            for k, v in deps:
                e.wait_ge(sems[k], v)
            fn(e).then_inc(sems[key], 16)

        self.lists[eng].append(thunk)
        self._commit(tok, reads, writes)
        return tok

    def wait_all(self, eng, toks):
        sems = self.sems
        toks = list(toks)

        def thunk(e):
            for k, v in toks:
                e.wait_ge(sems[k], v)

        self.lists[eng].append(thunk)

    def emit(self):
        L = self.lists
        with self.nc.Block() as block:
            @block.tensor
            def _(e):
                for t in L["pe"]:
                    t(e)

            @block.scalar
            def _(e):
                for t in L["act"]:
                    t(e)

            @block.vector
            def _(e):
                for t in L["dve"]:
                    t(e)

            @block.gpsimd
            def _(e):
                for t in L["pool"]:
                    t(e)

            @block.sync
            def _(e):
                for t in L["sp"]:
                    t(e)


class Cx:
    def __init__(self):
        self.nc = bass.Bass("TRN2", target_bir_lowering=False)
        self.es = ExitStack()
        self.S = Sched(self.nc, self.es)
        self.out_toks = {}
        self.rr = {}

    def din(self, name, shape, dt):
        return self.nc.dram_tensor(name, list(shape), dt, kind="ExternalInput").ap()

    def dout(self, name, shape, dt):
        return self.nc.dram_tensor(name, list(shape), dt, kind="ExternalOutput").ap()

    def dint(self, name, shape, dt):
        return self.nc.dram_tensor(name, list(shape), dt, kind="Internal").ap()

    def sb(self, name, shape, dt, n=None):
        if n is None:
            return Tile(self.es.enter_context(self.nc.sbuf_tensor("sb_" + name, list(shape), dt)), name)
        return [Tile(self.es.enter_context(self.nc.sbuf_tensor("sb_%s%d" % (name, i), list(shape), dt)),
                     "%s%d" % (name, i)) for i in range(n)]

    def ps(self, name, shape, dt, n=None):
        if n is None:
            return Tile(self.es.enter_context(self.nc.psum_tensor("ps_" + name, list(shape), dt)), name)
        return [Tile(self.es.enter_context(self.nc.psum_tensor("ps_%s%d" % (name, i), list(shape), dt)),
                     "%s%d" % (name, i)) for i in range(n)]

    def nxt(self, lst, key=None):
        key = key or id(lst)
        i = self.rr.get(key, 0)
        self.rr[key] = i + 1
        return lst[i % len(lst)]

    def store(self, semname, out_ap, tile, in_ap, eng="sp"):
        tok = self.S.dma(eng, "st_" + tile.b.name, lambda e: e.dma_start(out=out_ap, in_=in_ap), reads=[tile.b])
        self.out_toks[tok[0]] = max(self.out_toks.get(tok[0], 0), tok[1])

    def finish(self):
        self.S.wait_all("sp", list(self.out_toks.items()))
        self.S.emit()
        self.es.close()
        return self.nc


class Dense:
    def __init__(self, cx, ident_ap):
        self.cx = cx
        S = cx.S
        self.ident = cx.sb("ident", [128, 128], BF16)
        S.dma("sp", "const1", lambda e: e.dma_start(out=self.ident.t[:], in_=ident_ap), writes=[self.ident.b])
        self.junk = cx.sb("junk", [128, 1024], BF16, n=2)
        self.ss = cx.sb("ss", [128, 1], F32, n=4)
        self.lnv = cx.sb("lnv", [128, 1], F32, n=4)
        self.rstd = cx.sb("rstd", [128, 1], F32, n=4)
        self.hn = cx.sb("hn", [128, 1024], BF16, n=2)
        self.psT = cx.ps("psT", [128, 1024], BF16, n=2)
        self.wst = cx.sb("wst", [128, 512], F32, n=4)
        self.gv = {}
        self.evq = 0

    def load_gain(self, name, g_ap):
        cx = self.cx
        t = cx.sb("g_" + name, [128, 8], F32)
        cx.S.dma("sp", "cg_" + name, lambda e: e.dma_start(out=t.t[:], in_=g_ap.rearrange("(k p) -> p k", p=128),
                                                      allow_slow_non_contiguous=True), writes=[t.b])
        self.gv[name] = t
        return t

    def prep_weight(self, w_ap, K, N, dst_fn, gain=None, post=None, c0=0, c1=None):
        cx = self.cx
        S = cx.S
        c1 = N if c1 is None else c1
        for kc in range(K // 128):
            for n0 in range(c0, c1, 512):
                wd = min(512, c1 - n0)
                st = cx.nxt(self.wst)
                S.dma("sp", "wst_" + st.b.name,
                      lambda e, st=st, kc=kc, n0=n0, wd=wd: e.dma_start(
                          out=st.t[:, 0:wd], in_=w_ap[kc * 128:(kc + 1) * 128, n0:n0 + wd]),
                      writes=[st.b])
                dt_, dap = dst_fn(kc, n0, wd)
                eng = "pool" if (self.evq % 2 == 0) else "dve"
                self.evq += 1
                pv = post(n0) if post is not None else None
                if gain is not None:
                    g = gain
                    if pv is not None:
                        fn = lambda e, st=st, dap=dap, kc=kc, wd=wd, g=g, pv=pv: e.tensor_scalar(
                            dap, st.t[:, 0:wd], g.t[:, kc:kc + 1], pv, ALU.mult, ALU.mult)
                    else:
                        fn = lambda e, st=st, dap=dap, kc=kc, wd=wd, g=g: e.tensor_scalar(
                            dap, st.t[:, 0:wd], g.t[:, kc:kc + 1], None, ALU.mult)
                    S.op(eng, fn, reads=[st.b, g.b], writes=[dt_.b])
                else:
                    S.op(eng, lambda e, st=st, dap=dap, wd=wd: e.tensor_copy(dap, st.t[:, 0:wd]),
                         reads=[st.b], writes=[dt_.b])

    def norm_T(self, h, hnT, col0):
        cx = self.cx
        S = cx.S
        junk = cx.nxt(self.junk)
        ss = cx.nxt(self.ss)
        lnv = cx.nxt(self.lnv)
        rstd = cx.nxt(self.rstd)
        hn = cx.nxt(self.hn)
        pT = cx.nxt(self.psT)
        S.op("act", lambda e: e.activation(junk.t[:], h.t[:], AF.Square, accum_out=ss.t[:]),
             reads=[h.b], writes=[junk.b, ss.b])
        S.op("act", lambda e: e.activation(lnv.t[:], ss.t[:], AF.Ln, scale=1.0 / D, bias=self.eps.t[:, 0:1]),
             reads=[ss.b, self.eps.b], writes=[lnv.b])
        S.op("act", lambda e: e.activation(rstd.t[:], lnv.t[:], AF.Exp, scale=-0.5),
             reads=[lnv.b], writes=[rstd.b])
        S.op("dve", lambda e: e.tensor_scalar(hn.t[:], h.t[:], rstd.t[:, 0:1], None, ALU.mult),
             reads=[h.b, rstd.b], writes=[hn.b])
        for kc in range(8):
            S.op("pe", lambda e, kc=kc: e.transpose(pT.t[:, kc * 128:(kc + 1) * 128],
                                                    hn.t[:, kc * 128:(kc + 1) * 128], self.ident.t[:]),
                 reads=[hn.b, self.ident.b], writes=[pT.b])
        S.op("dve", lambda e: e.tensor_copy(hnT.t[:, :, col0:col0 + 128],
                                            pT.t[:, :].rearrange("p (k t) -> p k t", k=8)),
             reads=[pT.b], writes=[hnT.b])

    def consts(self):
        cx = self.cx
        self.eps = cx.sb("epsc", [128, 1], F32)
        cx.S.op("pool", lambda e: e.memset(self.eps.t[:], EPS), writes=[self.eps.b])


def build_pre0():
    cx = Cx()
    S = cx.S
    x = cx.din("x", [TL, D], F32)
    g = cx.din("g", [D], F32)
    w = cx.din("w", [D, 3 * D], F32)
    ident = cx.din("ident", [128, 128], BF16)
    qT = cx.dout("qT", [D, TL], BF16)
    kT = cx.dout("kT", [D, TL], BF16)
    v = cx.dout("v", [TL, D], BF16)
    dn = Dense(cx, ident)
    dn.consts()
    gt = dn.load_gain("a", g)
    Wb = cx.sb("Wb", [128, 8, 3 * D], BF16)
    dn.prep_weight(w, D, 3 * D, lambda kc, n0, wd: (Wb, Wb.t[:, kc, n0:n0 + wd]), gain=gt,
                   post=lambda n0: (0.125 if n0 < D else 1.0))
    hblk = cx.sb("hblk", [128, D], F32, n=3)
    hnT = cx.sb("hnT", [128, 8, 512], BF16, n=2)
    pm = cx.ps("pm", [128, 512], F32, n=4)
    ost = cx.sb("ost", [128, 512], BF16, n=4)
    ev = 0
    for gi in range(4):
        hT = cx.nxt(hnT)
        for b in range(4):
            h = cx.nxt(hblk)
            r0 = (gi * 4 + b) * 128
            S.dma("sp", "ld_" + h.b.name, lambda e, h=h, r0=r0: e.dma_start(out=h.t[:], in_=x[r0:r0 + 128, :]),
                  writes=[h.b])
            dn.norm_T(h, hT, b * 128)
        for n in range(16):
            p = cx.nxt(pm)
            for kc in range(8):
                S.op("pe", lambda e, p=p, kc=kc, n=n, hT=hT: e.matmul(
                    p.t[:], Wb.t[:, kc, n * 128:(n + 1) * 128], hT.t[:, kc, :], start=(kc == 0), stop=(kc == 7)),
                    reads=[Wb.b, hT.b], writes=[p.b])
            o = cx.nxt(ost)
            if ev % 2 == 0:
                S.op("act", lambda e, o=o, p=p: e.copy(o.t[:], p.t[:]), reads=[p.b], writes=[o.b])
            else:
                S.op("dve", lambda e, o=o, p=p: e.tensor_copy(o.t[:], p.t[:]), reads=[p.b], writes=[o.b])
            ev += 1
            dst = qT if n < 8 else kT
            rr = (n % 8) * 128
            cx.store("o_qk", dst[rr:rr + 128, gi * 512:(gi + 1) * 512], o, o.t[:])
        for b in range(4):
            for hf in range(2):
                p = cx.nxt(pm)
                for kc in range(8):
                    S.op("pe", lambda e, p=p, kc=kc, b=b, hf=hf, hT=hT: e.matmul(
                        p.t[:], hT.t[:, kc, b * 128:(b + 1) * 128],
                        Wb.t[:, kc, 2 * D + hf * 512:2 * D + (hf + 1) * 512], start=(kc == 0), stop=(kc == 7)),
                        reads=[Wb.b, hT.b], writes=[p.b])
                o = cx.nxt(ost)
                if ev % 2 == 0:
                    S.op("act", lambda e, o=o, p=p: e.copy(o.t[:], p.t[:]), reads=[p.b], writes=[o.b])
                else:
                    S.op("dve", lambda e, o=o, p=p: e.tensor_copy(o.t[:], p.t[:]), reads=[p.b], writes=[o.b])
                ev += 1
                r0 = (gi * 4 + b) * 128
                cx.store("o_v", v[r0:r0 + 128, hf * 512:(hf + 1) * 512], o, o.t[:])
    return cx.finish()


def build_attn0(n_pairs=8):
    cx = Cx()
    S = cx.S
    qT = cx.din("qT", [D, TL], BF16)
    kT = cx.din("kT", [D, S_ALL], BF16)
    vr = cx.din("vr", [8, 128, 128 * 128], BF16)
    masks = cx.din("masks", [128, 8 * 128], BF16)
    tri = cx.din("tri", [128, 128], BF16)
    omt = cx.din("omt", [128, 128], BF16)
    mixT = cx.dout("mixT", [D, TL], BF16)

    mk = cx.sb("mk", [128, 8 * 128], BF16)
    trt = cx.sb("trt", [128, 128], BF16)
    omtt = cx.sb("omtt", [128, 128], BF16)
    S.dma("sp", "const3", lambda e: e.dma_start(out=mk.t[:], in_=masks), writes=[mk.b])
    S.dma("sp", "const4", lambda e: e.dma_start(out=trt.t[:], in_=tri), writes=[trt.b])
    S.dma("sp", "const5", lambda e: e.dma_start(out=omtt.t[:], in_=omt), writes=[omtt.b])

    one = cx.sb("one", [128, 1], F32)
    S.op("pool", lambda e: e.memset(one.t[:], 1.0), writes=[one.b])
    kTs = cx.sb("kTs", [128, S_ALL], BF16, n=2)
    vs = cx.sb("vs", [128, 128 * 128], BF16, n=2)
    qs = cx.sb("qs", [128, TL], BF16, n=2)
    mx = cx.sb("mx", [128, TL], BF16, n=2)
    zp = cx.ps("zp", [128, 512], F32, n=3)
    Cp = cx.ps("Cp", [128, 512], F32, n=2)
    op_ = cx.ps("op", [128, 512], F32, n=2)
    eb = cx.sb("eb", [128, 512], F32, n=4)
    spb = cx.sb("spb", [128, 512], BF16, n=4)
    xb = cx.sb("xb", [128, 512], BF16, n=3)
    wb = cx.sb("wb", [128, 512], BF16, n=3)

    def load_pair(hp):
        sl = hp % 2
        k_, v_, q_ = kTs[sl], vs[sl], qs[sl]
        S.dma("sp", "ldq%d" % sl, lambda e: e.dma_start(out=q_.t[:], in_=qT[hp * 128:(hp + 1) * 128, :]),
              writes=[q_.b])
        for part in range(4):
            c0 = part * 4096
            S.dma("sp", "ldk%d" % sl,
                  lambda e, c0=c0: e.dma_start(out=k_.t[:, c0:c0 + 4096], in_=kT[hp * 128:(hp + 1) * 128, c0:c0 + 4096]),
                  writes=[k_.b])
            S.dma("sp", "ldv%d" % sl,
                  lambda e, c0=c0: e.dma_start(out=v_.t[:, c0:c0 + 4096], in_=vr[hp, :, c0:c0 + 4096]),
                  writes=[v_.b])

    load_pair(0)

    def do_pair(hp):
        sl = hp % 2
        k_, v_, q_, m_ = kTs[sl], vs[sl], qs[sl], mx[sl]
        if hp + 1 < n_pairs:
            load_pair(hp + 1)
        items = []
        for J in range(4):
            nkb = 32 * J + 32
            for kb in range(nkb - 1, -1, -1):
                for hh in range(2):
                    items.append((J, kb, hh, nkb))
        st = {}

        def s1(it):
            J, kb, hh, nkb = it
            r = kb - 32 * J
            c0 = 128 * (r // 8) if r >= 0 else 0
            z = cx.nxt(zp)
            e_ = cx.nxt(eb)
            sp_ = cx.nxt(spb)
            pb = 64 * hh
            S.op("pe", lambda e: e.matmul(z.t[:, c0:512], k_.t[pb:pb + 64, kb * 128:(kb + 1) * 128],
                                          q_.t[pb:pb + 64, 512 * J + c0:512 * J + 512], start=True, stop=True),
                 reads=[k_.b, q_.b], writes=[z.b])
            S.op("act", lambda e: e.activation(e_.t[:, c0:512], z.t[:, c0:512], AF.Exp),
                 reads=[z.b], writes=[e_.b])
            if r >= 0:
                i = r % 8
                S.op("pool", lambda e: e.tensor_tensor(e_.t[:, c0:c0 + 128], e_.t[:, c0:c0 + 128],
                                                       mk.t[:, i * 128:(i + 1) * 128], ALU.mult),
                     reads=[e_.b, mk.b], writes=[e_.b])
            st[it] = [c0, e_, sp_, None, None]

        def s1b(it):
            c0, e_, sp_, _, _ = st[it]
            S.op("act", lambda e: e.activation(sp_.t[:, c0:512], e_.t[:, c0:512], AF.Ln, bias=one.t[:, 0:1]),
                 reads=[e_.b, one.b], writes=[sp_.b])

        def s2(it):
            J, kb, hh, nkb = it
            c0, e_, sp_, _, _ = st[it]
            C = Cp[hh]
            x_ = cx.nxt(xb)
            S.op("pe", lambda e: e.matmul(C.t[:, c0:512], trt.t[:], sp_.t[:, c0:512],
                                          start=(kb == nkb - 1), stop=(kb == 0)),
                 reads=[trt.b, sp_.b], writes=[C.b])
            S.op("act", lambda e: e.activation(x_.t[:, c0:512], C.t[:, c0:512], AF.Exp, scale=-1.0),
                 reads=[C.b], writes=[x_.b])
            st[it][3] = x_

        def s3(it):
            J, kb, hh, nkb = it
            c0, e_, sp_, x_, _ = st[it]
            C = Cp[hh]
            w_ = cx.nxt(wb)
            if kb > 0:
                S.op("pe", lambda e: e.matmul(C.t[:, c0:512], omtt.t[:], sp_.t[:, c0:512], start=False, stop=False),
                     reads=[omtt.b, sp_.b], writes=[C.b])
            S.op("dve", lambda e: e.tensor_tensor(w_.t[:, c0:512], e_.t[:, c0:512], x_.t[:, c0:512], ALU.mult),
                 reads=[e_.b, x_.b], writes=[w_.b])
            st[it][4] = w_

        def s4(it):
            J, kb, hh, nkb = it
            c0, e_, sp_, x_, w_ = st.pop(it)
            o_ = op_[J % 2]
            pb = 64 * hh
            S.op("pe", lambda e: e.matmul(o_.t[pb:pb + 64, c0:512], v_.t[:, kb * 128 + pb:kb * 128 + pb + 64],
                                          w_.t[:, c0:512], start=(kb == nkb - 1), stop=(kb == 0)),
                 reads=[v_.b, w_.b], writes=[o_.b])
            if kb == 0:
                S.op("act", lambda e: e.copy(m_.t[pb:pb + 64, 512 * J:512 * J + 512], o_.t[pb:pb + 64, :]),
                     reads=[o_.b], writes=[m_.b])

        n = len(items)
        for tau in range(n + 3):
            if tau < n:
                s1(items[tau])
            if 0 <= tau - 1 < n:
                s2(items[tau - 1])
            if tau < n:
                s1b(items[tau])
            if 0 <= tau - 2 < n:
                s3(items[tau - 2])
            if 0 <= tau - 3 < n:
                s4(items[tau - 3])
        cx.store("o_mix", mixT[hp * 128:(hp + 1) * 128, :], m_, m_.t[:])

    for hp in range(n_pairs):
        do_pair(hp)
    return cx.finish()


def build_post(nxt):
    cx = Cx()
    S = cx.S
    h_in = cx.din("h", [TL, D], F32)
    mixT = cx.din("mixT", [D, TL], BF16)
    p_in = cx.din("p", [TL, PLE], F32)
    w_o = cx.din("w_o", [D, D], F32)
    ffn_g = cx.din("ffn_g", [D], F32)
    w_gu = cx.din("w_gu", [D, 2 * DFF], F32)
    w_d = cx.din("w_d", [DFF, D], F32)
    ple_g = cx.din("ple_g", [D], F32)
    w_gate = cx.din("w_gate", [D, D], F32)
    w_proj = cx.din("w_proj", [PLE, D], F32)
    ident = cx.din("ident", [128, 128], BF16)
    hout = cx.dout("hout", [TL, D], F32)
    if nxt:
        a_g = cx.din("a_g", [D], F32)
        w_q = cx.din("w_q", [D, D], F32)
        qn_g = cx.din("qn_g", [DH], F32)
        sh_g = cx.din("sh_g", [D], F32)
        w_kvf = cx.din("w_kvf", [D, 2 * D + NH], F32)
        b_f = cx.din("b_f", [NH], F32)
        kn_g = cx.din("kn_g", [DH], F32)
        q1T = cx.dout("q1T", [D, TL], BF16)
        k1T = cx.dout("k1T", [D, TL], BF16)
        v1 = cx.dout("v1", [TL, D], BF16)
        flogT = cx.dout("flogT", [NH, TL], F32)

    dn = Dense(cx, ident)
    dn.consts()
    g_ffn = dn.load_gain("ffn", ffn_g)
    g_ple = dn.load_gain("ple", ple_g)

    wbf = cx.sb("wbf", [128, 512], BF16, n=4)

    def to_scratch(name, dst_ap_fn):
        buf = Buf("scr_" + name)

        def dst_fn(kc, n0, wd):
            t = cx.nxt(wbf)
            return t, t.t[:, 0:wd]
        return buf, dst_fn

    def prep_dram(name, w_ap, K, N, dst_ap_fn, gain=None, c0=0, c1=None):
        buf = Buf("scr_" + name, multi=True)
        c1_ = N if c1 is None else c1
        for kc in range(K // 128):
            for n0 in range(c0, c1_, 512):
                wd = min(512, c1_ - n0)
                holder = {}

                def dst_fn(kc_, n0_, wd_, holder=holder):
                    t = cx.nxt(wbf)
                    holder["t"] = t
                    return t, t.t[:, 0:wd_]
                dn.prep_weight(w_ap[kc * 128:(kc + 1) * 128, :], 128, N, dst_fn, gain=None if gain is None else _GainCol(gain, kc),
                               c0=n0, c1=n0 + wd)
                t = holder["t"]
                dap = dst_ap_fn(kc, n0 - c0, wd)
                S.dma("sp", "wp_" + t.b.name, lambda e, t=t, dap=dap, wd=wd: e.dma_start(out=dap, in_=t.t[:, 0:wd]),
                      reads=[t.b], writes=[buf])
        return buf

    class _GainCol:
        def __init__(self, g, kc):
            self.b = g.b
            self.t = _Shift(g.t, kc)

    class _Shift:
        def __init__(self, t, kc):
            self._t = t
            self._kc = kc

        def __getitem__(self, idx):
            return self._t[idx[0], self._kc:self._kc + 1]

    def scr8(name, N):
        return cx.dint("scr_" + name, [N // 512, 128, 8, 512], BF16)

    WoB = scr8("wo", D)
    b_wo = prep_dram("wo", w_o, D, D, lambda kc, n0, wd: WoB[n0 // 512, :, kc, :])
    WguB = cx.dint("scr_wgu", [22, 128, 8, 256], BF16)

    def gu_dst(off):
        def f(kc, n0, wd):
            j0 = n0 // 128
            return WguB[j0:j0 + wd // 128, :, kc, off:off + 128].rearrange("j p i -> p j i")
        return f
    b_wg = prep_dram("wg", w_gu, D, 2 * DFF, gu_dst(0), gain=g_ffn, c0=0, c1=DFF)
    b_wu = prep_dram("wu", w_gu, D, 2 * DFF, gu_dst(128), gain=g_ffn, c0=DFF, c1=2 * DFF)
    WdB = cx.dint("scr_wd", [4, 128, 22, 256], BF16)
    b_wd = prep_dram("wd", w_d, DFF, D,
                     lambda kc, n0, wd: WdB[n0 // 256:n0 // 256 + 2, :, kc, :].rearrange("q p i -> p q i"))
    WgateB = scr8("wgate", D)
    b_wgate = prep_dram("wgate", w_gate, D, D, lambda kc, n0, wd: WgateB[n0 // 512, :, kc, :], gain=g_ple)
    Wproj = cx.sb("Wproj", [128, 2, D], BF16)
    dn.prep_weight(w_proj, PLE, D, lambda kc, n0, wd: (Wproj, Wproj.t[:, kc, n0:n0 + wd]))
    if nxt:
        g_a = dn.load_gain("a1", a_g)
        g_sh = dn.load_gain("sh", sh_g)
        WqB = scr8("wq", D)
        b_wq = prep_dram("wq", w_q, D, D, lambda kc, n0, wd: WqB[n0 // 512, :, kc, :], gain=g_a)
        WkvB = scr8("wkv", 2 * D)
        b_wkv = prep_dram("wkv", w_kvf, D, 2 * D + NH, lambda kc, n0, wd: WkvB[n0 // 512, :, kc, :], gain=g_sh,
                          c0=0, c1=2 * D)
        Wf = cx.sb("Wf", [128, 8, NH], BF16)
        dn.prep_weight(w_kvf, D, 2 * D + NH, lambda kc, n0, wd: (Wf, Wf.t[:, kc, 0:wd]), gain=g_sh,
                       c0=2 * D, c1=2 * D + NH)
        qg = cx.sb("qg", [128, 8, DH], F32)
        kg = cx.sb("kg", [128, 8, DH], F32)
        S.dma("sp", "const6", lambda e: e.dma_start(out=qg.t[:], in_=qn_g.unsqueeze(0).unsqueeze(0).to_broadcast([128, 8, DH])),
              writes=[qg.b])
        S.dma("sp", "const7", lambda e: e.dma_start(out=kg.t[:], in_=kn_g.unsqueeze(0).unsqueeze(0).to_broadcast([128, 8, DH])),
              writes=[kg.b])
        S.op("dve", lambda e: e.tensor_scalar(qg.t[:], qg.t[:], 0.125, None, ALU.mult), reads=[qg.b], writes=[qg.b])
        nbf = cx.sb("nbf", [NH, 1], F32)
        S.dma("sp", "const8", lambda e: e.dma_start(out=nbf.t[:], in_=b_f.rearrange("(h o) -> h o", o=1)), writes=[nbf.b])
        S.op("dve", lambda e: e.tensor_scalar(nbf.t[:], nbf.t[:], -1.0, None, ALU.mult), reads=[nbf.b], writes=[nbf.b])
        one = cx.sb("one", [128, 1], F32)
        S.op("pool", lambda e: e.memset(one.t[:], 1.0), writes=[one.b])

    hres = cx.sb("hres", [128, D], F32, n=6)
    mTs = cx.sb("mT", [128, 8, 512], BF16, n=1)
    wt8 = cx.sb("wt8", [128, 8, 512], BF16, n=2)
    wgut = cx.sb("wgut", [128, 8, 256], BF16, n=3)
    wdt = cx.sb("wdt", [128, 22, 256], BF16, n=2)
    hnTs = cx.sb("hnT", [128, 8, 512], BF16, n=2)
    aT = cx.sb("aT", [128, 22, 512], BF16)
    sgs = cx.sb("sg", [128, 512], F32, n=2)
    tmps = cx.sb("tmp", [128, 512], F32, n=2)
    pblk = cx.sb("pblk", [128, PLE], F32, n=2)
    pbf = cx.sb("pbf", [128, PLE], BF16, n=2)
    pTs = cx.sb("pT", [128, 2, 128], BF16, n=2)
    pm = cx.ps("pm", [128, 512], F32, n=4)
    if nxt:
        hd8 = cx.sb("hd8", [128, 8], F32, n=4)
        qnb = cx.sb("qnb", [128, 512], BF16, n=2)
        oT = cx.sb("oT", [128, 512], BF16, n=2)
        fl = cx.sb("fl", [NH, 512], F32, n=2)

    def load_w8(scr, buf, hf, name):
        t = cx.nxt(wt8)
        S.dma("sp", "ld_" + t.b.name, lambda e: e.dma_start(out=t.t[:], in_=scr[hf]), reads=[buf], writes=[t.b])
        return t

    def add_res(h, c0, wd, src_tile, src_ap, flip=[0]):
        S.op("dve", lambda e: e.tensor_tensor(h.t[:, c0:c0 + wd], h.t[:, c0:c0 + wd], src_ap, ALU.add),
             reads=[h.b, src_tile.b], writes=[h.b])

    for gi in range(4):
        t0 = gi * 512
        hb = []
        for b in range(4):
            h = cx.nxt(hres)
            r0 = t0 + b * 128
            S.dma("sp", "ld_" + h.b.name, lambda e, h=h, r0=r0: e.dma_start(out=h.t[:], in_=h_in[r0:r0 + 128, :]),
                  writes=[h.b])
            hb.append(h)
        mT = cx.nxt(mTs)
        S.dma("sp", "ld_mT", lambda e, mT=mT, t0=t0: e.dma_start(
            out=mT.t[:], in_=mixT[:, t0:t0 + 512].rearrange("(c p) t -> p c t", p=128)), writes=[mT.b])
        for hf in range(2):
            wt = load_w8(WoB, b_wo, hf, "wo")
            for b in range(4):
                p = cx.nxt(pm)
                for kc in range(8):
                    S.op("pe", lambda e, p=p, kc=kc, b=b, wt=wt, mT=mT: e.matmul(
                        p.t[:], mT.t[:, kc, b * 128:(b + 1) * 128], wt.t[:, kc, :], start=(kc == 0), stop=(kc == 7)),
                        reads=[mT.b, wt.b], writes=[p.b])
                add_res(hb[b], hf * 512, 512, p, p.t[:])
        hT = cx.nxt(hnTs)
        for b in range(4):
            dn.norm_T(hb[b], hT, b * 128)
        for j in range(22):
            wg = cx.nxt(wgut)
            S.dma("sp", "ld_" + wg.b.name, lambda e, wg=wg, j=j: e.dma_start(out=wg.t[:], in_=WguB[j]),
                  reads=[b_wg, b_wu], writes=[wg.b])
            pg = cx.nxt(pm)
            pu = cx.nxt(pm)
            for kc in range(8):
                S.op("pe", lambda e, pg=pg, kc=kc, wg=wg, hT=hT: e.matmul(
                    pg.t[:], wg.t[:, kc, 0:128], hT.t[:, kc, :], start=(kc == 0), stop=(kc == 7)),
                    reads=[wg.b, hT.b], writes=[pg.b])
            for kc in range(8):
                S.op("pe", lambda e, pu=pu, kc=kc, wg=wg, hT=hT: e.matmul(
                    pu.t[:], wg.t[:, kc, 128:256], hT.t[:, kc, :], start=(kc == 0), stop=(kc == 7)),
                    reads=[wg.b, hT.b], writes=[pu.b])
            sg = cx.nxt(sgs)
            S.op("act", lambda e, sg=sg, pg=pg: e.activation(sg.t[:], pg.t[:], AF.Silu), reads=[pg.b], writes=[sg.b])
            S.op("dve", lambda e, sg=sg, pu=pu, j=j: e.tensor_tensor(aT.t[:, j, :], sg.t[:], pu.t[:], ALU.mult),
                 reads=[sg.b, pu.b], writes=[aT.b])
        for qd in range(4):
            wd_ = cx.nxt(wdt)
            S.dma("sp", "ld_" + wd_.b.name, lambda e, wd_=wd_, qd=qd: e.dma_start(out=wd_.t[:], in_=WdB[qd]),
                  reads=[b_wd], writes=[wd_.b])
            for b in range(4):
                p = cx.nxt(pm)
                for j in range(22):
                    S.op("pe", lambda e, p=p, j=j, b=b, wd_=wd_: e.matmul(
                        p.t[:, 0:256], aT.t[:, j, b * 128:(b + 1) * 128], wd_.t[:, j, :], start=(j == 0), stop=(j == 21)),
                        reads=[aT.b, wd_.b], writes=[p.b])
                add_res(hb[b], qd * 256, 256, p, p.t[:, 0:256])
        hT = cx.nxt(hnTs)
        for b in range(4):
            dn.norm_T(hb[b], hT, b * 128)
        for hf in range(2):
            wt = load_w8(WgateB, b_wgate, hf, "wgate")
            for b in range(4):
                pb_ = cx.nxt(pblk)
                r0 = t0 + b * 128
                S.dma("sp", "ld_" + pb_.b.name, lambda e, pb_=pb_, r0=r0: e.dma_start(out=pb_.t[:], in_=p_in[r0:r0 + 128, :]),
                      writes=[pb_.b])
                pf = cx.nxt(pbf)
                S.op("pool", lambda e, pf=pf, pb_=pb_: e.tensor_copy(pf.t[:], pb_.t[:]), reads=[pb_.b], writes=[pf.b])
                pT_ps = cx.nxt(dn.psT)
                for k2 in range(2):
                    S.op("pe", lambda e, k2=k2, pT_ps=pT_ps, pf=pf: e.transpose(
                        pT_ps.t[:, k2 * 128:(k2 + 1) * 128], pf.t[:, k2 * 128:(k2 + 1) * 128], dn.ident.t[:]),
                        reads=[pf.b, dn.ident.b], writes=[pT_ps.b])
                pT = cx.nxt(pTs)
                S.op("act", lambda e, pT=pT, pT_ps=pT_ps: e.copy(pT.t[:, :, :], pT_ps.t[:, 0:256].rearrange("p (k t) -> p k t", k=2)),
                     reads=[pT_ps.b], writes=[pT.b])
                pgate = cx.nxt(pm)
                for kc in range(8):
                    S.op("pe", lambda e, pgate=pgate, kc=kc, b=b, wt=wt, hT=hT: e.matmul(
                        pgate.t[:], hT.t[:, kc, b * 128:(b + 1) * 128], wt.t[:, kc, :], start=(kc == 0), stop=(kc == 7)),
                        reads=[hT.b, wt.b], writes=[pgate.b])
                pproj = cx.nxt(pm)
                for k2 in range(2):
                    S.op("pe", lambda e, pproj=pproj, k2=k2, pT=pT, hf=hf: e.matmul(
                        pproj.t[:], pT.t[:, k2, :], Wproj.t[:, k2, hf * 512:(hf + 1) * 512], start=(k2 == 0), stop=(k2 == 1)),
                        reads=[pT.b, Wproj.b], writes=[pproj.b])
                sg = cx.nxt(sgs)
                S.op("act", lambda e, sg=sg, pgate=pgate: e.activation(sg.t[:], pgate.t[:], AF.Sigmoid),
                     reads=[pgate.b], writes=[sg.b])
                tmp = cx.nxt(tmps)
                S.op("dve", lambda e, tmp=tmp, sg=sg, pproj=pproj: e.tensor_tensor(tmp.t[:], sg.t[:], pproj.t[:], ALU.mult),
                     reads=[sg.b, pproj.b], writes=[tmp.b])
                hh_ = hb[b]
                S.op("pool", lambda e, hh_=hh_, tmp=tmp, hf=hf: e.tensor_tensor(
                    hh_.t[:, hf * 512:(hf + 1) * 512], hh_.t[:, hf * 512:(hf + 1) * 512], tmp.t[:], ALU.add),
                    reads=[hh_.b, tmp.b], writes=[hh_.b])
        for b in range(4):
            r0 = t0 + b * 128
            cx.store("o_h", hout[r0:r0 + 128, :], hb[b], hb[b].t[:])
        if not nxt:
            continue
        hT = cx.nxt(hnTs)
        for b in range(4):
            dn.norm_T(hb[b], hT, b * 128)

        def head_norm_store(p, gtile, dstT, b, hf):
            sq = cx.nxt(tmps)
            S.op("act", lambda e: e.activation(sq.t[:], p.t[:], AF.Square), reads=[p.b], writes=[sq.b])
            s8 = cx.nxt(hd8)
            S.op("dve", lambda e: e.tensor_reduce(s8.t[:], sq.t[:, :].rearrange("p (h d) -> p h d", d=DH),
                                                  mybir.AxisListType.X, ALU.add), reads=[sq.b], writes=[s8.b])
            l8 = cx.nxt(hd8)
            S.op("act", lambda e: e.activation(l8.t[:], s8.t[:], AF.Ln, scale=1.0 / DH, bias=dn.eps.t[:, 0:1]),
                 reads=[s8.b, dn.eps.b], writes=[l8.b])
            r8 = cx.nxt(hd8)
            S.op("act", lambda e: e.activation(r8.t[:], l8.t[:], AF.Exp, scale=-0.5), reads=[l8.b], writes=[r8.b])
            qf = cx.nxt(sgs)
            S.op("dve", lambda e: e.tensor_tensor(qf.t[:, :].rearrange("p (h d) -> p h d", d=DH),
                                                  p.t[:, :].rearrange("p (h d) -> p h d", d=DH),
                                                  r8.t[:, :].unsqueeze(2).to_broadcast([128, 8, DH]), ALU.mult),
                 reads=[p.b, r8.b], writes=[qf.b])
            qn = cx.nxt(qnb)
            S.op("pool", lambda e: e.tensor_tensor(qn.t[:, :], qf.t[:, :], gtile.t[:, :, :].rearrange("p h d -> p (h d)"), ALU.mult),
                 reads=[qf.b, gtile.b], writes=[qn.b])
            tp = cx.nxt(dn.psT)
            for c4 in range(4):
                S.op("pe", lambda e, c4=c4: e.transpose(tp.t[:, c4 * 128:(c4 + 1) * 128], qn.t[:, c4 * 128:(c4 + 1) * 128],
                                                        dn.ident.t[:]), reads=[qn.b, dn.ident.b], writes=[tp.b])
            o = cx.nxt(oT)
            S.op("act", lambda e: e.copy(o.t[:], tp.t[:, 0:512]), reads=[tp.b], writes=[o.b])
            r0 = t0 + b * 128
            cx.store("o_qk1", dstT[hf * 512:(hf + 1) * 512, r0:r0 + 128].rearrange("(c p) t -> p c t", p=128), o,
                     o.t[:, :].rearrange("p (c t) -> p c t", c=4))

        for hf in range(2):
            wt = load_w8(WqB, b_wq, hf, "wq")
            for b in range(4):
                p = cx.nxt(pm)
                for kc in range(8):
                    S.op("pe", lambda e, p=p, kc=kc, b=b, wt=wt, hT=hT: e.matmul(
                        p.t[:], hT.t[:, kc, b * 128:(b + 1) * 128], wt.t[:, kc, :], start=(kc == 0), stop=(kc == 7)),
                        reads=[hT.b, wt.b], writes=[p.b])
                head_norm_store(p, qg, q1T, b, hf)
        for hf in range(4):
            wt = load_w8(WkvB, b_wkv, hf, "wkv")
            for b in range(4):
                p = cx.nxt(pm)
                for kc in range(8):
                    S.op("pe", lambda e, p=p, kc=kc, b=b, wt=wt, hT=hT: e.matmul(
                        p.t[:], hT.t[:, kc, b * 128:(b + 1) * 128], wt.t[:, kc, :], start=(kc == 0), stop=(kc == 7)),
                        reads=[hT.b, wt.b], writes=[p.b])
                if hf < 2:
                    head_norm_store(p, kg, k1T, b, hf)
                else:
                    o = cx.nxt(oT)
                    S.op("act", lambda e, o=o, p=p: e.copy(o.t[:], p.t[:]), reads=[p.b], writes=[o.b])
                    r0 = t0 + b * 128
                    cx.store("o_v1", v1[r0:r0 + 128, (hf - 2) * 512:(hf - 1) * 512], o, o.t[:])
        p = cx.nxt(pm)
        for kc in range(8):
            S.op("pe", lambda e, p=p, kc=kc, hT=hT: e.matmul(p.t[0:NH, :], Wf.t[:, kc, :], hT.t[:, kc, :],
                                                              start=(kc == 0), stop=(kc == 7)),
                 reads=[Wf.b, hT.b], writes=[p.b])
        f1 = cx.nxt(fl)
        S.op("act", lambda e, f1=f1, p=p: e.activation(f1.t[:], p.t[0:NH, :], AF.Exp, scale=-1.0, bias=nbf.t[:, 0:1]),
             reads=[p.b, nbf.b], writes=[f1.b])
        f2 = cx.nxt(fl)
        S.op("act", lambda e, f1=f1, f2=f2: e.activation(f2.t[:], f1.t[:], AF.Ln, bias=one.t[0:NH, 0:1]),
             reads=[f1.b, one.b], writes=[f2.b])
        S.op("dve", lambda e, f2=f2: e.tensor_scalar(f2.t[:], f2.t[:], -1.0, None, ALU.mult), reads=[f2.b], writes=[f2.b])
        cx.store("o_fl", flogT[:, t0:t0 + 512], f2, f2.t[:])
    return cx.finish()


def build_attn1(n_heads=NH):
    cx = Cx()
    S = cx.S
    qT = cx.din("qT", [D, TL], BF16)
    kT = cx.din("kT", [D, S_ALL], BF16)
    vr = cx.din("vr", [8, 128, 128 * 128], BF16)
    flog = cx.din("flog", [NH, S_ALL], F32)
    onehot = cx.din("onehot", [NH, 8], F32)
    negm = cx.din("negm", [128, 8 * 128], BF16)
    ident = cx.din("ident", [128, 128], BF16)
    sel = cx.din("sel", [128, 256], F32)
    mixT = cx.dout("mixT", [D, TL], BF16)
    kaug = cx.dint("kaug", [NH, 6, S_ALL], BF16)
    qaug = cx.dint("qaug", [NH, 6, TL], BF16)
    b_kaug = Buf("kaug")
    b_qaug = Buf("qaug")

    nm = cx.sb("nm", [128, 8 * 128], BF16)
    idt = cx.sb("idt", [128, 128], BF16)
    selt = cx.sb("selt", [128, 256], F32)
    oh = cx.sb("oh", [NH, 8], F32)
    for t_, src in ((nm, negm), (idt, ident), (selt, sel), (oh, onehot)):
        S.dma("sp", "cc_" + t_.b.name, lambda e, t_=t_, src=src: e.dma_start(out=t_.t[:], in_=src), writes=[t_.b])

    CH = 1024
    Fc = cx.sb("Fc", [NH, CH], F32, n=2)
    Fs = cx.sb("Fs", [NH, CH], F32, n=2)
    r1 = cx.sb("r1", [NH, CH], F32)
    onesf = cx.sb("onesf", [NH, CH], F32)
    S.op("pool", lambda e: e.memset(onesf.t[:], 1.0), writes=[onesf.b])
    ka = cx.sb("ka", [NH, 6, CH], BF16)
    Fq = cx.sb("Fq", [NH, TL], F32)
    qa = cx.sb("qa", [NH, 6, 512], BF16)
    carry = cx.sb("carry", [NH, 1], F32, n=2)
    S.op("pool", lambda e: e.memset(carry[1].t[:], 0.0), writes=[carry[1].b])
    S.op("pool", lambda e: e.memset(ka.t[:, 0:3, :], 1.0), writes=[ka.b])
    S.op("pool", lambda e: e.memset(qa.t[:, 3:6, :], 1.0), writes=[qa.b])
    for ci in range(S_ALL // CH):
        fc = Fc[ci % 2]
        fs = Fs[ci % 2]
        S.dma("sp", "ld_" + fc.b.name, lambda e, fc=fc, ci=ci: e.dma_start(out=fc.t[:], in_=flog[:, ci * CH:(ci + 1) * CH]),
              writes=[fc.b])
        cprev = carry[(ci + 1) % 2]
        ccur = carry[ci % 2]
        S.op("dve", lambda e, fs=fs, fc=fc, cprev=cprev: e.tensor_tensor_scan(
            fs.t[:], onesf.t[:], fc.t[:], cprev.t[:, 0:1], ALU.mult, ALU.add),
            reads=[onesf.b, fc.b, cprev.b], writes=[fs.b])
        S.op("dve", lambda e, fs=fs, ccur=ccur: e.tensor_copy(ccur.t[:], fs.t[:, CH - 1:CH]), reads=[fs.b], writes=[ccur.b])
        fview = fs.t[:, :].rearrange("h (c i) -> h c i", c=8)
        fqv = Fq.t[:, ci * 128:(ci + 1) * 128]
        for c in range(8):
            if c == 0:
                S.op("dve", lambda e, fview=fview, fqv=fqv, c=c: e.tensor_scalar(
                    fqv, fview[:, c, :], oh.t[:, c:c + 1], None, ALU.mult), reads=[fs.b, oh.b], writes=[Fq.b])
            else:
                S.op("dve", lambda e, fview=fview, fqv=fqv, c=c: e.scalar_tensor_tensor(
                    fqv, fview[:, c, :], oh.t[:, c:c + 1], fqv, ALU.mult, ALU.add), reads=[fs.b, oh.b, Fq.b], writes=[Fq.b])
        S.op("dve", lambda e, fs=fs: e.tensor_scalar(r1.t[:], fs.t[:], -1.0, None, ALU.mult), reads=[fs.b], writes=[r1.b])
        for part in range(3):
            S.op("dve", lambda e, part=part: e.tensor_copy(ka.t[:, 3 + part, :], r1.t[:]), reads=[r1.b], writes=[ka.b])
            if part < 2:
                S.op("dve", lambda e, part=part: e.tensor_tensor(r1.t[:], r1.t[:], ka.t[:, 3 + part, :], ALU.subtract),
                     reads=[r1.b, ka.b], writes=[r1.b])
        S.dma("sp", "st_kaug", lambda e, ci=ci: e.dma_start(out=kaug[:, :, ci * CH:(ci + 1) * CH], in_=ka.t[:]),
              reads=[ka.b], writes=[b_kaug])
    for qi in range(4):
        S.op("dve", lambda e, qi=qi: e.tensor_copy(r1.t[:, 0:512], Fq.t[:, qi * 512:(qi + 1) * 512]), reads=[Fq.b], writes=[r1.b])
        for part in range(3):
            S.op("dve", lambda e, part=part: e.tensor_copy(qa.t[:, part, :], r1.t[:, 0:512]), reads=[r1.b], writes=[qa.b])
            if part < 2:
                S.op("dve", lambda e, part=part: e.tensor_tensor(r1.t[:, 0:512], r1.t[:, 0:512], qa.t[:, part, :], ALU.subtract),
                     reads=[r1.b, qa.b], writes=[r1.b])
        S.dma("sp", "st_qaug", lambda e, qi=qi: e.dma_start(out=qaug[:, :, qi * 512:(qi + 1) * 512], in_=qa.t[:]),
              reads=[qa.b], writes=[b_qaug])

    kTs = cx.sb("kTs", [128, S_ALL], BF16, n=2)
    qs = cx.sb("qs", [128, TL], BF16, n=2)
    vs = cx.sb("vs", [128, 128 * 130], BF16, n=2)
    mx = cx.sb("mx", [128, TL], BF16, n=2)
    for sl in range(2):
        v3 = vs[sl].t[:, :].rearrange("p (k c) -> p k c", c=130)
        S.op("pool", lambda e, v3=v3: e.memset(v3[:, :, 64:65], 1.0), writes=[vs[sl].b])
        S.op("pool", lambda e, v3=v3: e.memset(v3[:, :, 129:130], 1.0), writes=[vs[sl].b])
    zp = cx.ps("zp", [128, 512], F32, n=3)
    op_ = cx.ps("op", [128, 512], F32, n=2)
    bcp = cx.ps("bcp", [128, 512], F32, n=1)
    pb_ = cx.sb("pb", [128, 512], BF16, n=3)
    osb = cx.sb("osb", [128, 512], F32, n=1)
    rbs = cx.sb("rbs", [128, 512], F32, n=1)

    def load_head(h):
        sl = h % 2
        k_, q_ = kTs[sl], qs[sl]
        S.dma("sp", "ldq%d" % sl, lambda e: e.dma_start(out=q_.t[0:64, :], in_=qT[h * 64:(h + 1) * 64, :]), writes=[q_.b])
        S.dma("sp", "ldq%d" % sl, lambda e: e.dma_start(out=q_.t[64:70, :], in_=qaug[h]), reads=[b_qaug], writes=[q_.b])
        S.dma("sp", "ldk%d" % sl, lambda e: e.dma_start(out=k_.t[64:70, :], in_=kaug[h]), reads=[b_kaug], writes=[k_.b])
        for part in range(4):
            c0 = part * 4096
            S.dma("sp", "ldk%d" % sl,
                  lambda e, c0=c0: e.dma_start(out=k_.t[0:64, c0:c0 + 4096], in_=kT[h * 64:(h + 1) * 64, c0:c0 + 4096]),
                  writes=[k_.b])

    def load_v(hp):
        v_ = vs[hp % 2]
        v3 = v_.t[:, :].rearrange("p (k c) -> p k c", c=130)
        src = vr[hp].rearrange("p (k c) -> p k c", c=128)
        for part in range(8):
            k0 = part * 16
            for hh in range(2):
                S.dma("sp", "ldv%d" % (hp % 2),
                      lambda e, k0=k0, hh=hh: e.dma_start(out=v3[:, k0:k0 + 16, 65 * hh:65 * hh + 64],
                                                          in_=src[:, k0:k0 + 16, 64 * hh:64 * hh + 64]),
                      writes=[v_.b])

    load_v(0)
    load_head(0)

    def do_head(h):
        sl = h % 2
        hp = h // 2
        odd = h % 2
        k_, q_ = kTs[sl], qs[sl]
        v_ = vs[hp % 2]
        m_ = mx[hp % 2]
        if h + 1 < n_heads:
            if (h + 1) % 2 == 0:
                load_v((h + 1) // 2)
            load_head(h + 1)
        items = []
        for J in range(4):
            nkb = 32 * J + 32
            for kb in range(nkb - 1, -1, -1):
                items.append((J, kb, nkb))
        st = {}

        def s1(it):
            J, kb, nkb = it
            r = kb - 32 * J
            c0 = 128 * (r // 8) if r >= 0 else 0
            z = cx.nxt(zp)
            p_ = cx.nxt(pb_)
            S.op("pe", lambda e: e.matmul(z.t[:, c0:512], k_.t[0:70, kb * 128:(kb + 1) * 128],
                                          q_.t[0:70, 512 * J + c0:512 * J + 512], start=True, stop=(r < 0)),
                 reads=[k_.b, q_.b], writes=[z.b])
            if r >= 0:
                i = r % 8
                S.op("pe", lambda e: e.matmul(z.t[:, c0:c0 + 128], idt.t[:], nm.t[:, i * 128:(i + 1) * 128],
                                              start=False, stop=True), reads=[idt.b, nm.b], writes=[z.b])
            S.op("act", lambda e: e.activation(p_.t[:, c0:512], z.t[:, c0:512], AF.Exp), reads=[z.b], writes=[p_.b])
            st[it] = (c0, p_)

        def s2(it):
            J, kb, nkb = it
            c0, p_ = st.pop(it)
            o_ = op_[J % 2]
            if odd:
                S.op("pe", lambda e: e.matmul(o_.t[:, c0:512], v_.t[:, kb * 130 + 1:kb * 130 + 129], p_.t[:, c0:512],
                                              start=(kb == nkb - 1), stop=(kb == 0)), reads=[v_.b, p_.b], writes=[o_.b])
            else:
                S.op("pe", lambda e: e.matmul(o_.t[0:65, c0:512], v_.t[:, kb * 130:kb * 130 + 65], p_.t[:, c0:512],
                                              start=(kb == nkb - 1), stop=(kb == 0)), reads=[v_.b, p_.b], writes=[o_.b])
            if kb == 0:
                ob = cx.nxt(osb)
                rb = cx.nxt(rbs)
                bc = bcp[0]
                if odd:
                    S.op("act", lambda e: e.copy(ob.t[32:64, :], o_.t[32:64, :]), reads=[o_.b], writes=[ob.b])
                    S.op("act", lambda e: e.copy(ob.t[64:128, :], o_.t[64:128, :]), reads=[o_.b], writes=[ob.b])
                    S.op("pe", lambda e: e.matmul(bc.t[:, :], selt.t[32:64, 128:256], ob.t[32:64, :], start=True, stop=True),
                         reads=[selt.b, ob.b], writes=[bc.b])
                    S.op("dve", lambda e: e.reciprocal(rb.t[64:128, :], bc.t[64:128, :]), reads=[bc.b], writes=[rb.b])
                    S.op("dve", lambda e: e.tensor_tensor(m_.t[64:128, 512 * J:512 * J + 512], ob.t[64:128, :], rb.t[64:128, :], ALU.mult),
                         reads=[ob.b, rb.b], writes=[m_.b])
                else:
                    S.op("act", lambda e: e.copy(ob.t[0:65, :], o_.t[0:65, :]), reads=[o_.b], writes=[ob.b])
                    S.op("pe", lambda e: e.matmul(bc.t[0:64, :], selt.t[64:65, 0:64], ob.t[64:65, :], start=True, stop=True),
                         reads=[selt.b, ob.b], writes=[bc.b])
                    S.op("dve", lambda e: e.reciprocal(rb.t[0:64, :], bc.t[0:64, :]), reads=[bc.b], writes=[rb.b])
                    S.op("dve", lambda e: e.tensor_tensor(m_.t[0:64, 512 * J:512 * J + 512], ob.t[0:64, :], rb.t[0:64, :], ALU.mult),
                         reads=[ob.b, rb.b], writes=[m_.b])

        n = len(items)
        for tau in range(n + 1):
            if tau < n:
                s1(items[tau])
            if tau - 1 >= 0:
                s2(items[tau - 1])
        if odd:
            cx.store("o_mix", mixT[hp * 128:(hp + 1) * 128, :], m_, m_.t[:])

    for h in range(n_heads):
        do_head(h)
    return cx.finish()


_PROGS = {}


def _prog(name, fn):
    if name not in _PROGS:
        _PROGS[name] = fn()
    return _PROGS[name]


def _shard_tok(a):
    F_ = a.shape[-1]
    r = a.reshape(16, 8, 128, F_)
    return [np.ascontiguousarray(r[:, c].reshape(TL, F_)) for c in range(NC_)]


def _gather_T(parts):
    R = parts[0].shape[0]
    out = np.zeros((R, 16, 8, 128), dtype=parts[0].dtype)
    for c in range(NC_):
        out[:, :, c, :] = parts[c].reshape(R, 16, 128)
    return out.reshape(R, S_ALL)


def _gather_v(parts):
    va = np.zeros((16, 8, 128, D), dtype=parts[0].dtype)
    for c in range(NC_):
        va[:, c] = parts[c].reshape(16, 128, D)
    va = va.reshape(128, 128, 8, 128)
    return np.ascontiguousarray(va.transpose(2, 1, 0, 3)).reshape(8, 128, 128 * 128)


def _consts():
    ar = np.arange(128)
    c = {}
    c["ident"] = np.eye(128, dtype=np.float32).astype(NPBF)
    c["tri"] = (ar[:, None] >= ar[None, :]).astype(np.float32).astype(NPBF)
    c["omt"] = (ar[:, None] < ar[None, :]).astype(np.float32).astype(NPBF)
    masks, negm, oh = [], [], []
    for cc in range(NC_):
        m = np.zeros((128, 8, 128), np.float32)
        n = np.full((128, 8, 128), -30000.0, np.float32)
        for i in range(8):
            if i < cc:
                m[:, i, :] = 1.0
                n[:, i, :] = 0.0
            elif i == cc:
                m[:, i, :] = (ar[:, None] < ar[None, :])
                n[:, i, :] = np.where(ar[:, None] <= ar[None, :], 0.0, -30000.0)
        masks.append(m.reshape(128, 1024).astype(NPBF))
        negm.append(n.reshape(128, 1024).astype(NPBF))
        o = np.zeros((NH, 8), np.float32)
        o[:, cc] = 1.0
        oh.append(o)
    c["masks"], c["negm"], c["onehot"] = masks, negm, oh
    sel = np.zeros((128, 256), np.float32)
    sel[64, 0:64] = 1.0
    sel[63, 192:256] = 1.0
    c["sel"] = sel
    return c


def _run(nc, in_maps):
    return run_bass_kernel_spmd(nc, in_maps, core_ids=list(range(NC_))).results


def kernel(x, p, attn_norm_g, sb_w_qkv, sb_w_o, shared_norm_g, shared_w_kvf, shared_b_f, shared_k_norm_g,
           fox_w_q, fox_q_norm_g, fox_w_o, ffn_norm_g, ffn_w_gu, ffn_w_d, ple_norm_g, ple_w_gate, ple_w_proj):
    f32 = lambda a: np.ascontiguousarray(np.asarray(a, dtype=np.float32))
    x = f32(x)
    p = f32(p)
    C = _consts()
    xs = _shard_tok(x[0])
    p0 = _shard_tok(p[0, 0])
    p1 = _shard_tok(p[1, 0])
    r1 = _run(_prog("pre0", build_pre0),
              [{"x": xs[c], "g": f32(attn_norm_g[0]), "w": f32(sb_w_qkv[0]), "ident": C["ident"]} for c in range(NC_)])
    kT_all = _gather_T([r["kT"] for r in r1])
    vr = _gather_v([r["v"] for r in r1])
    r2 = _run(_prog("attn0", build_attn0),
              [{"qT": r1[c]["qT"], "kT": kT_all, "vr": vr, "masks": C["masks"][c], "tri": C["tri"], "omt": C["omt"]}
               for c in range(NC_)])

    def post_in(c, h, mix, pl, w_o, li):
        return {"h": h, "mixT": mix, "p": pl, "w_o": f32(w_o), "ffn_g": f32(ffn_norm_g[li]), "w_gu": f32(ffn_w_gu[li]),
                "w_d": f32(ffn_w_d[li]), "ple_g": f32(ple_norm_g[li]), "w_gate": f32(ple_w_gate[li]),
                "w_proj": f32(ple_w_proj[li]), "ident": C["ident"]}

    in3 = []
    for c in range(NC_):
        d = post_in(c, xs[c], r2[c]["mixT"], p0[c], sb_w_o[0], 0)
        d.update({"a_g": f32(attn_norm_g[1]), "w_q": f32(fox_w_q[0]), "qn_g": f32(fox_q_norm_g[0]),
                  "sh_g": f32(shared_norm_g), "w_kvf": f32(shared_w_kvf), "b_f": f32(shared_b_f),
                  "kn_g": f32(shared_k_norm_g)})
        in3.append(d)
    r3 = _run(_prog("post_n", lambda: build_post(True)), in3)
    k1_all = _gather_T([r["k1T"] for r in r3])
    vr1 = _gather_v([r["v1"] for r in r3])
    fl_all = _gather_T([r["flogT"] for r in r3])
    r4 = _run(_prog("attn1", build_attn1),
              [{"qT": r3[c]["q1T"], "kT": k1_all, "vr": vr1, "flog": fl_all, "onehot": C["onehot"][c],
                "negm": C["negm"][c], "ident": C["ident"], "sel": C["sel"]} for c in range(NC_)])
    r5 = _run(_prog("post_l", lambda: build_post(False)),
              [post_in(c, r3[c]["hout"], r4[c]["mixT"], p1[c], fox_w_o[0], 1) for c in range(NC_)])
    out = np.zeros((16, 8, 128, D), np.float32)
    for c in range(NC_):
        out[:, c] = r5[c]["hout"].reshape(16, 128, D)
    return out.reshape(1, S_ALL, D)
```

```python
import numpy as np
import ml_dtypes
from contextlib import ExitStack
import concourse.bass as bass
import concourse.mybir as mybir
from concourse.bass_utils import run_bass_kernel_spmd

F32 = mybir.dt.float32
BF16 = mybir.dt.bfloat16
AF = mybir.ActivationFunctionType
ALU = mybir.AluOpType
NPBF = ml_dtypes.bfloat16

NC_ = 8
D = 1024
S_ALL = 16384
TL = 2048
NH = 16
DH = 64
DFF = 2816
PLE = 256
EPS = 1e-6
EPOCH = 4096


class Buf:
    __slots__ = ("name", "w", "r", "wm")

    def __init__(self, name, multi=False):
        self.name = name
        self.w = None
        self.r = []
        self.wm = {} if multi else None


class Tile:
    __slots__ = ("t", "b")

    def __init__(self, t, name):
        self.t = t
        self.b = Buf(name)


class Sched:
    ENG = ("pe", "act", "dve", "pool", "sp")

    def __init__(self, nc, es):
        self.nc = nc
        self.es = es
        self.lists = {e: [] for e in self.ENG}
        self.sems = {}
        self.cnt = {}
        self.alias = {}
        self.waited = {e: {} for e in self.ENG}
        self.ecount = {e: 0 for e in self.ENG}
        self.nsem = 0

    def _mksem(self, key):
        self.nsem += 1
        self.sems[key] = self.es.enter_context(self.nc.semaphore("s%d" % self.nsem))
        self.cnt[key] = 0

    def _deps(self, eng, reads, writes):
        deps = {}

        def add(tok):
            if tok is None:
                return
            k, v = tok
            if deps.get(k, 0) < v:
                deps[k] = v

        for b in reads:
            add(b.w)
            if b.wm is not None:
                for t in b.wm.items():
                    add(t)
        for b in writes:
            add(b.w)
            for t in b.r:
                add(t)
        out = []
        w = self.waited[eng]
        for k, v in deps.items():
            if eng == "pe" and k.startswith("E_pe#"):
                continue
            if w.get(k, 0) >= v:
                continue
            w[k] = v
            out.append((k, v))
        return out

    @staticmethod
    def _commit(tok, reads, writes):
        for b in writes:
            if b.wm is not None:
                if b.wm.get(tok[0], 0) < tok[1]:
                    b.wm[tok[0]] = tok[1]
                continue
            b.w = tok
            b.r = []
        for b in reads:
            b.r.append(tok)

    def op(self, eng, fn, reads=(), writes=()):
        deps = self._deps(eng, reads, writes)
        ep = self.ecount[eng] // EPOCH
        key = "E_%s#%d" % (eng, ep)
        if key not in self.sems:
            self._mksem(key)
        self.ecount[eng] += 1
        self.cnt[key] += 1
        tok = (key, self.cnt[key])
        sems = self.sems

        def thunk(e, deps=deps, fn=fn, key=key):
            for k, v in deps:
                e.wait_ge(sems[k], v)
            fn(e).then_inc(sems[key], 1)

        self.lists[eng].append(thunk)
        self._commit(tok, reads, writes)
        return tok

    def dma(self, eng, semname, fn, reads=(), writes=()):
        deps = self._deps(eng, reads, writes)
        key = self.alias.get(semname)
        if key is None or self.cnt[key] >= 16 * 240:
            n = 0 if key is None else int(key.split("#")[1]) + 1
            key = "D_%s#%d" % (semname, n)
            self.alias[semname] = key
            self._mksem(key)
        self.cnt[key] += 16
        tok = (key, self.cnt[key])
        sems = self.sems

        def thunk(e, deps=deps, fn=fn, key=key):
            for k, v in deps:
                e.wait_ge(sems[k], v)
            fn(e).then_inc(sems[key], 16)

        self.lists[eng].append(thunk)
        self._commit(tok, reads, writes)
        return tok

    def wait_all(self, eng, toks):
        sems = self.sems
        toks = list(toks)

        def thunk(e):
            for k, v in toks:
                e.wait_ge(sems[k], v)

        self.lists[eng].append(thunk)

    def emit(self):
        L = self.lists
        with self.nc.Block() as block:
            @block.tensor
            def _(e):
                for t in L["pe"]:
                    t(e)

            @block.scalar
            def _(e):
                for t in L["act"]:
                    t(e)

            @block.vector
            def _(e):
                for t in L["dve"]:
                    t(e)

            @block.gpsimd
            def _(e):
                for t in L["pool"]:
                    t(e)

            @block.sync
            def _(e):
                for t in L["sp"]:
                    t(e)


class Cx:
    def __init__(self):
        self.nc = bass.Bass("TRN2", target_bir_lowering=False)
        self.es = ExitStack()
        self.S = Sched(self.nc, self.es)
        self.out_toks = {}
        self.rr = {}

    def din(self, name, shape, dt):
        return self.nc.dram_tensor(name, list(shape), dt, kind="ExternalInput").ap()

    def dout(self, name, shape, dt):
        return self.nc.dram_tensor(name, list(shape), dt, kind="ExternalOutput").ap()

    def dint(self, name, shape, dt):
        return self.nc.dram_tensor(name, list(shape), dt, kind="Internal").ap()

    def sb(self, name, shape, dt, n=None):
        if n is None:
            return Tile(self.es.enter_context(self.nc.sbuf_tensor("sb_" + name, list(shape), dt)), name)
        return [Tile(self.es.enter_context(self.nc.sbuf_tensor("sb_%s%d" % (name, i), list(shape), dt)),
                     "%s%d" % (name, i)) for i in range(n)]

    def ps(self, name, shape, dt, n=None):
        if n is None:
            return Tile(self.es.enter_context(self.nc.psum_tensor("ps_" + name, list(shape), dt)), name)
        return [Tile(self.es.enter_context(self.nc.psum_tensor("ps_%s%d" % (name, i), list(shape), dt)),
                     "%s%d" % (name, i)) for i in range(n)]

    def nxt(self, lst, key=None):
        key = key or id(lst)
        i = self.rr.get(key, 0)
        self.rr[key] = i + 1
        return lst[i % len(lst)]

    def store(self, semname, out_ap, tile, in_ap, eng="pool"):
        tok = self.S.dma(eng, "st_" + tile.b.name, lambda e: e.dma_start(out=out_ap, in_=in_ap), reads=[tile.b])
        self.out_toks[tok[0]] = max(self.out_toks.get(tok[0], 0), tok[1])

    def finish(self):
        self.S.wait_all("sp", list(self.out_toks.items()))
        self.S.emit()
        self.es.close()
        return self.nc


class Dense:
    def __init__(self, cx, ident_ap):
        self.cx = cx
        S = cx.S
        self.ident = cx.sb("ident", [128, 128], BF16)
        S.dma("sp", "const1", lambda e: e.dma_start(out=self.ident.t[:], in_=ident_ap), writes=[self.ident.b])
        self.junk = cx.sb("junk", [128, 1024], BF16, n=2)
        self.ss = cx.sb("ss", [128, 1], F32, n=4)
        self.lnv = cx.sb("lnv", [128, 1], F32, n=4)
        self.rstd = cx.sb("rstd", [128, 1], F32, n=4)
        self.hn = cx.sb("hn", [128, 1024], BF16, n=2)
        self.psT = cx.ps("psT", [128, 1024], BF16, n=2)
        self.wst = cx.sb("wst", [128, 512], F32, n=4)
        self.gv = {}
        self.evq = 0

    def load_gain(self, name, g_ap):
        cx = self.cx
        t = cx.sb("g_" + name, [128, 8], F32)
        cx.S.dma("sp", "cg_" + name, lambda e: e.dma_start(out=t.t[:], in_=g_ap.rearrange("(k p) -> p k", p=128),
                                                      allow_slow_non_contiguous=True), writes=[t.b])
        self.gv[name] = t
        return t

    def prep_weight(self, w_ap, K, N, dst_fn, gain=None, post=None, c0=0, c1=None):
        cx = self.cx
        S = cx.S
        c1 = N if c1 is None else c1
        for kc in range(K // 128):
            for n0 in range(c0, c1, 512):
                wd = min(512, c1 - n0)
                st = cx.nxt(self.wst)
                S.dma("sp", "wst_" + st.b.name,
                      lambda e, st=st, kc=kc, n0=n0, wd=wd: e.dma_start(
                          out=st.t[:, 0:wd], in_=w_ap[kc * 128:(kc + 1) * 128, n0:n0 + wd]),
                      writes=[st.b])
                dt_, dap = dst_fn(kc, n0, wd)
                eng = "pool" if (self.evq % 2 == 0) else "dve"
                self.evq += 1
                pv = post(n0) if post is not None else None
                if gain is not None:
                    g = gain
                    if pv is not None:
                        fn = lambda e, st=st, dap=dap, kc=kc, wd=wd, g=g, pv=pv: e.tensor_scalar(
                            dap, st.t[:, 0:wd], g.t[:, kc:kc + 1], pv, ALU.mult, ALU.mult)
                    else:
                        fn = lambda e, st=st, dap=dap, kc=kc, wd=wd, g=g: e.tensor_scalar(
                            dap, st.t[:, 0:wd], g.t[:, kc:kc + 1], None, ALU.mult)
                    S.op(eng, fn, reads=[st.b, g.b], writes=[dt_.b])
                else:
                    S.op(eng, lambda e, st=st, dap=dap, wd=wd: e.tensor_copy(dap, st.t[:, 0:wd]),
                         reads=[st.b], writes=[dt_.b])

    def norm_T(self, h, hnT, col0):
        cx = self.cx
        S = cx.S
        junk = cx.nxt(self.junk)
        ss = cx.nxt(self.ss)
        lnv = cx.nxt(self.lnv)
        rstd = cx.nxt(self.rstd)
        hn = cx.nxt(self.hn)
        pT = cx.nxt(self.psT)
        S.op("act", lambda e: e.activation(junk.t[:], h.t[:], AF.Square, accum_out=ss.t[:]),
             reads=[h.b], writes=[junk.b, ss.b])
        S.op("act", lambda e: e.activation(lnv.t[:], ss.t[:], AF.Ln, scale=1.0 / D, bias=self.eps.t[:, 0:1]),
             reads=[ss.b, self.eps.b], writes=[lnv.b])
        S.op("act", lambda e: e.activation(rstd.t[:], lnv.t[:], AF.Exp, scale=-0.5),
             reads=[lnv.b], writes=[rstd.b])
        S.op("dve", lambda e: e.tensor_scalar(hn.t[:], h.t[:], rstd.t[:, 0:1], None, ALU.mult),
             reads=[h.b, rstd.b], writes=[hn.b])
        for kc in range(8):
            S.op("pe", lambda e, kc=kc: e.transpose(pT.t[:, kc * 128:(kc + 1) * 128],
                                                    hn.t[:, kc * 128:(kc + 1) * 128], self.ident.t[:]),
                 reads=[hn.b, self.ident.b], writes=[pT.b])
        S.op("dve", lambda e: e.tensor_copy(hnT.t[:, :, col0:col0 + 128],
                                            pT.t[:, :].rearrange("p (k t) -> p k t", k=8)),
             reads=[pT.b], writes=[hnT.b])

    def consts(self):
        cx = self.cx
        self.eps = cx.sb("epsc", [128, 1], F32)
        cx.S.op("pool", lambda e: e.memset(self.eps.t[:], EPS), writes=[self.eps.b])


def build_pre0():
    cx = Cx()
    S = cx.S
    x = cx.din("x", [TL, D], F32)
    g = cx.din("g", [D], F32)
    w = cx.din("w", [D, 3 * D], F32)
    ident = cx.din("ident", [128, 128], BF16)
    qT = cx.dout("qT", [D, TL], BF16)
    kT = cx.dout("kT", [D, TL], BF16)
    v = cx.dout("v", [TL, D], BF16)
    dn = Dense(cx, ident)
    dn.consts()
    gt = dn.load_gain("a", g)
    Wb = cx.sb("Wb", [128, 8, 3 * D], BF16)
    dn.prep_weight(w, D, 3 * D, lambda kc, n0, wd: (Wb, Wb.t[:, kc, n0:n0 + wd]), gain=gt,
                   post=lambda n0: (0.125 if n0 < D else 1.0))
    hblk = cx.sb("hblk", [128, D], F32, n=3)
    hnT = cx.sb("hnT", [128, 8, 512], BF16, n=2)
    pm = cx.ps("pm", [128, 512], F32, n=4)
    ost = cx.sb("ost", [128, 512], BF16, n=4)
    ev = 0
    for gi in range(4):
        hT = cx.nxt(hnT)
        for b in range(4):
            h = cx.nxt(hblk)
            r0 = (gi * 4 + b) * 128
            S.dma("sp", "ld_" + h.b.name, lambda e, h=h, r0=r0: e.dma_start(out=h.t[:], in_=x[r0:r0 + 128, :]),
                  writes=[h.b])
            dn.norm_T(h, hT, b * 128)
        for n in range(16):
            p = cx.nxt(pm)
            for kc in range(8):
                S.op("pe", lambda e, p=p, kc=kc, n=n, hT=hT: e.matmul(
                    p.t[:], Wb.t[:, kc, n * 128:(n + 1) * 128], hT.t[:, kc, :], start=(kc == 0), stop=(kc == 7)),
                    reads=[Wb.b, hT.b], writes=[p.b])
            o = cx.nxt(ost)
            if ev % 2 == 0:
                S.op("act", lambda e, o=o, p=p: e.copy(o.t[:], p.t[:]), reads=[p.b], writes=[o.b])
            else:
                S.op("dve", lambda e, o=o, p=p: e.tensor_copy(o.t[:], p.t[:]), reads=[p.b], writes=[o.b])
            ev += 1
            dst = qT if n < 8 else kT
            rr = (n % 8) * 128
            cx.store("o_qk", dst[rr:rr + 128, gi * 512:(gi + 1) * 512], o, o.t[:])
        for b in range(4):
            for hf in range(2):
                p = cx.nxt(pm)
                for kc in range(8):
                    S.op("pe", lambda e, p=p, kc=kc, b=b, hf=hf, hT=hT: e.matmul(
                        p.t[:], hT.t[:, kc, b * 128:(b + 1) * 128],
                        Wb.t[:, kc, 2 * D + hf * 512:2 * D + (hf + 1) * 512], start=(kc == 0), stop=(kc == 7)),
                        reads=[Wb.b, hT.b], writes=[p.b])
                o = cx.nxt(ost)
                if ev % 2 == 0:
                    S.op("act", lambda e, o=o, p=p: e.copy(o.t[:], p.t[:]), reads=[p.b], writes=[o.b])
                else:
                    S.op("dve", lambda e, o=o, p=p: e.tensor_copy(o.t[:], p.t[:]), reads=[p.b], writes=[o.b])
                ev += 1
                r0 = (gi * 4 + b) * 128
                cx.store("o_v", v[r0:r0 + 128, hf * 512:(hf + 1) * 512], o, o.t[:])
    return cx.finish()


def build_attn0(n_pairs=8):
    cx = Cx()
    S = cx.S
    qT = cx.din("qT", [D, TL], BF16)
    kT = cx.din("kT", [D, S_ALL], BF16)
    vr = cx.din("vr", [8, 128, 128 * 128], BF16)
    masks = cx.din("masks", [128, 8 * 128], BF16)
    tri = cx.din("tri", [128, 128], BF16)
    omt = cx.din("omt", [128, 128], BF16)
    mixT = cx.dout("mixT", [D, TL], BF16)

    mk = cx.sb("mk", [128, 8 * 128], BF16)
    trt = cx.sb("trt", [128, 128], BF16)
    omtt = cx.sb("omtt", [128, 128], BF16)
    S.dma("sp", "const3", lambda e: e.dma_start(out=mk.t[:], in_=masks), writes=[mk.b])
    S.dma("sp", "const4", lambda e: e.dma_start(out=trt.t[:], in_=tri), writes=[trt.b])
    S.dma("sp", "const5", lambda e: e.dma_start(out=omtt.t[:], in_=omt), writes=[omtt.b])

    one = cx.sb("one", [128, 1], F32)
    S.op("pool", lambda e: e.memset(one.t[:], 1.0), writes=[one.b])
    kTs = cx.sb("kTs", [128, S_ALL], BF16, n=2)
    vs = cx.sb("vs", [128, 128 * 128], BF16, n=2)
    qs = cx.sb("qs", [128, TL], BF16, n=2)
    mx = cx.sb("mx", [128, TL], BF16, n=2)
    zp = cx.ps("zp", [128, 512], F32, n=3)
    Cp = cx.ps("Cp", [128, 512], F32, n=2)
    op_ = cx.ps("op", [128, 512], F32, n=2)
    eb = cx.sb("eb", [128, 512], F32, n=4)
    spb = cx.sb("spb", [128, 512], BF16, n=4)
    xb = cx.sb("xb", [128, 512], BF16, n=3)
    wb = cx.sb("wb", [128, 512], BF16, n=3)

    def load_pair(hp):
        sl = hp % 2
        k_, v_, q_ = kTs[sl], vs[sl], qs[sl]
        S.dma("sp", "ldq%d" % sl, lambda e: e.dma_start(out=q_.t[:], in_=qT[hp * 128:(hp + 1) * 128, :]),
              writes=[q_.b])
        for part in range(4):
            c0 = part * 4096
            S.dma("sp", "ldk%d" % sl,
                  lambda e, c0=c0: e.dma_start(out=k_.t[:, c0:c0 + 4096], in_=kT[hp * 128:(hp + 1) * 128, c0:c0 + 4096]),
                  writes=[k_.b])
            S.dma("sp", "ldv%d" % sl,
                  lambda e, c0=c0: e.dma_start(out=v_.t[:, c0:c0 + 4096], in_=vr[hp, :, c0:c0 + 4096]),
                  writes=[v_.b])

    load_pair(0)

    def do_pair(hp):
        sl = hp % 2
        k_, v_, q_, m_ = kTs[sl], vs[sl], qs[sl], mx[sl]
        if hp + 1 < n_pairs:
            load_pair(hp + 1)
        items = []
        for J in range(4):
            nkb = 32 * J + 32
            for kb in range(nkb - 1, -1, -1):
                for hh in range(2):
                    items.append((J, kb, hh, nkb))
        st = {}

        def s1(it):
            J, kb, hh, nkb = it
            r = kb - 32 * J
            c0 = 128 * (r // 8) if r >= 0 else 0
            z = cx.nxt(zp)
            e_ = cx.nxt(eb)
            sp_ = cx.nxt(spb)
            pb = 64 * hh
            S.op("pe", lambda e: e.matmul(z.t[:, c0:512], k_.t[pb:pb + 64, kb * 128:(kb + 1) * 128],
                                          q_.t[pb:pb + 64, 512 * J + c0:512 * J + 512], start=True, stop=True),
                 reads=[k_.b, q_.b], writes=[z.b])
            S.op("act", lambda e: e.activation(e_.t[:, c0:512], z.t[:, c0:512], AF.Exp),
                 reads=[z.b], writes=[e_.b])
            if r >= 0:
                i = r % 8
                S.op("pool", lambda e: e.tensor_tensor(e_.t[:, c0:c0 + 128], e_.t[:, c0:c0 + 128],
                                                       mk.t[:, i * 128:(i + 1) * 128], ALU.mult),
                     reads=[e_.b, mk.b], writes=[e_.b])
            st[it] = [c0, e_, sp_, None, None]

        def s1b(it):
            c0, e_, sp_, _, _ = st[it]
            S.op("act", lambda e: e.activation(sp_.t[:, c0:512], e_.t[:, c0:512], AF.Ln, bias=one.t[:, 0:1]),
                 reads=[e_.b, one.b], writes=[sp_.b])

        def s2(it):
            J, kb, hh, nkb = it
            c0, e_, sp_, _, _ = st[it]
            C = Cp[hh]
            x_ = cx.nxt(xb)
            S.op("pe", lambda e: e.matmul(C.t[:, c0:512], trt.t[:], sp_.t[:, c0:512],
                                          start=(kb == nkb - 1), stop=(kb == 0)),
                 reads=[trt.b, sp_.b], writes=[C.b])
            S.op("act", lambda e: e.activation(x_.t[:, c0:512], C.t[:, c0:512], AF.Exp, scale=-1.0),
                 reads=[C.b], writes=[x_.b])
            st[it][3] = x_

        def s3(it):
            J, kb, hh, nkb = it
            c0, e_, sp_, x_, _ = st[it]
            C = Cp[hh]
            w_ = cx.nxt(wb)
            if kb > 0:
                S.op("pe", lambda e: e.matmul(C.t[:, c0:512], omtt.t[:], sp_.t[:, c0:512], start=False, stop=False),
                     reads=[omtt.b, sp_.b], writes=[C.b])
            S.op("dve", lambda e: e.tensor_tensor(w_.t[:, c0:512], e_.t[:, c0:512], x_.t[:, c0:512], ALU.mult),
                 reads=[e_.b, x_.b], writes=[w_.b])
            st[it][4] = w_

        def s4(it):
            J, kb, hh, nkb = it
            c0, e_, sp_, x_, w_ = st.pop(it)
            o_ = op_[J % 2]
            pb = 64 * hh
            S.op("pe", lambda e: e.matmul(o_.t[pb:pb + 64, c0:512], v_.t[:, kb * 128 + pb:kb * 128 + pb + 64],
                                          w_.t[:, c0:512], start=(kb == nkb - 1), stop=(kb == 0)),
                 reads=[v_.b, w_.b], writes=[o_.b])
            if kb == 0:
                S.op("act", lambda e: e.copy(m_.t[pb:pb + 64, 512 * J:512 * J + 512], o_.t[pb:pb + 64, :]),
                     reads=[o_.b], writes=[m_.b])

        n = len(items)
        for tau in range(n + 3):
            if tau < n:
                s1(items[tau])
            if 0 <= tau - 1 < n:
                s2(items[tau - 1])
            if tau < n:
                s1b(items[tau])
            if 0 <= tau - 2 < n:
                s3(items[tau - 2])
            if 0 <= tau - 3 < n:
                s4(items[tau - 3])
        cx.store("o_mix", mixT[hp * 128:(hp + 1) * 128, :], m_, m_.t[:])

    for hp in range(n_pairs):
        do_pair(hp)
    return cx.finish()


def build_post(nxt):
    cx = Cx()
    S = cx.S
    h_in = cx.din("h", [TL, D], F32)
    mixT = cx.din("mixT", [D, TL], BF16)
    p_in = cx.din("p", [TL, PLE], F32)
    w_o = cx.din("w_o", [D, D], F32)
    ffn_g = cx.din("ffn_g", [D], F32)
    w_gu = cx.din("w_gu", [D, 2 * DFF], F32)
    w_d = cx.din("w_d", [DFF, D], F32)
    ple_g = cx.din("ple_g", [D], F32)
    w_gate = cx.din("w_gate", [D, D], F32)
    w_proj = cx.din("w_proj", [PLE, D], F32)
    ident = cx.din("ident", [128, 128], BF16)
    hout = cx.dout("hout", [TL, D], F32)
    if nxt:
        a_g = cx.din("a_g", [D], F32)
        w_q = cx.din("w_q", [D, D], F32)
        qn_g = cx.din("qn_g", [DH], F32)
        sh_g = cx.din("sh_g", [D], F32)
        w_kvf = cx.din("w_kvf", [D, 2 * D + NH], F32)
        b_f = cx.din("b_f", [NH], F32)
        kn_g = cx.din("kn_g", [DH], F32)
        q1T = cx.dout("q1T", [D, TL], BF16)
        k1T = cx.dout("k1T", [D, TL], BF16)
        v1 = cx.dout("v1", [TL, D], BF16)
        flogT = cx.dout("flogT", [NH, TL], F32)

    dn = Dense(cx, ident)
    dn.consts()
    g_ffn = dn.load_gain("ffn", ffn_g)
    g_ple = dn.load_gain("ple", ple_g)

    wbf = cx.sb("wbf", [128, 512], BF16, n=4)

    def to_scratch(name, dst_ap_fn):
        buf = Buf("scr_" + name)

        def dst_fn(kc, n0, wd):
            t = cx.nxt(wbf)
            return t, t.t[:, 0:wd]
        return buf, dst_fn

    def prep_dram(name, w_ap, K, N, dst_ap_fn, gain=None, c0=0, c1=None):
        buf = Buf("scr_" + name, multi=True)
        c1_ = N if c1 is None else c1
        for kc in range(K // 128):
            for n0 in range(c0, c1_, 512):
                wd = min(512, c1_ - n0)
                holder = {}

                def dst_fn(kc_, n0_, wd_, holder=holder):
                    t = cx.nxt(wbf)
                    holder["t"] = t
                    return t, t.t[:, 0:wd_]
                dn.prep_weight(w_ap[kc * 128:(kc + 1) * 128, :], 128, N, dst_fn, gain=None if gain is None else _GainCol(gain, kc),
                               c0=n0, c1=n0 + wd)
                t = holder["t"]
                dap = dst_ap_fn(kc, n0 - c0, wd)
                S.dma("pool", "wp_" + t.b.name, lambda e, t=t, dap=dap, wd=wd: e.dma_start(out=dap, in_=t.t[:, 0:wd]),
                      reads=[t.b], writes=[buf])
        return buf

    class _GainCol:
        def __init__(self, g, kc):
            self.b = g.b
            self.t = _Shift(g.t, kc)

    class _Shift:
        def __init__(self, t, kc):
            self._t = t
            self._kc = kc

        def __getitem__(self, idx):
            return self._t[idx[0], self._kc:self._kc + 1]

    def scr8(name, N):
        return cx.dint("scr_" + name, [N // 512, 128, 8, 512], BF16)

    WoB = scr8("wo", D)
    b_wo = prep_dram("wo", w_o, D, D, lambda kc, n0, wd: WoB[n0 // 512, :, kc, :])
    WguB = cx.dint("scr_wgu", [22, 128, 8, 256], BF16)

    def gu_dst(off):
        def f(kc, n0, wd):
            j0 = n0 // 128
            return WguB[j0:j0 + wd // 128, :, kc, off:off + 128].rearrange("j p i -> p j i")
        return f
    b_wg = prep_dram("wg", w_gu, D, 2 * DFF, gu_dst(0), gain=g_ffn, c0=0, c1=DFF)
    b_wu = prep_dram("wu", w_gu, D, 2 * DFF, gu_dst(128), gain=g_ffn, c0=DFF, c1=2 * DFF)
    WdB = cx.dint("scr_wd", [4, 128, 22, 256], BF16)
    b_wd = prep_dram("wd", w_d, DFF, D,
                     lambda kc, n0, wd: WdB[n0 // 256:n0 // 256 + 2, :, kc, :].rearrange("q p i -> p q i"))
    WgateB = scr8("wgate", D)
    b_wgate = prep_dram("wgate", w_gate, D, D, lambda kc, n0, wd: WgateB[n0 // 512, :, kc, :], gain=g_ple)
    Wproj = cx.sb("Wproj", [128, 2, D], BF16)
    dn.prep_weight(w_proj, PLE, D, lambda kc, n0, wd: (Wproj, Wproj.t[:, kc, n0:n0 + wd]))
    if nxt:
        g_a = dn.load_gain("a1", a_g)
        g_sh = dn.load_gain("sh", sh_g)
        WqB = scr8("wq", D)
        b_wq = prep_dram("wq", w_q, D, D, lambda kc, n0, wd: WqB[n0 // 512, :, kc, :], gain=g_a)
        WkvB = scr8("wkv", 2 * D)
        b_wkv = prep_dram("wkv", w_kvf, D, 2 * D + NH, lambda kc, n0, wd: WkvB[n0 // 512, :, kc, :], gain=g_sh,
                          c0=0, c1=2 * D)
        Wf = cx.sb("Wf", [128, 8, NH], BF16)
        dn.prep_weight(w_kvf, D, 2 * D + NH, lambda kc, n0, wd: (Wf, Wf.t[:, kc, 0:wd]), gain=g_sh,
                       c0=2 * D, c1=2 * D + NH)
        qg = cx.sb("qg", [128, 8, DH], F32)
        kg = cx.sb("kg", [128, 8, DH], F32)
        S.dma("sp", "const6", lambda e: e.dma_start(out=qg.t[:], in_=qn_g.unsqueeze(0).unsqueeze(0).to_broadcast([128, 8, DH])),
              writes=[qg.b])
        S.dma("sp", "const7", lambda e: e.dma_start(out=kg.t[:], in_=kn_g.unsqueeze(0).unsqueeze(0).to_broadcast([128, 8, DH])),
              writes=[kg.b])
        S.op("dve", lambda e: e.tensor_scalar(qg.t[:], qg.t[:], 0.125, None, ALU.mult), reads=[qg.b], writes=[qg.b])
        nbf = cx.sb("nbf", [NH, 1], F32)
        S.dma("sp", "const8", lambda e: e.dma_start(out=nbf.t[:], in_=b_f.rearrange("(h o) -> h o", o=1)), writes=[nbf.b])
        S.op("dve", lambda e: e.tensor_scalar(nbf.t[:], nbf.t[:], -1.0, None, ALU.mult), reads=[nbf.b], writes=[nbf.b])
        one = cx.sb("one", [128, 1], F32)
        S.op("pool", lambda e: e.memset(one.t[:], 1.0), writes=[one.b])

    hres = cx.sb("hres", [128, D], F32, n=8)
    mTs = cx.sb("mT", [128, 8, 512], BF16, n=1)
    wt8 = cx.sb("wt8", [128, 8, 512], BF16, n=3)
    wgut = cx.sb("wgut", [128, 8, 256], BF16, n=4)
    wdt = cx.sb("wdt", [128, 22, 256], BF16, n=2)
    hnTs = cx.sb("hnT", [128, 8, 512], BF16, n=2)
    aT = cx.sb("aT", [128, 22, 512], BF16)
    sgs = cx.sb("sg", [128, 512], F32, n=2)
    tmps = cx.sb("tmp", [128, 512], F32, n=2)
    pblk = cx.sb("pblk", [128, PLE], F32, n=2)
    pbf = cx.sb("pbf", [128, PLE], BF16, n=2)
    pTs = cx.sb("pT", [128, 2, 128], BF16, n=2)
    pm = cx.ps("pm", [128, 512], F32, n=4)
    if nxt:
        hd8 = cx.sb("hd8", [128, 8], F32, n=4)
        qnb = cx.sb("qnb", [128, 512], BF16, n=2)
        oT = cx.sb("oT", [128, 512], BF16, n=2)
        fl = cx.sb("fl", [NH, 512], F32, n=2)

    def load_w8(scr, buf, hf, name):
        t = cx.nxt(wt8)
        S.dma("sp", "ld_" + t.b.name, lambda e: e.dma_start(out=t.t[:], in_=scr[hf]), reads=[buf], writes=[t.b])
        return t

    def add_res(h, c0, wd, src_tile, src_ap, flip=[0]):
        S.op("dve", lambda e: e.tensor_tensor(h.t[:, c0:c0 + wd], h.t[:, c0:c0 + wd], src_ap, ALU.add),
             reads=[h.b, src_tile.b], writes=[h.b])

    for gi in range(4):
        t0 = gi * 512
        hb = []
        for b in range(4):
            h = cx.nxt(hres)
            r0 = t0 + b * 128
            S.dma("pool", "ld_" + h.b.name, lambda e, h=h, r0=r0: e.dma_start(out=h.t[:], in_=h_in[r0:r0 + 128, :]),
                  writes=[h.b])
            hb.append(h)
        mT = cx.nxt(mTs)
        S.dma("pool", "ld_mT", lambda e, mT=mT, t0=t0: e.dma_start(
            out=mT.t[:], in_=mixT[:, t0:t0 + 512].rearrange("(c p) t -> p c t", p=128)), writes=[mT.b])
        for hf in range(2):
            wt = load_w8(WoB, b_wo, hf, "wo")
            for b in range(4):
                p = cx.nxt(pm)
                for kc in range(8):
                    S.op("pe", lambda e, p=p, kc=kc, b=b, wt=wt, mT=mT: e.matmul(
                        p.t[:], mT.t[:, kc, b * 128:(b + 1) * 128], wt.t[:, kc, :], start=(kc == 0), stop=(kc == 7)),
                        reads=[mT.b, wt.b], writes=[p.b])
                add_res(hb[b], hf * 512, 512, p, p.t[:])
        hT = cx.nxt(hnTs)
        for b in range(4):
            dn.norm_T(hb[b], hT, b * 128)
        for j in range(22):
            wg = cx.nxt(wgut)
            S.dma("sp", "ld_" + wg.b.name, lambda e, wg=wg, j=j: e.dma_start(out=wg.t[:], in_=WguB[j]),
                  reads=[b_wg, b_wu], writes=[wg.b])
            pg = cx.nxt(pm)
            pu = cx.nxt(pm)
            for kc in range(8):
                S.op("pe", lambda e, pg=pg, kc=kc, wg=wg, hT=hT: e.matmul(
                    pg.t[:], wg.t[:, kc, 0:128], hT.t[:, kc, :], start=(kc == 0), stop=(kc == 7)),
                    reads=[wg.b, hT.b], writes=[pg.b])
            for kc in range(8):
                S.op("pe", lambda e, pu=pu, kc=kc, wg=wg, hT=hT: e.matmul(
                    pu.t[:], wg.t[:, kc, 128:256], hT.t[:, kc, :], start=(kc == 0), stop=(kc == 7)),
                    reads=[wg.b, hT.b], writes=[pu.b])
            sg = cx.nxt(sgs)
            S.op("act", lambda e, sg=sg, pg=pg: e.activation(sg.t[:], pg.t[:], AF.Silu), reads=[pg.b], writes=[sg.b])
            S.op("dve", lambda e, sg=sg, pu=pu, j=j: e.tensor_tensor(aT.t[:, j, :], sg.t[:], pu.t[:], ALU.mult),
                 reads=[sg.b, pu.b], writes=[aT.b])
        for qd in range(4):
            wd_ = cx.nxt(wdt)
            S.dma("sp", "ld_" + wd_.b.name, lambda e, wd_=wd_, qd=qd: e.dma_start(out=wd_.t[:], in_=WdB[qd]),
                  reads=[b_wd], writes=[wd_.b])
            for b in range(4):
                p = cx.nxt(pm)
                for j in range(22):
                    S.op("pe", lambda e, p=p, j=j, b=b, wd_=wd_: e.matmul(
                        p.t[:, 0:256], aT.t[:, j, b * 128:(b + 1) * 128], wd_.t[:, j, :], start=(j == 0), stop=(j == 21)),
                        reads=[aT.b, wd_.b], writes=[p.b])
                add_res(hb[b], qd * 256, 256, p, p.t[:, 0:256])
        hT = cx.nxt(hnTs)
        for b in range(4):
            dn.norm_T(hb[b], hT, b * 128)
        for hf in range(2):
            wt = load_w8(WgateB, b_wgate, hf, "wgate")
            for b in range(4):
                pb_ = cx.nxt(pblk)
                r0 = t0 + b * 128
                S.dma("pool", "ld_" + pb_.b.name, lambda e, pb_=pb_, r0=r0: e.dma_start(out=pb_.t[:], in_=p_in[r0:r0 + 128, :]),
                      writes=[pb_.b])
                pf = cx.nxt(pbf)
                S.op("pool", lambda e, pf=pf, pb_=pb_: e.tensor_copy(pf.t[:], pb_.t[:]), reads=[pb_.b], writes=[pf.b])
                pT_ps = cx.nxt(dn.psT)
                for k2 in range(2):
                    S.op("pe", lambda e, k2=k2, pT_ps=pT_ps, pf=pf: e.transpose(
                        pT_ps.t[:, k2 * 128:(k2 + 1) * 128], pf.t[:, k2 * 128:(k2 + 1) * 128], dn.ident.t[:]),
                        reads=[pf.b, dn.ident.b], writes=[pT_ps.b])
                pT = cx.nxt(pTs)
                S.op("act", lambda e, pT=pT, pT_ps=pT_ps: e.copy(pT.t[:, :, :], pT_ps.t[:, 0:256].rearrange("p (k t) -> p k t", k=2)),
                     reads=[pT_ps.b], writes=[pT.b])
                pgate = cx.nxt(pm)
                for kc in range(8):
                    S.op("pe", lambda e, pgate=pgate, kc=kc, b=b, wt=wt, hT=hT: e.matmul(
                        pgate.t[:], hT.t[:, kc, b * 128:(b + 1) * 128], wt.t[:, kc, :], start=(kc == 0), stop=(kc == 7)),
                        reads=[hT.b, wt.b], writes=[pgate.b])
                pproj = cx.nxt(pm)
                for k2 in range(2):
                    S.op("pe", lambda e, pproj=pproj, k2=k2, pT=pT, hf=hf: e.matmul(
                        pproj.t[:], pT.t[:, k2, :], Wproj.t[:, k2, hf * 512:(hf + 1) * 512], start=(k2 == 0), stop=(k2 == 1)),
                        reads=[pT.b, Wproj.b], writes=[pproj.b])
                sg = cx.nxt(sgs)
                S.op("act", lambda e, sg=sg, pgate=pgate: e.activation(sg.t[:], pgate.t[:], AF.Sigmoid),
                     reads=[pgate.b], writes=[sg.b])
                tmp = cx.nxt(tmps)
                S.op("dve", lambda e, tmp=tmp, sg=sg, pproj=pproj: e.tensor_tensor(tmp.t[:], sg.t[:], pproj.t[:], ALU.mult),
                     reads=[sg.b, pproj.b], writes=[tmp.b])
                hh_ = hb[b]
                S.op("pool", lambda e, hh_=hh_, tmp=tmp, hf=hf: e.tensor_tensor(
                    hh_.t[:, hf * 512:(hf + 1) * 512], hh_.t[:, hf * 512:(hf + 1) * 512], tmp.t[:], ALU.add),
                    reads=[hh_.b, tmp.b], writes=[hh_.b])
        for b in range(4):
            r0 = t0 + b * 128
            cx.store("o_h", hout[r0:r0 + 128, :], hb[b], hb[b].t[:])
        if not nxt:
            continue
        hT = cx.nxt(hnTs)
        for b in range(4):
            dn.norm_T(hb[b], hT, b * 128)

        def head_norm_store(p, gtile, dstT, b, hf):
            sq = cx.nxt(tmps)
            S.op("act", lambda e: e.activation(sq.t[:], p.t[:], AF.Square), reads=[p.b], writes=[sq.b])
            s8 = cx.nxt(hd8)
            S.op("dve", lambda e: e.tensor_reduce(s8.t[:], sq.t[:, :].rearrange("p (h d) -> p h d", d=DH),
                                                  mybir.AxisListType.X, ALU.add), reads=[sq.b], writes=[s8.b])
            l8 = cx.nxt(hd8)
            S.op("act", lambda e: e.activation(l8.t[:], s8.t[:], AF.Ln, scale=1.0 / DH, bias=dn.eps.t[:, 0:1]),
                 reads=[s8.b, dn.eps.b], writes=[l8.b])
            r8 = cx.nxt(hd8)
            S.op("act", lambda e: e.activation(r8.t[:], l8.t[:], AF.Exp, scale=-0.5), reads=[l8.b], writes=[r8.b])
            qf = cx.nxt(sgs)
            S.op("dve", lambda e: e.tensor_tensor(qf.t[:, :].rearrange("p (h d) -> p h d", d=DH),
                                                  p.t[:, :].rearrange("p (h d) -> p h d", d=DH),
                                                  r8.t[:, :].unsqueeze(2).to_broadcast([128, 8, DH]), ALU.mult),
                 reads=[p.b, r8.b], writes=[qf.b])
            qn = cx.nxt(qnb)
            S.op("pool", lambda e: e.tensor_tensor(qn.t[:, :], qf.t[:, :], gtile.t[:, :, :].rearrange("p h d -> p (h d)"), ALU.mult),
                 reads=[qf.b, gtile.b], writes=[qn.b])
            tp = cx.nxt(dn.psT)
            for c4 in range(4):
                S.op("pe", lambda e, c4=c4: e.transpose(tp.t[:, c4 * 128:(c4 + 1) * 128], qn.t[:, c4 * 128:(c4 + 1) * 128],
                                                        dn.ident.t[:]), reads=[qn.b, dn.ident.b], writes=[tp.b])
            o = cx.nxt(oT)
            S.op("act", lambda e: e.copy(o.t[:], tp.t[:, 0:512]), reads=[tp.b], writes=[o.b])
            r0 = t0 + b * 128
            cx.store("o_qk1", dstT[hf * 512:(hf + 1) * 512, r0:r0 + 128].rearrange("(c p) t -> p c t", p=128), o,
                     o.t[:, :].rearrange("p (c t) -> p c t", c=4))

        for hf in range(2):
            wt = load_w8(WqB, b_wq, hf, "wq")
            for b in range(4):
                p = cx.nxt(pm)
                for kc in range(8):
                    S.op("pe", lambda e, p=p, kc=kc, b=b, wt=wt, hT=hT: e.matmul(
                        p.t[:], hT.t[:, kc, b * 128:(b + 1) * 128], wt.t[:, kc, :], start=(kc == 0), stop=(kc == 7)),
                        reads=[hT.b, wt.b], writes=[p.b])
                head_norm_store(p, qg, q1T, b, hf)
        for hf in range(4):
            wt = load_w8(WkvB, b_wkv, hf, "wkv")
            for b in range(4):
                p = cx.nxt(pm)
                for kc in range(8):
                    S.op("pe", lambda e, p=p, kc=kc, b=b, wt=wt, hT=hT: e.matmul(
                        p.t[:], hT.t[:, kc, b * 128:(b + 1) * 128], wt.t[:, kc, :], start=(kc == 0), stop=(kc == 7)),
                        reads=[hT.b, wt.b], writes=[p.b])
                if hf < 2:
                    head_norm_store(p, kg, k1T, b, hf)
                else:
                    o = cx.nxt(oT)
                    S.op("act", lambda e, o=o, p=p: e.copy(o.t[:], p.t[:]), reads=[p.b], writes=[o.b])
                    r0 = t0 + b * 128
                    cx.store("o_v1", v1[r0:r0 + 128, (hf - 2) * 512:(hf - 1) * 512], o, o.t[:])
        p = cx.nxt(pm)
        for kc in range(8):
            S.op("pe", lambda e, p=p, kc=kc, hT=hT: e.matmul(p.t[0:NH, :], Wf.t[:, kc, :], hT.t[:, kc, :],
                                                              start=(kc == 0), stop=(kc == 7)),
                 reads=[Wf.b, hT.b], writes=[p.b])
        f1 = cx.nxt(fl)
        S.op("act", lambda e, f1=f1, p=p: e.activation(f1.t[:], p.t[0:NH, :], AF.Exp, scale=-1.0, bias=nbf.t[:, 0:1]),
             reads=[p.b, nbf.b], writes=[f1.b])
        f2 = cx.nxt(fl)
        S.op("act", lambda e, f1=f1, f2=f2: e.activation(f2.t[:], f1.t[:], AF.Ln, bias=one.t[0:NH, 0:1]),
             reads=[f1.b, one.b], writes=[f2.b])
        S.op("dve", lambda e, f2=f2: e.tensor_scalar(f2.t[:], f2.t[:], -1.0, None, ALU.mult), reads=[f2.b], writes=[f2.b])
        cx.store("o_fl", flogT[:, t0:t0 + 512], f2, f2.t[:])
    return cx.finish()


def build_attn1(n_heads=NH):
    cx = Cx()
    S = cx.S
    qT = cx.din("qT", [D, TL], BF16)
    kT = cx.din("kT", [D, S_ALL], BF16)
    vr = cx.din("vr", [8, 128, 128 * 128], BF16)
    flog = cx.din("flog", [NH, S_ALL], F32)
    onehot = cx.din("onehot", [NH, 8], F32)
    negm = cx.din("negm", [128, 8 * 128], BF16)
    ident = cx.din("ident", [128, 128], BF16)
    sel = cx.din("sel", [128, 256], F32)
    mixT = cx.dout("mixT", [D, TL], BF16)
    kaug = cx.dint("kaug", [NH, 6, S_ALL], BF16)
    qaug = cx.dint("qaug", [NH, 6, TL], BF16)
    b_kaug = Buf("kaug")
    b_qaug = Buf("qaug")

    nm = cx.sb("nm", [128, 8 * 128], BF16)
    idt = cx.sb("idt", [128, 128], BF16)
    selt = cx.sb("selt", [128, 256], F32)
    oh = cx.sb("oh", [NH, 8], F32)
    for t_, src in ((nm, negm), (idt, ident), (selt, sel), (oh, onehot)):
        S.dma("sp", "cc_" + t_.b.name, lambda e, t_=t_, src=src: e.dma_start(out=t_.t[:], in_=src), writes=[t_.b])

    CH = 1024
    Fc = cx.sb("Fc", [NH, CH], F32, n=2)
    Fs = cx.sb("Fs", [NH, CH], F32, n=2)
    r1 = cx.sb("r1", [NH, CH], F32)
    onesf = cx.sb("onesf", [NH, CH], F32)
    S.op("pool", lambda e: e.memset(onesf.t[:], 1.0), writes=[onesf.b])
    ka = cx.sb("ka", [NH, 6, CH], BF16)
    Fq = cx.sb("Fq", [NH, TL], F32)
    qa = cx.sb("qa", [NH, 6, 512], BF16)
    carry = cx.sb("carry", [NH, 1], F32, n=2)
    S.op("pool", lambda e: e.memset(carry[1].t[:], 0.0), writes=[carry[1].b])
    S.op("pool", lambda e: e.memset(ka.t[:, 0:3, :], 1.0), writes=[ka.b])
    S.op("pool", lambda e: e.memset(qa.t[:, 3:6, :], 1.0), writes=[qa.b])
    for ci in range(S_ALL // CH):
        fc = Fc[ci % 2]
        fs = Fs[ci % 2]
        S.dma("sp", "ld_" + fc.b.name, lambda e, fc=fc, ci=ci: e.dma_start(out=fc.t[:], in_=flog[:, ci * CH:(ci + 1) * CH]),
              writes=[fc.b])
        cprev = carry[(ci + 1) % 2]
        ccur = carry[ci % 2]
        S.op("dve", lambda e, fs=fs, fc=fc, cprev=cprev: e.tensor_tensor_scan(
            fs.t[:], onesf.t[:], fc.t[:], cprev.t[:, 0:1], ALU.mult, ALU.add),
            reads=[onesf.b, fc.b, cprev.b], writes=[fs.b])
        S.op("dve", lambda e, fs=fs, ccur=ccur: e.tensor_copy(ccur.t[:], fs.t[:, CH - 1:CH]), reads=[fs.b], writes=[ccur.b])
        fview = fs.t[:, :].rearrange("h (c i) -> h c i", c=8)
        fqv = Fq.t[:, ci * 128:(ci + 1) * 128]
        for c in range(8):
            if c == 0:
                S.op("dve", lambda e, fview=fview, fqv=fqv, c=c: e.tensor_scalar(
                    fqv, fview[:, c, :], oh.t[:, c:c + 1], None, ALU.mult), reads=[fs.b, oh.b], writes=[Fq.b])
            else:
                S.op("dve", lambda e, fview=fview, fqv=fqv, c=c: e.scalar_tensor_tensor(
                    fqv, fview[:, c, :], oh.t[:, c:c + 1], fqv, ALU.mult, ALU.add), reads=[fs.b, oh.b, Fq.b], writes=[Fq.b])
        S.op("dve", lambda e, fs=fs: e.tensor_scalar(r1.t[:], fs.t[:], -1.0, None, ALU.mult), reads=[fs.b], writes=[r1.b])
        for part in range(3):
            S.op("dve", lambda e, part=part: e.tensor_copy(ka.t[:, 3 + part, :], r1.t[:]), reads=[r1.b], writes=[ka.b])
            if part < 2:
                S.op("dve", lambda e, part=part: e.tensor_tensor(r1.t[:], r1.t[:], ka.t[:, 3 + part, :], ALU.subtract),
                     reads=[r1.b, ka.b], writes=[r1.b])
        S.dma("sp", "st_kaug", lambda e, ci=ci: e.dma_start(out=kaug[:, :, ci * CH:(ci + 1) * CH], in_=ka.t[:]),
              reads=[ka.b], writes=[b_kaug])
    for qi in range(4):
        S.op("dve", lambda e, qi=qi: e.tensor_copy(r1.t[:, 0:512], Fq.t[:, qi * 512:(qi + 1) * 512]), reads=[Fq.b], writes=[r1.b])
        for part in range(3):
            S.op("dve", lambda e, part=part: e.tensor_copy(qa.t[:, part, :], r1.t[:, 0:512]), reads=[r1.b], writes=[qa.b])
            if part < 2:
                S.op("dve", lambda e, part=part: e.tensor_tensor(r1.t[:, 0:512], r1.t[:, 0:512], qa.t[:, part, :], ALU.subtract),
                     reads=[r1.b, qa.b], writes=[r1.b])
        S.dma("sp", "st_qaug", lambda e, qi=qi: e.dma_start(out=qaug[:, :, qi * 512:(qi + 1) * 512], in_=qa.t[:]),
              reads=[qa.b], writes=[b_qaug])

    kTs = cx.sb("kTs", [128, S_ALL], BF16, n=2)
    qs = cx.sb("qs", [128, TL], BF16, n=2)
    vs = cx.sb("vs", [128, 128 * 130], BF16, n=2)
    mx = cx.sb("mx", [128, TL], BF16, n=2)
    for sl in range(2):
        v3 = vs[sl].t[:, :].rearrange("p (k c) -> p k c", c=130)
        S.op("pool", lambda e, v3=v3: e.memset(v3[:, :, 64:65], 1.0), writes=[vs[sl].b])
        S.op("pool", lambda e, v3=v3: e.memset(v3[:, :, 129:130], 1.0), writes=[vs[sl].b])
    zp = cx.ps("zp", [128, 512], F32, n=3)
    op_ = cx.ps("op", [128, 512], F32, n=2)
    bcp = cx.ps("bcp", [128, 512], F32, n=1)
    pb_ = cx.sb("pb", [128, 512], BF16, n=3)
    osb = cx.sb("osb", [128, 512], F32, n=1)
    rbs = cx.sb("rbs", [128, 512], F32, n=1)

    def load_head(h):
        sl = h % 2
        k_, q_ = kTs[sl], qs[sl]
        S.dma("sp", "ldq%d" % sl, lambda e: e.dma_start(out=q_.t[0:64, :], in_=qT[h * 64:(h + 1) * 64, :]), writes=[q_.b])
        S.dma("sp", "ldq%d" % sl, lambda e: e.dma_start(out=q_.t[64:70, :], in_=qaug[h]), reads=[b_qaug], writes=[q_.b])
        S.dma("sp", "ldk%d" % sl, lambda e: e.dma_start(out=k_.t[64:70, :], in_=kaug[h]), reads=[b_kaug], writes=[k_.b])
        for part in range(4):
            c0 = part * 4096
            S.dma("sp", "ldk%d" % sl,
                  lambda e, c0=c0: e.dma_start(out=k_.t[0:64, c0:c0 + 4096], in_=kT[h * 64:(h + 1) * 64, c0:c0 + 4096]),
                  writes=[k_.b])

    def load_v(hp):
        v_ = vs[hp % 2]
        v3 = v_.t[:, :].rearrange("p (k c) -> p k c", c=130)
        src = vr[hp].rearrange("p (k c) -> p k c", c=128)
        for part in range(8):
            k0 = part * 16
            for hh in range(2):
                S.dma("sp", "ldv%d" % (hp % 2),
                      lambda e, k0=k0, hh=hh: e.dma_start(out=v3[:, k0:k0 + 16, 65 * hh:65 * hh + 64],
                                                          in_=src[:, k0:k0 + 16, 64 * hh:64 * hh + 64]),
                      writes=[v_.b])

    load_v(0)
    load_head(0)

    def do_head(h):
        sl = h % 2
        hp = h // 2
        odd = h % 2
        k_, q_ = kTs[sl], qs[sl]
        v_ = vs[hp % 2]
        m_ = mx[hp % 2]
        if h + 1 < n_heads:
            if (h + 1) % 2 == 0:
                load_v((h + 1) // 2)
            load_head(h + 1)
        items = []
        for J in range(4):
            nkb = 32 * J + 32
            for kb in range(nkb - 1, -1, -1):
                items.append((J, kb, nkb))
        st = {}

        def s1(it):
            J, kb, nkb = it
            r = kb - 32 * J
            c0 = 128 * (r // 8) if r >= 0 else 0
            z = cx.nxt(zp)
            p_ = cx.nxt(pb_)
            S.op("pe", lambda e: e.matmul(z.t[:, c0:512], k_.t[0:70, kb * 128:(kb + 1) * 128],
                                          q_.t[0:70, 512 * J + c0:512 * J + 512], start=True, stop=(r < 0)),
                 reads=[k_.b, q_.b], writes=[z.b])
            if r >= 0:
                i = r % 8
                S.op("pe", lambda e: e.matmul(z.t[:, c0:c0 + 128], idt.t[:], nm.t[:, i * 128:(i + 1) * 128],
                                              start=False, stop=True), reads=[idt.b, nm.b], writes=[z.b])
            S.op("act", lambda e: e.activation(p_.t[:, c0:512], z.t[:, c0:512], AF.Exp), reads=[z.b], writes=[p_.b])
            st[it] = (c0, p_)

        def s2(it):
            J, kb, nkb = it
            c0, p_ = st.pop(it)
            o_ = op_[J % 2]
            if odd:
                S.op("pe", lambda e: e.matmul(o_.t[:, c0:512], v_.t[:, kb * 130 + 1:kb * 130 + 129], p_.t[:, c0:512],
                                              start=(kb == nkb - 1), stop=(kb == 0)), reads=[v_.b, p_.b], writes=[o_.b])
            else:
                S.op("pe", lambda e: e.matmul(o_.t[0:65, c0:512], v_.t[:, kb * 130:kb * 130 + 65], p_.t[:, c0:512],
                                              start=(kb == nkb - 1), stop=(kb == 0)), reads=[v_.b, p_.b], writes=[o_.b])
            if kb == 0:
                ob = cx.nxt(osb)
                rb = cx.nxt(rbs)
                bc = bcp[0]
                if odd:
                    S.op("act", lambda e: e.copy(ob.t[32:64, :], o_.t[32:64, :]), reads=[o_.b], writes=[ob.b])
                    S.op("act", lambda e: e.copy(ob.t[64:128, :], o_.t[64:128, :]), reads=[o_.b], writes=[ob.b])
                    S.op("pe", lambda e: e.matmul(bc.t[:, :], selt.t[32:64, 128:256], ob.t[32:64, :], start=True, stop=True),
                         reads=[selt.b, ob.b], writes=[bc.b])
                    S.op("dve", lambda e: e.reciprocal(rb.t[64:128, :], bc.t[64:128, :]), reads=[bc.b], writes=[rb.b])
                    S.op("dve", lambda e: e.tensor_tensor(m_.t[64:128, 512 * J:512 * J + 512], ob.t[64:128, :], rb.t[64:128, :], ALU.mult),
                         reads=[ob.b, rb.b], writes=[m_.b])
                else:
                    S.op("act", lambda e: e.copy(ob.t[0:65, :], o_.t[0:65, :]), reads=[o_.b], writes=[ob.b])
                    S.op("pe", lambda e: e.matmul(bc.t[0:64, :], selt.t[64:65, 0:64], ob.t[64:65, :], start=True, stop=True),
                         reads=[selt.b, ob.b], writes=[bc.b])
                    S.op("dve", lambda e: e.reciprocal(rb.t[0:64, :], bc.t[0:64, :]), reads=[bc.b], writes=[rb.b])
                    S.op("dve", lambda e: e.tensor_tensor(m_.t[0:64, 512 * J:512 * J + 512], ob.t[0:64, :], rb.t[0:64, :], ALU.mult),
                         reads=[ob.b, rb.b], writes=[m_.b])

        n = len(items)
        for tau in range(n + 1):
            if tau < n:
                s1(items[tau])
            if tau - 1 >= 0:
                s2(items[tau - 1])
        if odd:
            cx.store("o_mix", mixT[hp * 128:(hp + 1) * 128, :], m_, m_.t[:])

    for h in range(n_heads):
        do_head(h)
    return cx.finish()


_PROGS = {}


def _prog(name, fn):
    if name not in _PROGS:
        _PROGS[name] = fn()
    return _PROGS[name]


def _shard_tok(a):
    F_ = a.shape[-1]
    r = a.reshape(16, 8, 128, F_)
    return [np.ascontiguousarray(r[:, c].reshape(TL, F_)) for c in range(NC_)]


def _gather_T(parts):
    R = parts[0].shape[0]
    out = np.zeros((R, 16, 8, 128), dtype=parts[0].dtype)
    for c in range(NC_):
        out[:, :, c, :] = parts[c].reshape(R, 16, 128)
    return out.reshape(R, S_ALL)


def _gather_v(parts):
    va = np.zeros((16, 8, 128, D), dtype=parts[0].dtype)
    for c in range(NC_):
        va[:, c] = parts[c].reshape(16, 128, D)
    va = va.reshape(128, 128, 8, 128)
    return np.ascontiguousarray(va.transpose(2, 1, 0, 3)).reshape(8, 128, 128 * 128)


def _consts():
    ar = np.arange(128)
    c = {}
    c["ident"] = np.eye(128, dtype=np.float32).astype(NPBF)
    c["tri"] = (ar[:, None] >= ar[None, :]).astype(np.float32).astype(NPBF)
    c["omt"] = (ar[:, None] < ar[None, :]).astype(np.float32).astype(NPBF)
    masks, negm, oh = [], [], []
    for cc in range(NC_):
        m = np.zeros((128, 8, 128), np.float32)
        n = np.full((128, 8, 128), -30000.0, np.float32)
        for i in range(8):
            if i < cc:
                m[:, i, :] = 1.0
                n[:, i, :] = 0.0
            elif i == cc:
                m[:, i, :] = (ar[:, None] < ar[None, :])
                n[:, i, :] = np.where(ar[:, None] <= ar[None, :], 0.0, -30000.0)
        masks.append(m.reshape(128, 1024).astype(NPBF))
        negm.append(n.reshape(128, 1024).astype(NPBF))
        o = np.zeros((NH, 8), np.float32)
        o[:, cc] = 1.0
        oh.append(o)
    c["masks"], c["negm"], c["onehot"] = masks, negm, oh
    sel = np.zeros((128, 256), np.float32)
    sel[64, 0:64] = 1.0
    sel[63, 192:256] = 1.0
    c["sel"] = sel
    return c


def _run(nc, in_maps):
    return run_bass_kernel_spmd(nc, in_maps, core_ids=list(range(NC_))).results


def kernel(x, p, attn_norm_g, sb_w_qkv, sb_w_o, shared_norm_g, shared_w_kvf, shared_b_f, shared_k_norm_g,
           fox_w_q, fox_q_norm_g, fox_w_o, ffn_norm_g, ffn_w_gu, ffn_w_d, ple_norm_g, ple_w_gate, ple_w_proj):
    f32 = lambda a: np.ascontiguousarray(np.asarray(a, dtype=np.float32))
    x = f32(x)
    p = f32(p)
    C = _consts()
    xs = _shard_tok(x[0])
    p0 = _shard_tok(p[0, 0])
    p1 = _shard_tok(p[1, 0])
    r1 = _run(_prog("pre0", build_pre0),
              [{"x": xs[c], "g": f32(attn_norm_g[0]), "w": f32(sb_w_qkv[0]), "ident": C["ident"]} for c in range(NC_)])
    kT_all = _gather_T([r["kT"] for r in r1])
    vr = _gather_v([r["v"] for r in r1])
    r2 = _run(_prog("attn0", build_attn0),
              [{"qT": r1[c]["qT"], "kT": kT_all, "vr": vr, "masks": C["masks"][c], "tri": C["tri"], "omt": C["omt"]}
               for c in range(NC_)])

    def post_in(c, h, mix, pl, w_o, li):
        return {"h": h, "mixT": mix, "p": pl, "w_o": f32(w_o), "ffn_g": f32(ffn_norm_g[li]), "w_gu": f32(ffn_w_gu[li]),
                "w_d": f32(ffn_w_d[li]), "ple_g": f32(ple_norm_g[li]), "w_gate": f32(ple_w_gate[li]),
                "w_proj": f32(ple_w_proj[li]), "ident": C["ident"]}

    in3 = []
    for c in range(NC_):
        d = post_in(c, xs[c], r2[c]["mixT"], p0[c], sb_w_o[0], 0)
        d.update({"a_g": f32(attn_norm_g[1]), "w_q": f32(fox_w_q[0]), "qn_g": f32(fox_q_norm_g[0]),
                  "sh_g": f32(shared_norm_g), "w_kvf": f32(shared_w_kvf), "b_f": f32(shared_b_f),
                  "kn_g": f32(shared_k_norm_g)})
        in3.append(d)
    r3 = _run(_prog("post_n", lambda: build_post(True)), in3)
    k1_all = _gather_T([r["k1T"] for r in r3])
    vr1 = _gather_v([r["v1"] for r in r3])
    fl_all = _gather_T([r["flogT"] for r in r3])
    r4 = _run(_prog("attn1", build_attn1),
              [{"qT": r3[c]["q1T"], "kT": k1_all, "vr": vr1, "flog": fl_all, "onehot": C["onehot"][c],
                "negm": C["negm"][c], "ident": C["ident"], "sel": C["sel"]} for c in range(NC_)])
    r5 = _run(_prog("post_l", lambda: build_post(False)),
              [post_in(c, r3[c]["hout"], r4[c]["mixT"], p1[c], fox_w_o[0], 1) for c in range(NC_)])
    out = np.zeros((16, 8, 128, D), np.float32)
    for c in range(NC_):
        out[:, c] = r5[c]["hout"].reshape(16, 128, D)
    return out.reshape(1, S_ALL, D)
```

```python
import numpy as np
import ml_dtypes
from contextlib import ExitStack
import concourse.bass as bass
import concourse.mybir as mybir
from concourse.bass_utils import run_bass_kernel_spmd

F32 = mybir.dt.float32
BF16 = mybir.dt.bfloat16
AF = mybir.ActivationFunctionType
ALU = mybir.AluOpType
NPBF = ml_dtypes.bfloat16

NC_ = 8
D = 1024
S_ALL = 16384
TL = 2048
NH = 16
DH = 64
DFF = 2816
PLE = 256
EPS = 1e-6
EPOCH = 4096


class Buf:
    __slots__ = ("name", "w", "r", "wm")

    def __init__(self, name, multi=False):
        self.name = name
        self.w = None
        self.r = []
        self.wm = {} if multi else None


class Tile:
    __slots__ = ("t", "b")

    def __init__(self, t, name):
        self.t = t
        self.b = Buf(name)


class Sched:
    ENG = ("pe", "act", "dve", "pool", "sp")

    def __init__(self, nc, es):
        self.nc = nc
        self.es = es
        self.lists = {e: [] for e in self.ENG}
        self.sems = {}
        self.cnt = {}
        self.alias = {}
        self.waited = {e: {} for e in self.ENG}
        self.ecount = {e: 0 for e in self.ENG}
        self.nsem = 0

    def _mksem(self, key):
        self.nsem += 1
        self.sems[key] = self.es.enter_context(self.nc.semaphore("s%d" % self.nsem))
        self.cnt[key] = 0

    def _deps(self, eng, reads, writes):
        deps = {}

        def add(tok):
            if tok is None:
                return
            k, v = tok
            if deps.get(k, 0) < v:
                deps[k] = v

        for b in reads:
            add(b.w)
            if b.wm is not None:
                for t in b.wm.items():
                    add(t)
        for b in writes:
            add(b.w)
            for t in b.r:
                add(t)
        out = []
        w = self.waited[eng]
        for k, v in deps.items():
            if eng == "pe" and k.startswith("E_pe#"):
                continue
            if w.get(k, 0) >= v:
                continue
            w[k] = v
            out.append((k, v))
        return out

    @staticmethod
    def _commit(tok, reads, writes):
        for b in writes:
            if b.wm is not None:
                if b.wm.get(tok[0], 0) < tok[1]:
                    b.wm[tok[0]] = tok[1]
                continue
            b.w = tok
            b.r = []
        for b in reads:
            b.r.append(tok)

    def op(self, eng, fn, reads=(), writes=()):
        deps = self._deps(eng, reads, writes)
        ep = self.ecount[eng] // EPOCH
        key = "E_%s#%d" % (eng, ep)
        if key not in self.sems:
            self._mksem(key)
        self.ecount[eng] += 1
        self.cnt[key] += 1
        tok = (key, self.cnt[key])
        sems = self.sems

        def thunk(e, deps=deps, fn=fn, key=key):
            for k, v in deps:
                e.wait_ge(sems[k], v)
            fn(e).then_inc(sems[key], 1)

        self.lists[eng].append(thunk)
        self._commit(tok, reads, writes)
        return tok

    def dma(self, eng, semname, fn, reads=(), writes=()):
        deps = self._deps(eng, reads, writes)
        key = self.alias.get(semname)
        if key is None or self.cnt[key] >= 16 * 240:
            n = 0 if key is None else int(key.split("#")[1]) + 1
            key = "D_%s#%d" % (semname, n)
            self.alias[semname] = key
            self._mksem(key)
        self.cnt[key] += 16
        tok = (key, self.cnt[key])
        sems = self.sems

        def thunk(e, deps=deps, fn=fn, key=key):
            for k, v in deps:
                e.wait_ge(sems[k], v)
            fn(e).then_inc(sems[key], 16)

        self.lists[eng].append(thunk)
        self._commit(tok, reads, writes)
        return tok

    def wait_all(self, eng, toks):
        sems = self.sems
        toks = list(toks)

        def thunk(e):
            for k, v in toks:
                e.wait_ge(sems[k], v)

        self.lists[eng].append(thunk)

    def emit(self):
        L = self.lists
        with self.nc.Block() as block:
            @block.tensor
            def _(e):
                for t in L["pe"]:
                    t(e)

            @block.scalar
            def _(e):
                for t in L["act"]:
                    t(e)

            @block.vector
            def _(e):
                for t in L["dve"]:
                    t(e)

            @block.gpsimd
            def _(e):
                for t in L["pool"]:
                    t(e)

            @block.sync
            def _(e):
                for t in L["sp"]:
                    t(e)


class Cx:
    def __init__(self):
        self.nc = bass.Bass("TRN2", target_bir_lowering=False)
        self.es = ExitStack()
        self.S = Sched(self.nc, self.es)
        self.out_toks = {}
        self.rr = {}

    def din(self, name, shape, dt):
        return self.nc.dram_tensor(name, list(shape), dt, kind="ExternalInput").ap()

    def dout(self, name, shape, dt):
        return self.nc.dram_tensor(name, list(shape), dt, kind="ExternalOutput").ap()

    def dint(self, name, shape, dt):
        return self.nc.dram_tensor(name, list(shape), dt, kind="Internal").ap()

    def sb(self, name, shape, dt, n=None):
        if n is None:
            return Tile(self.es.enter_context(self.nc.sbuf_tensor("sb_" + name, list(shape), dt)), name)
        return [Tile(self.es.enter_context(self.nc.sbuf_tensor("sb_%s%d" % (name, i), list(shape), dt)),
                     "%s%d" % (name, i)) for i in range(n)]

    def ps(self, name, shape, dt, n=None):
        if n is None:
            return Tile(self.es.enter_context(self.nc.psum_tensor("ps_" + name, list(shape), dt)), name)
        return [Tile(self.es.enter_context(self.nc.psum_tensor("ps_%s%d" % (name, i), list(shape), dt)),
                     "%s%d" % (name, i)) for i in range(n)]

    def nxt(self, lst, key=None):
        key = key or id(lst)
        i = self.rr.get(key, 0)
        self.rr[key] = i + 1
        return lst[i % len(lst)]

    def store(self, semname, out_ap, tile, in_ap, eng="pool"):
        tok = self.S.dma(eng, "st_" + tile.b.name, lambda e: e.dma_start(out=out_ap, in_=in_ap), reads=[tile.b])
        self.out_toks[tok[0]] = max(self.out_toks.get(tok[0], 0), tok[1])

    def finish(self):
        self.S.wait_all("sp", list(self.out_toks.items()))
        self.S.emit()
        self.es.close()
        return self.nc


class Dense:
    def __init__(self, cx, ident_ap):
        self.cx = cx
        S = cx.S
        self.ident = cx.sb("ident", [128, 128], BF16)
        S.dma("sp", "const1", lambda e: e.dma_start(out=self.ident.t[:], in_=ident_ap), writes=[self.ident.b])
        self.junk = cx.sb("junk", [128, 1024], BF16, n=2)
        self.ss = cx.sb("ss", [128, 1], F32, n=4)
        self.lnv = cx.sb("lnv", [128, 1], F32, n=4)
        self.rstd = cx.sb("rstd", [128, 1], F32, n=4)
        self.hn = cx.sb("hn", [128, 1024], BF16, n=2)
        self.psT = cx.ps("psT", [128, 1024], BF16, n=2)
        self.wst = cx.sb("wst", [128, 512], F32, n=4)
        self.gv = {}
        self.evq = 0

    def load_gain(self, name, g_ap):
        cx = self.cx
        t = cx.sb("g_" + name, [128, 8], F32)
        cx.S.dma("sp", "cg_" + name, lambda e: e.dma_start(out=t.t[:], in_=g_ap.rearrange("(k p) -> p k", p=128),
                                                      allow_slow_non_contiguous=True), writes=[t.b])
        self.gv[name] = t
        return t

    def prep_weight(self, w_ap, K, N, dst_fn, gain=None, post=None, c0=0, c1=None):
        cx = self.cx
        S = cx.S
        c1 = N if c1 is None else c1
        for kc in range(K // 128):
            for n0 in range(c0, c1, 512):
                wd = min(512, c1 - n0)
                st = cx.nxt(self.wst)
                S.dma("sp", "wst_" + st.b.name,
                      lambda e, st=st, kc=kc, n0=n0, wd=wd: e.dma_start(
                          out=st.t[:, 0:wd], in_=w_ap[kc * 128:(kc + 1) * 128, n0:n0 + wd]),
                      writes=[st.b])
                dt_, dap = dst_fn(kc, n0, wd)
                eng = "pool" if (self.evq % 2 == 0) else "dve"
                self.evq += 1
                pv = post(n0) if post is not None else None
                if gain is not None:
                    g = gain
                    if pv is not None:
                        fn = lambda e, st=st, dap=dap, kc=kc, wd=wd, g=g, pv=pv: e.tensor_scalar(
                            dap, st.t[:, 0:wd], g.t[:, kc:kc + 1], pv, ALU.mult, ALU.mult)
                    else:
                        fn = lambda e, st=st, dap=dap, kc=kc, wd=wd, g=g: e.tensor_scalar(
                            dap, st.t[:, 0:wd], g.t[:, kc:kc + 1], None, ALU.mult)
                    S.op(eng, fn, reads=[st.b, g.b], writes=[dt_.b])
                else:
                    S.op(eng, lambda e, st=st, dap=dap, wd=wd: e.tensor_copy(dap, st.t[:, 0:wd]),
                         reads=[st.b], writes=[dt_.b])

    def norm_T(self, h, hnT, col0):
        cx = self.cx
        S = cx.S
        junk = cx.nxt(self.junk)
        ss = cx.nxt(self.ss)
        lnv = cx.nxt(self.lnv)
        rstd = cx.nxt(self.rstd)
        hn = cx.nxt(self.hn)
        pT = cx.nxt(self.psT)
        S.op("act", lambda e: e.activation(junk.t[:], h.t[:], AF.Square, accum_out=ss.t[:]),
             reads=[h.b], writes=[junk.b, ss.b])
        S.op("act", lambda e: e.activation(lnv.t[:], ss.t[:], AF.Ln, scale=1.0 / D, bias=self.eps.t[:, 0:1]),
             reads=[ss.b, self.eps.b], writes=[lnv.b])
        S.op("act", lambda e: e.activation(rstd.t[:], lnv.t[:], AF.Exp, scale=-0.5),
             reads=[lnv.b], writes=[rstd.b])
        S.op("dve", lambda e: e.tensor_scalar(hn.t[:], h.t[:], rstd.t[:, 0:1], None, ALU.mult),
             reads=[h.b, rstd.b], writes=[hn.b])
        for kc in range(8):
            S.op("pe", lambda e, kc=kc: e.transpose(pT.t[:, kc * 128:(kc + 1) * 128],
                                                    hn.t[:, kc * 128:(kc + 1) * 128], self.ident.t[:]),
                 reads=[hn.b, self.ident.b], writes=[pT.b])
        S.op("dve", lambda e: e.tensor_copy(hnT.t[:, :, col0:col0 + 128],
                                            pT.t[:, :].rearrange("p (k t) -> p k t", k=8)),
             reads=[pT.b], writes=[hnT.b])

    def consts(self):
        cx = self.cx
        self.eps = cx.sb("epsc", [128, 1], F32)
        cx.S.op("pool", lambda e: e.memset(self.eps.t[:], EPS), writes=[self.eps.b])


def build_pre0():
    cx = Cx()
    S = cx.S
    x = cx.din("x", [TL, D], F32)
    g = cx.din("g", [D], F32)
    w = cx.din("w", [D, 3 * D], F32)
    ident = cx.din("ident", [128, 128], BF16)
    qT = cx.dout("qT", [D, TL], BF16)
    kT = cx.dout("kT", [D, TL], BF16)
    v = cx.dout("v", [TL, D], BF16)
    dn = Dense(cx, ident)
    dn.consts()
    gt = dn.load_gain("a", g)
    Wb = cx.sb("Wb", [128, 8, 3 * D], BF16)
    dn.prep_weight(w, D, 3 * D, lambda kc, n0, wd: (Wb, Wb.t[:, kc, n0:n0 + wd]), gain=gt,
                   post=lambda n0: (0.125 if n0 < D else 1.0))
    hblk = cx.sb("hblk", [128, D], F32, n=3)
    hnT = cx.sb("hnT", [128, 8, 512], BF16, n=2)
    pm = cx.ps("pm", [128, 512], F32, n=4)
    ost = cx.sb("ost", [128, 512], BF16, n=4)
    ev = 0
    for gi in range(4):
        hT = cx.nxt(hnT)
        for b in range(4):
            h = cx.nxt(hblk)
            r0 = (gi * 4 + b) * 128
            S.dma("sp", "ld_" + h.b.name, lambda e, h=h, r0=r0: e.dma_start(out=h.t[:], in_=x[r0:r0 + 128, :]),
                  writes=[h.b])
            dn.norm_T(h, hT, b * 128)
        for n in range(16):
            p = cx.nxt(pm)
            for kc in range(8):
                S.op("pe", lambda e, p=p, kc=kc, n=n, hT=hT: e.matmul(
                    p.t[:], Wb.t[:, kc, n * 128:(n + 1) * 128], hT.t[:, kc, :], start=(kc == 0), stop=(kc == 7)),
                    reads=[Wb.b, hT.b], writes=[p.b])
            o = cx.nxt(ost)
            if ev % 2 == 0:
                S.op("act", lambda e, o=o, p=p: e.copy(o.t[:], p.t[:]), reads=[p.b], writes=[o.b])
            else:
                S.op("dve", lambda e, o=o, p=p: e.tensor_copy(o.t[:], p.t[:]), reads=[p.b], writes=[o.b])
            ev += 1
            dst = qT if n < 8 else kT
            rr = (n % 8) * 128
            cx.store("o_qk", dst[rr:rr + 128, gi * 512:(gi + 1) * 512], o, o.t[:])
        for b in range(4):
            for hf in range(2):
                p = cx.nxt(pm)
                for kc in range(8):
                    S.op("pe", lambda e, p=p, kc=kc, b=b, hf=hf, hT=hT: e.matmul(
                        p.t[:], hT.t[:, kc, b * 128:(b + 1) * 128],
                        Wb.t[:, kc, 2 * D + hf * 512:2 * D + (hf + 1) * 512], start=(kc == 0), stop=(kc == 7)),
                        reads=[Wb.b, hT.b], writes=[p.b])
                o = cx.nxt(ost)
                if ev % 2 == 0:
                    S.op("act", lambda e, o=o, p=p: e.copy(o.t[:], p.t[:]), reads=[p.b], writes=[o.b])
                else:
                    S.op("dve", lambda e, o=o, p=p: e.tensor_copy(o.t[:], p.t[:]), reads=[p.b], writes=[o.b])
                ev += 1
                r0 = (gi * 4 + b) * 128
                cx.store("o_v", v[r0:r0 + 128, hf * 512:(hf + 1) * 512], o, o.t[:])
    return cx.finish()


def build_attn0(n_pairs=8):
    cx = Cx()
    S = cx.S
    qT = cx.din("qT", [D, TL], BF16)
    kT = cx.din("kT", [D, S_ALL], BF16)
    vr = cx.din("vr", [8, 128, 128 * 128], BF16)
    masks = cx.din("masks", [128, 8 * 128], BF16)
    tri = cx.din("tri", [128, 128], BF16)
    omt = cx.din("omt", [128, 128], BF16)
    mixT = cx.dout("mixT", [D, TL], BF16)

    mk = cx.sb("mk", [128, 8 * 128], BF16)
    trt = cx.sb("trt", [128, 128], BF16)
    omtt = cx.sb("omtt", [128, 128], BF16)
    S.dma("sp", "const3", lambda e: e.dma_start(out=mk.t[:], in_=masks), writes=[mk.b])
    S.dma("sp", "const4", lambda e: e.dma_start(out=trt.t[:], in_=tri), writes=[trt.b])
    S.dma("sp", "const5", lambda e: e.dma_start(out=omtt.t[:], in_=omt), writes=[omtt.b])

    one = cx.sb("one", [128, 1], F32)
    S.op("pool", lambda e: e.memset(one.t[:], 1.0), writes=[one.b])
    kTs = cx.sb("kTs", [128, S_ALL], BF16, n=2)
    vs = cx.sb("vs", [128, 128 * 128], BF16, n=2)
    qs = cx.sb("qs", [128, TL], BF16, n=2)
    mx = cx.sb("mx", [128, TL], BF16, n=2)
    z2 = cx.ps("z2", [128, 1024], F32, n=2)
    C2 = cx.ps("C2", [128, 1024], F32)
    op_ = cx.ps("op", [128, 512], F32, n=2)
    e2 = cx.sb("e2", [128, 1024], F32, n=5)
    sp2 = cx.sb("sp2", [128, 1024], BF16, n=5)
    x2 = cx.sb("x2", [128, 1024], BF16, n=3)
    w2 = cx.sb("w2", [128, 1024], BF16, n=3)

    def V(t, c0, w=None):
        v = t.t[:, :].rearrange("p (h c) -> p h c", h=2)
        return v[:, :, c0:512] if w is None else v[:, :, c0:c0 + w]

    def load_pair(hp):
        sl = hp % 2
        k_, v_, q_ = kTs[sl], vs[sl], qs[sl]
        S.dma("sp", "ldq%d" % sl, lambda e: e.dma_start(out=q_.t[:], in_=qT[hp * 128:(hp + 1) * 128, :]),
              writes=[q_.b])
        for part in range(4):
            c0 = part * 4096
            S.dma("sp", "ldk%d" % sl,
                  lambda e, c0=c0: e.dma_start(out=k_.t[:, c0:c0 + 4096], in_=kT[hp * 128:(hp + 1) * 128, c0:c0 + 4096]),
                  writes=[k_.b])
            S.dma("sp", "ldv%d" % sl,
                  lambda e, c0=c0: e.dma_start(out=v_.t[:, c0:c0 + 4096], in_=vr[hp, :, c0:c0 + 4096]),
                  writes=[v_.b])

    load_pair(0)

    def do_pair(hp):
        sl = hp % 2
        k_, v_, q_, m_ = kTs[sl], vs[sl], qs[sl], mx[sl]
        if hp + 1 < n_pairs:
            load_pair(hp + 1)
        items = []
        for J in range(4):
            nkb = 32 * J + 32
            for kb in range(nkb - 1, -1, -1):
                items.append((J, kb, nkb))
        st = {}

        def s1(it):
            J, kb, nkb = it
            r = kb - 32 * J
            c0 = 128 * (r // 8) if r >= 0 else 0
            z = cx.nxt(z2)
            e_ = cx.nxt(e2)
            sp_ = cx.nxt(sp2)
            for hh in range(2):
                pb = 64 * hh
                S.op("pe", lambda e, hh=hh, pb=pb: e.matmul(
                    z.t[:, 512 * hh + c0:512 * hh + 512], k_.t[pb:pb + 64, kb * 128:(kb + 1) * 128],
                    q_.t[pb:pb + 64, 512 * J + c0:512 * J + 512], start=True, stop=True),
                    reads=[k_.b, q_.b], writes=[z.b])
            S.op("act", lambda e: e.activation(V(e_, c0), V(z, c0), AF.Exp), reads=[z.b], writes=[e_.b])
            if r >= 0:
                i = r % 8
                S.op("pool", lambda e: e.tensor_tensor(
                    V(e_, c0, 128), V(e_, c0, 128),
                    mk.t[:, i * 128:(i + 1) * 128].unsqueeze(1).to_broadcast([128, 2, 128]), ALU.mult),
                    reads=[e_.b, mk.b], writes=[e_.b])
            S.op("act", lambda e: e.activation(V(sp_, c0), V(e_, c0), AF.Ln, bias=one.t[:, 0:1]),
                 reads=[e_.b, one.b], writes=[sp_.b])
            st[it] = [c0, e_, sp_, None, None]

        def s2(it):
            J, kb, nkb = it
            c0, e_, sp_, _, _ = st[it]
            x_ = cx.nxt(x2)
            for hh in range(2):
                S.op("pe", lambda e, hh=hh: e.matmul(C2.t[:, 512 * hh + c0:512 * hh + 512], trt.t[:],
                                                     sp_.t[:, 512 * hh + c0:512 * hh + 512],
                                                     start=(kb == nkb - 1), stop=(kb == 0)),
                     reads=[trt.b, sp_.b], writes=[C2.b])
            S.op("act", lambda e: e.activation(V(x_, c0), V(C2, c0), AF.Exp, scale=-1.0), reads=[C2.b], writes=[x_.b])
            st[it][3] = x_

        def s3(it):
            J, kb, nkb = it
            c0, e_, sp_, x_, _ = st[it]
            w_ = cx.nxt(w2)
            if kb > 0:
                for hh in range(2):
                    S.op("pe", lambda e, hh=hh: e.matmul(C2.t[:, 512 * hh + c0:512 * hh + 512], omtt.t[:],
                                                         sp_.t[:, 512 * hh + c0:512 * hh + 512], start=False, stop=False),
                         reads=[omtt.b, sp_.b], writes=[C2.b])
            S.op("dve", lambda e: e.tensor_tensor(V(w_, c0), V(e_, c0), V(x_, c0), ALU.mult),
                 reads=[e_.b, x_.b], writes=[w_.b])
            st[it][4] = w_

        def s4(it):
            J, kb, nkb = it
            c0, e_, sp_, x_, w_ = st.pop(it)
            o_ = op_[J % 2]
            for hh in range(2):
                pb = 64 * hh
                S.op("pe", lambda e, hh=hh, pb=pb: e.matmul(
                    o_.t[pb:pb + 64, c0:512], v_.t[:, kb * 128 + pb:kb * 128 + pb + 64],
                    w_.t[:, 512 * hh + c0:512 * hh + 512], start=(kb == nkb - 1), stop=(kb == 0)),
                    reads=[v_.b, w_.b], writes=[o_.b])
            if kb == 0:
                S.op("act", lambda e: e.copy(m_.t[:, 512 * J:512 * J + 512], o_.t[:, :]), reads=[o_.b], writes=[m_.b])

        n = len(items)
        for tau in range(-1, n + 3):
            if 0 <= tau - 2 < n:
                s3(items[tau - 2])
            if 0 <= tau - 1 < n:
                s2(items[tau - 1])
            if 0 <= tau + 1 < n:
                s1(items[tau + 1])
            if 0 <= tau - 3 < n:
                s4(items[tau - 3])
        cx.store("o_mix", mixT[hp * 128:(hp + 1) * 128, :], m_, m_.t[:])

    for hp in range(n_pairs):
        do_pair(hp)
    return cx.finish()


def build_post(nxt):
    cx = Cx()
    S = cx.S
    h_in = cx.din("h", [TL, D], F32)
    mixT = cx.din("mixT", [D, TL], BF16)
    p_in = cx.din("p", [TL, PLE], F32)
    w_o = cx.din("w_o", [D, D], F32)
    ffn_g = cx.din("ffn_g", [D], F32)
    w_gu = cx.din("w_gu", [D, 2 * DFF], F32)
    w_d = cx.din("w_d", [DFF, D], F32)
    ple_g = cx.din("ple_g", [D], F32)
    w_gate = cx.din("w_gate", [D, D], F32)
    w_proj = cx.din("w_proj", [PLE, D], F32)
    ident = cx.din("ident", [128, 128], BF16)
    hout = cx.dout("hout", [TL, D], F32)
    if nxt:
        a_g = cx.din("a_g", [D], F32)
        w_q = cx.din("w_q", [D, D], F32)
        qn_g = cx.din("qn_g", [DH], F32)
        sh_g = cx.din("sh_g", [D], F32)
        w_kvf = cx.din("w_kvf", [D, 2 * D + NH], F32)
        b_f = cx.din("b_f", [NH], F32)
        kn_g = cx.din("kn_g", [DH], F32)
        q1T = cx.dout("q1T", [D, TL], BF16)
        k1T = cx.dout("k1T", [D, TL], BF16)
        v1 = cx.dout("v1", [TL, D], BF16)
        flogT = cx.dout("flogT", [NH, TL], F32)

    dn = Dense(cx, ident)
    dn.consts()
    g_ffn = dn.load_gain("ffn", ffn_g)
    g_ple = dn.load_gain("ple", ple_g)

    wbf = cx.sb("wbf", [128, 512], BF16, n=4)

    def to_scratch(name, dst_ap_fn):
        buf = Buf("scr_" + name)

        def dst_fn(kc, n0, wd):
            t = cx.nxt(wbf)
            return t, t.t[:, 0:wd]
        return buf, dst_fn

    def prep_dram(name, w_ap, K, N, dst_ap_fn, gain=None, c0=0, c1=None):
        buf = Buf("scr_" + name, multi=True)
        c1_ = N if c1 is None else c1
        for kc in range(K // 128):
            for n0 in range(c0, c1_, 512):
                wd = min(512, c1_ - n0)
                holder = {}

                def dst_fn(kc_, n0_, wd_, holder=holder):
                    t = cx.nxt(wbf)
                    holder["t"] = t
                    return t, t.t[:, 0:wd_]
                dn.prep_weight(w_ap[kc * 128:(kc + 1) * 128, :], 128, N, dst_fn, gain=None if gain is None else _GainCol(gain, kc),
                               c0=n0, c1=n0 + wd)
                t = holder["t"]
                dap = dst_ap_fn(kc, n0 - c0, wd)
                S.dma("pool", "wp_" + t.b.name, lambda e, t=t, dap=dap, wd=wd: e.dma_start(out=dap, in_=t.t[:, 0:wd]),
                      reads=[t.b], writes=[buf])
        return buf

    class _GainCol:
        def __init__(self, g, kc):
            self.b = g.b
            self.t = _Shift(g.t, kc)

    class _Shift:
        def __init__(self, t, kc):
            self._t = t
            self._kc = kc

        def __getitem__(self, idx):
            return self._t[idx[0], self._kc:self._kc + 1]

    def scr8(name, N):
        return cx.dint("scr_" + name, [N // 512, 128, 8, 512], BF16)

    WoB = scr8("wo", D)
    b_wo = prep_dram("wo", w_o, D, D, lambda kc, n0, wd: WoB[n0 // 512, :, kc, :])
    WguB = cx.dint("scr_wgu", [22, 128, 8, 256], BF16)

    def gu_dst(off):
        def f(kc, n0, wd):
            j0 = n0 // 128
            return WguB[j0:j0 + wd // 128, :, kc, off:off + 128].rearrange("j p i -> p j i")
        return f
    b_wg = prep_dram("wg", w_gu, D, 2 * DFF, gu_dst(0), gain=g_ffn, c0=0, c1=DFF)
    b_wu = prep_dram("wu", w_gu, D, 2 * DFF, gu_dst(128), gain=g_ffn, c0=DFF, c1=2 * DFF)
    WdB = cx.dint("scr_wd", [4, 128, 22, 256], BF16)
    b_wd = prep_dram("wd", w_d, DFF, D,
                     lambda kc, n0, wd: WdB[n0 // 256:n0 // 256 + 2, :, kc, :].rearrange("q p i -> p q i"))
    WgateB = scr8("wgate", D)
    b_wgate = prep_dram("wgate", w_gate, D, D, lambda kc, n0, wd: WgateB[n0 // 512, :, kc, :], gain=g_ple)
    Wproj = cx.sb("Wproj", [128, 2, D], BF16)
    dn.prep_weight(w_proj, PLE, D, lambda kc, n0, wd: (Wproj, Wproj.t[:, kc, n0:n0 + wd]))
    if nxt:
        g_a = dn.load_gain("a1", a_g)
        g_sh = dn.load_gain("sh", sh_g)
        WqB = scr8("wq", D)
        b_wq = prep_dram("wq", w_q, D, D, lambda kc, n0, wd: WqB[n0 // 512, :, kc, :], gain=g_a)
        WkvB = scr8("wkv", 2 * D)
        b_wkv = prep_dram("wkv", w_kvf, D, 2 * D + NH, lambda kc, n0, wd: WkvB[n0 // 512, :, kc, :], gain=g_sh,
                          c0=0, c1=2 * D)
        Wf = cx.sb("Wf", [128, 8, NH], BF16)
        dn.prep_weight(w_kvf, D, 2 * D + NH, lambda kc, n0, wd: (Wf, Wf.t[:, kc, 0:wd]), gain=g_sh,
                       c0=2 * D, c1=2 * D + NH)
        qg = cx.sb("qg", [128, 8, DH], F32)
        kg = cx.sb("kg", [128, 8, DH], F32)
        S.dma("sp", "const6", lambda e: e.dma_start(out=qg.t[:], in_=qn_g.unsqueeze(0).unsqueeze(0).to_broadcast([128, 8, DH])),
              writes=[qg.b])
        S.dma("sp", "const7", lambda e: e.dma_start(out=kg.t[:], in_=kn_g.unsqueeze(0).unsqueeze(0).to_broadcast([128, 8, DH])),
              writes=[kg.b])
        S.op("dve", lambda e: e.tensor_scalar(qg.t[:], qg.t[:], 0.125, None, ALU.mult), reads=[qg.b], writes=[qg.b])
        nbf = cx.sb("nbf", [NH, 1], F32)
        S.dma("sp", "const8", lambda e: e.dma_start(out=nbf.t[:], in_=b_f.rearrange("(h o) -> h o", o=1)), writes=[nbf.b])
        S.op("dve", lambda e: e.tensor_scalar(nbf.t[:], nbf.t[:], -1.0, None, ALU.mult), reads=[nbf.b], writes=[nbf.b])
        one = cx.sb("one", [128, 1], F32)
        S.op("pool", lambda e: e.memset(one.t[:], 1.0), writes=[one.b])

    hres = cx.sb("hres", [128, D], F32, n=8)
    mTs = cx.sb("mT", [128, 8, 512], BF16, n=1)
    wt8 = cx.sb("wt8", [128, 8, 512], BF16, n=3)
    wgut = cx.sb("wgut", [128, 8, 256], BF16, n=4)
    wdt = cx.sb("wdt", [128, 22, 256], BF16, n=2)
    hnTs = cx.sb("hnT", [128, 8, 512], BF16, n=2)
    aT = cx.sb("aT", [128, 22, 512], BF16)
    sgs = cx.sb("sg", [128, 512], F32, n=2)
    tmps = cx.sb("tmp", [128, 512], F32, n=2)
    pblk = cx.sb("pblk", [128, PLE], F32, n=2)
    pbf = cx.sb("pbf", [128, PLE], BF16, n=2)
    pTs = cx.sb("pT", [128, 2, 128], BF16, n=2)
    pm = cx.ps("pm", [128, 512], F32, n=4)
    if nxt:
        hd8 = cx.sb("hd8", [128, 8], F32, n=4)
        qnb = cx.sb("qnb", [128, 512], BF16, n=2)
        oT = cx.sb("oT", [128, 512], BF16, n=2)
        fl = cx.sb("fl", [NH, 512], F32, n=2)

    def load_w8(scr, buf, hf, name):
        t = cx.nxt(wt8)
        S.dma("sp", "ld_" + t.b.name, lambda e: e.dma_start(out=t.t[:], in_=scr[hf]), reads=[buf], writes=[t.b])
        return t

    def add_res(h, c0, wd, src_tile, src_ap, flip=[0]):
        S.op("dve", lambda e: e.tensor_tensor(h.t[:, c0:c0 + wd], h.t[:, c0:c0 + wd], src_ap, ALU.add),
             reads=[h.b, src_tile.b], writes=[h.b])

    for gi in range(4):
        t0 = gi * 512
        hb = []
        for b in range(4):
            h = cx.nxt(hres)
            r0 = t0 + b * 128
            S.dma("pool", "ld_" + h.b.name, lambda e, h=h, r0=r0: e.dma_start(out=h.t[:], in_=h_in[r0:r0 + 128, :]),
                  writes=[h.b])
            hb.append(h)
        mT = cx.nxt(mTs)
        S.dma("pool", "ld_mT", lambda e, mT=mT, t0=t0: e.dma_start(
            out=mT.t[:], in_=mixT[:, t0:t0 + 512].rearrange("(c p) t -> p c t", p=128)), writes=[mT.b])
        for hf in range(2):
            wt = load_w8(WoB, b_wo, hf, "wo")
            for b in range(4):
                p = cx.nxt(pm)
                for kc in range(8):
                    S.op("pe", lambda e, p=p, kc=kc, b=b, wt=wt, mT=mT: e.matmul(
                        p.t[:], mT.t[:, kc, b * 128:(b + 1) * 128], wt.t[:, kc, :], start=(kc == 0), stop=(kc == 7)),
                        reads=[mT.b, wt.b], writes=[p.b])
                add_res(hb[b], hf * 512, 512, p, p.t[:])
        hT = cx.nxt(hnTs)
        for b in range(4):
            dn.norm_T(hb[b], hT, b * 128)
        for j in range(22):
            wg = cx.nxt(wgut)
            S.dma("sp", "ld_" + wg.b.name, lambda e, wg=wg, j=j: e.dma_start(out=wg.t[:], in_=WguB[j]),
                  reads=[b_wg, b_wu], writes=[wg.b])
            pg = cx.nxt(pm)
            pu = cx.nxt(pm)
            for kc in range(8):
                S.op("pe", lambda e, pg=pg, kc=kc, wg=wg, hT=hT: e.matmul(
                    pg.t[:], wg.t[:, kc, 0:128], hT.t[:, kc, :], start=(kc == 0), stop=(kc == 7)),
                    reads=[wg.b, hT.b], writes=[pg.b])
            for kc in range(8):
                S.op("pe", lambda e, pu=pu, kc=kc, wg=wg, hT=hT: e.matmul(
                    pu.t[:], wg.t[:, kc, 128:256], hT.t[:, kc, :], start=(kc == 0), stop=(kc == 7)),
                    reads=[wg.b, hT.b], writes=[pu.b])
            sg = cx.nxt(sgs)
            S.op("act", lambda e, sg=sg, pg=pg: e.activation(sg.t[:], pg.t[:], AF.Silu), reads=[pg.b], writes=[sg.b])
            S.op("dve", lambda e, sg=sg, pu=pu, j=j: e.tensor_tensor(aT.t[:, j, :], sg.t[:], pu.t[:], ALU.mult),
                 reads=[sg.b, pu.b], writes=[aT.b])
        for qd in range(4):
            wd_ = cx.nxt(wdt)
            S.dma("sp", "ld_" + wd_.b.name, lambda e, wd_=wd_, qd=qd: e.dma_start(out=wd_.t[:], in_=WdB[qd]),
                  reads=[b_wd], writes=[wd_.b])
            for b in range(4):
                p = cx.nxt(pm)
                for j in range(22):
                    S.op("pe", lambda e, p=p, j=j, b=b, wd_=wd_: e.matmul(
                        p.t[:, 0:256], aT.t[:, j, b * 128:(b + 1) * 128], wd_.t[:, j, :], start=(j == 0), stop=(j == 21)),
                        reads=[aT.b, wd_.b], writes=[p.b])
                add_res(hb[b], qd * 256, 256, p, p.t[:, 0:256])
        hT = cx.nxt(hnTs)
        for b in range(4):
            dn.norm_T(hb[b], hT, b * 128)
        for hf in range(2):
            wt = load_w8(WgateB, b_wgate, hf, "wgate")
            for b in range(4):
                pb_ = cx.nxt(pblk)
                r0 = t0 + b * 128
                S.dma("pool", "ld_" + pb_.b.name, lambda e, pb_=pb_, r0=r0: e.dma_start(out=pb_.t[:], in_=p_in[r0:r0 + 128, :]),
                      writes=[pb_.b])
                pf = cx.nxt(pbf)
                S.op("pool", lambda e, pf=pf, pb_=pb_: e.tensor_copy(pf.t[:], pb_.t[:]), reads=[pb_.b], writes=[pf.b])
                pT_ps = cx.nxt(dn.psT)
                for k2 in range(2):
                    S.op("pe", lambda e, k2=k2, pT_ps=pT_ps, pf=pf: e.transpose(
                        pT_ps.t[:, k2 * 128:(k2 + 1) * 128], pf.t[:, k2 * 128:(k2 + 1) * 128], dn.ident.t[:]),
                        reads=[pf.b, dn.ident.b], writes=[pT_ps.b])
                pT = cx.nxt(pTs)
                S.op("act", lambda e, pT=pT, pT_ps=pT_ps: e.copy(pT.t[:, :, :], pT_ps.t[:, 0:256].rearrange("p (k t) -> p k t", k=2)),
                     reads=[pT_ps.b], writes=[pT.b])
                pgate = cx.nxt(pm)
                for kc in range(8):
                    S.op("pe", lambda e, pgate=pgate, kc=kc, b=b, wt=wt, hT=hT: e.matmul(
                        pgate.t[:], hT.t[:, kc, b * 128:(b + 1) * 128], wt.t[:, kc, :], start=(kc == 0), stop=(kc == 7)),
                        reads=[hT.b, wt.b], writes=[pgate.b])
                pproj = cx.nxt(pm)
                for k2 in range(2):
                    S.op("pe", lambda e, pproj=pproj, k2=k2, pT=pT, hf=hf: e.matmul(
                        pproj.t[:], pT.t[:, k2, :], Wproj.t[:, k2, hf * 512:(hf + 1) * 512], start=(k2 == 0), stop=(k2 == 1)),
                        reads=[pT.b, Wproj.b], writes=[pproj.b])
                sg = cx.nxt(sgs)
                S.op("act", lambda e, sg=sg, pgate=pgate: e.activation(sg.t[:], pgate.t[:], AF.Sigmoid),
                     reads=[pgate.b], writes=[sg.b])
                tmp = cx.nxt(tmps)
                S.op("dve", lambda e, tmp=tmp, sg=sg, pproj=pproj: e.tensor_tensor(tmp.t[:], sg.t[:], pproj.t[:], ALU.mult),
                     reads=[sg.b, pproj.b], writes=[tmp.b])
                hh_ = hb[b]
                S.op("pool", lambda e, hh_=hh_, tmp=tmp, hf=hf: e.tensor_tensor(
                    hh_.t[:, hf * 512:(hf + 1) * 512], hh_.t[:, hf * 512:(hf + 1) * 512], tmp.t[:], ALU.add),
                    reads=[hh_.b, tmp.b], writes=[hh_.b])
        for b in range(4):
            r0 = t0 + b * 128
            cx.store("o_h", hout[r0:r0 + 128, :], hb[b], hb[b].t[:])
        if not nxt:
            continue
        hT = cx.nxt(hnTs)
        for b in range(4):
            dn.norm_T(hb[b], hT, b * 128)

        def head_norm_store(p, gtile, dstT, b, hf):
            sq = cx.nxt(tmps)
            S.op("act", lambda e: e.activation(sq.t[:], p.t[:], AF.Square), reads=[p.b], writes=[sq.b])
            s8 = cx.nxt(hd8)
            S.op("dve", lambda e: e.tensor_reduce(s8.t[:], sq.t[:, :].rearrange("p (h d) -> p h d", d=DH),
                                                  mybir.AxisListType.X, ALU.add), reads=[sq.b], writes=[s8.b])
            l8 = cx.nxt(hd8)
            S.op("act", lambda e: e.activation(l8.t[:], s8.t[:], AF.Ln, scale=1.0 / DH, bias=dn.eps.t[:, 0:1]),
                 reads=[s8.b, dn.eps.b], writes=[l8.b])
            r8 = cx.nxt(hd8)
            S.op("act", lambda e: e.activation(r8.t[:], l8.t[:], AF.Exp, scale=-0.5), reads=[l8.b], writes=[r8.b])
            qf = cx.nxt(sgs)
            S.op("dve", lambda e: e.tensor_tensor(qf.t[:, :].rearrange("p (h d) -> p h d", d=DH),
                                                  p.t[:, :].rearrange("p (h d) -> p h d", d=DH),
                                                  r8.t[:, :].unsqueeze(2).to_broadcast([128, 8, DH]), ALU.mult),
                 reads=[p.b, r8.b], writes=[qf.b])
            qn = cx.nxt(qnb)
            S.op("pool", lambda e: e.tensor_tensor(qn.t[:, :], qf.t[:, :], gtile.t[:, :, :].rearrange("p h d -> p (h d)"), ALU.mult),
                 reads=[qf.b, gtile.b], writes=[qn.b])
            tp = cx.nxt(dn.psT)
            for c4 in range(4):
                S.op("pe", lambda e, c4=c4: e.transpose(tp.t[:, c4 * 128:(c4 + 1) * 128], qn.t[:, c4 * 128:(c4 + 1) * 128],
                                                        dn.ident.t[:]), reads=[qn.b, dn.ident.b], writes=[tp.b])
            o = cx.nxt(oT)
            S.op("act", lambda e: e.copy(o.t[:], tp.t[:, 0:512]), reads=[tp.b], writes=[o.b])
            r0 = t0 + b * 128
            cx.store("o_qk1", dstT[hf * 512:(hf + 1) * 512, r0:r0 + 128].rearrange("(c p) t -> p c t", p=128), o,
                     o.t[:, :].rearrange("p (c t) -> p c t", c=4))

        for hf in range(2):
            wt = load_w8(WqB, b_wq, hf, "wq")
            for b in range(4):
                p = cx.nxt(pm)
                for kc in range(8):
                    S.op("pe", lambda e, p=p, kc=kc, b=b, wt=wt, hT=hT: e.matmul(
                        p.t[:], hT.t[:, kc, b * 128:(b + 1) * 128], wt.t[:, kc, :], start=(kc == 0), stop=(kc == 7)),
                        reads=[hT.b, wt.b], writes=[p.b])
                head_norm_store(p, qg, q1T, b, hf)
        for hf in range(4):
            wt = load_w8(WkvB, b_wkv, hf, "wkv")
            for b in range(4):
                p = cx.nxt(pm)
                for kc in range(8):
                    S.op("pe", lambda e, p=p, kc=kc, b=b, wt=wt, hT=hT: e.matmul(
                        p.t[:], hT.t[:, kc, b * 128:(b + 1) * 128], wt.t[:, kc, :], start=(kc == 0), stop=(kc == 7)),
                        reads=[hT.b, wt.b], writes=[p.b])
                if hf < 2:
                    head_norm_store(p, kg, k1T, b, hf)
                else:
                    o = cx.nxt(oT)
                    S.op("act", lambda e, o=o, p=p: e.copy(o.t[:], p.t[:]), reads=[p.b], writes=[o.b])
                    r0 = t0 + b * 128
                    cx.store("o_v1", v1[r0:r0 + 128, (hf - 2) * 512:(hf - 1) * 512], o, o.t[:])
        p = cx.nxt(pm)
        for kc in range(8):
            S.op("pe", lambda e, p=p, kc=kc, hT=hT: e.matmul(p.t[0:NH, :], Wf.t[:, kc, :], hT.t[:, kc, :],
                                                              start=(kc == 0), stop=(kc == 7)),
                 reads=[Wf.b, hT.b], writes=[p.b])
        f1 = cx.nxt(fl)
        S.op("act", lambda e, f1=f1, p=p: e.activation(f1.t[:], p.t[0:NH, :], AF.Exp, scale=-1.0, bias=nbf.t[:, 0:1]),
             reads=[p.b, nbf.b], writes=[f1.b])
        f2 = cx.nxt(fl)
        S.op("act", lambda e, f1=f1, f2=f2: e.activation(f2.t[:], f1.t[:], AF.Ln, bias=one.t[0:NH, 0:1]),
             reads=[f1.b, one.b], writes=[f2.b])
        S.op("dve", lambda e, f2=f2: e.tensor_scalar(f2.t[:], f2.t[:], -1.0, None, ALU.mult), reads=[f2.b], writes=[f2.b])
        cx.store("o_fl", flogT[:, t0:t0 + 512], f2, f2.t[:])
    return cx.finish()


def build_attn1(n_heads=NH):
    cx = Cx()
    S = cx.S
    qT = cx.din("qT", [D, TL], BF16)
    kT = cx.din("kT", [D, S_ALL], BF16)
    vr = cx.din("vr", [8, 128, 128 * 128], BF16)
    flog = cx.din("flog", [NH, S_ALL], F32)
    onehot = cx.din("onehot", [NH, 8], F32)
    negm = cx.din("negm", [128, 8 * 128], BF16)
    ident = cx.din("ident", [128, 128], BF16)
    sel = cx.din("sel", [128, 256], F32)
    mixT = cx.dout("mixT", [D, TL], BF16)
    kaug = cx.dint("kaug", [NH, 6, S_ALL], BF16)
    qaug = cx.dint("qaug", [NH, 6, TL], BF16)
    b_kaug = Buf("kaug")
    b_qaug = Buf("qaug")

    nm = cx.sb("nm", [128, 8 * 128], BF16)
    idt = cx.sb("idt", [128, 128], BF16)
    selt = cx.sb("selt", [128, 256], F32)
    oh = cx.sb("oh", [NH, 8], F32)
    for t_, src in ((nm, negm), (idt, ident), (selt, sel), (oh, onehot)):
        S.dma("sp", "cc_" + t_.b.name, lambda e, t_=t_, src=src: e.dma_start(out=t_.t[:], in_=src), writes=[t_.b])

    CH = 1024
    Fc = cx.sb("Fc", [NH, CH], F32, n=2)
    Fs = cx.sb("Fs", [NH, CH], F32, n=2)
    r1 = cx.sb("r1", [NH, CH], F32)
    onesf = cx.sb("onesf", [NH, CH], F32)
    S.op("pool", lambda e: e.memset(onesf.t[:], 1.0), writes=[onesf.b])
    ka = cx.sb("ka", [NH, 6, CH], BF16)
    Fq = cx.sb("Fq", [NH, TL], F32)
    qa = cx.sb("qa", [NH, 6, 512], BF16)
    carry = cx.sb("carry", [NH, 1], F32, n=2)
    S.op("pool", lambda e: e.memset(carry[1].t[:], 0.0), writes=[carry[1].b])
    S.op("pool", lambda e: e.memset(ka.t[:, 0:3, :], 1.0), writes=[ka.b])
    S.op("pool", lambda e: e.memset(qa.t[:, 3:6, :], 1.0), writes=[qa.b])
    for ci in range(S_ALL // CH):
        fc = Fc[ci % 2]
        fs = Fs[ci % 2]
        S.dma("sp", "ld_" + fc.b.name, lambda e, fc=fc, ci=ci: e.dma_start(out=fc.t[:], in_=flog[:, ci * CH:(ci + 1) * CH]),
              writes=[fc.b])
        cprev = carry[(ci + 1) % 2]
        ccur = carry[ci % 2]
        S.op("dve", lambda e, fs=fs, fc=fc, cprev=cprev: e.tensor_tensor_scan(
            fs.t[:], onesf.t[:], fc.t[:], cprev.t[:, 0:1], ALU.mult, ALU.add),
            reads=[onesf.b, fc.b, cprev.b], writes=[fs.b])
        S.op("dve", lambda e, fs=fs, ccur=ccur: e.tensor_copy(ccur.t[:], fs.t[:, CH - 1:CH]), reads=[fs.b], writes=[ccur.b])
        fview = fs.t[:, :].rearrange("h (c i) -> h c i", c=8)
        fqv = Fq.t[:, ci * 128:(ci + 1) * 128]
        for c in range(8):
            if c == 0:
                S.op("dve", lambda e, fview=fview, fqv=fqv, c=c: e.tensor_scalar(
                    fqv, fview[:, c, :], oh.t[:, c:c + 1], None, ALU.mult), reads=[fs.b, oh.b], writes=[Fq.b])
            else:
                S.op("dve", lambda e, fview=fview, fqv=fqv, c=c: e.scalar_tensor_tensor(
                    fqv, fview[:, c, :], oh.t[:, c:c + 1], fqv, ALU.mult, ALU.add), reads=[fs.b, oh.b, Fq.b], writes=[Fq.b])
        S.op("dve", lambda e, fs=fs: e.tensor_scalar(r1.t[:], fs.t[:], -1.0, None, ALU.mult), reads=[fs.b], writes=[r1.b])
        for part in range(3):
            S.op("dve", lambda e, part=part: e.tensor_copy(ka.t[:, 3 + part, :], r1.t[:]), reads=[r1.b], writes=[ka.b])
            if part < 2:
                S.op("dve", lambda e, part=part: e.tensor_tensor(r1.t[:], r1.t[:], ka.t[:, 3 + part, :], ALU.subtract),
                     reads=[r1.b, ka.b], writes=[r1.b])
        S.dma("sp", "st_kaug", lambda e, ci=ci: e.dma_start(out=kaug[:, :, ci * CH:(ci + 1) * CH], in_=ka.t[:]),
              reads=[ka.b], writes=[b_kaug])
    for qi in range(4):
        S.op("dve", lambda e, qi=qi: e.tensor_copy(r1.t[:, 0:512], Fq.t[:, qi * 512:(qi + 1) * 512]), reads=[Fq.b], writes=[r1.b])
        for part in range(3):
            S.op("dve", lambda e, part=part: e.tensor_copy(qa.t[:, part, :], r1.t[:, 0:512]), reads=[r1.b], writes=[qa.b])
            if part < 2:
                S.op("dve", lambda e, part=part: e.tensor_tensor(r1.t[:, 0:512], r1.t[:, 0:512], qa.t[:, part, :], ALU.subtract),
                     reads=[r1.b, qa.b], writes=[r1.b])
        S.dma("sp", "st_qaug", lambda e, qi=qi: e.dma_start(out=qaug[:, :, qi * 512:(qi + 1) * 512], in_=qa.t[:]),
              reads=[qa.b], writes=[b_qaug])

    kTs = cx.sb("kTs", [128, S_ALL], BF16, n=2)
    qs = cx.sb("qs", [128, TL], BF16, n=2)
    vs = cx.sb("vs", [128, 128 * 130], BF16, n=2)
    mx = cx.sb("mx", [128, TL], BF16, n=2)
    for sl in range(2):
        v3 = vs[sl].t[:, :].rearrange("p (k c) -> p k c", c=130)
        S.op("pool", lambda e, v3=v3: e.memset(v3[:, :, 64:65], 1.0), writes=[vs[sl].b])
        S.op("pool", lambda e, v3=v3: e.memset(v3[:, :, 129:130], 1.0), writes=[vs[sl].b])
    zp = cx.ps("zp", [128, 512], F32, n=3)
    op_ = cx.ps("op", [128, 512], F32, n=2)
    bcp = cx.ps("bcp", [128, 512], F32, n=1)
    pb_ = cx.sb("pb", [128, 512], BF16, n=3)
    osb = cx.sb("osb", [128, 512], F32, n=1)
    rbs = cx.sb("rbs", [128, 512], F32, n=1)

    def load_head(h):
        sl = h % 2
        k_, q_ = kTs[sl], qs[sl]
        S.dma("sp", "ldq%d" % sl, lambda e: e.dma_start(out=q_.t[0:64, :], in_=qT[h * 64:(h + 1) * 64, :]), writes=[q_.b])
        S.dma("sp", "ldq%d" % sl, lambda e: e.dma_start(out=q_.t[64:70, :], in_=qaug[h]), reads=[b_qaug], writes=[q_.b])
        S.dma("sp", "ldk%d" % sl, lambda e: e.dma_start(out=k_.t[64:70, :], in_=kaug[h]), reads=[b_kaug], writes=[k_.b])
        for part in range(4):
            c0 = part * 4096
            S.dma("sp", "ldk%d" % sl,
                  lambda e, c0=c0: e.dma_start(out=k_.t[0:64, c0:c0 + 4096], in_=kT[h * 64:(h + 1) * 64, c0:c0 + 4096]),
                  writes=[k_.b])

    def load_v(hp):
        v_ = vs[hp % 2]
        v3 = v_.t[:, :].rearrange("p (k c) -> p k c", c=130)
        src = vr[hp].rearrange("p (k c) -> p k c", c=128)
        for part in range(8):
            k0 = part * 16
            for hh in range(2):
                S.dma("sp", "ldv%d" % (hp % 2),
                      lambda e, k0=k0, hh=hh: e.dma_start(out=v3[:, k0:k0 + 16, 65 * hh:65 * hh + 64],
                                                          in_=src[:, k0:k0 + 16, 64 * hh:64 * hh + 64]),
                      writes=[v_.b])

    load_v(0)
    load_head(0)

    def do_head(h):
        sl = h % 2
        hp = h // 2
        odd = h % 2
        k_, q_ = kTs[sl], qs[sl]
        v_ = vs[hp % 2]
        m_ = mx[hp % 2]
        if h + 1 < n_heads:
            if (h + 1) % 2 == 0:
                load_v((h + 1) // 2)
            load_head(h + 1)
        items = []
        for J in range(4):
            nkb = 32 * J + 32
            for kb in range(nkb - 1, -1, -1):
                items.append((J, kb, nkb))
        st = {}

        def s1(it):
            J, kb, nkb = it
            r = kb - 32 * J
            c0 = 128 * (r // 8) if r >= 0 else 0
            z = cx.nxt(zp)
            p_ = cx.nxt(pb_)
            S.op("pe", lambda e: e.matmul(z.t[:, c0:512], k_.t[0:70, kb * 128:(kb + 1) * 128],
                                          q_.t[0:70, 512 * J + c0:512 * J + 512], start=True, stop=(r < 0)),
                 reads=[k_.b, q_.b], writes=[z.b])
            if r >= 0:
                i = r % 8
                S.op("pe", lambda e: e.matmul(z.t[:, c0:c0 + 128], idt.t[:], nm.t[:, i * 128:(i + 1) * 128],
                                              start=False, stop=True), reads=[idt.b, nm.b], writes=[z.b])
            S.op("act", lambda e: e.activation(p_.t[:, c0:512], z.t[:, c0:512], AF.Exp), reads=[z.b], writes=[p_.b])
            st[it] = (c0, p_)

        def s2(it):
            J, kb, nkb = it
            c0, p_ = st.pop(it)
            o_ = op_[J % 2]
            if odd:
                S.op("pe", lambda e: e.matmul(o_.t[:, c0:512], v_.t[:, kb * 130 + 1:kb * 130 + 129], p_.t[:, c0:512],
                                              start=(kb == nkb - 1), stop=(kb == 0)), reads=[v_.b, p_.b], writes=[o_.b])
            else:
                S.op("pe", lambda e: e.matmul(o_.t[0:65, c0:512], v_.t[:, kb * 130:kb * 130 + 65], p_.t[:, c0:512],
                                              start=(kb == nkb - 1), stop=(kb == 0)), reads=[v_.b, p_.b], writes=[o_.b])
            if kb == 0:
                ob = cx.nxt(osb)
                rb = cx.nxt(rbs)
                bc = bcp[0]
                if odd:
                    S.op("act", lambda e: e.copy(ob.t[32:64, :], o_.t[32:64, :]), reads=[o_.b], writes=[ob.b])
                    S.op("act", lambda e: e.copy(ob.t[64:128, :], o_.t[64:128, :]), reads=[o_.b], writes=[ob.b])
                    S.op("pe", lambda e: e.matmul(bc.t[:, :], selt.t[32:64, 128:256], ob.t[32:64, :], start=True, stop=True),
                         reads=[selt.b, ob.b], writes=[bc.b])
                    S.op("dve", lambda e: e.reciprocal(rb.t[64:128, :], bc.t[64:128, :]), reads=[bc.b], writes=[rb.b])
                    S.op("dve", lambda e: e.tensor_tensor(m_.t[64:128, 512 * J:512 * J + 512], ob.t[64:128, :], rb.t[64:128, :], ALU.mult),
                         reads=[ob.b, rb.b], writes=[m_.b])
                else:
                    S.op("act", lambda e: e.copy(ob.t[0:65, :], o_.t[0:65, :]), reads=[o_.b], writes=[ob.b])
                    S.op("pe", lambda e: e.matmul(bc.t[0:64, :], selt.t[64:65, 0:64], ob.t[64:65, :], start=True, stop=True),
                         reads=[selt.b, ob.b], writes=[bc.b])
                    S.op("dve", lambda e: e.reciprocal(rb.t[0:64, :], bc.t[0:64, :]), reads=[bc.b], writes=[rb.b])
                    S.op("dve", lambda e: e.tensor_tensor(m_.t[0:64, 512 * J:512 * J + 512], ob.t[0:64, :], rb.t[0:64, :], ALU.mult),
                         reads=[ob.b, rb.b], writes=[m_.b])

        n = len(items)
        for tau in range(n + 1):
            if tau < n:
                s1(items[tau])
            if tau - 1 >= 0:
                s2(items[tau - 1])
        if odd:
            cx.store("o_mix", mixT[hp * 128:(hp + 1) * 128, :], m_, m_.t[:])

    for h in range(n_heads):
        do_head(h)
    return cx.finish()


_PROGS = {}


def _prog(name, fn):
    if name not in _PROGS:
        _PROGS[name] = fn()
    return _PROGS[name]


def _shard_tok(a):
    F_ = a.shape[-1]
    r = a.reshape(16, 8, 128, F_)
    return [np.ascontiguousarray(r[:, c].reshape(TL, F_)) for c in range(NC_)]


def _gather_T(parts):
    R = parts[0].shape[0]
    out = np.zeros((R, 16, 8, 128), dtype=parts[0].dtype)
    for c in range(NC_):
        out[:, :, c, :] = parts[c].reshape(R, 16, 128)
    return out.reshape(R, S_ALL)


def _gather_v(parts):
    va = np.zeros((16, 8, 128, D), dtype=parts[0].dtype)
    for c in range(NC_):
        va[:, c] = parts[c].reshape(16, 128, D)
    va = va.reshape(128, 128, 8, 128)
    return np.ascontiguousarray(va.transpose(2, 1, 0, 3)).reshape(8, 128, 128 * 128)


def _consts():
    ar = np.arange(128)
    c = {}
    c["ident"] = np.eye(128, dtype=np.float32).astype(NPBF)
    c["tri"] = (ar[:, None] >= ar[None, :]).astype(np.float32).astype(NPBF)
    c["omt"] = (ar[:, None] < ar[None, :]).astype(np.float32).astype(NPBF)
    masks, negm, oh = [], [], []
    for cc in range(NC_):
        m = np.zeros((128, 8, 128), np.float32)
        n = np.full((128, 8, 128), -30000.0, np.float32)
        for i in range(8):
            if i < cc:
                m[:, i, :] = 1.0
                n[:, i, :] = 0.0
            elif i == cc:
                m[:, i, :] = (ar[:, None] < ar[None, :])
                n[:, i, :] = np.where(ar[:, None] <= ar[None, :], 0.0, -30000.0)
        masks.append(m.reshape(128, 1024).astype(NPBF))
        negm.append(n.reshape(128, 1024).astype(NPBF))
        o = np.zeros((NH, 8), np.float32)
        o[:, cc] = 1.0
        oh.append(o)
    c["masks"], c["negm"], c["onehot"] = masks, negm, oh
    sel = np.zeros((128, 256), np.float32)
    sel[64, 0:64] = 1.0
    sel[63, 192:256] = 1.0
    c["sel"] = sel
    return c


def _run(nc, in_maps):
    return run_bass_kernel_spmd(nc, in_maps, core_ids=list(range(NC_))).results


def kernel(x, p, attn_norm_g, sb_w_qkv, sb_w_o, shared_norm_g, shared_w_kvf, shared_b_f, shared_k_norm_g,
           fox_w_q, fox_q_norm_g, fox_w_o, ffn_norm_g, ffn_w_gu, ffn_w_d, ple_norm_g, ple_w_gate, ple_w_proj):
    f32 = lambda a: np.ascontiguousarray(np.asarray(a, dtype=np.float32))
    x = f32(x)
    p = f32(p)
    C = _consts()
    xs = _shard_tok(x[0])
    p0 = _shard_tok(p[0, 0])
    p1 = _shard_tok(p[1, 0])
    r1 = _run(_prog("pre0", build_pre0),
              [{"x": xs[c], "g": f32(attn_norm_g[0]), "w": f32(sb_w_qkv[0]), "ident": C["ident"]} for c in range(NC_)])
    kT_all = _gather_T([r["kT"] for r in r1])
    vr = _gather_v([r["v"] for r in r1])
    r2 = _run(_prog("attn0", build_attn0),
              [{"qT": r1[c]["qT"], "kT": kT_all, "vr": vr, "masks": C["masks"][c], "tri": C["tri"], "omt": C["omt"]}
               for c in range(NC_)])

    def post_in(c, h, mix, pl, w_o, li):
        return {"h": h, "mixT": mix, "p": pl, "w_o": f32(w_o), "ffn_g": f32(ffn_norm_g[li]), "w_gu": f32(ffn_w_gu[li]),
                "w_d": f32(ffn_w_d[li]), "ple_g": f32(ple_norm_g[li]), "w_gate": f32(ple_w_gate[li]),
                "w_proj": f32(ple_w_proj[li]), "ident": C["ident"]}

    in3 = []
    for c in range(NC_):
        d = post_in(c, xs[c], r2[c]["mixT"], p0[c], sb_w_o[0], 0)
        d.update({"a_g": f32(attn_norm_g[1]), "w_q": f32(fox_w_q[0]), "qn_g": f32(fox_q_norm_g[0]),
                  "sh_g": f32(shared_norm_g), "w_kvf": f32(shared_w_kvf), "b_f": f32(shared_b_f),
                  "kn_g": f32(shared_k_norm_g)})
        in3.append(d)
    r3 = _run(_prog("post_n", lambda: build_post(True)), in3)
    k1_all = _gather_T([r["k1T"] for r in r3])
    vr1 = _gather_v([r["v1"] for r in r3])
    fl_all = _gather_T([r["flogT"] for r in r3])
    r4 = _run(_prog("attn1", build_attn1),
              [{"qT": r3[c]["q1T"], "kT": k1_all, "vr": vr1, "flog": fl_all, "onehot": C["onehot"][c],
                "negm": C["negm"][c], "ident": C["ident"], "sel": C["sel"]} for c in range(NC_)])
    r5 = _run(_prog("post_l", lambda: build_post(False)),
              [post_in(c, r3[c]["hout"], r4[c]["mixT"], p1[c], fox_w_o[0], 1) for c in range(NC_)])
    out = np.zeros((16, 8, 128, D), np.float32)
    for c in range(NC_):
        out[:, c] = r5[c]["hout"].reshape(16, 128, D)
    return out.reshape(1, S_ALL, D)
```

```python
import numpy as np
import ml_dtypes
from contextlib import ExitStack
import concourse.bass as bass
import concourse.mybir as mybir
from concourse.bass_utils import run_bass_kernel_spmd

F32 = mybir.dt.float32
BF16 = mybir.dt.bfloat16
AF = mybir.ActivationFunctionType
ALU = mybir.AluOpType
NPBF = ml_dtypes.bfloat16

NC_ = 8
D = 1024
S_ALL = 16384
TL = 2048
NH = 16
DH = 64
DFF = 2816
PLE = 256
EPS = 1e-6
EPOCH = 4096


class Buf:
    __slots__ = ("name", "w", "r", "wm")

    def __init__(self, name, multi=False):
        self.name = name
        self.w = None
        self.r = []
        self.wm = {} if multi else None


class Tile:
    __slots__ = ("t", "b")

    def __init__(self, t, name):
        self.t = t
        self.b = Buf(name)


class Sched:
    ENG = ("pe", "act", "dve", "pool", "sp")

    def __init__(self, nc, es):
        self.nc = nc
        self.es = es
        self.lists = {e: [] for e in self.ENG}
        self.sems = {}
        self.cnt = {}
        self.alias = {}
        self.waited = {e: {} for e in self.ENG}
        self.ecount = {e: 0 for e in self.ENG}
        self.nsem = 0

    def _mksem(self, key):
        self.nsem += 1
        self.sems[key] = self.es.enter_context(self.nc.semaphore("s%d" % self.nsem))
        self.cnt[key] = 0

    def _deps(self, eng, reads, writes):
        deps = {}

        def add(tok):
            if tok is None:
                return
            k, v = tok
            if deps.get(k, 0) < v:
                deps[k] = v

        for b in reads:
            add(b.w)
            if b.wm is not None:
                for t in b.wm.items():
                    add(t)
        for b in writes:
            add(b.w)
            for t in b.r:
                add(t)
        out = []
        w = self.waited[eng]
        for k, v in deps.items():
            if eng == "pe" and k.startswith("E_pe#"):
                continue
            if w.get(k, 0) >= v:
                continue
            w[k] = v
            out.append((k, v))
        return out

    @staticmethod
    def _commit(tok, reads, writes):
        for b in writes:
            if b.wm is not None:
                if b.wm.get(tok[0], 0) < tok[1]:
                    b.wm[tok[0]] = tok[1]
                continue
            b.w = tok
            b.r = []
        for b in reads:
            b.r.append(tok)

    def op(self, eng, fn, reads=(), writes=()):
        deps = self._deps(eng, reads, writes)
        ep = self.ecount[eng] // EPOCH
        key = "E_%s#%d" % (eng, ep)
        if key not in self.sems:
            self._mksem(key)
        self.ecount[eng] += 1
        self.cnt[key] += 1
        tok = (key, self.cnt[key])
        sems = self.sems

        def thunk(e, deps=deps, fn=fn, key=key):
            for k, v in deps:
                e.wait_ge(sems[k], v)
            fn(e).then_inc(sems[key], 1)

        self.lists[eng].append(thunk)
        self._commit(tok, reads, writes)
        return tok

    def dma(self, eng, semname, fn, reads=(), writes=()):
        deps = self._deps(eng, reads, writes)
        key = self.alias.get(semname)
        if key is None or self.cnt[key] >= 16 * 240:
            n = 0 if key is None else int(key.split("#")[1]) + 1
            key = "D_%s#%d" % (semname, n)
            self.alias[semname] = key
            self._mksem(key)
        self.cnt[key] += 16
        tok = (key, self.cnt[key])
        sems = self.sems

        def thunk(e, deps=deps, fn=fn, key=key):
            for k, v in deps:
                e.wait_ge(sems[k], v)
            fn(e).then_inc(sems[key], 16)

        self.lists[eng].append(thunk)
        self._commit(tok, reads, writes)
        return tok

    def wait_all(self, eng, toks):
        sems = self.sems
        toks = list(toks)

        def thunk(e):
            for k, v in toks:
                e.wait_ge(sems[k], v)

        self.lists[eng].append(thunk)

    def emit(self):
        L = self.lists
        with self.nc.Block() as block:
            @block.tensor
            def _(e):
                for t in L["pe"]:
                    t(e)

            @block.scalar
            def _(e):
                for t in L["act"]:
                    t(e)

            @block.vector
            def _(e):
                for t in L["dve"]:
                    t(e)

            @block.gpsimd
            def _(e):
                for t in L["pool"]:
                    t(e)

            @block.sync
            def _(e):
                for t in L["sp"]:
                    t(e)


class Cx:
    def __init__(self):
        self.nc = bass.Bass("TRN2", target_bir_lowering=False)
        self.es = ExitStack()
        self.S = Sched(self.nc, self.es)
        self.out_toks = {}
        self.rr = {}

    def din(self, name, shape, dt):
        return self.nc.dram_tensor(name, list(shape), dt, kind="ExternalInput").ap()

    def dout(self, name, shape, dt):
        return self.nc.dram_tensor(name, list(shape), dt, kind="ExternalOutput").ap()

    def dint(self, name, shape, dt):
        return self.nc.dram_tensor(name, list(shape), dt, kind="Internal").ap()

    def sb(self, name, shape, dt, n=None):
        if n is None:
            return Tile(self.es.enter_context(self.nc.sbuf_tensor("sb_" + name, list(shape), dt)), name)
        return [Tile(self.es.enter_context(self.nc.sbuf_tensor("sb_%s%d" % (name, i), list(shape), dt)),
                     "%s%d" % (name, i)) for i in range(n)]

    def ps(self, name, shape, dt, n=None):
        if n is None:
            return Tile(self.es.enter_context(self.nc.psum_tensor("ps_" + name, list(shape), dt)), name)
        return [Tile(self.es.enter_context(self.nc.psum_tensor("ps_%s%d" % (name, i), list(shape), dt)),
                     "%s%d" % (name, i)) for i in range(n)]

    def nxt(self, lst, key=None):
        key = key or id(lst)
        i = self.rr.get(key, 0)
        self.rr[key] = i + 1
        return lst[i % len(lst)]

    def store(self, semname, out_ap, tile, in_ap, eng="pool"):
        tok = self.S.dma(eng, "st_" + tile.b.name, lambda e: e.dma_start(out=out_ap, in_=in_ap), reads=[tile.b])
        self.out_toks[tok[0]] = max(self.out_toks.get(tok[0], 0), tok[1])

    def finish(self):
        self.S.wait_all("sp", list(self.out_toks.items()))
        self.S.emit()
        self.es.close()
        return self.nc


class Dense:
    def __init__(self, cx, ident_ap):
        self.cx = cx
        S = cx.S
        self.ident = cx.sb("ident", [128, 128], BF16)
        S.dma("sp", "const1", lambda e: e.dma_start(out=self.ident.t[:], in_=ident_ap), writes=[self.ident.b])
        self.junk = cx.sb("junk", [128, 1024], BF16, n=2)
        self.ss = cx.sb("ss", [128, 1], F32, n=4)
        self.lnv = cx.sb("lnv", [128, 1], F32, n=4)
        self.rstd = cx.sb("rstd", [128, 1], F32, n=4)
        self.hn = cx.sb("hn", [128, 1024], BF16, n=2)
        self.psT = cx.ps("psT", [128, 1024], BF16, n=2)
        self.wst = cx.sb("wst", [128, 512], F32, n=4)
        self.gv = {}
        self.evq = 0

    def load_gain(self, name, g_ap):
        cx = self.cx
        t = cx.sb("g_" + name, [128, 8], F32)
        cx.S.dma("sp", "cg_" + name, lambda e: e.dma_start(out=t.t[:], in_=g_ap.rearrange("(k p) -> p k", p=128),
                                                      allow_slow_non_contiguous=True), writes=[t.b])
        self.gv[name] = t
        return t

    def prep_weight(self, w_ap, K, N, dst_fn, gain=None, post=None, c0=0, c1=None):
        cx = self.cx
        S = cx.S
        c1 = N if c1 is None else c1
        for kc in range(K // 128):
            for n0 in range(c0, c1, 512):
                wd = min(512, c1 - n0)
                st = cx.nxt(self.wst)
                S.dma("sp", "wst_" + st.b.name,
                      lambda e, st=st, kc=kc, n0=n0, wd=wd: e.dma_start(
                          out=st.t[:, 0:wd], in_=w_ap[kc * 128:(kc + 1) * 128, n0:n0 + wd]),
                      writes=[st.b])
                dt_, dap = dst_fn(kc, n0, wd)
                eng = "pool" if (self.evq % 2 == 0) else "dve"
                self.evq += 1
                pv = post(n0) if post is not None else None
                if gain is not None:
                    g = gain
                    if pv is not None:
                        fn = lambda e, st=st, dap=dap, kc=kc, wd=wd, g=g, pv=pv: e.tensor_scalar(
                            dap, st.t[:, 0:wd], g.t[:, kc:kc + 1], pv, ALU.mult, ALU.mult)
                    else:
                        fn = lambda e, st=st, dap=dap, kc=kc, wd=wd, g=g: e.tensor_scalar(
                            dap, st.t[:, 0:wd], g.t[:, kc:kc + 1], None, ALU.mult)
                    S.op(eng, fn, reads=[st.b, g.b], writes=[dt_.b])
                else:
                    S.op(eng, lambda e, st=st, dap=dap, wd=wd: e.tensor_copy(dap, st.t[:, 0:wd]),
                         reads=[st.b], writes=[dt_.b])

    def norm_T(self, h, hnT, col0):
        cx = self.cx
        S = cx.S
        junk = cx.nxt(self.junk)
        ss = cx.nxt(self.ss)
        lnv = cx.nxt(self.lnv)
        rstd = cx.nxt(self.rstd)
        hn = cx.nxt(self.hn)
        pT = cx.nxt(self.psT)
        S.op("act", lambda e: e.activation(junk.t[:], h.t[:], AF.Square, accum_out=ss.t[:]),
             reads=[h.b], writes=[junk.b, ss.b])
        S.op("act", lambda e: e.activation(lnv.t[:], ss.t[:], AF.Ln, scale=1.0 / D, bias=self.eps.t[:, 0:1]),
             reads=[ss.b, self.eps.b], writes=[lnv.b])
        S.op("act", lambda e: e.activation(rstd.t[:], lnv.t[:], AF.Exp, scale=-0.5),
             reads=[lnv.b], writes=[rstd.b])
        S.op("dve", lambda e: e.tensor_scalar(hn.t[:], h.t[:], rstd.t[:, 0:1], None, ALU.mult),
             reads=[h.b, rstd.b], writes=[hn.b])
        for kc in range(8):
            S.op("pe", lambda e, kc=kc: e.transpose(pT.t[:, kc * 128:(kc + 1) * 128],
                                                    hn.t[:, kc * 128:(kc + 1) * 128], self.ident.t[:]),
                 reads=[hn.b, self.ident.b], writes=[pT.b])
        S.op("dve", lambda e: e.tensor_copy(hnT.t[:, :, col0:col0 + 128],
                                            pT.t[:, :].rearrange("p (k t) -> p k t", k=8)),
             reads=[pT.b], writes=[hnT.b])

    def consts(self):
        cx = self.cx
        self.eps = cx.sb("epsc", [128, 1], F32)
        cx.S.op("pool", lambda e: e.memset(self.eps.t[:], EPS), writes=[self.eps.b])


def build_pre0():
    cx = Cx()
    S = cx.S
    x = cx.din("x", [TL, D], F32)
    g = cx.din("g", [D], F32)
    w = cx.din("w", [D, 3 * D], F32)
    ident = cx.din("ident", [128, 128], BF16)
    qT = cx.dout("qT", [D, TL], BF16)
    kT = cx.dout("kT", [D, TL], BF16)
    v = cx.dout("v", [TL, D], BF16)
    dn = Dense(cx, ident)
    dn.consts()
    gt = dn.load_gain("a", g)
    Wb = cx.sb("Wb", [128, 8, 3 * D], BF16)
    dn.prep_weight(w, D, 3 * D, lambda kc, n0, wd: (Wb, Wb.t[:, kc, n0:n0 + wd]), gain=gt,
                   post=lambda n0: (0.125 if n0 < D else 1.0))
    hblk = cx.sb("hblk", [128, D], F32, n=3)
    hnT = cx.sb("hnT", [128, 8, 512], BF16, n=2)
    pm = cx.ps("pm", [128, 512], F32, n=4)
    ost = cx.sb("ost", [128, 512], BF16, n=4)
    ev = 0
    for gi in range(4):
        hT = cx.nxt(hnT)
        for b in range(4):
            h = cx.nxt(hblk)
            r0 = (gi * 4 + b) * 128
            S.dma("sp", "ld_" + h.b.name, lambda e, h=h, r0=r0: e.dma_start(out=h.t[:], in_=x[r0:r0 + 128, :]),
                  writes=[h.b])
            dn.norm_T(h, hT, b * 128)
        for n in range(16):
            p = cx.nxt(pm)
            for kc in range(8):
                S.op("pe", lambda e, p=p, kc=kc, n=n, hT=hT: e.matmul(
                    p.t[:], Wb.t[:, kc, n * 128:(n + 1) * 128], hT.t[:, kc, :], start=(kc == 0), stop=(kc == 7)),
                    reads=[Wb.b, hT.b], writes=[p.b])
            o = cx.nxt(ost)
            if ev % 2 == 0:
                S.op("act", lambda e, o=o, p=p: e.copy(o.t[:], p.t[:]), reads=[p.b], writes=[o.b])
            else:
                S.op("dve", lambda e, o=o, p=p: e.tensor_copy(o.t[:], p.t[:]), reads=[p.b], writes=[o.b])
            ev += 1
            dst = qT if n < 8 else kT
            rr = (n % 8) * 128
            cx.store("o_qk", dst[rr:rr + 128, gi * 512:(gi + 1) * 512], o, o.t[:])
        for b in range(4):
            for hf in range(2):
                p = cx.nxt(pm)
                for kc in range(8):
                    S.op("pe", lambda e, p=p, kc=kc, b=b, hf=hf, hT=hT: e.matmul(
                        p.t[:], hT.t[:, kc, b * 128:(b + 1) * 128],
                        Wb.t[:, kc, 2 * D + hf * 512:2 * D + (hf + 1) * 512], start=(kc == 0), stop=(kc == 7)),
                        reads=[Wb.b, hT.b], writes=[p.b])
                o = cx.nxt(ost)
                if ev % 2 == 0:
                    S.op("act", lambda e, o=o, p=p: e.copy(o.t[:], p.t[:]), reads=[p.b], writes=[o.b])
                else:
                    S.op("dve", lambda e, o=o, p=p: e.tensor_copy(o.t[:], p.t[:]), reads=[p.b], writes=[o.b])
                ev += 1
                r0 = (gi * 4 + b) * 128
                cx.store("o_v", v[r0:r0 + 128, hf * 512:(hf + 1) * 512], o, o.t[:])
    return cx.finish()


def build_attn0(n_pairs=8):
    cx = Cx()
    S = cx.S
    qT = cx.din("qT", [D, TL], BF16)
    kT = cx.din("kT", [D, S_ALL], BF16)
    vr = cx.din("vr", [8, 128, 128 * 128], BF16)
    masks = cx.din("masks", [128, 8 * 128], BF16)
    tri = cx.din("tri", [128, 128], BF16)
    omt = cx.din("omt", [128, 128], BF16)
    mixT = cx.dout("mixT", [D, TL], BF16)

    mk = cx.sb("mk", [128, 8 * 128], BF16)
    trt = cx.sb("trt", [128, 128], BF16)
    omtt = cx.sb("omtt", [128, 128], BF16)
    S.dma("sp", "const3", lambda e: e.dma_start(out=mk.t[:], in_=masks), writes=[mk.b])
    S.dma("sp", "const4", lambda e: e.dma_start(out=trt.t[:], in_=tri), writes=[trt.b])
    S.dma("sp", "const5", lambda e: e.dma_start(out=omtt.t[:], in_=omt), writes=[omtt.b])

    one = cx.sb("one", [128, 1], F32)
    S.op("pool", lambda e: e.memset(one.t[:], 1.0), writes=[one.b])
    kTs = cx.sb("kTs", [128, S_ALL], BF16, n=2)
    vs = cx.sb("vs", [128, 128 * 128], BF16, n=2)
    qs = cx.sb("qs", [128, TL], BF16, n=2)
    mx = cx.sb("mx", [128, TL], BF16, n=2)
    z2 = cx.ps("z2", [128, 1024], F32, n=2)
    C2 = cx.ps("C2", [128, 1024], F32)
    op_ = cx.ps("op", [128, 512], F32, n=2)
    e2 = cx.sb("e2", [128, 1024], F32, n=5)
    sp2 = cx.sb("sp2", [128, 1024], BF16, n=5)
    x2 = cx.sb("x2", [128, 1024], BF16, n=3)
    w2 = cx.sb("w2", [128, 1024], BF16, n=3)

    def V(t, c0, w=None):
        v = t.t[:, :].rearrange("p (h c) -> p h c", h=2)
        return v[:, :, c0:512] if w is None else v[:, :, c0:c0 + w]

    def load_pair(hp):
        sl = hp % 2
        k_, v_, q_ = kTs[sl], vs[sl], qs[sl]
        S.dma("sp", "ldq%d" % sl, lambda e: e.dma_start(out=q_.t[:], in_=qT[hp * 128:(hp + 1) * 128, :]),
              writes=[q_.b])
        for part in range(4):
            c0 = part * 4096
            S.dma("sp", "ldk%d" % sl,
                  lambda e, c0=c0: e.dma_start(out=k_.t[:, c0:c0 + 4096], in_=kT[hp * 128:(hp + 1) * 128, c0:c0 + 4096]),
                  writes=[k_.b])
            S.dma("sp", "ldv%d" % sl,
                  lambda e, c0=c0: e.dma_start(out=v_.t[:, c0:c0 + 4096], in_=vr[hp, :, c0:c0 + 4096]),
                  writes=[v_.b])

    load_pair(0)

    def do_pair(hp):
        sl = hp % 2
        k_, v_, q_, m_ = kTs[sl], vs[sl], qs[sl], mx[sl]
        if hp + 1 < n_pairs:
            load_pair(hp + 1)
        items = []
        for J in range(4):
            nkb = 32 * J + 32
            for kb in range(nkb - 1, -1, -1):
                items.append((J, kb, nkb))
        st = {}

        def s1(it):
            J, kb, nkb = it
            r = kb - 32 * J
            c0 = 128 * (r // 8) if r >= 0 else 0
            z = cx.nxt(z2)
            e_ = cx.nxt(e2)
            sp_ = cx.nxt(sp2)
            for hh in range(2):
                pb = 64 * hh
                S.op("pe", lambda e, hh=hh, pb=pb: e.matmul(
                    z.t[:, 512 * hh + c0:512 * hh + 512], k_.t[pb:pb + 64, kb * 128:(kb + 1) * 128],
                    q_.t[pb:pb + 64, 512 * J + c0:512 * J + 512], start=True, stop=True),
                    reads=[k_.b, q_.b], writes=[z.b])
            S.op("act", lambda e: e.activation(V(e_, c0), V(z, c0), AF.Exp), reads=[z.b], writes=[e_.b])
            if r >= 0:
                i = r % 8
                S.op("pool", lambda e: e.tensor_tensor(
                    V(e_, c0, 128), V(e_, c0, 128),
                    mk.t[:, i * 128:(i + 1) * 128].unsqueeze(1).to_broadcast([128, 2, 128]), ALU.mult),
                    reads=[e_.b, mk.b], writes=[e_.b])
            S.op("act", lambda e: e.activation(V(sp_, c0), V(e_, c0), AF.Ln, bias=one.t[:, 0:1]),
                 reads=[e_.b, one.b], writes=[sp_.b])
            st[it] = [c0, e_, sp_, None, None]

        def s2(it):
            J, kb, nkb = it
            c0, e_, sp_, _, _ = st[it]
            x_ = cx.nxt(x2)
            for hh in range(2):
                S.op("pe", lambda e, hh=hh: e.matmul(C2.t[:, 512 * hh + c0:512 * hh + 512], trt.t[:],
                                                     sp_.t[:, 512 * hh + c0:512 * hh + 512],
                                                     start=(kb == nkb - 1), stop=True,
                                                     skip_group_check=(kb != nkb - 1)),
                     reads=[trt.b, sp_.b], writes=[C2.b])
            S.op("act", lambda e: e.activation(V(x_, c0), V(C2, c0), AF.Exp, scale=-1.0), reads=[C2.b], writes=[x_.b])
            st[it][3] = x_

        def s3(it):
            J, kb, nkb = it
            c0, e_, sp_, x_, _ = st[it]
            w_ = cx.nxt(w2)
            if kb > 0:
                for hh in range(2):
                    S.op("pe", lambda e, hh=hh: e.matmul(C2.t[:, 512 * hh + c0:512 * hh + 512], omtt.t[:],
                                                         sp_.t[:, 512 * hh + c0:512 * hh + 512], start=False, stop=True,
                                                         skip_group_check=True),
                         reads=[omtt.b, sp_.b], writes=[C2.b])
            S.op("dve", lambda e: e.tensor_tensor(V(w_, c0), V(e_, c0), V(x_, c0), ALU.mult),
                 reads=[e_.b, x_.b], writes=[w_.b])
            st[it][4] = w_

        def s4(it):
            J, kb, nkb = it
            c0, e_, sp_, x_, w_ = st.pop(it)
            o_ = op_[J % 2]
            for hh in range(2):
                pb = 64 * hh
                S.op("pe", lambda e, hh=hh, pb=pb: e.matmul(
                    o_.t[pb:pb + 64, c0:512], v_.t[:, kb * 128 + pb:kb * 128 + pb + 64],
                    w_.t[:, 512 * hh + c0:512 * hh + 512], start=(kb == nkb - 1), stop=(kb == 0)),
                    reads=[v_.b, w_.b], writes=[o_.b])
            if kb == 0:
                S.op("act", lambda e: e.copy(m_.t[:, 512 * J:512 * J + 512], o_.t[:, :]), reads=[o_.b], writes=[m_.b])

        n = len(items)
        for tau in range(-1, n + 3):
            if 0 <= tau - 2 < n:
                s3(items[tau - 2])
            if 0 <= tau - 1 < n:
                s2(items[tau - 1])
            if 0 <= tau + 1 < n:
                s1(items[tau + 1])
            if 0 <= tau - 3 < n:
                s4(items[tau - 3])
        cx.store("o_mix", mixT[hp * 128:(hp + 1) * 128, :], m_, m_.t[:])

    for hp in range(n_pairs):
        do_pair(hp)
    return cx.finish()


def build_post(nxt):
    cx = Cx()
    S = cx.S
    h_in = cx.din("h", [TL, D], F32)
    mixT = cx.din("mixT", [D, TL], BF16)
    p_in = cx.din("p", [TL, PLE], F32)
    w_o = cx.din("w_o", [D, D], F32)
    ffn_g = cx.din("ffn_g", [D], F32)
    w_gu = cx.din("w_gu", [D, 2 * DFF], F32)
    w_d = cx.din("w_d", [DFF, D], F32)
    ple_g = cx.din("ple_g", [D], F32)
    w_gate = cx.din("w_gate", [D, D], F32)
    w_proj = cx.din("w_proj", [PLE, D], F32)
    ident = cx.din("ident", [128, 128], BF16)
    hout = cx.dout("hout", [TL, D], F32)
    if nxt:
        a_g = cx.din("a_g", [D], F32)
        w_q = cx.din("w_q", [D, D], F32)
        qn_g = cx.din("qn_g", [DH], F32)
        sh_g = cx.din("sh_g", [D], F32)
        w_kvf = cx.din("w_kvf", [D, 2 * D + NH], F32)
        b_f = cx.din("b_f", [NH], F32)
        kn_g = cx.din("kn_g", [DH], F32)
        q1T = cx.dout("q1T", [D, TL], BF16)
        k1T = cx.dout("k1T", [D, TL], BF16)
        v1 = cx.dout("v1", [TL, D], BF16)
        flogT = cx.dout("flogT", [NH, TL], F32)

    dn = Dense(cx, ident)
    dn.consts()
    g_ffn = dn.load_gain("ffn", ffn_g)
    g_ple = dn.load_gain("ple", ple_g)

    wbf = cx.sb("wbf", [128, 512], BF16, n=4)

    def to_scratch(name, dst_ap_fn):
        buf = Buf("scr_" + name)

        def dst_fn(kc, n0, wd):
            t = cx.nxt(wbf)
            return t, t.t[:, 0:wd]
        return buf, dst_fn

    def prep_dram(name, w_ap, K, N, dst_ap_fn, gain=None, c0=0, c1=None):
        buf = Buf("scr_" + name, multi=True)
        c1_ = N if c1 is None else c1
        for kc in range(K // 128):
            for n0 in range(c0, c1_, 512):
                wd = min(512, c1_ - n0)
                holder = {}

                def dst_fn(kc_, n0_, wd_, holder=holder):
                    t = cx.nxt(wbf)
                    holder["t"] = t
                    return t, t.t[:, 0:wd_]
                dn.prep_weight(w_ap[kc * 128:(kc + 1) * 128, :], 128, N, dst_fn, gain=None if gain is None else _GainCol(gain, kc),
                               c0=n0, c1=n0 + wd)
                t = holder["t"]
                dap = dst_ap_fn(kc, n0 - c0, wd)
                S.dma("pool", "wp_" + t.b.name, lambda e, t=t, dap=dap, wd=wd: e.dma_start(out=dap, in_=t.t[:, 0:wd]),
                      reads=[t.b], writes=[buf])
        return buf

    class _GainCol:
        def __init__(self, g, kc):
            self.b = g.b
            self.t = _Shift(g.t, kc)

    class _Shift:
        def __init__(self, t, kc):
            self._t = t
            self._kc = kc

        def __getitem__(self, idx):
            return self._t[idx[0], self._kc:self._kc + 1]

    def scr8(name, N):
        return cx.dint("scr_" + name, [N // 512, 128, 8, 512], BF16)

    WoB = scr8("wo", D)
    b_wo = prep_dram("wo", w_o, D, D, lambda kc, n0, wd: WoB[n0 // 512, :, kc, :])
    WguB = cx.dint("scr_wgu", [22, 128, 8, 256], BF16)

    def gu_dst(off):
        def f(kc, n0, wd):
            j0 = n0 // 128
            return WguB[j0:j0 + wd // 128, :, kc, off:off + 128].rearrange("j p i -> p j i")
        return f
    b_wg = prep_dram("wg", w_gu, D, 2 * DFF, gu_dst(0), gain=g_ffn, c0=0, c1=DFF)
    b_wu = prep_dram("wu", w_gu, D, 2 * DFF, gu_dst(128), gain=g_ffn, c0=DFF, c1=2 * DFF)
    WdB = cx.dint("scr_wd", [4, 128, 22, 256], BF16)
    b_wd = prep_dram("wd", w_d, DFF, D,
                     lambda kc, n0, wd: WdB[n0 // 256:n0 // 256 + 2, :, kc, :].rearrange("q p i -> p q i"))
    WgateB = scr8("wgate", D)
    b_wgate = prep_dram("wgate", w_gate, D, D, lambda kc, n0, wd: WgateB[n0 // 512, :, kc, :], gain=g_ple)
    Wproj = cx.sb("Wproj", [128, 2, D], BF16)
    dn.prep_weight(w_proj, PLE, D, lambda kc, n0, wd: (Wproj, Wproj.t[:, kc, n0:n0 + wd]))
    if nxt:
        g_a = dn.load_gain("a1", a_g)
        g_sh = dn.load_gain("sh", sh_g)
        WqB = scr8("wq", D)
        b_wq = prep_dram("wq", w_q, D, D, lambda kc, n0, wd: WqB[n0 // 512, :, kc, :], gain=g_a)
        WkvB = scr8("wkv", 2 * D)
        b_wkv = prep_dram("wkv", w_kvf, D, 2 * D + NH, lambda kc, n0, wd: WkvB[n0 // 512, :, kc, :], gain=g_sh,
                          c0=0, c1=2 * D)
        Wf = cx.sb("Wf", [128, 8, NH], BF16)
        dn.prep_weight(w_kvf, D, 2 * D + NH, lambda kc, n0, wd: (Wf, Wf.t[:, kc, 0:wd]), gain=g_sh,
                       c0=2 * D, c1=2 * D + NH)
        qg = cx.sb("qg", [128, 8, DH], F32)
        kg = cx.sb("kg", [128, 8, DH], F32)
        S.dma("sp", "const6", lambda e: e.dma_start(out=qg.t[:], in_=qn_g.unsqueeze(0).unsqueeze(0).to_broadcast([128, 8, DH])),
              writes=[qg.b])
        S.dma("sp", "const7", lambda e: e.dma_start(out=kg.t[:], in_=kn_g.unsqueeze(0).unsqueeze(0).to_broadcast([128, 8, DH])),
              writes=[kg.b])
        S.op("dve", lambda e: e.tensor_scalar(qg.t[:], qg.t[:], 0.125, None, ALU.mult), reads=[qg.b], writes=[qg.b])
        nbf = cx.sb("nbf", [NH, 1], F32)
        S.dma("sp", "const8", lambda e: e.dma_start(out=nbf.t[:], in_=b_f.rearrange("(h o) -> h o", o=1)), writes=[nbf.b])
        S.op("dve", lambda e: e.tensor_scalar(nbf.t[:], nbf.t[:], -1.0, None, ALU.mult), reads=[nbf.b], writes=[nbf.b])
        one = cx.sb("one", [128, 1], F32)
        S.op("pool", lambda e: e.memset(one.t[:], 1.0), writes=[one.b])

    hres = cx.sb("hres", [128, D], F32, n=8)
    mTs = cx.sb("mT", [128, 8, 512], BF16, n=1)
    wt8 = cx.sb("wt8", [128, 8, 512], BF16, n=3)
    wgut = cx.sb("wgut", [128, 8, 256], BF16, n=4)
    wdt = cx.sb("wdt", [128, 22, 256], BF16, n=2)
    hnTs = cx.sb("hnT", [128, 8, 512], BF16, n=2)
    aT = cx.sb("aT", [128, 22, 512], BF16)
    sgs = cx.sb("sg", [128, 512], F32, n=2)
    tmps = cx.sb("tmp", [128, 512], F32, n=2)
    pblk = cx.sb("pblk", [128, PLE], F32, n=2)
    pbf = cx.sb("pbf", [128, PLE], BF16, n=2)
    pTs = cx.sb("pT", [128, 2, 128], BF16, n=2)
    pm = cx.ps("pm", [128, 512], F32, n=4)
    if nxt:
        hd8 = cx.sb("hd8", [128, 8], F32, n=4)
        qnb = cx.sb("qnb", [128, 512], BF16, n=2)
        oT = cx.sb("oT", [128, 512], BF16, n=2)
        fl = cx.sb("fl", [NH, 512], F32, n=2)

    def load_w8(scr, buf, hf, name):
        t = cx.nxt(wt8)
        S.dma("sp", "ld_" + t.b.name, lambda e: e.dma_start(out=t.t[:], in_=scr[hf]), reads=[buf], writes=[t.b])
        return t

    def add_res(h, c0, wd, src_tile, src_ap, flip=[0]):
        S.op("dve", lambda e: e.tensor_tensor(h.t[:, c0:c0 + wd], h.t[:, c0:c0 + wd], src_ap, ALU.add),
             reads=[h.b, src_tile.b], writes=[h.b])

    for gi in range(4):
        t0 = gi * 512
        hb = []
        for b in range(4):
            h = cx.nxt(hres)
            r0 = t0 + b * 128
            S.dma("pool", "ld_" + h.b.name, lambda e, h=h, r0=r0: e.dma_start(out=h.t[:], in_=h_in[r0:r0 + 128, :]),
                  writes=[h.b])
            hb.append(h)
        mT = cx.nxt(mTs)
        S.dma("pool", "ld_mT", lambda e, mT=mT, t0=t0: e.dma_start(
            out=mT.t[:], in_=mixT[:, t0:t0 + 512].rearrange("(c p) t -> p c t", p=128)), writes=[mT.b])
        for hf in range(2):
            wt = load_w8(WoB, b_wo, hf, "wo")
            for b in range(4):
                p = cx.nxt(pm)
                for kc in range(8):
                    S.op("pe", lambda e, p=p, kc=kc, b=b, wt=wt, mT=mT: e.matmul(
                        p.t[:], mT.t[:, kc, b * 128:(b + 1) * 128], wt.t[:, kc, :], start=(kc == 0), stop=(kc == 7)),
                        reads=[mT.b, wt.b], writes=[p.b])
                add_res(hb[b], hf * 512, 512, p, p.t[:])
        hT = cx.nxt(hnTs)
        for b in range(4):
            dn.norm_T(hb[b], hT, b * 128)
        for j in range(22):
            wg = cx.nxt(wgut)
            S.dma("sp", "ld_" + wg.b.name, lambda e, wg=wg, j=j: e.dma_start(out=wg.t[:], in_=WguB[j]),
                  reads=[b_wg, b_wu], writes=[wg.b])
            pg = cx.nxt(pm)
            pu = cx.nxt(pm)
            for kc in range(8):
                S.op("pe", lambda e, pg=pg, kc=kc, wg=wg, hT=hT: e.matmul(
                    pg.t[:], wg.t[:, kc, 0:128], hT.t[:, kc, :], start=(kc == 0), stop=(kc == 7)),
                    reads=[wg.b, hT.b], writes=[pg.b])
            for kc in range(8):
                S.op("pe", lambda e, pu=pu, kc=kc, wg=wg, hT=hT: e.matmul(
                    pu.t[:], wg.t[:, kc, 128:256], hT.t[:, kc, :], start=(kc == 0), stop=(kc == 7)),
                    reads=[wg.b, hT.b], writes=[pu.b])
            sg = cx.nxt(sgs)
            S.op("act", lambda e, sg=sg, pg=pg: e.activation(sg.t[:], pg.t[:], AF.Silu), reads=[pg.b], writes=[sg.b])
            S.op("dve", lambda e, sg=sg, pu=pu, j=j: e.tensor_tensor(aT.t[:, j, :], sg.t[:], pu.t[:], ALU.mult),
                 reads=[sg.b, pu.b], writes=[aT.b])
        for qd in range(4):
            wd_ = cx.nxt(wdt)
            S.dma("sp", "ld_" + wd_.b.name, lambda e, wd_=wd_, qd=qd: e.dma_start(out=wd_.t[:], in_=WdB[qd]),
                  reads=[b_wd], writes=[wd_.b])
            for b in range(4):
                p = cx.nxt(pm)
                for j in range(22):
                    S.op("pe", lambda e, p=p, j=j, b=b, wd_=wd_: e.matmul(
                        p.t[:, 0:256], aT.t[:, j, b * 128:(b + 1) * 128], wd_.t[:, j, :], start=(j == 0), stop=(j == 21)),
                        reads=[aT.b, wd_.b], writes=[p.b])
                add_res(hb[b], qd * 256, 256, p, p.t[:, 0:256])
        hT = cx.nxt(hnTs)
        for b in range(4):
            dn.norm_T(hb[b], hT, b * 128)
        for hf in range(2):
            wt = load_w8(WgateB, b_wgate, hf, "wgate")
            for b in range(4):
                pb_ = cx.nxt(pblk)
                r0 = t0 + b * 128
                S.dma("pool", "ld_" + pb_.b.name, lambda e, pb_=pb_, r0=r0: e.dma_start(out=pb_.t[:], in_=p_in[r0:r0 + 128, :]),
                      writes=[pb_.b])
                pf = cx.nxt(pbf)
                S.op("pool", lambda e, pf=pf, pb_=pb_: e.tensor_copy(pf.t[:], pb_.t[:]), reads=[pb_.b], writes=[pf.b])
                pT_ps = cx.nxt(dn.psT)
                for k2 in range(2):
                    S.op("pe", lambda e, k2=k2, pT_ps=pT_ps, pf=pf: e.transpose(
                        pT_ps.t[:, k2 * 128:(k2 + 1) * 128], pf.t[:, k2 * 128:(k2 + 1) * 128], dn.ident.t[:]),
                        reads=[pf.b, dn.ident.b], writes=[pT_ps.b])
                pT = cx.nxt(pTs)
                S.op("act", lambda e, pT=pT, pT_ps=pT_ps: e.copy(pT.t[:, :, :], pT_ps.t[:, 0:256].rearrange("p (k t) -> p k t", k=2)),
                     reads=[pT_ps.b], writes=[pT.b])
                pgate = cx.nxt(pm)
                for kc in range(8):
                    S.op("pe", lambda e, pgate=pgate, kc=kc, b=b, wt=wt, hT=hT: e.matmul(
                        pgate.t[:], hT.t[:, kc, b * 128:(b + 1) * 128], wt.t[:, kc, :], start=(kc == 0), stop=(kc == 7)),
                        reads=[hT.b, wt.b], writes=[pgate.b])
                pproj = cx.nxt(pm)
                for k2 in range(2):
                    S.op("pe", lambda e, pproj=pproj, k2=k2, pT=pT, hf=hf: e.matmul(
                        pproj.t[:], pT.t[:, k2, :], Wproj.t[:, k2, hf * 512:(hf + 1) * 512], start=(k2 == 0), stop=(k2 == 1)),
                        reads=[pT.b, Wproj.b], writes=[pproj.b])
                sg = cx.nxt(sgs)
                S.op("act", lambda e, sg=sg, pgate=pgate: e.activation(sg.t[:], pgate.t[:], AF.Sigmoid),
                     reads=[pgate.b], writes=[sg.b])
                tmp = cx.nxt(tmps)
                S.op("dve", lambda e, tmp=tmp, sg=sg, pproj=pproj: e.tensor_tensor(tmp.t[:], sg.t[:], pproj.t[:], ALU.mult),
                     reads=[sg.b, pproj.b], writes=[tmp.b])
                hh_ = hb[b]
                S.op("pool", lambda e, hh_=hh_, tmp=tmp, hf=hf: e.tensor_tensor(
                    hh_.t[:, hf * 512:(hf + 1) * 512], hh_.t[:, hf * 512:(hf + 1) * 512], tmp.t[:], ALU.add),
                    reads=[hh_.b, tmp.b], writes=[hh_.b])
        for b in range(4):
            r0 = t0 + b * 128
            cx.store("o_h", hout[r0:r0 + 128, :], hb[b], hb[b].t[:])
        if not nxt:
            continue
        hT = cx.nxt(hnTs)
        for b in range(4):
            dn.norm_T(hb[b], hT, b * 128)

        def head_norm_store(p, gtile, dstT, b, hf):
            sq = cx.nxt(tmps)
            S.op("act", lambda e: e.activation(sq.t[:], p.t[:], AF.Square), reads=[p.b], writes=[sq.b])
            s8 = cx.nxt(hd8)
            S.op("dve", lambda e: e.tensor_reduce(s8.t[:], sq.t[:, :].rearrange("p (h d) -> p h d", d=DH),
                                                  mybir.AxisListType.X, ALU.add), reads=[sq.b], writes=[s8.b])
            l8 = cx.nxt(hd8)
            S.op("act", lambda e: e.activation(l8.t[:], s8.t[:], AF.Ln, scale=1.0 / DH, bias=dn.eps.t[:, 0:1]),
                 reads=[s8.b, dn.eps.b], writes=[l8.b])
            r8 = cx.nxt(hd8)
            S.op("act", lambda e: e.activation(r8.t[:], l8.t[:], AF.Exp, scale=-0.5), reads=[l8.b], writes=[r8.b])
            qf = cx.nxt(sgs)
            S.op("dve", lambda e: e.tensor_tensor(qf.t[:, :].rearrange("p (h d) -> p h d", d=DH),
                                                  p.t[:, :].rearrange("p (h d) -> p h d", d=DH),
                                                  r8.t[:, :].unsqueeze(2).to_broadcast([128, 8, DH]), ALU.mult),
                 reads=[p.b, r8.b], writes=[qf.b])
            qn = cx.nxt(qnb)
            S.op("pool", lambda e: e.tensor_tensor(qn.t[:, :], qf.t[:, :], gtile.t[:, :, :].rearrange("p h d -> p (h d)"), ALU.mult),
                 reads=[qf.b, gtile.b], writes=[qn.b])
            tp = cx.nxt(dn.psT)
            for c4 in range(4):
                S.op("pe", lambda e, c4=c4: e.transpose(tp.t[:, c4 * 128:(c4 + 1) * 128], qn.t[:, c4 * 128:(c4 + 1) * 128],
                                                        dn.ident.t[:]), reads=[qn.b, dn.ident.b], writes=[tp.b])
            o = cx.nxt(oT)
            S.op("act", lambda e: e.copy(o.t[:], tp.t[:, 0:512]), reads=[tp.b], writes=[o.b])
            r0 = t0 + b * 128
            cx.store("o_qk1", dstT[hf * 512:(hf + 1) * 512, r0:r0 + 128].rearrange("(c p) t -> p c t", p=128), o,
                     o.t[:, :].rearrange("p (c t) -> p c t", c=4))

        for hf in range(2):
            wt = load_w8(WqB, b_wq, hf, "wq")
            for b in range(4):
                p = cx.nxt(pm)
                for kc in range(8):
                    S.op("pe", lambda e, p=p, kc=kc, b=b, wt=wt, hT=hT: e.matmul(
                        p.t[:], hT.t[:, kc, b * 128:(b + 1) * 128], wt.t[:, kc, :], start=(kc == 0), stop=(kc == 7)),
                        reads=[hT.b, wt.b], writes=[p.b])
                head_norm_store(p, qg, q1T, b, hf)
        for hf in range(4):
            wt = load_w8(WkvB, b_wkv, hf, "wkv")
            for b in range(4):
                p = cx.nxt(pm)
                for kc in range(8):
                    S.op("pe", lambda e, p=p, kc=kc, b=b, wt=wt, hT=hT: e.matmul(
                        p.t[:], hT.t[:, kc, b * 128:(b + 1) * 128], wt.t[:, kc, :], start=(kc == 0), stop=(kc == 7)),
                        reads=[hT.b, wt.b], writes=[p.b])
                if hf < 2:
                    head_norm_store(p, kg, k1T, b, hf)
                else:
                    o = cx.nxt(oT)
                    S.op("act", lambda e, o=o, p=p: e.copy(o.t[:], p.t[:]), reads=[p.b], writes=[o.b])
                    r0 = t0 + b * 128
                    cx.store("o_v1", v1[r0:r0 + 128, (hf - 2) * 512:(hf - 1) * 512], o, o.t[:])
        p = cx.nxt(pm)
        for kc in range(8):
            S.op("pe", lambda e, p=p, kc=kc, hT=hT: e.matmul(p.t[0:NH, :], Wf.t[:, kc, :], hT.t[:, kc, :],
                                                              start=(kc == 0), stop=(kc == 7)),
                 reads=[Wf.b, hT.b], writes=[p.b])
        f1 = cx.nxt(fl)
        S.op("act", lambda e, f1=f1, p=p: e.activation(f1.t[:], p.t[0:NH, :], AF.Exp, scale=-1.0, bias=nbf.t[:, 0:1]),
             reads=[p.b, nbf.b], writes=[f1.b])
        f2 = cx.nxt(fl)
        S.op("act", lambda e, f1=f1, f2=f2: e.activation(f2.t[:], f1.t[:], AF.Ln, bias=one.t[0:NH, 0:1]),
             reads=[f1.b, one.b], writes=[f2.b])
        S.op("dve", lambda e, f2=f2: e.tensor_scalar(f2.t[:], f2.t[:], -1.0, None, ALU.mult), reads=[f2.b], writes=[f2.b])
        cx.store("o_fl", flogT[:, t0:t0 + 512], f2, f2.t[:])
    return cx.finish()


def build_attn1(n_heads=NH):
    cx = Cx()
    S = cx.S
    qT = cx.din("qT", [D, TL], BF16)
    kT = cx.din("kT", [D, S_ALL], BF16)
    vr = cx.din("vr", [8, 128, 128 * 128], BF16)
    flog = cx.din("flog", [NH, S_ALL], F32)
    onehot = cx.din("onehot", [NH, 8], F32)
    negm = cx.din("negm", [128, 8 * 128], BF16)
    ident = cx.din("ident", [128, 128], BF16)
    sel = cx.din("sel", [128, 256], F32)
    mixT = cx.dout("mixT", [D, TL], BF16)
    kaug = cx.dint("kaug", [NH, 6, S_ALL], BF16)
    qaug = cx.dint("qaug", [NH, 6, TL], BF16)
    b_kaug = Buf("kaug")
    b_qaug = Buf("qaug")

    nm = cx.sb("nm", [128, 8 * 128], BF16)
    idt = cx.sb("idt", [128, 128], BF16)
    selt = cx.sb("selt", [128, 256], F32)
    oh = cx.sb("oh", [NH, 8], F32)
    for t_, src in ((nm, negm), (idt, ident), (selt, sel), (oh, onehot)):
        S.dma("sp", "cc_" + t_.b.name, lambda e, t_=t_, src=src: e.dma_start(out=t_.t[:], in_=src), writes=[t_.b])

    CH = 1024
    Fc = cx.sb("Fc", [NH, CH], F32, n=2)
    Fs = cx.sb("Fs", [NH, CH], F32, n=2)
    r1 = cx.sb("r1", [NH, CH], F32)
    onesf = cx.sb("onesf", [NH, CH], F32)
    S.op("pool", lambda e: e.memset(onesf.t[:], 1.0), writes=[onesf.b])
    ka = cx.sb("ka", [NH, 6, CH], BF16)
    Fq = cx.sb("Fq", [NH, TL], F32)
    qa = cx.sb("qa", [NH, 6, 512], BF16)
    carry = cx.sb("carry", [NH, 1], F32, n=2)
    S.op("pool", lambda e: e.memset(carry[1].t[:], 0.0), writes=[carry[1].b])
    S.op("pool", lambda e: e.memset(ka.t[:, 0:3, :], 1.0), writes=[ka.b])
    S.op("pool", lambda e: e.memset(qa.t[:, 3:6, :], 1.0), writes=[qa.b])
    for ci in range(S_ALL // CH):
        fc = Fc[ci % 2]
        fs = Fs[ci % 2]
        S.dma("sp", "ld_" + fc.b.name, lambda e, fc=fc, ci=ci: e.dma_start(out=fc.t[:], in_=flog[:, ci * CH:(ci + 1) * CH]),
              writes=[fc.b])
        cprev = carry[(ci + 1) % 2]
        ccur = carry[ci % 2]
        S.op("dve", lambda e, fs=fs, fc=fc, cprev=cprev: e.tensor_tensor_scan(
            fs.t[:], onesf.t[:], fc.t[:], cprev.t[:, 0:1], ALU.mult, ALU.add),
            reads=[onesf.b, fc.b, cprev.b], writes=[fs.b])
        S.op("dve", lambda e, fs=fs, ccur=ccur: e.tensor_copy(ccur.t[:], fs.t[:, CH - 1:CH]), reads=[fs.b], writes=[ccur.b])
        fview = fs.t[:, :].rearrange("h (c i) -> h c i", c=8)
        fqv = Fq.t[:, ci * 128:(ci + 1) * 128]
        for c in range(8):
            if c == 0:
                S.op("dve", lambda e, fview=fview, fqv=fqv, c=c: e.tensor_scalar(
                    fqv, fview[:, c, :], oh.t[:, c:c + 1], None, ALU.mult), reads=[fs.b, oh.b], writes=[Fq.b])
            else:
                S.op("dve", lambda e, fview=fview, fqv=fqv, c=c: e.scalar_tensor_tensor(
                    fqv, fview[:, c, :], oh.t[:, c:c + 1], fqv, ALU.mult, ALU.add), reads=[fs.b, oh.b, Fq.b], writes=[Fq.b])
        S.op("dve", lambda e, fs=fs: e.tensor_scalar(r1.t[:], fs.t[:], -1.0, None, ALU.mult), reads=[fs.b], writes=[r1.b])
        for part in range(3):
            S.op("dve", lambda e, part=part: e.tensor_copy(ka.t[:, 3 + part, :], r1.t[:]), reads=[r1.b], writes=[ka.b])
            if part < 2:
                S.op("dve", lambda e, part=part: e.tensor_tensor(r1.t[:], r1.t[:], ka.t[:, 3 + part, :], ALU.subtract),
                     reads=[r1.b, ka.b], writes=[r1.b])
        S.dma("sp", "st_kaug", lambda e, ci=ci: e.dma_start(out=kaug[:, :, ci * CH:(ci + 1) * CH], in_=ka.t[:]),
              reads=[ka.b], writes=[b_kaug])
    for qi in range(4):
        S.op("dve", lambda e, qi=qi: e.tensor_copy(r1.t[:, 0:512], Fq.t[:, qi * 512:(qi + 1) * 512]), reads=[Fq.b], writes=[r1.b])
        for part in range(3):
            S.op("dve", lambda e, part=part: e.tensor_copy(qa.t[:, part, :], r1.t[:, 0:512]), reads=[r1.b], writes=[qa.b])
            if part < 2:
                S.op("dve", lambda e, part=part: e.tensor_tensor(r1.t[:, 0:512], r1.t[:, 0:512], qa.t[:, part, :], ALU.subtract),
                     reads=[r1.b, qa.b], writes=[r1.b])
        S.dma("sp", "st_qaug", lambda e, qi=qi: e.dma_start(out=qaug[:, :, qi * 512:(qi + 1) * 512], in_=qa.t[:]),
              reads=[qa.b], writes=[b_qaug])

    kTs = cx.sb("kTs", [128, S_ALL], BF16, n=2)
    qs = cx.sb("qs", [128, TL], BF16, n=2)
    vs = cx.sb("vs", [128, 128 * 130], BF16, n=2)
    mx = cx.sb("mx", [128, TL], BF16, n=2)
    for sl in range(2):
        v3 = vs[sl].t[:, :].rearrange("p (k c) -> p k c", c=130)
        S.op("pool", lambda e, v3=v3: e.memset(v3[:, :, 64:65], 1.0), writes=[vs[sl].b])
        S.op("pool", lambda e, v3=v3: e.memset(v3[:, :, 129:130], 1.0), writes=[vs[sl].b])
    zp = cx.ps("zp", [128, 512], F32, n=3)
    op_ = cx.ps("op", [128, 512], F32, n=2)
    bcp = cx.ps("bcp", [128, 512], F32, n=1)
    pb_ = cx.sb("pb", [128, 512], BF16, n=4)
    osb = cx.sb("osb", [128, 512], F32, n=1)
    rbs = cx.sb("rbs", [128, 512], F32, n=1)

    def load_head(h):
        sl = h % 2
        k_, q_ = kTs[sl], qs[sl]
        S.dma("sp", "ldq%d" % sl, lambda e: e.dma_start(out=q_.t[0:64, :], in_=qT[h * 64:(h + 1) * 64, :]), writes=[q_.b])
        S.dma("sp", "ldq%d" % sl, lambda e: e.dma_start(out=q_.t[64:70, :], in_=qaug[h]), reads=[b_qaug], writes=[q_.b])
        S.dma("sp", "ldk%d" % sl, lambda e: e.dma_start(out=k_.t[64:70, :], in_=kaug[h]), reads=[b_kaug], writes=[k_.b])
        for part in range(4):
            c0 = part * 4096
            S.dma("sp", "ldk%d" % sl,
                  lambda e, c0=c0: e.dma_start(out=k_.t[0:64, c0:c0 + 4096], in_=kT[h * 64:(h + 1) * 64, c0:c0 + 4096]),
                  writes=[k_.b])

    def load_v(hp):
        v_ = vs[hp % 2]
        v3 = v_.t[:, :].rearrange("p (k c) -> p k c", c=130)
        src = vr[hp].rearrange("p (k c) -> p k c", c=128)
        for part in range(8):
            k0 = part * 16
            for hh in range(2):
                S.dma("sp", "ldv%d" % (hp % 2),
                      lambda e, k0=k0, hh=hh: e.dma_start(out=v3[:, k0:k0 + 16, 65 * hh:65 * hh + 64],
                                                          in_=src[:, k0:k0 + 16, 64 * hh:64 * hh + 64]),
                      writes=[v_.b])

    load_v(0)
    load_head(0)

    def do_head(h):
        sl = h % 2
        hp = h // 2
        odd = h % 2
        k_, q_ = kTs[sl], qs[sl]
        v_ = vs[hp % 2]
        m_ = mx[hp % 2]
        if h + 1 < n_heads:
            if (h + 1) % 2 == 0:
                load_v((h + 1) // 2)
            load_head(h + 1)
        items = []
        for J in range(4):
            nkb = 32 * J + 32
            for kb in range(nkb - 1, -1, -1):
                items.append((J, kb, nkb))
        st = {}

        def s1(it):
            J, kb, nkb = it
            r = kb - 32 * J
            c0 = 128 * (r // 8) if r >= 0 else 0
            z = cx.nxt(zp)
            p_ = cx.nxt(pb_)
            S.op("pe", lambda e: e.matmul(z.t[:, c0:512], k_.t[0:70, kb * 128:(kb + 1) * 128],
                                          q_.t[0:70, 512 * J + c0:512 * J + 512], start=True, stop=(r < 0)),
                 reads=[k_.b, q_.b], writes=[z.b])
            if r >= 0:
                i = r % 8
                S.op("pe", lambda e: e.matmul(z.t[:, c0:c0 + 128], idt.t[:], nm.t[:, i * 128:(i + 1) * 128],
                                              start=False, stop=True), reads=[idt.b, nm.b], writes=[z.b])
            S.op("act", lambda e: e.activation(p_.t[:, c0:512], z.t[:, c0:512], AF.Exp), reads=[z.b], writes=[p_.b])
            st[it] = (c0, p_)

        def s2(it):
            J, kb, nkb = it
            c0, p_ = st.pop(it)
            o_ = op_[J % 2]
            if odd:
                S.op("pe", lambda e: e.matmul(o_.t[:, c0:512], v_.t[:, kb * 130 + 1:kb * 130 + 129], p_.t[:, c0:512],
                                              start=(kb == nkb - 1), stop=(kb == 0)), reads=[v_.b, p_.b], writes=[o_.b])
            else:
                S.op("pe", lambda e: e.matmul(o_.t[0:65, c0:512], v_.t[:, kb * 130:kb * 130 + 65], p_.t[:, c0:512],
                                              start=(kb == nkb - 1), stop=(kb == 0)), reads=[v_.b, p_.b], writes=[o_.b])
            if kb == 0:
                ob = cx.nxt(osb)
                rb = cx.nxt(rbs)
                bc = bcp[0]
                if odd:
                    S.op("act", lambda e: e.copy(ob.t[32:64, :], o_.t[32:64, :]), reads=[o_.b], writes=[ob.b])
                    S.op("act", lambda e: e.copy(ob.t[64:128, :], o_.t[64:128, :]), reads=[o_.b], writes=[ob.b])
                    S.op("pe", lambda e: e.matmul(bc.t[:, :], selt.t[32:64, 128:256], ob.t[32:64, :], start=True, stop=True),
                         reads=[selt.b, ob.b], writes=[bc.b])
                    S.op("dve", lambda e: e.reciprocal(rb.t[64:128, :], bc.t[64:128, :]), reads=[bc.b], writes=[rb.b])
                    S.op("dve", lambda e: e.tensor_tensor(m_.t[64:128, 512 * J:512 * J + 512], ob.t[64:128, :], rb.t[64:128, :], ALU.mult),
                         reads=[ob.b, rb.b], writes=[m_.b])
                else:
                    S.op("act", lambda e: e.copy(ob.t[0:65, :], o_.t[0:65, :]), reads=[o_.b], writes=[ob.b])
                    S.op("pe", lambda e: e.matmul(bc.t[0:64, :], selt.t[64:65, 0:64], ob.t[64:65, :], start=True, stop=True),
                         reads=[selt.b, ob.b], writes=[bc.b])
                    S.op("dve", lambda e: e.reciprocal(rb.t[0:64, :], bc.t[0:64, :]), reads=[bc.b], writes=[rb.b])
                    S.op("dve", lambda e: e.tensor_tensor(m_.t[0:64, 512 * J:512 * J + 512], ob.t[0:64, :], rb.t[0:64, :], ALU.mult),
                         reads=[ob.b, rb.b], writes=[m_.b])

        n = len(items)
        for tau in range(-1, n + 1):
            if 0 <= tau + 1 < n:
                s1(items[tau + 1])
            if 0 <= tau - 1 < n:
                s2(items[tau - 1])
        if odd:
            cx.store("o_mix", mixT[hp * 128:(hp + 1) * 128, :], m_, m_.t[:])

    for h in range(n_heads):
        do_head(h)
    return cx.finish()


_PROGS = {}


def _prog(name, fn):
    if name not in _PROGS:
        _PROGS[name] = fn()
    return _PROGS[name]


def _shard_tok(a):
    F_ = a.shape[-1]
    r = a.reshape(16, 8, 128, F_)
    return [np.ascontiguousarray(r[:, c].reshape(TL, F_)) for c in range(NC_)]


def _gather_T(parts):
    R = parts[0].shape[0]
    out = np.zeros((R, 16, 8, 128), dtype=parts[0].dtype)
    for c in range(NC_):
        out[:, :, c, :] = parts[c].reshape(R, 16, 128)
    return out.reshape(R, S_ALL)


def _gather_v(parts):
    va = np.zeros((16, 8, 128, D), dtype=parts[0].dtype)
    for c in range(NC_):
        va[:, c] = parts[c].reshape(16, 128, D)
    va = va.reshape(128, 128, 8, 128)
    return np.ascontiguousarray(va.transpose(2, 1, 0, 3)).reshape(8, 128, 128 * 128)


def _consts():
    ar = np.arange(128)
    c = {}
    c["ident"] = np.eye(128, dtype=np.float32).astype(NPBF)
    c["tri"] = (ar[:, None] >= ar[None, :]).astype(np.float32).astype(NPBF)
    c["omt"] = (ar[:, None] < ar[None, :]).astype(np.float32).astype(NPBF)
    masks, negm, oh = [], [], []
    for cc in range(NC_):
        m = np.zeros((128, 8, 128), np.float32)
        n = np.full((128, 8, 128), -30000.0, np.float32)
        for i in range(8):
            if i < cc:
                m[:, i, :] = 1.0
                n[:, i, :] = 0.0
            elif i == cc:
                m[:, i, :] = (ar[:, None] < ar[None, :])
                n[:, i, :] = np.where(ar[:, None] <= ar[None, :], 0.0, -30000.0)
        masks.append(m.reshape(128, 1024).astype(NPBF))
        negm.append(n.reshape(128, 1024).astype(NPBF))
        o = np.zeros((NH, 8), np.float32)
        o[:, cc] = 1.0
        oh.append(o)
    c["masks"], c["negm"], c["onehot"] = masks, negm, oh
    sel = np.zeros((128, 256), np.float32)
    sel[64, 0:64] = 1.0
    sel[63, 192:256] = 1.0
    c["sel"] = sel
    return c


def _run(nc, in_maps):
    return run_bass_kernel_spmd(nc, in_maps, core_ids=list(range(NC_))).results


def kernel(x, p, attn_norm_g, sb_w_qkv, sb_w_o, shared_norm_g, shared_w_kvf, shared_b_f, shared_k_norm_g,
           fox_w_q, fox_q_norm_g, fox_w_o, ffn_norm_g, ffn_w_gu, ffn_w_d, ple_norm_g, ple_w_gate, ple_w_proj):
    f32 = lambda a: np.ascontiguousarray(np.asarray(a, dtype=np.float32))
    x = f32(x)
    p = f32(p)
    C = _consts()
    xs = _shard_tok(x[0])
    p0 = _shard_tok(p[0, 0])
    p1 = _shard_tok(p[1, 0])
    r1 = _run(_prog("pre0", build_pre0),
              [{"x": xs[c], "g": f32(attn_norm_g[0]), "w": f32(sb_w_qkv[0]), "ident": C["ident"]} for c in range(NC_)])
    kT_all = _gather_T([r["kT"] for r in r1])
    vr = _gather_v([r["v"] for r in r1])
    r2 = _run(_prog("attn0", build_attn0),
              [{"qT": r1[c]["qT"], "kT": kT_all, "vr": vr, "masks": C["masks"][c], "tri": C["tri"], "omt": C["omt"]}
               for c in range(NC_)])

    def post_in(c, h, mix, pl, w_o, li):
        return {"h": h, "mixT": mix, "p": pl, "w_o": f32(w_o), "ffn_g": f32(ffn_norm_g[li]), "w_gu": f32(ffn_w_gu[li]),
                "w_d": f32(ffn_w_d[li]), "ple_g": f32(ple_norm_g[li]), "w_gate": f32(ple_w_gate[li]),
                "w_proj": f32(ple_w_proj[li]), "ident": C["ident"]}

    in3 = []
    for c in range(NC_):
        d = post_in(c, xs[c], r2[c]["mixT"], p0[c], sb_w_o[0], 0)
        d.update({"a_g": f32(attn_norm_g[1]), "w_q": f32(fox_w_q[0]), "qn_g": f32(fox_q_norm_g[0]),
                  "sh_g": f32(shared_norm_g), "w_kvf": f32(shared_w_kvf), "b_f": f32(shared_b_f),
                  "kn_g": f32(shared_k_norm_g)})
        in3.append(d)
    r3 = _run(_prog("post_n", lambda: build_post(True)), in3)
    k1_all = _gather_T([r["k1T"] for r in r3])
    vr1 = _gather_v([r["v1"] for r in r3])
    fl_all = _gather_T([r["flogT"] for r in r3])
    r4 = _run(_prog("attn1", build_attn1),
              [{"qT": r3[c]["q1T"], "kT": k1_all, "vr": vr1, "flog": fl_all, "onehot": C["onehot"][c],
                "negm": C["negm"][c], "ident": C["ident"], "sel": C["sel"]} for c in range(NC_)])
    r5 = _run(_prog("post_l", lambda: build_post(False)),
              [post_in(c, r3[c]["hout"], r4[c]["mixT"], p1[c], fox_w_o[0], 1) for c in range(NC_)])
    out = np.zeros((16, 8, 128, D), np.float32)
    for c in range(NC_):
        out[:, c] = r5[c]["hout"].reshape(16, 128, D)
    return out.reshape(1, S_ALL, D)
```

```python
import numpy as np
import ml_dtypes
from contextlib import ExitStack
import concourse.bass as bass
import concourse.mybir as mybir
from concourse.bass_utils import run_bass_kernel_spmd

F32 = mybir.dt.float32
BF16 = mybir.dt.bfloat16
AF = mybir.ActivationFunctionType
ALU = mybir.AluOpType
NPBF = ml_dtypes.bfloat16

NC_ = 8
D = 1024
S_ALL = 16384
TL = 2048
NH = 16
DH = 64
DFF = 2816
PLE = 256
EPS = 1e-6
EPOCH = 4096


class Buf:
    __slots__ = ("name", "w", "r", "wm")

    def __init__(self, name, multi=False):
        self.name = name
        self.w = None
        self.r = []
        self.wm = {} if multi else None


class Tile:
    __slots__ = ("t", "b")

    def __init__(self, t, name):
        self.t = t
        self.b = Buf(name)


class Sched:
    ENG = ("pe", "act", "dve", "pool", "sp")

    def __init__(self, nc, es):
        self.nc = nc
        self.es = es
        self.lists = {e: [] for e in self.ENG}
        self.sems = {}
        self.cnt = {}
        self.alias = {}
        self.waited = {e: {} for e in self.ENG}
        self.ecount = {e: 0 for e in self.ENG}
        self.nsem = 0

    def _mksem(self, key):
        self.nsem += 1
        self.sems[key] = self.es.enter_context(self.nc.semaphore("s%d" % self.nsem))
        self.cnt[key] = 0

    def _deps(self, eng, reads, writes):
        deps = {}

        def add(tok):
            if tok is None:
                return
            k, v = tok
            if deps.get(k, 0) < v:
                deps[k] = v

        for b in reads:
            add(b.w)
            if b.wm is not None:
                for t in b.wm.items():
                    add(t)
        for b in writes:
            add(b.w)
            for t in b.r:
                add(t)
        out = []
        w = self.waited[eng]
        for k, v in deps.items():
            if eng == "pe" and k.startswith("E_pe#"):
                continue
            if w.get(k, 0) >= v:
                continue
            w[k] = v
            out.append((k, v))
        return out

    @staticmethod
    def _commit(tok, reads, writes):
        for b in writes:
            if b.wm is not None:
                if b.wm.get(tok[0], 0) < tok[1]:
                    b.wm[tok[0]] = tok[1]
                continue
            b.w = tok
            b.r = []
        for b in reads:
            b.r.append(tok)

    def op(self, eng, fn, reads=(), writes=()):
        deps = self._deps(eng, reads, writes)
        ep = self.ecount[eng] // EPOCH
        key = "E_%s#%d" % (eng, ep)
        if key not in self.sems:
            self._mksem(key)
        self.ecount[eng] += 1
        self.cnt[key] += 1
        tok = (key, self.cnt[key])
        sems = self.sems

        def thunk(e, deps=deps, fn=fn, key=key):
            for k, v in deps:
                e.wait_ge(sems[k], v)
            fn(e).then_inc(sems[key], 1)

        self.lists[eng].append(thunk)
        self._commit(tok, reads, writes)
        return tok

    def dma(self, eng, semname, fn, reads=(), writes=()):
        deps = self._deps(eng, reads, writes)
        key = self.alias.get(semname)
        if key is None or self.cnt[key] >= 16 * 240:
            n = 0 if key is None else int(key.split("#")[1]) + 1
            key = "D_%s#%d" % (semname, n)
            self.alias[semname] = key
            self._mksem(key)
        self.cnt[key] += 16
        tok = (key, self.cnt[key])
        sems = self.sems

        def thunk(e, deps=deps, fn=fn, key=key):
            for k, v in deps:
                e.wait_ge(sems[k], v)
            fn(e).then_inc(sems[key], 16)

        self.lists[eng].append(thunk)
        self._commit(tok, reads, writes)
        return tok

    def wait_all(self, eng, toks):
        sems = self.sems
        toks = list(toks)

        def thunk(e):
            for k, v in toks:
                e.wait_ge(sems[k], v)

        self.lists[eng].append(thunk)

    def emit(self):
        L = self.lists
        with self.nc.Block() as block:
            @block.tensor
            def _(e):
                for t in L["pe"]:
                    t(e)

            @block.scalar
            def _(e):
                for t in L["act"]:
                    t(e)

            @block.vector
            def _(e):
                for t in L["dve"]:
                    t(e)

            @block.gpsimd
            def _(e):
                for t in L["pool"]:
                    t(e)

            @block.sync
            def _(e):
                for t in L["sp"]:
                    t(e)


class Cx:
    def __init__(self):
        self.nc = bass.Bass("TRN2", target_bir_lowering=False)
        self.es = ExitStack()
        self.S = Sched(self.nc, self.es)
        self.out_toks = {}
        self.rr = {}

    def din(self, name, shape, dt):
        return self.nc.dram_tensor(name, list(shape), dt, kind="ExternalInput").ap()

    def dout(self, name, shape, dt):
        return self.nc.dram_tensor(name, list(shape), dt, kind="ExternalOutput").ap()

    def dint(self, name, shape, dt):
        return self.nc.dram_tensor(name, list(shape), dt, kind="Internal").ap()

    def sb(self, name, shape, dt, n=None):
        if n is None:
            return Tile(self.es.enter_context(self.nc.sbuf_tensor("sb_" + name, list(shape), dt)), name)
        return [Tile(self.es.enter_context(self.nc.sbuf_tensor("sb_%s%d" % (name, i), list(shape), dt)),
                     "%s%d" % (name, i)) for i in range(n)]

    def ps(self, name, shape, dt, n=None):
        if n is None:
            return Tile(self.es.enter_context(self.nc.psum_tensor("ps_" + name, list(shape), dt)), name)
        return [Tile(self.es.enter_context(self.nc.psum_tensor("ps_%s%d" % (name, i), list(shape), dt)),
                     "%s%d" % (name, i)) for i in range(n)]

    def nxt(self, lst, key=None):
        key = key or id(lst)
        i = self.rr.get(key, 0)
        self.rr[key] = i + 1
        return lst[i % len(lst)]

    def store(self, semname, out_ap, tile, in_ap, eng="pool"):
        tok = self.S.dma(eng, "st_" + tile.b.name, lambda e: e.dma_start(out=out_ap, in_=in_ap), reads=[tile.b])
        self.out_toks[tok[0]] = max(self.out_toks.get(tok[0], 0), tok[1])

    def finish(self):
        self.S.wait_all("sp", list(self.out_toks.items()))
        self.S.emit()
        self.es.close()
        return self.nc


class Dense:
    def __init__(self, cx, ident_ap):
        self.cx = cx
        S = cx.S
        self.ident = cx.sb("ident", [128, 128], BF16)
        S.dma("sp", "const1", lambda e: e.dma_start(out=self.ident.t[:], in_=ident_ap), writes=[self.ident.b])
        self.junk = cx.sb("junk", [128, 1024], BF16, n=2)
        self.ss = cx.sb("ss", [128, 1], F32, n=4)
        self.lnv = cx.sb("lnv", [128, 1], F32, n=4)
        self.rstd = cx.sb("rstd", [128, 1], F32, n=4)
        self.hn = cx.sb("hn", [128, 1024], BF16, n=2)
        self.psT = cx.ps("psT", [128, 1024], BF16, n=2)
        self.wst = cx.sb("wst", [128, 512], F32, n=4)
        self.gv = {}
        self.evq = 0

    def load_gain(self, name, g_ap):
        cx = self.cx
        t = cx.sb("g_" + name, [128, 8], F32)
        cx.S.dma("sp", "cg_" + name, lambda e: e.dma_start(out=t.t[:], in_=g_ap.rearrange("(k p) -> p k", p=128),
                                                      allow_slow_non_contiguous=True), writes=[t.b])
        self.gv[name] = t
        return t

    def prep_weight(self, w_ap, K, N, dst_fn, gain=None, post=None, c0=0, c1=None):
        cx = self.cx
        S = cx.S
        c1 = N if c1 is None else c1
        for kc in range(K // 128):
            for n0 in range(c0, c1, 512):
                wd = min(512, c1 - n0)
                st = cx.nxt(self.wst)
                S.dma("sp", "wst_" + st.b.name,
                      lambda e, st=st, kc=kc, n0=n0, wd=wd: e.dma_start(
                          out=st.t[:, 0:wd], in_=w_ap[kc * 128:(kc + 1) * 128, n0:n0 + wd]),
                      writes=[st.b])
                dt_, dap = dst_fn(kc, n0, wd)
                eng = "pool" if (self.evq % 2 == 0) else "dve"
                self.evq += 1
                pv = post(n0) if post is not None else None
                if gain is not None:
                    g = gain
                    if pv is not None:
                        fn = lambda e, st=st, dap=dap, kc=kc, wd=wd, g=g, pv=pv: e.tensor_scalar(
                            dap, st.t[:, 0:wd], g.t[:, kc:kc + 1], pv, ALU.mult, ALU.mult)
                    else:
                        fn = lambda e, st=st, dap=dap, kc=kc, wd=wd, g=g: e.tensor_scalar(
                            dap, st.t[:, 0:wd], g.t[:, kc:kc + 1], None, ALU.mult)
                    S.op(eng, fn, reads=[st.b, g.b], writes=[dt_.b])
                else:
                    S.op(eng, lambda e, st=st, dap=dap, wd=wd: e.tensor_copy(dap, st.t[:, 0:wd]),
                         reads=[st.b], writes=[dt_.b])

    def norm_T(self, h, hnT, col0):
        cx = self.cx
        S = cx.S
        junk = cx.nxt(self.junk)
        ss = cx.nxt(self.ss)
        lnv = cx.nxt(self.lnv)
        rstd = cx.nxt(self.rstd)
        hn = cx.nxt(self.hn)
        pT = cx.nxt(self.psT)
        S.op("act", lambda e: e.activation(junk.t[:], h.t[:], AF.Square, accum_out=ss.t[:]),
             reads=[h.b], writes=[junk.b, ss.b])
        S.op("act", lambda e: e.activation(lnv.t[:], ss.t[:], AF.Ln, scale=1.0 / D, bias=self.eps.t[:, 0:1]),
             reads=[ss.b, self.eps.b], writes=[lnv.b])
        S.op("act", lambda e: e.activation(rstd.t[:], lnv.t[:], AF.Exp, scale=-0.5),
             reads=[lnv.b], writes=[rstd.b])
        S.op("dve", lambda e: e.tensor_scalar(hn.t[:], h.t[:], rstd.t[:, 0:1], None, ALU.mult),
             reads=[h.b, rstd.b], writes=[hn.b])
        for kc in range(8):
            S.op("pe", lambda e, kc=kc: e.transpose(pT.t[:, kc * 128:(kc + 1) * 128],
                                                    hn.t[:, kc * 128:(kc + 1) * 128], self.ident.t[:]),
                 reads=[hn.b, self.ident.b], writes=[pT.b])
        S.op("dve", lambda e: e.tensor_copy(hnT.t[:, :, col0:col0 + 128],
                                            pT.t[:, :].rearrange("p (k t) -> p k t", k=8)),
             reads=[pT.b], writes=[hnT.b])

    def consts(self):
        cx = self.cx
        self.eps = cx.sb("epsc", [128, 1], F32)
        cx.S.op("pool", lambda e: e.memset(self.eps.t[:], EPS), writes=[self.eps.b])


def build_pre0():
    cx = Cx()
    S = cx.S
    x = cx.din("x", [TL, D], F32)
    g = cx.din("g", [D], F32)
    w = cx.din("w", [D, 3 * D], F32)
    ident = cx.din("ident", [128, 128], BF16)
    qT = cx.dout("qT", [D, TL], BF16)
    kT = cx.dout("kT", [D, TL], BF16)
    v = cx.dout("v", [TL, D], BF16)
    dn = Dense(cx, ident)
    dn.consts()
    gt = dn.load_gain("a", g)
    Wb = cx.sb("Wb", [128, 8, 3 * D], BF16)
    dn.prep_weight(w, D, 3 * D, lambda kc, n0, wd: (Wb, Wb.t[:, kc, n0:n0 + wd]), gain=gt,
                   post=lambda n0: (0.125 if n0 < D else 1.0))
    hblk = cx.sb("hblk", [128, D], F32, n=3)
    hnT = cx.sb("hnT", [128, 8, 512], BF16, n=2)
    pm = cx.ps("pm", [128, 512], F32, n=4)
    ost = cx.sb("ost", [128, 512], BF16, n=4)
    ev = 0
    for gi in range(4):
        hT = cx.nxt(hnT)
        for b in range(4):
            h = cx.nxt(hblk)
            r0 = (gi * 4 + b) * 128
            S.dma("sp", "ld_" + h.b.name, lambda e, h=h, r0=r0: e.dma_start(out=h.t[:], in_=x[r0:r0 + 128, :]),
                  writes=[h.b])
            dn.norm_T(h, hT, b * 128)
        for n in range(16):
            p = cx.nxt(pm)
            for kc in range(8):
                S.op("pe", lambda e, p=p, kc=kc, n=n, hT=hT: e.matmul(
                    p.t[:], Wb.t[:, kc, n * 128:(n + 1) * 128], hT.t[:, kc, :], start=(kc == 0), stop=(kc == 7)),
                    reads=[Wb.b, hT.b], writes=[p.b])
            o = cx.nxt(ost)
            if ev % 2 == 0:
                S.op("act", lambda e, o=o, p=p: e.copy(o.t[:], p.t[:]), reads=[p.b], writes=[o.b])
            else:
                S.op("dve", lambda e, o=o, p=p: e.tensor_copy(o.t[:], p.t[:]), reads=[p.b], writes=[o.b])
            ev += 1
            dst = qT if n < 8 else kT
            rr = (n % 8) * 128
            cx.store("o_qk", dst[rr:rr + 128, gi * 512:(gi + 1) * 512], o, o.t[:])
        for b in range(4):
            for hf in range(2):
                p = cx.nxt(pm)
                for kc in range(8):
                    S.op("pe", lambda e, p=p, kc=kc, b=b, hf=hf, hT=hT: e.matmul(
                        p.t[:], hT.t[:, kc, b * 128:(b + 1) * 128],
                        Wb.t[:, kc, 2 * D + hf * 512:2 * D + (hf + 1) * 512], start=(kc == 0), stop=(kc == 7)),
                        reads=[Wb.b, hT.b], writes=[p.b])
                o = cx.nxt(ost)
                if ev % 2 == 0:
                    S.op("act", lambda e, o=o, p=p: e.copy(o.t[:], p.t[:]), reads=[p.b], writes=[o.b])
                else:
                    S.op("dve", lambda e, o=o, p=p: e.tensor_copy(o.t[:], p.t[:]), reads=[p.b], writes=[o.b])
                ev += 1
                r0 = (gi * 4 + b) * 128
                cx.store("o_v", v[r0:r0 + 128, hf * 512:(hf + 1) * 512], o, o.t[:])
    return cx.finish()


def build_attn0(n_pairs=8):
    cx = Cx()
    S = cx.S
    qT = cx.din("qT", [D, TL], BF16)
    kT = cx.din("kT", [D, S_ALL], BF16)
    vr = cx.din("vr", [8, 128, 128 * 128], BF16)
    masks = cx.din("masks", [128, 8 * 128], BF16)
    tri = cx.din("tri", [128, 128], BF16)
    omt = cx.din("omt", [128, 128], BF16)
    mixT = cx.dout("mixT", [D, TL], BF16)

    mk = cx.sb("mk", [128, 8 * 128], BF16)
    trt = cx.sb("trt", [128, 128], BF16)
    omtt = cx.sb("omtt", [128, 128], BF16)
    S.dma("sp", "const3", lambda e: e.dma_start(out=mk.t[:], in_=masks), writes=[mk.b])
    S.dma("sp", "const4", lambda e: e.dma_start(out=trt.t[:], in_=tri), writes=[trt.b])
    S.dma("sp", "const5", lambda e: e.dma_start(out=omtt.t[:], in_=omt), writes=[omtt.b])

    one = cx.sb("one", [128, 1], F32)
    S.op("pool", lambda e: e.memset(one.t[:], 1.0), writes=[one.b])
    kTs = cx.sb("kTs", [128, S_ALL], BF16, n=2)
    vs = cx.sb("vs", [128, 128 * 128], BF16, n=2)
    qs = cx.sb("qs", [128, TL], BF16, n=2)
    mx = cx.sb("mx", [128, TL], BF16, n=2)
    z2 = cx.ps("z2", [128, 1024], F32, n=2)
    C2 = cx.ps("C2", [128, 1024], F32)
    op_ = cx.ps("op", [128, 512], F32, n=2)
    e2 = cx.sb("e2", [128, 1024], F32, n=5)
    sp2 = cx.sb("sp2", [128, 1024], BF16, n=5)
    x2 = cx.sb("x2", [128, 1024], BF16, n=3)
    w2 = cx.sb("w2", [128, 1024], BF16, n=3)

    def V(t, c0, w=None):
        v = t.t[:, :].rearrange("p (h c) -> p h c", h=2)
        return v[:, :, c0:512] if w is None else v[:, :, c0:c0 + w]

    def load_pair(hp):
        sl = hp % 2
        k_, v_, q_ = kTs[sl], vs[sl], qs[sl]
        S.dma("sp", "ldq%d" % sl, lambda e: e.dma_start(out=q_.t[:], in_=qT[hp * 128:(hp + 1) * 128, :]),
              writes=[q_.b])
        for part in range(4):
            c0 = part * 4096
            S.dma("sp", "ldk%d" % sl,
                  lambda e, c0=c0: e.dma_start(out=k_.t[:, c0:c0 + 4096], in_=kT[hp * 128:(hp + 1) * 128, c0:c0 + 4096]),
                  writes=[k_.b])
            S.dma("sp", "ldv%d" % sl,
                  lambda e, c0=c0: e.dma_start(out=v_.t[:, c0:c0 + 4096], in_=vr[hp, :, c0:c0 + 4096]),
                  writes=[v_.b])

    load_pair(0)

    def do_pair(hp):
        sl = hp % 2
        k_, v_, q_, m_ = kTs[sl], vs[sl], qs[sl], mx[sl]
        if hp + 1 < n_pairs:
            load_pair(hp + 1)
        items = []
        for J in range(4):
            nkb = 32 * J + 32
            for kb in range(nkb - 1, -1, -1):
                items.append((J, kb, nkb))
        st = {}

        def s1(it):
            J, kb, nkb = it
            r = kb - 32 * J
            c0 = 128 * (r // 8) if r >= 0 else 0
            z = cx.nxt(z2)
            e_ = cx.nxt(e2)
            sp_ = cx.nxt(sp2)
            for hh in range(2):
                pb = 64 * hh
                S.op("pe", lambda e, hh=hh, pb=pb: e.matmul(
                    z.t[:, 512 * hh + c0:512 * hh + 512], k_.t[pb:pb + 64, kb * 128:(kb + 1) * 128],
                    q_.t[pb:pb + 64, 512 * J + c0:512 * J + 512], start=True, stop=True),
                    reads=[k_.b, q_.b], writes=[z.b])
            S.op("act", lambda e: e.activation(V(e_, c0), V(z, c0), AF.Exp), reads=[z.b], writes=[e_.b])
            if r >= 0:
                i = r % 8
                S.op("pool", lambda e: e.tensor_tensor(
                    V(e_, c0, 128), V(e_, c0, 128),
                    mk.t[:, i * 128:(i + 1) * 128].unsqueeze(1).to_broadcast([128, 2, 128]), ALU.mult),
                    reads=[e_.b, mk.b], writes=[e_.b])
            S.op("act", lambda e: e.activation(V(sp_, c0), V(e_, c0), AF.Ln, bias=one.t[:, 0:1]),
                 reads=[e_.b, one.b], writes=[sp_.b])
            st[it] = [c0, e_, sp_, None, None]

        def s2(it):
            J, kb, nkb = it
            c0, e_, sp_, _, _ = st[it]
            x_ = cx.nxt(x2)
            for hh in range(2):
                S.op("pe", lambda e, hh=hh: e.matmul(C2.t[:, 512 * hh + c0:512 * hh + 512], trt.t[:],
                                                     sp_.t[:, 512 * hh + c0:512 * hh + 512],
                                                     start=(kb == nkb - 1), stop=True,
                                                     skip_group_check=(kb != nkb - 1)),
                     reads=[trt.b, sp_.b], writes=[C2.b])
            S.op("act", lambda e: e.activation(V(x_, c0), V(C2, c0), AF.Exp, scale=-1.0), reads=[C2.b], writes=[x_.b])
            st[it][3] = x_

        def s3(it):
            J, kb, nkb = it
            c0, e_, sp_, x_, _ = st[it]
            w_ = cx.nxt(w2)
            if kb > 0:
                for hh in range(2):
                    S.op("pe", lambda e, hh=hh: e.matmul(C2.t[:, 512 * hh + c0:512 * hh + 512], omtt.t[:],
                                                         sp_.t[:, 512 * hh + c0:512 * hh + 512], start=False, stop=True,
                                                         skip_group_check=True),
                         reads=[omtt.b, sp_.b], writes=[C2.b])
            S.op("dve", lambda e: e.tensor_tensor(V(w_, c0), V(e_, c0), V(x_, c0), ALU.mult),
                 reads=[e_.b, x_.b], writes=[w_.b])
            st[it][4] = w_

        def s4(it):
            J, kb, nkb = it
            c0, e_, sp_, x_, w_ = st.pop(it)
            o_ = op_[J % 2]
            for hh in range(2):
                pb = 64 * hh
                S.op("pe", lambda e, hh=hh, pb=pb: e.matmul(
                    o_.t[pb:pb + 64, c0:512], v_.t[:, kb * 128 + pb:kb * 128 + pb + 64],
                    w_.t[:, 512 * hh + c0:512 * hh + 512], start=(kb == nkb - 1), stop=(kb == 0),
                    skip_group_check=(kb != nkb - 1)),
                    reads=[v_.b, w_.b], writes=[o_.b])
            if kb == 0:
                S.op("act", lambda e: e.copy(m_.t[:, 512 * J:512 * J + 512], o_.t[:, :]), reads=[o_.b], writes=[m_.b])

        n = len(items)
        for tau in range(-1, n + 3):
            if 0 <= tau - 2 < n:
                s3(items[tau - 2])
            if 0 <= tau - 1 < n:
                s2(items[tau - 1])
            if 0 <= tau + 1 < n:
                s1(items[tau + 1])
            if 0 <= tau - 3 < n:
                s4(items[tau - 3])
        cx.store("o_mix", mixT[hp * 128:(hp + 1) * 128, :], m_, m_.t[:])

    for hp in range(n_pairs):
        do_pair(hp)
    return cx.finish()


def build_post(nxt):
    cx = Cx()
    S = cx.S
    h_in = cx.din("h", [TL, D], F32)
    mixT = cx.din("mixT", [D, TL], BF16)
    p_in = cx.din("p", [TL, PLE], F32)
    w_o = cx.din("w_o", [D, D], F32)
    ffn_g = cx.din("ffn_g", [D], F32)
    w_gu = cx.din("w_gu", [D, 2 * DFF], F32)
    w_d = cx.din("w_d", [DFF, D], F32)
    ple_g = cx.din("ple_g", [D], F32)
    w_gate = cx.din("w_gate", [D, D], F32)
    w_proj = cx.din("w_proj", [PLE, D], F32)
    ident = cx.din("ident", [128, 128], BF16)
    hout = cx.dout("hout", [TL, D], F32)
    if nxt:
        a_g = cx.din("a_g", [D], F32)
        w_q = cx.din("w_q", [D, D], F32)
        qn_g = cx.din("qn_g", [DH], F32)
        sh_g = cx.din("sh_g", [D], F32)
        w_kvf = cx.din("w_kvf", [D, 2 * D + NH], F32)
        b_f = cx.din("b_f", [NH], F32)
        kn_g = cx.din("kn_g", [DH], F32)
        q1T = cx.dout("q1T", [D, TL], BF16)
        k1T = cx.dout("k1T", [D, TL], BF16)
        v1 = cx.dout("v1", [TL, D], BF16)
        flogT = cx.dout("flogT", [NH, TL], F32)

    dn = Dense(cx, ident)
    dn.consts()
    g_ffn = dn.load_gain("ffn", ffn_g)
    g_ple = dn.load_gain("ple", ple_g)

    wbf = cx.sb("wbf", [128, 512], BF16, n=4)

    def to_scratch(name, dst_ap_fn):
        buf = Buf("scr_" + name)

        def dst_fn(kc, n0, wd):
            t = cx.nxt(wbf)
            return t, t.t[:, 0:wd]
        return buf, dst_fn

    def prep_dram(name, w_ap, K, N, dst_ap_fn, gain=None, c0=0, c1=None):
        buf = Buf("scr_" + name, multi=True)
        c1_ = N if c1 is None else c1
        for kc in range(K // 128):
            for n0 in range(c0, c1_, 512):
                wd = min(512, c1_ - n0)
                holder = {}

                def dst_fn(kc_, n0_, wd_, holder=holder):
                    t = cx.nxt(wbf)
                    holder["t"] = t
                    return t, t.t[:, 0:wd_]
                dn.prep_weight(w_ap[kc * 128:(kc + 1) * 128, :], 128, N, dst_fn, gain=None if gain is None else _GainCol(gain, kc),
                               c0=n0, c1=n0 + wd)
                t = holder["t"]
                dap = dst_ap_fn(kc, n0 - c0, wd)
                S.dma("pool", "wp_" + t.b.name, lambda e, t=t, dap=dap, wd=wd: e.dma_start(out=dap, in_=t.t[:, 0:wd]),
                      reads=[t.b], writes=[buf])
        return buf

    class _GainCol:
        def __init__(self, g, kc):
            self.b = g.b
            self.t = _Shift(g.t, kc)

    class _Shift:
        def __init__(self, t, kc):
            self._t = t
            self._kc = kc

        def __getitem__(self, idx):
            return self._t[idx[0], self._kc:self._kc + 1]

    def scr8(name, N):
        return cx.dint("scr_" + name, [N // 512, 128, 8, 512], BF16)

    WB = {}
    LAZY = {}

    def ensure(name):
        f = LAZY.pop(name, None)
        if f is not None:
            f()

    WoB = scr8("wo", D)
    LAZY["wo"] = lambda: WB.__setitem__("wo", prep_dram("wo", w_o, D, D, lambda kc, n0, wd: WoB[n0 // 512, :, kc, :]))
    WguB = cx.dint("scr_wgu", [22, 128, 8, 256], BF16)

    def gu_dst(off):
        def f(kc, n0, wd):
            j0 = n0 // 128
            return WguB[j0:j0 + wd // 128, :, kc, off:off + 128].rearrange("j p i -> p j i")
        return f
    def _p_wgu():
        WB["wg"] = prep_dram("wg", w_gu, D, 2 * DFF, gu_dst(0), gain=g_ffn, c0=0, c1=DFF)
        WB["wu"] = prep_dram("wu", w_gu, D, 2 * DFF, gu_dst(128), gain=g_ffn, c0=DFF, c1=2 * DFF)
    LAZY["wgu"] = _p_wgu
    WdB = cx.dint("scr_wd", [4, 128, 22, 256], BF16)
    LAZY["wd"] = lambda: WB.__setitem__("wd", prep_dram(
        "wd", w_d, DFF, D, lambda kc, n0, wd: WdB[n0 // 256:n0 // 256 + 2, :, kc, :].rearrange("q p i -> p q i")))
    WgateB = scr8("wgate", D)
    Wproj = cx.sb("Wproj", [128, 2, D], BF16)

    def _p_wgate():
        WB["wgate"] = prep_dram("wgate", w_gate, D, D, lambda kc, n0, wd: WgateB[n0 // 512, :, kc, :], gain=g_ple)
        dn.prep_weight(w_proj, PLE, D, lambda kc, n0, wd: (Wproj, Wproj.t[:, kc, n0:n0 + wd]))
    LAZY["wgate"] = _p_wgate
    if nxt:
        g_a = dn.load_gain("a1", a_g)
        g_sh = dn.load_gain("sh", sh_g)
        WqB = scr8("wq", D)
        LAZY["wq"] = lambda: WB.__setitem__("wq", prep_dram(
            "wq", w_q, D, D, lambda kc, n0, wd: WqB[n0 // 512, :, kc, :], gain=g_a))
        WkvB = scr8("wkv", 2 * D)
        Wf = cx.sb("Wf", [128, 8, NH], BF16)

        def _p_wkv():
            WB["wkv"] = prep_dram("wkv", w_kvf, D, 2 * D + NH, lambda kc, n0, wd: WkvB[n0 // 512, :, kc, :], gain=g_sh,
                                  c0=0, c1=2 * D)
            dn.prep_weight(w_kvf, D, 2 * D + NH, lambda kc, n0, wd: (Wf, Wf.t[:, kc, 0:wd]), gain=g_sh,
                           c0=2 * D, c1=2 * D + NH)
        LAZY["wkv"] = _p_wkv
        qg = cx.sb("qg", [128, 8, DH], F32)
        kg = cx.sb("kg", [128, 8, DH], F32)
        S.dma("sp", "const6", lambda e: e.dma_start(out=qg.t[:], in_=qn_g.unsqueeze(0).unsqueeze(0).to_broadcast([128, 8, DH])),
              writes=[qg.b])
        S.dma("sp", "const7", lambda e: e.dma_start(out=kg.t[:], in_=kn_g.unsqueeze(0).unsqueeze(0).to_broadcast([128, 8, DH])),
              writes=[kg.b])
        S.op("dve", lambda e: e.tensor_scalar(qg.t[:], qg.t[:], 0.125, None, ALU.mult), reads=[qg.b], writes=[qg.b])
        nbf = cx.sb("nbf", [NH, 1], F32)
        S.dma("sp", "const8", lambda e: e.dma_start(out=nbf.t[:], in_=b_f.rearrange("(h o) -> h o", o=1)), writes=[nbf.b])
        S.op("dve", lambda e: e.tensor_scalar(nbf.t[:], nbf.t[:], -1.0, None, ALU.mult), reads=[nbf.b], writes=[nbf.b])
        one = cx.sb("one", [128, 1], F32)
        S.op("pool", lambda e: e.memset(one.t[:], 1.0), writes=[one.b])

    hres = cx.sb("hres", [128, D], F32, n=8)
    mTs = cx.sb("mT", [128, 8, 512], BF16, n=1)
    wt8 = cx.sb("wt8", [128, 8, 512], BF16, n=3)
    wgut = cx.sb("wgut", [128, 8, 256], BF16, n=4)
    wdt = cx.sb("wdt", [128, 22, 256], BF16, n=2)
    hnTs = cx.sb("hnT", [128, 8, 512], BF16, n=2)
    aT = cx.sb("aT", [128, 22, 512], BF16)
    sgs = cx.sb("sg", [128, 512], F32, n=2)
    tmps = cx.sb("tmp", [128, 512], F32, n=2)
    pblk = cx.sb("pblk", [128, PLE], F32, n=2)
    pbf = cx.sb("pbf", [128, PLE], BF16, n=2)
    pTs = cx.sb("pT", [128, 2, 128], BF16, n=2)
    pm = cx.ps("pm", [128, 512], F32, n=4)
    if nxt:
        hd8 = cx.sb("hd8", [128, 8], F32, n=4)
        qnb = cx.sb("qnb", [128, 512], BF16, n=2)
        oT = cx.sb("oT", [128, 512], BF16, n=2)
        fl = cx.sb("fl", [NH, 512], F32, n=2)

    def load_w8(scr, buf, hf, name):
        t = cx.nxt(wt8)
        S.dma("sp", "ld_" + t.b.name, lambda e: e.dma_start(out=t.t[:], in_=scr[hf]), reads=[buf], writes=[t.b])
        return t

    def add_res(h, c0, wd, src_tile, src_ap, flip=[0]):
        S.op("dve", lambda e: e.tensor_tensor(h.t[:, c0:c0 + wd], h.t[:, c0:c0 + wd], src_ap, ALU.add),
             reads=[h.b, src_tile.b], writes=[h.b])

    ensure("wo")
    ensure("wgu")
    for gi in range(4):
        t0 = gi * 512
        hb = []
        for b in range(4):
            h = cx.nxt(hres)
            r0 = t0 + b * 128
            S.dma("pool", "ld_" + h.b.name, lambda e, h=h, r0=r0: e.dma_start(out=h.t[:], in_=h_in[r0:r0 + 128, :]),
                  writes=[h.b])
            hb.append(h)
        mT = cx.nxt(mTs)
        S.dma("pool", "ld_mT", lambda e, mT=mT, t0=t0: e.dma_start(
            out=mT.t[:], in_=mixT[:, t0:t0 + 512].rearrange("(c p) t -> p c t", p=128)), writes=[mT.b])
        for hf in range(2):
            wt = load_w8(WoB, WB["wo"], hf, "wo")
            for b in range(4):
                p = cx.nxt(pm)
                for kc in range(8):
                    S.op("pe", lambda e, p=p, kc=kc, b=b, wt=wt, mT=mT: e.matmul(
                        p.t[:], mT.t[:, kc, b * 128:(b + 1) * 128], wt.t[:, kc, :], start=(kc == 0), stop=(kc == 7)),
                        reads=[mT.b, wt.b], writes=[p.b])
                add_res(hb[b], hf * 512, 512, p, p.t[:])
        ensure("wd")
        hT = cx.nxt(hnTs)
        for b in range(4):
            dn.norm_T(hb[b], hT, b * 128)
        for j in range(22):
            wg = cx.nxt(wgut)
            S.dma("sp", "ld_" + wg.b.name, lambda e, wg=wg, j=j: e.dma_start(out=wg.t[:], in_=WguB[j]),
                  reads=[WB["wg"], WB["wu"]], writes=[wg.b])
            pg = cx.nxt(pm)
            pu = cx.nxt(pm)
            for kc in range(8):
                S.op("pe", lambda e, pg=pg, kc=kc, wg=wg, hT=hT: e.matmul(
                    pg.t[:], wg.t[:, kc, 0:128], hT.t[:, kc, :], start=(kc == 0), stop=(kc == 7)),
                    reads=[wg.b, hT.b], writes=[pg.b])
            for kc in range(8):
                S.op("pe", lambda e, pu=pu, kc=kc, wg=wg, hT=hT: e.matmul(
                    pu.t[:], wg.t[:, kc, 128:256], hT.t[:, kc, :], start=(kc == 0), stop=(kc == 7)),
                    reads=[wg.b, hT.b], writes=[pu.b])
            sg = cx.nxt(sgs)
            S.op("act", lambda e, sg=sg, pg=pg: e.activation(sg.t[:], pg.t[:], AF.Silu), reads=[pg.b], writes=[sg.b])
            S.op("dve", lambda e, sg=sg, pu=pu, j=j: e.tensor_tensor(aT.t[:, j, :], sg.t[:], pu.t[:], ALU.mult),
                 reads=[sg.b, pu.b], writes=[aT.b])
        ensure("wgate")
        for qd in range(4):
            wd_ = cx.nxt(wdt)
            S.dma("sp", "ld_" + wd_.b.name, lambda e, wd_=wd_, qd=qd: e.dma_start(out=wd_.t[:], in_=WdB[qd]),
                  reads=[WB["wd"]], writes=[wd_.b])
            for b in range(4):
                p = cx.nxt(pm)
                for j in range(22):
                    S.op("pe", lambda e, p=p, j=j, b=b, wd_=wd_: e.matmul(
                        p.t[:, 0:256], aT.t[:, j, b * 128:(b + 1) * 128], wd_.t[:, j, :], start=(j == 0), stop=(j == 21)),
                        reads=[aT.b, wd_.b], writes=[p.b])
                add_res(hb[b], qd * 256, 256, p, p.t[:, 0:256])
        if nxt:
            ensure("wq")
        hT = cx.nxt(hnTs)
        for b in range(4):
            dn.norm_T(hb[b], hT, b * 128)
        for hf in range(2):
            wt = load_w8(WgateB, WB["wgate"], hf, "wgate")
            for b in range(4):
                pb_ = cx.nxt(pblk)
                r0 = t0 + b * 128
                S.dma("pool", "ld_" + pb_.b.name, lambda e, pb_=pb_, r0=r0: e.dma_start(out=pb_.t[:], in_=p_in[r0:r0 + 128, :]),
                      writes=[pb_.b])
                pf = cx.nxt(pbf)
                S.op("pool", lambda e, pf=pf, pb_=pb_: e.tensor_copy(pf.t[:], pb_.t[:]), reads=[pb_.b], writes=[pf.b])
                pT_ps = cx.nxt(dn.psT)
                for k2 in range(2):
                    S.op("pe", lambda e, k2=k2, pT_ps=pT_ps, pf=pf: e.transpose(
                        pT_ps.t[:, k2 * 128:(k2 + 1) * 128], pf.t[:, k2 * 128:(k2 + 1) * 128], dn.ident.t[:]),
                        reads=[pf.b, dn.ident.b], writes=[pT_ps.b])
                pT = cx.nxt(pTs)
                S.op("act", lambda e, pT=pT, pT_ps=pT_ps: e.copy(pT.t[:, :, :], pT_ps.t[:, 0:256].rearrange("p (k t) -> p k t", k=2)),
                     reads=[pT_ps.b], writes=[pT.b])
                pgate = cx.nxt(pm)
                for kc in range(8):
                    S.op("pe", lambda e, pgate=pgate, kc=kc, b=b, wt=wt, hT=hT: e.matmul(
                        pgate.t[:], hT.t[:, kc, b * 128:(b + 1) * 128], wt.t[:, kc, :], start=(kc == 0), stop=(kc == 7)),
                        reads=[hT.b, wt.b], writes=[pgate.b])
                pproj = cx.nxt(pm)
                for k2 in range(2):
                    S.op("pe", lambda e, pproj=pproj, k2=k2, pT=pT, hf=hf: e.matmul(
                        pproj.t[:], pT.t[:, k2, :], Wproj.t[:, k2, hf * 512:(hf + 1) * 512], start=(k2 == 0), stop=(k2 == 1)),
                        reads=[pT.b, Wproj.b], writes=[pproj.b])
                sg = cx.nxt(sgs)
                S.op("act", lambda e, sg=sg, pgate=pgate: e.activation(sg.t[:], pgate.t[:], AF.Sigmoid),
                     reads=[pgate.b], writes=[sg.b])
                tmp = cx.nxt(tmps)
                S.op("dve", lambda e, tmp=tmp, sg=sg, pproj=pproj: e.tensor_tensor(tmp.t[:], sg.t[:], pproj.t[:], ALU.mult),
                     reads=[sg.b, pproj.b], writes=[tmp.b])
                hh_ = hb[b]
                S.op("pool", lambda e, hh_=hh_, tmp=tmp, hf=hf: e.tensor_tensor(
                    hh_.t[:, hf * 512:(hf + 1) * 512], hh_.t[:, hf * 512:(hf + 1) * 512], tmp.t[:], ALU.add),
                    reads=[hh_.b, tmp.b], writes=[hh_.b])
        for b in range(4):
            r0 = t0 + b * 128
            cx.store("o_h", hout[r0:r0 + 128, :], hb[b], hb[b].t[:])
        if not nxt:
            continue
        ensure("wkv")
        hT = cx.nxt(hnTs)
        for b in range(4):
            dn.norm_T(hb[b], hT, b * 128)

        def head_norm_store(p, gtile, dstT, b, hf):
            sq = cx.nxt(tmps)
            S.op("act", lambda e: e.activation(sq.t[:], p.t[:], AF.Square), reads=[p.b], writes=[sq.b])
            s8 = cx.nxt(hd8)
            S.op("dve", lambda e: e.tensor_reduce(s8.t[:], sq.t[:, :].rearrange("p (h d) -> p h d", d=DH),
                                                  mybir.AxisListType.X, ALU.add), reads=[sq.b], writes=[s8.b])
            l8 = cx.nxt(hd8)
            S.op("act", lambda e: e.activation(l8.t[:], s8.t[:], AF.Ln, scale=1.0 / DH, bias=dn.eps.t[:, 0:1]),
                 reads=[s8.b, dn.eps.b], writes=[l8.b])
            r8 = cx.nxt(hd8)
            S.op("act", lambda e: e.activation(r8.t[:], l8.t[:], AF.Exp, scale=-0.5), reads=[l8.b], writes=[r8.b])
            qf = cx.nxt(sgs)
            S.op("dve", lambda e: e.tensor_tensor(qf.t[:, :].rearrange("p (h d) -> p h d", d=DH),
                                                  p.t[:, :].rearrange("p (h d) -> p h d", d=DH),
                                                  r8.t[:, :].unsqueeze(2).to_broadcast([128, 8, DH]), ALU.mult),
                 reads=[p.b, r8.b], writes=[qf.b])
            qn = cx.nxt(qnb)
            S.op("pool", lambda e: e.tensor_tensor(qn.t[:, :], qf.t[:, :], gtile.t[:, :, :].rearrange("p h d -> p (h d)"), ALU.mult),
                 reads=[qf.b, gtile.b], writes=[qn.b])
            tp = cx.nxt(dn.psT)
            for c4 in range(4):
                S.op("pe", lambda e, c4=c4: e.transpose(tp.t[:, c4 * 128:(c4 + 1) * 128], qn.t[:, c4 * 128:(c4 + 1) * 128],
                                                        dn.ident.t[:]), reads=[qn.b, dn.ident.b], writes=[tp.b])
            o = cx.nxt(oT)
            S.op("act", lambda e: e.copy(o.t[:], tp.t[:, 0:512]), reads=[tp.b], writes=[o.b])
            r0 = t0 + b * 128
            cx.store("o_qk1", dstT[hf * 512:(hf + 1) * 512, r0:r0 + 128].rearrange("(c p) t -> p c t", p=128), o,
                     o.t[:, :].rearrange("p (c t) -> p c t", c=4))

        for hf in range(2):
            wt = load_w8(WqB, WB["wq"], hf, "wq")
            for b in range(4):
                p = cx.nxt(pm)
                for kc in range(8):
                    S.op("pe", lambda e, p=p, kc=kc, b=b, wt=wt, hT=hT: e.matmul(
                        p.t[:], hT.t[:, kc, b * 128:(b + 1) * 128], wt.t[:, kc, :], start=(kc == 0), stop=(kc == 7)),
                        reads=[hT.b, wt.b], writes=[p.b])
                head_norm_store(p, qg, q1T, b, hf)
        for hf in range(4):
            wt = load_w8(WkvB, WB["wkv"], hf, "wkv")
            for b in range(4):
                p = cx.nxt(pm)
                for kc in range(8):
                    S.op("pe", lambda e, p=p, kc=kc, b=b, wt=wt, hT=hT: e.matmul(
                        p.t[:], hT.t[:, kc, b * 128:(b + 1) * 128], wt.t[:, kc, :], start=(kc == 0), stop=(kc == 7)),
                        reads=[hT.b, wt.b], writes=[p.b])
                if hf < 2:
                    head_norm_store(p, kg, k1T, b, hf)
                else:
                    o = cx.nxt(oT)
                    S.op("act", lambda e, o=o, p=p: e.copy(o.t[:], p.t[:]), reads=[p.b], writes=[o.b])
                    r0 = t0 + b * 128
                    cx.store("o_v1", v1[r0:r0 + 128, (hf - 2) * 512:(hf - 1) * 512], o, o.t[:])
        p = cx.nxt(pm)
        for kc in range(8):
            S.op("pe", lambda e, p=p, kc=kc, hT=hT: e.matmul(p.t[0:NH, :], Wf.t[:, kc, :], hT.t[:, kc, :],
                                                              start=(kc == 0), stop=(kc == 7)),
                 reads=[Wf.b, hT.b], writes=[p.b])
        f1 = cx.nxt(fl)
        S.op("act", lambda e, f1=f1, p=p: e.activation(f1.t[:], p.t[0:NH, :], AF.Exp, scale=-1.0, bias=nbf.t[:, 0:1]),
             reads=[p.b, nbf.b], writes=[f1.b])
        f2 = cx.nxt(fl)
        S.op("act", lambda e, f1=f1, f2=f2: e.activation(f2.t[:], f1.t[:], AF.Ln, bias=one.t[0:NH, 0:1]),
             reads=[f1.b, one.b], writes=[f2.b])
        S.op("dve", lambda e, f2=f2: e.tensor_scalar(f2.t[:], f2.t[:], -1.0, None, ALU.mult), reads=[f2.b], writes=[f2.b])
        cx.store("o_fl", flogT[:, t0:t0 + 512], f2, f2.t[:])
    return cx.finish()


def build_attn1(n_heads=NH):
    cx = Cx()
    S = cx.S
    qT = cx.din("qT", [D, TL], BF16)
    kT = cx.din("kT", [D, S_ALL], BF16)
    vr = cx.din("vr", [8, 128, 128 * 128], BF16)
    flog = cx.din("flog", [NH, S_ALL], F32)
    onehot = cx.din("onehot", [NH, 8], F32)
    negm = cx.din("negm", [128, 8 * 128], BF16)
    ident = cx.din("ident", [128, 128], BF16)
    sel = cx.din("sel", [128, 256], F32)
    mixT = cx.dout("mixT", [D, TL], BF16)
    kaug = cx.dint("kaug", [NH, 6, S_ALL], BF16)
    qaug = cx.dint("qaug", [NH, 6, TL], BF16)
    b_kaug = Buf("kaug")
    b_qaug = Buf("qaug")

    nm = cx.sb("nm", [128, 8 * 128], BF16)
    idt = cx.sb("idt", [128, 128], BF16)
    selt = cx.sb("selt", [128, 256], F32)
    oh = cx.sb("oh", [NH, 8], F32)
    for t_, src in ((nm, negm), (idt, ident), (selt, sel), (oh, onehot)):
        S.dma("sp", "cc_" + t_.b.name, lambda e, t_=t_, src=src: e.dma_start(out=t_.t[:], in_=src), writes=[t_.b])

    CH = 1024
    Fc = cx.sb("Fc", [NH, CH], F32, n=2)
    Fs = cx.sb("Fs", [NH, CH], F32, n=2)
    r1 = cx.sb("r1", [NH, CH], F32)
    onesf = cx.sb("onesf", [NH, CH], F32)
    S.op("pool", lambda e: e.memset(onesf.t[:], 1.0), writes=[onesf.b])
    ka = cx.sb("ka", [NH, 6, CH], BF16)
    Fq = cx.sb("Fq", [NH, TL], F32)
    qa = cx.sb("qa", [NH, 6, 512], BF16)
    carry = cx.sb("carry", [NH, 1], F32, n=2)
    S.op("pool", lambda e: e.memset(carry[1].t[:], 0.0), writes=[carry[1].b])
    S.op("pool", lambda e: e.memset(ka.t[:, 0:3, :], 1.0), writes=[ka.b])
    S.op("pool", lambda e: e.memset(qa.t[:, 3:6, :], 1.0), writes=[qa.b])
    for ci in range(S_ALL // CH):
        fc = Fc[ci % 2]
        fs = Fs[ci % 2]
        S.dma("sp", "ld_" + fc.b.name, lambda e, fc=fc, ci=ci: e.dma_start(out=fc.t[:], in_=flog[:, ci * CH:(ci + 1) * CH]),
              writes=[fc.b])
        cprev = carry[(ci + 1) % 2]
        ccur = carry[ci % 2]
        S.op("dve", lambda e, fs=fs, fc=fc, cprev=cprev: e.tensor_tensor_scan(
            fs.t[:], onesf.t[:], fc.t[:], cprev.t[:, 0:1], ALU.mult, ALU.add),
            reads=[onesf.b, fc.b, cprev.b], writes=[fs.b])
        S.op("dve", lambda e, fs=fs, ccur=ccur: e.tensor_copy(ccur.t[:], fs.t[:, CH - 1:CH]), reads=[fs.b], writes=[ccur.b])
        fview = fs.t[:, :].rearrange("h (c i) -> h c i", c=8)
        fqv = Fq.t[:, ci * 128:(ci + 1) * 128]
        for c in range(8):
            if c == 0:
                S.op("dve", lambda e, fview=fview, fqv=fqv, c=c: e.tensor_scalar(
                    fqv, fview[:, c, :], oh.t[:, c:c + 1], None, ALU.mult), reads=[fs.b, oh.b], writes=[Fq.b])
            else:
                S.op("dve", lambda e, fview=fview, fqv=fqv, c=c: e.scalar_tensor_tensor(
                    fqv, fview[:, c, :], oh.t[:, c:c + 1], fqv, ALU.mult, ALU.add), reads=[fs.b, oh.b, Fq.b], writes=[Fq.b])
        S.op("dve", lambda e, fs=fs: e.tensor_scalar(r1.t[:], fs.t[:], -1.0, None, ALU.mult), reads=[fs.b], writes=[r1.b])
        for part in range(3):
            S.op("dve", lambda e, part=part: e.tensor_copy(ka.t[:, 3 + part, :], r1.t[:]), reads=[r1.b], writes=[ka.b])
            if part < 2:
                S.op("dve", lambda e, part=part: e.tensor_tensor(r1.t[:], r1.t[:], ka.t[:, 3 + part, :], ALU.subtract),
                     reads=[r1.b, ka.b], writes=[r1.b])
        S.dma("sp", "st_kaug", lambda e, ci=ci: e.dma_start(out=kaug[:, :, ci * CH:(ci + 1) * CH], in_=ka.t[:]),
              reads=[ka.b], writes=[b_kaug])
    for qi in range(4):
        S.op("dve", lambda e, qi=qi: e.tensor_copy(r1.t[:, 0:512], Fq.t[:, qi * 512:(qi + 1) * 512]), reads=[Fq.b], writes=[r1.b])
        for part in range(3):
            S.op("dve", lambda e, part=part: e.tensor_copy(qa.t[:, part, :], r1.t[:, 0:512]), reads=[r1.b], writes=[qa.b])
            if part < 2:
                S.op("dve", lambda e, part=part: e.tensor_tensor(r1.t[:, 0:512], r1.t[:, 0:512], qa.t[:, part, :], ALU.subtract),
                     reads=[r1.b, qa.b], writes=[r1.b])
        S.dma("sp", "st_qaug", lambda e, qi=qi: e.dma_start(out=qaug[:, :, qi * 512:(qi + 1) * 512], in_=qa.t[:]),
              reads=[qa.b], writes=[b_qaug])

    kTs = cx.sb("kTs", [128, S_ALL], BF16, n=2)
    qs = cx.sb("qs", [128, TL], BF16, n=2)
    vs = cx.sb("vs", [128, 128 * 130], BF16, n=2)
    mx = cx.sb("mx", [128, TL], BF16, n=2)
    for sl in range(2):
        v3 = vs[sl].t[:, :].rearrange("p (k c) -> p k c", c=130)
        S.op("pool", lambda e, v3=v3: e.memset(v3[:, :, 64:65], 1.0), writes=[vs[sl].b])
        S.op("pool", lambda e, v3=v3: e.memset(v3[:, :, 129:130], 1.0), writes=[vs[sl].b])
    zp = cx.ps("zp", [128, 512], F32, n=3)
    op_ = cx.ps("op", [128, 512], F32, n=2)
    bcp = cx.ps("bcp", [128, 512], F32, n=1)
    pb_ = cx.sb("pb", [128, 512], BF16, n=4)
    osb = cx.sb("osb", [128, 512], F32, n=1)
    rbs = cx.sb("rbs", [128, 512], F32, n=1)

    def load_head(h):
        sl = h % 2
        k_, q_ = kTs[sl], qs[sl]
        S.dma("sp", "ldq%d" % sl, lambda e: e.dma_start(out=q_.t[0:64, :], in_=qT[h * 64:(h + 1) * 64, :]), writes=[q_.b])
        S.dma("sp", "ldq%d" % sl, lambda e: e.dma_start(out=q_.t[64:70, :], in_=qaug[h]), reads=[b_qaug], writes=[q_.b])
        S.dma("sp", "ldk%d" % sl, lambda e: e.dma_start(out=k_.t[64:70, :], in_=kaug[h]), reads=[b_kaug], writes=[k_.b])
        for part in range(4):
            c0 = part * 4096
            S.dma("sp", "ldk%d" % sl,
                  lambda e, c0=c0: e.dma_start(out=k_.t[0:64, c0:c0 + 4096], in_=kT[h * 64:(h + 1) * 64, c0:c0 + 4096]),
                  writes=[k_.b])

    def load_v(hp):
        v_ = vs[hp % 2]
        v3 = v_.t[:, :].rearrange("p (k c) -> p k c", c=130)
        src = vr[hp].rearrange("p (k c) -> p k c", c=128)
        for part in range(8):
            k0 = part * 16
            for hh in range(2):
                S.dma("sp", "ldv%d" % (hp % 2),
                      lambda e, k0=k0, hh=hh: e.dma_start(out=v3[:, k0:k0 + 16, 65 * hh:65 * hh + 64],
                                                          in_=src[:, k0:k0 + 16, 64 * hh:64 * hh + 64]),
                      writes=[v_.b])

    load_v(0)
    load_head(0)

    def do_head(h):
        sl = h % 2
        hp = h // 2
        odd = h % 2
        k_, q_ = kTs[sl], qs[sl]
        v_ = vs[hp % 2]
        m_ = mx[hp % 2]
        if h + 1 < n_heads:
            if (h + 1) % 2 == 0:
                load_v((h + 1) // 2)
            load_head(h + 1)
        items = []
        for J in range(4):
            nkb = 32 * J + 32
            for kb in range(nkb - 1, -1, -1):
                items.append((J, kb, nkb))
        st = {}

        def s1(it):
            J, kb, nkb = it
            r = kb - 32 * J
            c0 = 128 * (r // 8) if r >= 0 else 0
            z = cx.nxt(zp)
            p_ = cx.nxt(pb_)
            S.op("pe", lambda e: e.matmul(z.t[:, c0:512], k_.t[0:70, kb * 128:(kb + 1) * 128],
                                          q_.t[0:70, 512 * J + c0:512 * J + 512], start=True, stop=(r < 0)),
                 reads=[k_.b, q_.b], writes=[z.b])
            if r >= 0:
                i = r % 8
                S.op("pe", lambda e: e.matmul(z.t[:, c0:c0 + 128], idt.t[:], nm.t[:, i * 128:(i + 1) * 128],
                                              start=False, stop=True), reads=[idt.b, nm.b], writes=[z.b])
            S.op("act", lambda e: e.activation(p_.t[:, c0:512], z.t[:, c0:512], AF.Exp), reads=[z.b], writes=[p_.b])
            st[it] = (c0, p_)

        def s2(it):
            J, kb, nkb = it
            c0, p_ = st.pop(it)
            o_ = op_[J % 2]
            if odd:
                S.op("pe", lambda e: e.matmul(o_.t[:, c0:512], v_.t[:, kb * 130 + 1:kb * 130 + 129], p_.t[:, c0:512],
                                              start=(kb == nkb - 1), stop=(kb == 0), skip_group_check=(kb != nkb - 1)),
                     reads=[v_.b, p_.b], writes=[o_.b])
            else:
                S.op("pe", lambda e: e.matmul(o_.t[0:65, c0:512], v_.t[:, kb * 130:kb * 130 + 65], p_.t[:, c0:512],
                                              start=(kb == nkb - 1), stop=(kb == 0), skip_group_check=(kb != nkb - 1)),
                     reads=[v_.b, p_.b], writes=[o_.b])
            if kb == 0:
                ob = cx.nxt(osb)
                rb = cx.nxt(rbs)
                bc = bcp[0]
                if odd:
                    S.op("act", lambda e: e.copy(ob.t[32:64, :], o_.t[32:64, :]), reads=[o_.b], writes=[ob.b])
                    S.op("act", lambda e: e.copy(ob.t[64:128, :], o_.t[64:128, :]), reads=[o_.b], writes=[ob.b])
                    S.op("pe", lambda e: e.matmul(bc.t[:, :], selt.t[32:64, 128:256], ob.t[32:64, :], start=True, stop=True),
                         reads=[selt.b, ob.b], writes=[bc.b])
                    S.op("dve", lambda e: e.reciprocal(rb.t[64:128, :], bc.t[64:128, :]), reads=[bc.b], writes=[rb.b])
                    S.op("dve", lambda e: e.tensor_tensor(m_.t[64:128, 512 * J:512 * J + 512], ob.t[64:128, :], rb.t[64:128, :], ALU.mult),
                         reads=[ob.b, rb.b], writes=[m_.b])
                else:
                    S.op("act", lambda e: e.copy(ob.t[0:65, :], o_.t[0:65, :]), reads=[o_.b], writes=[ob.b])
                    S.op("pe", lambda e: e.matmul(bc.t[0:64, :], selt.t[64:65, 0:64], ob.t[64:65, :], start=True, stop=True),
                         reads=[selt.b, ob.b], writes=[bc.b])
                    S.op("dve", lambda e: e.reciprocal(rb.t[0:64, :], bc.t[0:64, :]), reads=[bc.b], writes=[rb.b])
                    S.op("dve", lambda e: e.tensor_tensor(m_.t[0:64, 512 * J:512 * J + 512], ob.t[0:64, :], rb.t[0:64, :], ALU.mult),
                         reads=[ob.b, rb.b], writes=[m_.b])

        n = len(items)
        for tau in range(-1, n + 1):
            if 0 <= tau + 1 < n:
                s1(items[tau + 1])
            if 0 <= tau - 1 < n:
                s2(items[tau - 1])
        if odd:
            cx.store("o_mix", mixT[hp * 128:(hp + 1) * 128, :], m_, m_.t[:])

    for h in range(n_heads):
        do_head(h)
    return cx.finish()


_PROGS = {}


def _prog(name, fn):
    if name not in _PROGS:
        _PROGS[name] = fn()
    return _PROGS[name]


def _shard_tok(a):
    F_ = a.shape[-1]
    r = a.reshape(16, 8, 128, F_)
    return [np.ascontiguousarray(r[:, c].reshape(TL, F_)) for c in range(NC_)]


def _gather_T(parts):
    R = parts[0].shape[0]
    out = np.zeros((R, 16, 8, 128), dtype=parts[0].dtype)
    for c in range(NC_):
        out[:, :, c, :] = parts[c].reshape(R, 16, 128)
    return out.reshape(R, S_ALL)


def _gather_v(parts):
    va = np.zeros((16, 8, 128, D), dtype=parts[0].dtype)
    for c in range(NC_):
        va[:, c] = parts[c].reshape(16, 128, D)
    va = va.reshape(128, 128, 8, 128)
    return np.ascontiguousarray(va.transpose(2, 1, 0, 3)).reshape(8, 128, 128 * 128)


def _consts():
    ar = np.arange(128)
    c = {}
    c["ident"] = np.eye(128, dtype=np.float32).astype(NPBF)
    c["tri"] = (ar[:, None] >= ar[None, :]).astype(np.float32).astype(NPBF)
    c["omt"] = (ar[:, None] < ar[None, :]).astype(np.float32).astype(NPBF)
    masks, negm, oh = [], [], []
    for cc in range(NC_):
        m = np.zeros((128, 8, 128), np.float32)
        n = np.full((128, 8, 128), -30000.0, np.float32)
        for i in range(8):
            if i < cc:
                m[:, i, :] = 1.0
                n[:, i, :] = 0.0
            elif i == cc:
                m[:, i, :] = (ar[:, None] < ar[None, :])
                n[:, i, :] = np.where(ar[:, None] <= ar[None, :], 0.0, -30000.0)
        masks.append(m.reshape(128, 1024).astype(NPBF))
        negm.append(n.reshape(128, 1024).astype(NPBF))
        o = np.zeros((NH, 8), np.float32)
        o[:, cc] = 1.0
        oh.append(o)
    c["masks"], c["negm"], c["onehot"] = masks, negm, oh
    sel = np.zeros((128, 256), np.float32)
    sel[64, 0:64] = 1.0
    sel[63, 192:256] = 1.0
    c["sel"] = sel
    return c


def _run(nc, in_maps):
    return run_bass_kernel_spmd(nc, in_maps, core_ids=list(range(NC_))).results


def kernel(x, p, attn_norm_g, sb_w_qkv, sb_w_o, shared_norm_g, shared_w_kvf, shared_b_f, shared_k_norm_g,
           fox_w_q, fox_q_norm_g, fox_w_o, ffn_norm_g, ffn_w_gu, ffn_w_d, ple_norm_g, ple_w_gate, ple_w_proj):
    f32 = lambda a: np.ascontiguousarray(np.asarray(a, dtype=np.float32))
    x = f32(x)
    p = f32(p)
    C = _consts()
    xs = _shard_tok(x[0])
    p0 = _shard_tok(p[0, 0])
    p1 = _shard_tok(p[1, 0])
    r1 = _run(_prog("pre0", build_pre0),
              [{"x": xs[c], "g": f32(attn_norm_g[0]), "w": f32(sb_w_qkv[0]), "ident": C["ident"]} for c in range(NC_)])
    kT_all = _gather_T([r["kT"] for r in r1])
    vr = _gather_v([r["v"] for r in r1])
    r2 = _run(_prog("attn0", build_attn0),
              [{"qT": r1[c]["qT"], "kT": kT_all, "vr": vr, "masks": C["masks"][c], "tri": C["tri"], "omt": C["omt"]}
               for c in range(NC_)])

    def post_in(c, h, mix, pl, w_o, li):
        return {"h": h, "mixT": mix, "p": pl, "w_o": f32(w_o), "ffn_g": f32(ffn_norm_g[li]), "w_gu": f32(ffn_w_gu[li]),
                "w_d": f32(ffn_w_d[li]), "ple_g": f32(ple_norm_g[li]), "w_gate": f32(ple_w_gate[li]),
                "w_proj": f32(ple_w_proj[li]), "ident": C["ident"]}

    in3 = []
    for c in range(NC_):
        d = post_in(c, xs[c], r2[c]["mixT"], p0[c], sb_w_o[0], 0)
        d.update({"a_g": f32(attn_norm_g[1]), "w_q": f32(fox_w_q[0]), "qn_g": f32(fox_q_norm_g[0]),
                  "sh_g": f32(shared_norm_g), "w_kvf": f32(shared_w_kvf), "b_f": f32(shared_b_f),
                  "kn_g": f32(shared_k_norm_g)})
        in3.append(d)
    r3 = _run(_prog("post_n", lambda: build_post(True)), in3)
    k1_all = _gather_T([r["k1T"] for r in r3])
    vr1 = _gather_v([r["v1"] for r in r3])
    fl_all = _gather_T([r["flogT"] for r in r3])
    r4 = _run(_prog("attn1", build_attn1),
              [{"qT": r3[c]["q1T"], "kT": k1_all, "vr": vr1, "flog": fl_all, "onehot": C["onehot"][c],
                "negm": C["negm"][c], "ident": C["ident"], "sel": C["sel"]} for c in range(NC_)])
    r5 = _run(_prog("post_l", lambda: build_post(False)),
              [post_in(c, r3[c]["hout"], r4[c]["mixT"], p1[c], fox_w_o[0], 1) for c in range(NC_)])
    out = np.zeros((16, 8, 128, D), np.float32)
    for c in range(NC_):
        out[:, c] = r5[c]["hout"].reshape(16, 128, D)
    return out.reshape(1, S_ALL, D)
```

```python
import numpy as np
import ml_dtypes
from contextlib import ExitStack
import concourse.bass as bass
import concourse.mybir as mybir
from concourse.bass_utils import run_bass_kernel_spmd

F32 = mybir.dt.float32
BF16 = mybir.dt.bfloat16
AF = mybir.ActivationFunctionType
ALU = mybir.AluOpType
NPBF = ml_dtypes.bfloat16

NC_ = 8
D = 1024
S_ALL = 16384
TL = 2048
NH = 16
DH = 64
DFF = 2816
PLE = 256
EPS = 1e-6
EPOCH = 4096


class Buf:
    __slots__ = ("name", "w", "r", "wm")

    def __init__(self, name, multi=False):
        self.name = name
        self.w = None
        self.r = []
        self.wm = {} if multi else None


class Tile:
    __slots__ = ("t", "b")

    def __init__(self, t, name):
        self.t = t
        self.b = Buf(name)


class Sched:
    ENG = ("pe", "act", "dve", "pool", "sp")

    def __init__(self, nc, es):
        self.nc = nc
        self.es = es
        self.lists = {e: [] for e in self.ENG}
        self.sems = {}
        self.cnt = {}
        self.alias = {}
        self.waited = {e: {} for e in self.ENG}
        self.ecount = {e: 0 for e in self.ENG}
        self.nsem = 0

    def _mksem(self, key):
        self.nsem += 1
        self.sems[key] = self.es.enter_context(self.nc.semaphore("s%d" % self.nsem))
        self.cnt[key] = 0

    def _deps(self, eng, reads, writes):
        deps = {}

        def add(tok):
            if tok is None:
                return
            k, v = tok
            if deps.get(k, 0) < v:
                deps[k] = v

        for b in reads:
            add(b.w)
            if b.wm is not None:
                for t in b.wm.items():
                    add(t)
        for b in writes:
            add(b.w)
            for t in b.r:
                add(t)
        out = []
        w = self.waited[eng]
        for k, v in deps.items():
            if eng == "pe" and k.startswith("E_pe#"):
                continue
            if w.get(k, 0) >= v:
                continue
            w[k] = v
            out.append((k, v))
        return out

    @staticmethod
    def _commit(tok, reads, writes):
        for b in writes:
            if b.wm is not None:
                if b.wm.get(tok[0], 0) < tok[1]:
                    b.wm[tok[0]] = tok[1]
                continue
            b.w = tok
            b.r = []
        for b in reads:
            b.r.append(tok)

    def op(self, eng, fn, reads=(), writes=()):
        deps = self._deps(eng, reads, writes)
        ep = self.ecount[eng] // EPOCH
        key = "E_%s#%d" % (eng, ep)
        if key not in self.sems:
            self._mksem(key)
        self.ecount[eng] += 1
        self.cnt[key] += 1
        tok = (key, self.cnt[key])
        sems = self.sems

        def thunk(e, deps=deps, fn=fn, key=key):
            for k, v in deps:
                e.wait_ge(sems[k], v)
            fn(e).then_inc(sems[key], 1)

        self.lists[eng].append(thunk)
        self._commit(tok, reads, writes)
        return tok

    def dma(self, eng, semname, fn, reads=(), writes=()):
        deps = self._deps(eng, reads, writes)
        key = self.alias.get(semname)
        if key is None or self.cnt[key] >= 16 * 240:
            n = 0 if key is None else int(key.split("#")[1]) + 1
            key = "D_%s#%d" % (semname, n)
            self.alias[semname] = key
            self._mksem(key)
        self.cnt[key] += 16
        tok = (key, self.cnt[key])
        sems = self.sems

        def thunk(e, deps=deps, fn=fn, key=key):
            for k, v in deps:
                e.wait_ge(sems[k], v)
            fn(e).then_inc(sems[key], 16)

        self.lists[eng].append(thunk)
        self._commit(tok, reads, writes)
        return tok

    def wait_all(self, eng, toks):
        sems = self.sems
        toks = list(toks)

        def thunk(e):
            for k, v in toks:
                e.wait_ge(sems[k], v)

        self.lists[eng].append(thunk)

    def emit(self):
        L = self.lists
        with self.nc.Block() as block:
            @block.tensor
            def _(e):
                for t in L["pe"]:
                    t(e)

            @block.scalar
            def _(e):
                for t in L["act"]:
                    t(e)

            @block.vector
            def _(e):
                for t in L["dve"]:
                    t(e)

            @block.gpsimd
            def _(e):
                for t in L["pool"]:
                    t(e)

            @block.sync
            def _(e):
                for t in L["sp"]:
                    t(e)


class Cx:
    def __init__(self):
        self.nc = bass.Bass("TRN2", target_bir_lowering=False)
        self.es = ExitStack()
        self.S = Sched(self.nc, self.es)
        self.out_toks = {}
        self.rr = {}

    def din(self, name, shape, dt):
        return self.nc.dram_tensor(name, list(shape), dt, kind="ExternalInput").ap()

    def dout(self, name, shape, dt):
        return self.nc.dram_tensor(name, list(shape), dt, kind="ExternalOutput").ap()

    def dint(self, name, shape, dt):
        return self.nc.dram_tensor(name, list(shape), dt, kind="Internal").ap()

    def sb(self, name, shape, dt, n=None):
        if n is None:
            return Tile(self.es.enter_context(self.nc.sbuf_tensor("sb_" + name, list(shape), dt)), name)
        return [Tile(self.es.enter_context(self.nc.sbuf_tensor("sb_%s%d" % (name, i), list(shape), dt)),
                     "%s%d" % (name, i)) for i in range(n)]

    def ps(self, name, shape, dt, n=None):
        if n is None:
            return Tile(self.es.enter_context(self.nc.psum_tensor("ps_" + name, list(shape), dt)), name)
        return [Tile(self.es.enter_context(self.nc.psum_tensor("ps_%s%d" % (name, i), list(shape), dt)),
                     "%s%d" % (name, i)) for i in range(n)]

    def nxt(self, lst, key=None):
        key = key or id(lst)
        i = self.rr.get(key, 0)
        self.rr[key] = i + 1
        return lst[i % len(lst)]

    def store(self, semname, out_ap, tile, in_ap, eng="act"):
        tok = self.S.dma(eng, "st_" + tile.b.name, lambda e: e.dma_start(out=out_ap, in_=in_ap), reads=[tile.b])
        self.out_toks[tok[0]] = max(self.out_toks.get(tok[0], 0), tok[1])

    def finish(self):
        self.S.wait_all("sp", list(self.out_toks.items()))
        self.S.emit()
        self.es.close()
        return self.nc


class Dense:
    def __init__(self, cx, ident_ap):
        self.cx = cx
        S = cx.S
        self.ident = cx.sb("ident", [128, 128], BF16)
        S.dma("sp", "const1", lambda e: e.dma_start(out=self.ident.t[:], in_=ident_ap), writes=[self.ident.b])
        self.junk = cx.sb("junk", [128, 1024], BF16, n=2)
        self.ss = cx.sb("ss", [128, 1], F32, n=4)
        self.lnv = cx.sb("lnv", [128, 1], F32, n=4)
        self.rstd = cx.sb("rstd", [128, 1], F32, n=4)
        self.hn = cx.sb("hn", [128, 1024], BF16, n=2)
        self.psT = cx.ps("psT", [128, 1024], BF16, n=2)
        self.wst = cx.sb("wst", [128, 512], F32, n=4)
        self.gv = {}
        self.evq = 0
        self.conv_engs = ("pool", "dve")

    def load_gain(self, name, g_ap):
        cx = self.cx
        t = cx.sb("g_" + name, [128, 8], F32)
        cx.S.dma("sp", "cg_" + name, lambda e: e.dma_start(out=t.t[:], in_=g_ap.rearrange("(k p) -> p k", p=128),
                                                      allow_slow_non_contiguous=True), writes=[t.b])
        self.gv[name] = t
        return t

    def prep_weight(self, w_ap, K, N, dst_fn, gain=None, post=None, c0=0, c1=None):
        cx = self.cx
        S = cx.S
        c1 = N if c1 is None else c1
        for kc in range(K // 128):
            for n0 in range(c0, c1, 512):
                wd = min(512, c1 - n0)
                st = cx.nxt(self.wst)
                S.dma("sp", "wst_" + st.b.name,
                      lambda e, st=st, kc=kc, n0=n0, wd=wd: e.dma_start(
                          out=st.t[:, 0:wd], in_=w_ap[kc * 128:(kc + 1) * 128, n0:n0 + wd]),
                      writes=[st.b])
                dt_, dap = dst_fn(kc, n0, wd)
                eng = self.conv_engs[self.evq % 2]
                self.evq += 1
                pv = post(n0) if post is not None else None
                if eng == "act":
                    if gain is not None:
                        g = gain
                        S.op("act", lambda e, st=st, dap=dap, kc=kc, wd=wd, g=g: e.activation(
                            dap, st.t[:, 0:wd], AF.Copy, scale=g.t[:, kc:kc + 1]), reads=[st.b, g.b], writes=[dt_.b])
                    else:
                        S.op("act", lambda e, st=st, dap=dap, wd=wd: e.copy(dap, st.t[:, 0:wd]),
                             reads=[st.b], writes=[dt_.b])
                elif gain is not None:
                    g = gain
                    if pv is not None:
                        fn = lambda e, st=st, dap=dap, kc=kc, wd=wd, g=g, pv=pv: e.tensor_scalar(
                            dap, st.t[:, 0:wd], g.t[:, kc:kc + 1], pv, ALU.mult, ALU.mult)
                    else:
                        fn = lambda e, st=st, dap=dap, kc=kc, wd=wd, g=g: e.tensor_scalar(
                            dap, st.t[:, 0:wd], g.t[:, kc:kc + 1], None, ALU.mult)
                    S.op(eng, fn, reads=[st.b, g.b], writes=[dt_.b])
                else:
                    S.op(eng, lambda e, st=st, dap=dap, wd=wd: e.tensor_copy(dap, st.t[:, 0:wd]),
                         reads=[st.b], writes=[dt_.b])

    def norm_T(self, h, hnT, col0):
        cx = self.cx
        S = cx.S
        junk = cx.nxt(self.junk)
        ss = cx.nxt(self.ss)
        lnv = cx.nxt(self.lnv)
        rstd = cx.nxt(self.rstd)
        hn = cx.nxt(self.hn)
        pT = cx.nxt(self.psT)
        S.op("act", lambda e: e.activation(junk.t[:], h.t[:], AF.Square, accum_out=ss.t[:]),
             reads=[h.b], writes=[junk.b, ss.b])
        S.op("act", lambda e: e.activation(lnv.t[:], ss.t[:], AF.Ln, scale=1.0 / D, bias=self.eps.t[:, 0:1]),
             reads=[ss.b, self.eps.b], writes=[lnv.b])
        S.op("act", lambda e: e.activation(rstd.t[:], lnv.t[:], AF.Exp, scale=-0.5),
             reads=[lnv.b], writes=[rstd.b])
        S.op("dve", lambda e: e.tensor_scalar(hn.t[:], h.t[:], rstd.t[:, 0:1], None, ALU.mult),
             reads=[h.b, rstd.b], writes=[hn.b])
        for kc in range(8):
            S.op("pe", lambda e, kc=kc: e.transpose(pT.t[:, kc * 128:(kc + 1) * 128],
                                                    hn.t[:, kc * 128:(kc + 1) * 128], self.ident.t[:]),
                 reads=[hn.b, self.ident.b], writes=[pT.b])
        S.op("dve", lambda e: e.tensor_copy(hnT.t[:, :, col0:col0 + 128],
                                            pT.t[:, :].rearrange("p (k t) -> p k t", k=8)),
             reads=[pT.b], writes=[hnT.b])

    def consts(self):
        cx = self.cx
        self.eps = cx.sb("epsc", [128, 1], F32)
        cx.S.op("pool", lambda e: e.memset(self.eps.t[:], EPS), writes=[self.eps.b])


def build_pre0():
    cx = Cx()
    S = cx.S
    x = cx.din("x", [TL, D], F32)
    g = cx.din("g", [D], F32)
    w = cx.din("w", [D, 3 * D], F32)
    ident = cx.din("ident", [128, 128], BF16)
    qT = cx.dout("qT", [D, TL], BF16)
    kT = cx.dout("kT", [D, TL], BF16)
    v = cx.dout("v", [TL, D], BF16)
    dn = Dense(cx, ident)
    dn.consts()
    gt = dn.load_gain("a", g)
    Wb = cx.sb("Wb", [128, 8, 3 * D], BF16)
    dn.prep_weight(w, D, 3 * D, lambda kc, n0, wd: (Wb, Wb.t[:, kc, n0:n0 + wd]), gain=gt,
                   post=lambda n0: (0.125 if n0 < D else 1.0))
    hblk = cx.sb("hblk", [128, D], F32, n=3)
    hnT = cx.sb("hnT", [128, 8, 512], BF16, n=2)
    pm = cx.ps("pm", [128, 512], F32, n=4)
    ost = cx.sb("ost", [128, 512], BF16, n=4)
    ev = 0
    for gi in range(4):
        hT = cx.nxt(hnT)
        for b in range(4):
            h = cx.nxt(hblk)
            r0 = (gi * 4 + b) * 128
            S.dma("sp", "ld_" + h.b.name, lambda e, h=h, r0=r0: e.dma_start(out=h.t[:], in_=x[r0:r0 + 128, :]),
                  writes=[h.b])
            dn.norm_T(h, hT, b * 128)
        for n in range(16):
            p = cx.nxt(pm)
            for kc in range(8):
                S.op("pe", lambda e, p=p, kc=kc, n=n, hT=hT: e.matmul(
                    p.t[:], Wb.t[:, kc, n * 128:(n + 1) * 128], hT.t[:, kc, :], start=(kc == 0), stop=(kc == 7)),
                    reads=[Wb.b, hT.b], writes=[p.b])
            o = cx.nxt(ost)
            if ev % 2 == 0:
                S.op("act", lambda e, o=o, p=p: e.copy(o.t[:], p.t[:]), reads=[p.b], writes=[o.b])
            else:
                S.op("dve", lambda e, o=o, p=p: e.tensor_copy(o.t[:], p.t[:]), reads=[p.b], writes=[o.b])
            ev += 1
            dst = qT if n < 8 else kT
            rr = (n % 8) * 128
            cx.store("o_qk", dst[rr:rr + 128, gi * 512:(gi + 1) * 512], o, o.t[:])
        for b in range(4):
            for hf in range(2):
                p = cx.nxt(pm)
                for kc in range(8):
                    S.op("pe", lambda e, p=p, kc=kc, b=b, hf=hf, hT=hT: e.matmul(
                        p.t[:], hT.t[:, kc, b * 128:(b + 1) * 128],
                        Wb.t[:, kc, 2 * D + hf * 512:2 * D + (hf + 1) * 512], start=(kc == 0), stop=(kc == 7)),
                        reads=[Wb.b, hT.b], writes=[p.b])
                o = cx.nxt(ost)
                if ev % 2 == 0:
                    S.op("act", lambda e, o=o, p=p: e.copy(o.t[:], p.t[:]), reads=[p.b], writes=[o.b])
                else:
                    S.op("dve", lambda e, o=o, p=p: e.tensor_copy(o.t[:], p.t[:]), reads=[p.b], writes=[o.b])
                ev += 1
                r0 = (gi * 4 + b) * 128
                cx.store("o_v", v[r0:r0 + 128, hf * 512:(hf + 1) * 512], o, o.t[:])
    return cx.finish()


def build_attn0(n_pairs=8):
    cx = Cx()
    S = cx.S
    qT = cx.din("qT", [D, TL], BF16)
    kT = cx.din("kT", [D, S_ALL], BF16)
    vr = cx.din("vr", [8, 128, 128 * 128], BF16)
    masks = cx.din("masks", [128, 8 * 128], BF16)
    tri = cx.din("tri", [128, 128], BF16)
    omt = cx.din("omt", [128, 128], BF16)
    mixT = cx.dout("mixT", [D, TL], BF16)

    mk = cx.sb("mk", [128, 8 * 128], BF16)
    trt = cx.sb("trt", [128, 128], BF16)
    omtt = cx.sb("omtt", [128, 128], BF16)
    S.dma("sp", "const3", lambda e: e.dma_start(out=mk.t[:], in_=masks), writes=[mk.b])
    S.dma("sp", "const4", lambda e: e.dma_start(out=trt.t[:], in_=tri), writes=[trt.b])
    S.dma("sp", "const5", lambda e: e.dma_start(out=omtt.t[:], in_=omt), writes=[omtt.b])

    one = cx.sb("one", [128, 1], F32)
    S.op("pool", lambda e: e.memset(one.t[:], 1.0), writes=[one.b])
    kTs = cx.sb("kTs", [128, S_ALL], BF16, n=2)
    vs = cx.sb("vs", [128, 128 * 128], BF16, n=2)
    qs = cx.sb("qs", [128, TL], BF16, n=2)
    mx = cx.sb("mx", [128, TL], BF16, n=2)
    z2 = cx.ps("z2", [128, 1024], F32, n=2)
    C2 = cx.ps("C2", [128, 1024], F32)
    op_ = cx.ps("op", [128, 512], F32, n=2)
    e2 = cx.sb("e2", [128, 1024], F32, n=5)
    sp2 = cx.sb("sp2", [128, 1024], BF16, n=5)
    x2 = cx.sb("x2", [128, 1024], BF16, n=3)
    w2 = cx.sb("w2", [128, 1024], BF16, n=3)

    def V(t, c0, w=None):
        v = t.t[:, :].rearrange("p (h c) -> p h c", h=2)
        return v[:, :, c0:512] if w is None else v[:, :, c0:c0 + w]

    def load_pair(hp):
        sl = hp % 2
        k_, v_, q_ = kTs[sl], vs[sl], qs[sl]
        S.dma("sp", "ldq%d" % sl, lambda e: e.dma_start(out=q_.t[:], in_=qT[hp * 128:(hp + 1) * 128, :]),
              writes=[q_.b])
        for part in range(4):
            c0 = part * 4096
            S.dma("sp", "ldk%d" % sl,
                  lambda e, c0=c0: e.dma_start(out=k_.t[:, c0:c0 + 4096], in_=kT[hp * 128:(hp + 1) * 128, c0:c0 + 4096]),
                  writes=[k_.b])
            S.dma("sp", "ldv%d" % sl,
                  lambda e, c0=c0: e.dma_start(out=v_.t[:, c0:c0 + 4096], in_=vr[hp, :, c0:c0 + 4096]),
                  writes=[v_.b])

    load_pair(0)

    def do_pair(hp):
        sl = hp % 2
        k_, v_, q_, m_ = kTs[sl], vs[sl], qs[sl], mx[sl]
        if hp + 1 < n_pairs:
            load_pair(hp + 1)
        items = []
        for J in range(4):
            nkb = 32 * J + 32
            for kb in range(nkb - 1, -1, -1):
                items.append((J, kb, nkb))
        st = {}

        def s1(it):
            J, kb, nkb = it
            r = kb - 32 * J
            c0 = 128 * (r // 8) if r >= 0 else 0
            z = cx.nxt(z2)
            e_ = cx.nxt(e2)
            sp_ = cx.nxt(sp2)
            for hh in range(2):
                pb = 64 * hh
                S.op("pe", lambda e, hh=hh, pb=pb: e.matmul(
                    z.t[:, 512 * hh + c0:512 * hh + 512], k_.t[pb:pb + 64, kb * 128:(kb + 1) * 128],
                    q_.t[pb:pb + 64, 512 * J + c0:512 * J + 512], start=True, stop=True),
                    reads=[k_.b, q_.b], writes=[z.b])
            S.op("act", lambda e: e.activation(V(e_, c0), V(z, c0), AF.Exp), reads=[z.b], writes=[e_.b])
            if r >= 0:
                i = r % 8
                S.op("pool", lambda e: e.tensor_tensor(
                    V(e_, c0, 128), V(e_, c0, 128),
                    mk.t[:, i * 128:(i + 1) * 128].unsqueeze(1).to_broadcast([128, 2, 128]), ALU.mult),
                    reads=[e_.b, mk.b], writes=[e_.b])
            S.op("act", lambda e: e.activation(V(sp_, c0), V(e_, c0), AF.Ln, bias=one.t[:, 0:1]),
                 reads=[e_.b, one.b], writes=[sp_.b])
            st[it] = [c0, e_, sp_, None, None]

        def s2(it):
            J, kb, nkb = it
            c0, e_, sp_, _, _ = st[it]
            x_ = cx.nxt(x2)
            for hh in range(2):
                S.op("pe", lambda e, hh=hh: e.matmul(C2.t[:, 512 * hh + c0:512 * hh + 512], trt.t[:],
                                                     sp_.t[:, 512 * hh + c0:512 * hh + 512],
                                                     start=(kb == nkb - 1), stop=True,
                                                     skip_group_check=(kb != nkb - 1 and kb != 0)),
                     reads=[trt.b, sp_.b], writes=[C2.b])
            S.op("act", lambda e: e.activation(V(x_, c0), V(C2, c0), AF.Exp, scale=-1.0), reads=[C2.b], writes=[x_.b])
            st[it][3] = x_

        def s3(it):
            J, kb, nkb = it
            c0, e_, sp_, x_, _ = st[it]
            w_ = cx.nxt(w2)
            if kb > 0:
                for hh in range(2):
                    S.op("pe", lambda e, hh=hh: e.matmul(C2.t[:, 512 * hh + c0:512 * hh + 512], omtt.t[:],
                                                         sp_.t[:, 512 * hh + c0:512 * hh + 512], start=False, stop=True,
                                                         skip_group_check=True),
                         reads=[omtt.b, sp_.b], writes=[C2.b])
            S.op("dve", lambda e: e.tensor_tensor(V(w_, c0), V(e_, c0), V(x_, c0), ALU.mult),
                 reads=[e_.b, x_.b], writes=[w_.b])
            st[it][4] = w_

        def s4(it):
            J, kb, nkb = it
            c0, e_, sp_, x_, w_ = st.pop(it)
            o_ = op_[J % 2]
            for hh in range(2):
                pb = 64 * hh
                S.op("pe", lambda e, hh=hh, pb=pb: e.matmul(
                    o_.t[pb:pb + 64, c0:512], v_.t[:, kb * 128 + pb:kb * 128 + pb + 64],
                    w_.t[:, 512 * hh + c0:512 * hh + 512], start=(kb == nkb - 1), stop=(kb == 0),
                    skip_group_check=(kb != nkb - 1 and kb != 0)),
                    reads=[v_.b, w_.b], writes=[o_.b])
            if kb == 0:
                S.op("act", lambda e: e.copy(m_.t[:, 512 * J:512 * J + 512], o_.t[:, :]), reads=[o_.b], writes=[m_.b])

        n = len(items)
        for tau in range(-1, n + 3):
            if 0 <= tau - 2 < n:
                s3(items[tau - 2])
            if 0 <= tau - 1 < n:
                s2(items[tau - 1])
            if 0 <= tau + 1 < n:
                s1(items[tau + 1])
            if 0 <= tau - 3 < n:
                s4(items[tau - 3])
        cx.store("o_mix", mixT[hp * 128:(hp + 1) * 128, :], m_, m_.t[:])

    for hp in range(n_pairs):
        do_pair(hp)
    return cx.finish()


def build_post(nxt):
    cx = Cx()
    S = cx.S
    h_in = cx.din("h", [TL, D], F32)
    mixT = cx.din("mixT", [D, TL], BF16)
    p_in = cx.din("p", [TL, PLE], F32)
    w_o = cx.din("w_o", [D, D], F32)
    ffn_g = cx.din("ffn_g", [D], F32)
    w_gu = cx.din("w_gu", [D, 2 * DFF], F32)
    w_d = cx.din("w_d", [DFF, D], F32)
    ple_g = cx.din("ple_g", [D], F32)
    w_gate = cx.din("w_gate", [D, D], F32)
    w_proj = cx.din("w_proj", [PLE, D], F32)
    ident = cx.din("ident", [128, 128], BF16)
    hout = cx.dout("hout", [TL, D], F32)
    if nxt:
        a_g = cx.din("a_g", [D], F32)
        w_q = cx.din("w_q", [D, D], F32)
        qn_g = cx.din("qn_g", [DH], F32)
        sh_g = cx.din("sh_g", [D], F32)
        w_kvf = cx.din("w_kvf", [D, 2 * D + NH], F32)
        b_f = cx.din("b_f", [NH], F32)
        kn_g = cx.din("kn_g", [DH], F32)
        q1T = cx.dout("q1T", [D, TL], BF16)
        k1T = cx.dout("k1T", [D, TL], BF16)
        v1 = cx.dout("v1", [TL, D], BF16)
        flogT = cx.dout("flogT", [NH, TL], F32)

    dn = Dense(cx, ident)
    dn.consts()
    dn.conv_engs = ("act", "dve")
    g_ffn = dn.load_gain("ffn", ffn_g)
    g_ple = dn.load_gain("ple", ple_g)

    wbf = cx.sb("wbf", [128, 512], BF16, n=4)

    def to_scratch(name, dst_ap_fn):
        buf = Buf("scr_" + name)

        def dst_fn(kc, n0, wd):
            t = cx.nxt(wbf)
            return t, t.t[:, 0:wd]
        return buf, dst_fn

    def prep_dram(name, w_ap, K, N, dst_ap_fn, gain=None, c0=0, c1=None):
        buf = Buf("scr_" + name, multi=True)
        c1_ = N if c1 is None else c1
        for kc in range(K // 128):
            for n0 in range(c0, c1_, 512):
                wd = min(512, c1_ - n0)
                holder = {}

                def dst_fn(kc_, n0_, wd_, holder=holder):
                    t = cx.nxt(wbf)
                    holder["t"] = t
                    return t, t.t[:, 0:wd_]
                dn.prep_weight(w_ap[kc * 128:(kc + 1) * 128, :], 128, N, dst_fn, gain=None if gain is None else _GainCol(gain, kc),
                               c0=n0, c1=n0 + wd)
                t = holder["t"]
                dap = dst_ap_fn(kc, n0 - c0, wd)
                S.dma("act", "wp_" + t.b.name, lambda e, t=t, dap=dap, wd=wd: e.dma_start(out=dap, in_=t.t[:, 0:wd]),
                      reads=[t.b], writes=[buf])
        return buf

    class _GainCol:
        def __init__(self, g, kc):
            self.b = g.b
            self.t = _Shift(g.t, kc)

    class _Shift:
        def __init__(self, t, kc):
            self._t = t
            self._kc = kc

        def __getitem__(self, idx):
            return self._t[idx[0], self._kc:self._kc + 1]

    def scr8(name, N):
        return cx.dint("scr_" + name, [N // 512, 128, 8, 512], BF16)

    WB = {}
    LAZY = {}

    def ensure(name):
        f = LAZY.pop(name, None)
        if f is not None:
            f()

    WoB = scr8("wo", D)
    LAZY["wo"] = lambda: WB.__setitem__("wo", prep_dram("wo", w_o, D, D, lambda kc, n0, wd: WoB[n0 // 512, :, kc, :]))
    WguB = cx.dint("scr_wgu", [22, 128, 8, 256], BF16)

    def gu_dst(off):
        def f(kc, n0, wd):
            j0 = n0 // 128
            return WguB[j0:j0 + wd // 128, :, kc, off:off + 128].rearrange("j p i -> p j i")
        return f
    def _p_wgu():
        WB["wg"] = prep_dram("wg", w_gu, D, 2 * DFF, gu_dst(0), gain=g_ffn, c0=0, c1=DFF)
        WB["wu"] = prep_dram("wu", w_gu, D, 2 * DFF, gu_dst(128), gain=g_ffn, c0=DFF, c1=2 * DFF)
    LAZY["wgu"] = _p_wgu
    WdB = cx.dint("scr_wd", [4, 128, 22, 256], BF16)
    LAZY["wd"] = lambda: WB.__setitem__("wd", prep_dram(
        "wd", w_d, DFF, D, lambda kc, n0, wd: WdB[n0 // 256:n0 // 256 + 2, :, kc, :].rearrange("q p i -> p q i")))
    WgateB = scr8("wgate", D)
    Wproj = cx.sb("Wproj", [128, 2, D], BF16)

    def _p_wgate():
        WB["wgate"] = prep_dram("wgate", w_gate, D, D, lambda kc, n0, wd: WgateB[n0 // 512, :, kc, :], gain=g_ple)
        dn.prep_weight(w_proj, PLE, D, lambda kc, n0, wd: (Wproj, Wproj.t[:, kc, n0:n0 + wd]))
    LAZY["wgate"] = _p_wgate
    if nxt:
        g_a = dn.load_gain("a1", a_g)
        g_sh = dn.load_gain("sh", sh_g)
        WqB = scr8("wq", D)
        LAZY["wq"] = lambda: WB.__setitem__("wq", prep_dram(
            "wq", w_q, D, D, lambda kc, n0, wd: WqB[n0 // 512, :, kc, :], gain=g_a))
        WkvB = scr8("wkv", 2 * D)
        Wf = cx.sb("Wf", [128, 8, NH], BF16)

        def _p_wkv():
            WB["wkv"] = prep_dram("wkv", w_kvf, D, 2 * D + NH, lambda kc, n0, wd: WkvB[n0 // 512, :, kc, :], gain=g_sh,
                                  c0=0, c1=2 * D)
            dn.prep_weight(w_kvf, D, 2 * D + NH, lambda kc, n0, wd: (Wf, Wf.t[:, kc, 0:wd]), gain=g_sh,
                           c0=2 * D, c1=2 * D + NH)
        LAZY["wkv"] = _p_wkv
        qg = cx.sb("qg", [128, 8, DH], F32)
        kg = cx.sb("kg", [128, 8, DH], F32)
        S.dma("sp", "const6", lambda e: e.dma_start(out=qg.t[:], in_=qn_g.unsqueeze(0).unsqueeze(0).to_broadcast([128, 8, DH])),
              writes=[qg.b])
        S.dma("sp", "const7", lambda e: e.dma_start(out=kg.t[:], in_=kn_g.unsqueeze(0).unsqueeze(0).to_broadcast([128, 8, DH])),
              writes=[kg.b])
        S.op("dve", lambda e: e.tensor_scalar(qg.t[:], qg.t[:], 0.125, None, ALU.mult), reads=[qg.b], writes=[qg.b])
        nbf = cx.sb("nbf", [NH, 1], F32)
        S.dma("sp", "const8", lambda e: e.dma_start(out=nbf.t[:], in_=b_f.rearrange("(h o) -> h o", o=1)), writes=[nbf.b])
        S.op("dve", lambda e: e.tensor_scalar(nbf.t[:], nbf.t[:], -1.0, None, ALU.mult), reads=[nbf.b], writes=[nbf.b])
        one = cx.sb("one", [128, 1], F32)
        S.op("pool", lambda e: e.memset(one.t[:], 1.0), writes=[one.b])

    hres = cx.sb("hres", [128, D], F32, n=8)
    mTs = cx.sb("mT", [128, 8, 512], BF16, n=1)
    wt8 = cx.sb("wt8", [128, 8, 512], BF16, n=3)
    wgut = cx.sb("wgut", [128, 8, 256], BF16, n=4)
    wdt = cx.sb("wdt", [128, 22, 256], BF16, n=2)
    hnTs = cx.sb("hnT", [128, 8, 512], BF16, n=2)
    aT = cx.sb("aT", [128, 22, 512], BF16)
    sgs = cx.sb("sg", [128, 512], F32, n=2)
    tmps = cx.sb("tmp", [128, 512], F32, n=2)
    pblk = cx.sb("pblk", [128, PLE], F32, n=2)
    pbf = cx.sb("pbf", [128, PLE], BF16, n=2)
    pTs = cx.sb("pT", [128, 2, 128], BF16, n=2)
    pm = cx.ps("pm", [128, 512], F32, n=4)
    if nxt:
        hd8 = cx.sb("hd8", [128, 8], F32, n=4)
        qnb = cx.sb("qnb", [128, 512], BF16, n=2)
        oT = cx.sb("oT", [128, 512], BF16, n=2)
        fl = cx.sb("fl", [NH, 512], F32, n=2)

    def load_w8(scr, buf, hf, name):
        t = cx.nxt(wt8)
        S.dma("sp", "ld_" + t.b.name, lambda e: e.dma_start(out=t.t[:], in_=scr[hf]), reads=[buf], writes=[t.b])
        return t

    def add_res(h, c0, wd, src_tile, src_ap, flip=[0]):
        S.op("dve", lambda e: e.tensor_tensor(h.t[:, c0:c0 + wd], h.t[:, c0:c0 + wd], src_ap, ALU.add),
             reads=[h.b, src_tile.b], writes=[h.b])

    ensure("wo")
    ensure("wgu")
    for gi in range(4):
        t0 = gi * 512
        hb = []
        for b in range(4):
            h = cx.nxt(hres)
            r0 = t0 + b * 128
            S.dma("act", "ld_" + h.b.name, lambda e, h=h, r0=r0: e.dma_start(out=h.t[:], in_=h_in[r0:r0 + 128, :]),
                  writes=[h.b])
            hb.append(h)
        mT = cx.nxt(mTs)
        S.dma("act", "ld_mT", lambda e, mT=mT, t0=t0: e.dma_start(
            out=mT.t[:], in_=mixT[:, t0:t0 + 512].rearrange("(c p) t -> p c t", p=128)), writes=[mT.b])
        for hf in range(2):
            wt = load_w8(WoB, WB["wo"], hf, "wo")
            for b in range(4):
                p = cx.nxt(pm)
                for kc in range(8):
                    S.op("pe", lambda e, p=p, kc=kc, b=b, wt=wt, mT=mT: e.matmul(
                        p.t[:], mT.t[:, kc, b * 128:(b + 1) * 128], wt.t[:, kc, :], start=(kc == 0), stop=(kc == 7)),
                        reads=[mT.b, wt.b], writes=[p.b])
                add_res(hb[b], hf * 512, 512, p, p.t[:])
        ensure("wd")
        hT = cx.nxt(hnTs)
        for b in range(4):
            dn.norm_T(hb[b], hT, b * 128)
        for j in range(22):
            wg = cx.nxt(wgut)
            S.dma("sp", "ld_" + wg.b.name, lambda e, wg=wg, j=j: e.dma_start(out=wg.t[:], in_=WguB[j]),
                  reads=[WB["wg"], WB["wu"]], writes=[wg.b])
            pg = cx.nxt(pm)
            pu = cx.nxt(pm)
            for kc in range(8):
                S.op("pe", lambda e, pg=pg, kc=kc, wg=wg, hT=hT: e.matmul(
                    pg.t[:], wg.t[:, kc, 0:128], hT.t[:, kc, :], start=(kc == 0), stop=(kc == 7)),
                    reads=[wg.b, hT.b], writes=[pg.b])
            for kc in range(8):
                S.op("pe", lambda e, pu=pu, kc=kc, wg=wg, hT=hT: e.matmul(
                    pu.t[:], wg.t[:, kc, 128:256], hT.t[:, kc, :], start=(kc == 0), stop=(kc == 7)),
                    reads=[wg.b, hT.b], writes=[pu.b])
            sg = cx.nxt(sgs)
            S.op("act", lambda e, sg=sg, pg=pg: e.activation(sg.t[:], pg.t[:], AF.Silu), reads=[pg.b], writes=[sg.b])
            S.op("dve", lambda e, sg=sg, pu=pu, j=j: e.tensor_tensor(aT.t[:, j, :], sg.t[:], pu.t[:], ALU.mult),
                 reads=[sg.b, pu.b], writes=[aT.b])
        ensure("wgate")
        for qd in range(4):
            wd_ = cx.nxt(wdt)
            S.dma("sp", "ld_" + wd_.b.name, lambda e, wd_=wd_, qd=qd: e.dma_start(out=wd_.t[:], in_=WdB[qd]),
                  reads=[WB["wd"]], writes=[wd_.b])
            for b in range(4):
                p = cx.nxt(pm)
                for j in range(22):
                    S.op("pe", lambda e, p=p, j=j, b=b, wd_=wd_: e.matmul(
                        p.t[:, 0:256], aT.t[:, j, b * 128:(b + 1) * 128], wd_.t[:, j, :], start=(j == 0), stop=(j == 21)),
                        reads=[aT.b, wd_.b], writes=[p.b])
                add_res(hb[b], qd * 256, 256, p, p.t[:, 0:256])
        if nxt:
            ensure("wq")
        hT = cx.nxt(hnTs)
        for b in range(4):
            dn.norm_T(hb[b], hT, b * 128)
        for hf in range(2):
            wt = load_w8(WgateB, WB["wgate"], hf, "wgate")
            for b in range(4):
                pb_ = cx.nxt(pblk)
                r0 = t0 + b * 128
                S.dma("act", "ld_" + pb_.b.name, lambda e, pb_=pb_, r0=r0: e.dma_start(out=pb_.t[:], in_=p_in[r0:r0 + 128, :]),
                      writes=[pb_.b])
                pf = cx.nxt(pbf)
                S.op("act", lambda e, pf=pf, pb_=pb_: e.copy(pf.t[:], pb_.t[:]), reads=[pb_.b], writes=[pf.b])
                pT_ps = cx.nxt(dn.psT)
                for k2 in range(2):
                    S.op("pe", lambda e, k2=k2, pT_ps=pT_ps, pf=pf: e.transpose(
                        pT_ps.t[:, k2 * 128:(k2 + 1) * 128], pf.t[:, k2 * 128:(k2 + 1) * 128], dn.ident.t[:]),
                        reads=[pf.b, dn.ident.b], writes=[pT_ps.b])
                pT = cx.nxt(pTs)
                S.op("act", lambda e, pT=pT, pT_ps=pT_ps: e.copy(pT.t[:, :, :], pT_ps.t[:, 0:256].rearrange("p (k t) -> p k t", k=2)),
                     reads=[pT_ps.b], writes=[pT.b])
                pgate = cx.nxt(pm)
                for kc in range(8):
                    S.op("pe", lambda e, pgate=pgate, kc=kc, b=b, wt=wt, hT=hT: e.matmul(
                        pgate.t[:], hT.t[:, kc, b * 128:(b + 1) * 128], wt.t[:, kc, :], start=(kc == 0), stop=(kc == 7)),
                        reads=[hT.b, wt.b], writes=[pgate.b])
                pproj = cx.nxt(pm)
                for k2 in range(2):
                    S.op("pe", lambda e, pproj=pproj, k2=k2, pT=pT, hf=hf: e.matmul(
                        pproj.t[:], pT.t[:, k2, :], Wproj.t[:, k2, hf * 512:(hf + 1) * 512], start=(k2 == 0), stop=(k2 == 1)),
                        reads=[pT.b, Wproj.b], writes=[pproj.b])
                sg = cx.nxt(sgs)
                S.op("act", lambda e, sg=sg, pgate=pgate: e.activation(sg.t[:], pgate.t[:], AF.Sigmoid),
                     reads=[pgate.b], writes=[sg.b])
                tmp = cx.nxt(tmps)
                S.op("dve", lambda e, tmp=tmp, sg=sg, pproj=pproj: e.tensor_tensor(tmp.t[:], sg.t[:], pproj.t[:], ALU.mult),
                     reads=[sg.b, pproj.b], writes=[tmp.b])
                hh_ = hb[b]
                S.op("dve", lambda e, hh_=hh_, tmp=tmp, hf=hf: e.tensor_tensor(
                    hh_.t[:, hf * 512:(hf + 1) * 512], hh_.t[:, hf * 512:(hf + 1) * 512], tmp.t[:], ALU.add),
                    reads=[hh_.b, tmp.b], writes=[hh_.b])
        for b in range(4):
            r0 = t0 + b * 128
            cx.store("o_h", hout[r0:r0 + 128, :], hb[b], hb[b].t[:])
        if not nxt:
            continue
        ensure("wkv")
        hT = cx.nxt(hnTs)
        for b in range(4):
            dn.norm_T(hb[b], hT, b * 128)

        def head_norm_store(p, gtile, dstT, b, hf):
            sq = cx.nxt(tmps)
            S.op("act", lambda e: e.activation(sq.t[:], p.t[:], AF.Square), reads=[p.b], writes=[sq.b])
            s8 = cx.nxt(hd8)
            S.op("dve", lambda e: e.tensor_reduce(s8.t[:], sq.t[:, :].rearrange("p (h d) -> p h d", d=DH),
                                                  mybir.AxisListType.X, ALU.add), reads=[sq.b], writes=[s8.b])
            l8 = cx.nxt(hd8)
            S.op("act", lambda e: e.activation(l8.t[:], s8.t[:], AF.Ln, scale=1.0 / DH, bias=dn.eps.t[:, 0:1]),
                 reads=[s8.b, dn.eps.b], writes=[l8.b])
            r8 = cx.nxt(hd8)
            S.op("act", lambda e: e.activation(r8.t[:], l8.t[:], AF.Exp, scale=-0.5), reads=[l8.b], writes=[r8.b])
            qf = cx.nxt(sgs)
            S.op("dve", lambda e: e.tensor_tensor(qf.t[:, :].rearrange("p (h d) -> p h d", d=DH),
                                                  p.t[:, :].rearrange("p (h d) -> p h d", d=DH),
                                                  r8.t[:, :].unsqueeze(2).to_broadcast([128, 8, DH]), ALU.mult),
                 reads=[p.b, r8.b], writes=[qf.b])
            qn = cx.nxt(qnb)
            S.op("pool", lambda e: e.tensor_tensor(qn.t[:, :], qf.t[:, :], gtile.t[:, :, :].rearrange("p h d -> p (h d)"), ALU.mult),
                 reads=[qf.b, gtile.b], writes=[qn.b])
            tp = cx.nxt(dn.psT)
            for c4 in range(4):
                S.op("pe", lambda e, c4=c4: e.transpose(tp.t[:, c4 * 128:(c4 + 1) * 128], qn.t[:, c4 * 128:(c4 + 1) * 128],
                                                        dn.ident.t[:]), reads=[qn.b, dn.ident.b], writes=[tp.b])
            o = cx.nxt(oT)
            S.op("act", lambda e: e.copy(o.t[:], tp.t[:, 0:512]), reads=[tp.b], writes=[o.b])
            r0 = t0 + b * 128
            cx.store("o_qk1", dstT[hf * 512:(hf + 1) * 512, r0:r0 + 128].rearrange("(c p) t -> p c t", p=128), o,
                     o.t[:, :].rearrange("p (c t) -> p c t", c=4))

        for hf in range(2):
            wt = load_w8(WqB, WB["wq"], hf, "wq")
            for b in range(4):
                p = cx.nxt(pm)
                for kc in range(8):
                    S.op("pe", lambda e, p=p, kc=kc, b=b, wt=wt, hT=hT: e.matmul(
                        p.t[:], hT.t[:, kc, b * 128:(b + 1) * 128], wt.t[:, kc, :], start=(kc == 0), stop=(kc == 7)),
                        reads=[hT.b, wt.b], writes=[p.b])
                head_norm_store(p, qg, q1T, b, hf)
        for hf in range(4):
            wt = load_w8(WkvB, WB["wkv"], hf, "wkv")
            for b in range(4):
                p = cx.nxt(pm)
                for kc in range(8):
                    S.op("pe", lambda e, p=p, kc=kc, b=b, wt=wt, hT=hT: e.matmul(
                        p.t[:], hT.t[:, kc, b * 128:(b + 1) * 128], wt.t[:, kc, :], start=(kc == 0), stop=(kc == 7)),
                        reads=[hT.b, wt.b], writes=[p.b])
                if hf < 2:
                    head_norm_store(p, kg, k1T, b, hf)
                else:
                    o = cx.nxt(oT)
                    S.op("act", lambda e, o=o, p=p: e.copy(o.t[:], p.t[:]), reads=[p.b], writes=[o.b])
                    r0 = t0 + b * 128
                    cx.store("o_v1", v1[r0:r0 + 128, (hf - 2) * 512:(hf - 1) * 512], o, o.t[:])
        p = cx.nxt(pm)
        for kc in range(8):
            S.op("pe", lambda e, p=p, kc=kc, hT=hT: e.matmul(p.t[0:NH, :], Wf.t[:, kc, :], hT.t[:, kc, :],
                                                              start=(kc == 0), stop=(kc == 7)),
                 reads=[Wf.b, hT.b], writes=[p.b])
        f1 = cx.nxt(fl)
        S.op("act", lambda e, f1=f1, p=p: e.activation(f1.t[:], p.t[0:NH, :], AF.Exp, scale=-1.0, bias=nbf.t[:, 0:1]),
             reads=[p.b, nbf.b], writes=[f1.b])
        f2 = cx.nxt(fl)
        S.op("act", lambda e, f1=f1, f2=f2: e.activation(f2.t[:], f1.t[:], AF.Ln, bias=one.t[0:NH, 0:1]),
             reads=[f1.b, one.b], writes=[f2.b])
        S.op("dve", lambda e, f2=f2: e.tensor_scalar(f2.t[:], f2.t[:], -1.0, None, ALU.mult), reads=[f2.b], writes=[f2.b])
        cx.store("o_fl", flogT[:, t0:t0 + 512], f2, f2.t[:])
    return cx.finish()


def build_attn1(n_heads=NH):
    cx = Cx()
    S = cx.S
    qT = cx.din("qT", [D, TL], BF16)
    kT = cx.din("kT", [D, S_ALL], BF16)
    vr = cx.din("vr", [8, 128, 128 * 128], BF16)
    flog = cx.din("flog", [NH, S_ALL], F32)
    onehot = cx.din("onehot", [NH, 8], F32)
    negm = cx.din("negm", [128, 8 * 128], BF16)
    ident = cx.din("ident", [128, 128], BF16)
    sel = cx.din("sel", [128, 256], F32)
    mixT = cx.dout("mixT", [D, TL], BF16)
    kaug = cx.dint("kaug", [NH, 6, S_ALL], BF16)
    qaug = cx.dint("qaug", [NH, 6, TL], BF16)
    b_kaug = Buf("kaug")
    b_qaug = Buf("qaug")

    nm = cx.sb("nm", [128, 8 * 128], BF16)
    idt = cx.sb("idt", [128, 128], BF16)
    selt = cx.sb("selt", [128, 256], F32)
    oh = cx.sb("oh", [NH, 8], F32)
    for t_, src in ((nm, negm), (idt, ident), (selt, sel), (oh, onehot)):
        S.dma("sp", "cc_" + t_.b.name, lambda e, t_=t_, src=src: e.dma_start(out=t_.t[:], in_=src), writes=[t_.b])

    CH = 1024
    Fc = cx.sb("Fc", [NH, CH], F32, n=2)
    Fs = cx.sb("Fs", [NH, CH], F32, n=2)
    r1 = cx.sb("r1", [NH, CH], F32)
    onesf = cx.sb("onesf", [NH, CH], F32)
    S.op("pool", lambda e: e.memset(onesf.t[:], 1.0), writes=[onesf.b])
    ka = cx.sb("ka", [NH, 6, CH], BF16)
    Fq = cx.sb("Fq", [NH, TL], F32)
    qa = cx.sb("qa", [NH, 6, 512], BF16)
    carry = cx.sb("carry", [NH, 1], F32, n=2)
    S.op("pool", lambda e: e.memset(carry[1].t[:], 0.0), writes=[carry[1].b])
    S.op("pool", lambda e: e.memset(ka.t[:, 0:3, :], 1.0), writes=[ka.b])
    S.op("pool", lambda e: e.memset(qa.t[:, 3:6, :], 1.0), writes=[qa.b])
    for ci in range(S_ALL // CH):
        fc = Fc[ci % 2]
        fs = Fs[ci % 2]
        S.dma("sp", "ld_" + fc.b.name, lambda e, fc=fc, ci=ci: e.dma_start(out=fc.t[:], in_=flog[:, ci * CH:(ci + 1) * CH]),
              writes=[fc.b])
        cprev = carry[(ci + 1) % 2]
        ccur = carry[ci % 2]
        S.op("dve", lambda e, fs=fs, fc=fc, cprev=cprev: e.tensor_tensor_scan(
            fs.t[:], onesf.t[:], fc.t[:], cprev.t[:, 0:1], ALU.mult, ALU.add),
            reads=[onesf.b, fc.b, cprev.b], writes=[fs.b])
        S.op("dve", lambda e, fs=fs, ccur=ccur: e.tensor_copy(ccur.t[:], fs.t[:, CH - 1:CH]), reads=[fs.b], writes=[ccur.b])
        fview = fs.t[:, :].rearrange("h (c i) -> h c i", c=8)
        fqv = Fq.t[:, ci * 128:(ci + 1) * 128]
        for c in range(8):
            if c == 0:
                S.op("dve", lambda e, fview=fview, fqv=fqv, c=c: e.tensor_scalar(
                    fqv, fview[:, c, :], oh.t[:, c:c + 1], None, ALU.mult), reads=[fs.b, oh.b], writes=[Fq.b])
            else:
                S.op("dve", lambda e, fview=fview, fqv=fqv, c=c: e.scalar_tensor_tensor(
                    fqv, fview[:, c, :], oh.t[:, c:c + 1], fqv, ALU.mult, ALU.add), reads=[fs.b, oh.b, Fq.b], writes=[Fq.b])
        S.op("dve", lambda e, fs=fs: e.tensor_scalar(r1.t[:], fs.t[:], -1.0, None, ALU.mult), reads=[fs.b], writes=[r1.b])
        for part in range(3):
            S.op("dve", lambda e, part=part: e.tensor_copy(ka.t[:, 3 + part, :], r1.t[:]), reads=[r1.b], writes=[ka.b])
            if part < 2:
                S.op("dve", lambda e, part=part: e.tensor_tensor(r1.t[:], r1.t[:], ka.t[:, 3 + part, :], ALU.subtract),
                     reads=[r1.b, ka.b], writes=[r1.b])
        S.dma("sp", "st_kaug", lambda e, ci=ci: e.dma_start(out=kaug[:, :, ci * CH:(ci + 1) * CH], in_=ka.t[:]),
              reads=[ka.b], writes=[b_kaug])
    for qi in range(4):
        S.op("dve", lambda e, qi=qi: e.tensor_copy(r1.t[:, 0:512], Fq.t[:, qi * 512:(qi + 1) * 512]), reads=[Fq.b], writes=[r1.b])
        for part in range(3):
            S.op("dve", lambda e, part=part: e.tensor_copy(qa.t[:, part, :], r1.t[:, 0:512]), reads=[r1.b], writes=[qa.b])
            if part < 2:
                S.op("dve", lambda e, part=part: e.tensor_tensor(r1.t[:, 0:512], r1.t[:, 0:512], qa.t[:, part, :], ALU.subtract),
                     reads=[r1.b, qa.b], writes=[r1.b])
        S.dma("sp", "st_qaug", lambda e, qi=qi: e.dma_start(out=qaug[:, :, qi * 512:(qi + 1) * 512], in_=qa.t[:]),
              reads=[qa.b], writes=[b_qaug])

    kTs = cx.sb("kTs", [128, S_ALL], BF16, n=2)
    qs = cx.sb("qs", [128, TL], BF16, n=2)
    vs = cx.sb("vs", [128, 128 * 130], BF16, n=2)
    mx = cx.sb("mx", [128, TL], BF16, n=2)
    for sl in range(2):
        v3 = vs[sl].t[:, :].rearrange("p (k c) -> p k c", c=130)
        S.op("pool", lambda e, v3=v3: e.memset(v3[:, :, 64:65], 1.0), writes=[vs[sl].b])
        S.op("pool", lambda e, v3=v3: e.memset(v3[:, :, 129:130], 1.0), writes=[vs[sl].b])
    zp = cx.ps("zp", [128, 512], F32, n=3)
    op_ = cx.ps("op", [128, 512], F32, n=2)
    bcp = cx.ps("bcp", [128, 512], F32, n=1)
    pb_ = cx.sb("pb", [128, 512], BF16, n=4)
    osb = cx.sb("osb", [128, 512], F32, n=1)
    rbs = cx.sb("rbs", [128, 512], F32, n=1)

    def load_head(h):
        sl = h % 2
        k_, q_ = kTs[sl], qs[sl]
        S.dma("sp", "ldq%d" % sl, lambda e: e.dma_start(out=q_.t[0:64, :], in_=qT[h * 64:(h + 1) * 64, :]), writes=[q_.b])
        S.dma("sp", "ldq%d" % sl, lambda e: e.dma_start(out=q_.t[64:70, :], in_=qaug[h]), reads=[b_qaug], writes=[q_.b])
        S.dma("sp", "ldk%d" % sl, lambda e: e.dma_start(out=k_.t[64:70, :], in_=kaug[h]), reads=[b_kaug], writes=[k_.b])
        for part in range(4):
            c0 = part * 4096
            S.dma("sp", "ldk%d" % sl,
                  lambda e, c0=c0: e.dma_start(out=k_.t[0:64, c0:c0 + 4096], in_=kT[h * 64:(h + 1) * 64, c0:c0 + 4096]),
                  writes=[k_.b])

    def load_v(hp):
        v_ = vs[hp % 2]
        v3 = v_.t[:, :].rearrange("p (k c) -> p k c", c=130)
        src = vr[hp].rearrange("p (k c) -> p k c", c=128)
        for part in range(8):
            k0 = part * 16
            for hh in range(2):
                S.dma("sp", "ldv%d" % (hp % 2),
                      lambda e, k0=k0, hh=hh: e.dma_start(out=v3[:, k0:k0 + 16, 65 * hh:65 * hh + 64],
                                                          in_=src[:, k0:k0 + 16, 64 * hh:64 * hh + 64]),
                      writes=[v_.b])

    load_v(0)
    load_head(0)

    def do_head(h):
        sl = h % 2
        hp = h // 2
        odd = h % 2
        k_, q_ = kTs[sl], qs[sl]
        v_ = vs[hp % 2]
        m_ = mx[hp % 2]
        if h + 1 < n_heads:
            if (h + 1) % 2 == 0:
                load_v((h + 1) // 2)
            load_head(h + 1)
        items = []
        for J in range(4):
            nkb = 32 * J + 32
            for kb in range(nkb - 1, -1, -1):
                items.append((J, kb, nkb))
        st = {}

        def s1(it):
            J, kb, nkb = it
            r = kb - 32 * J
            c0 = 128 * (r // 8) if r >= 0 else 0
            z = cx.nxt(zp)
            p_ = cx.nxt(pb_)
            S.op("pe", lambda e: e.matmul(z.t[:, c0:512], k_.t[0:70, kb * 128:(kb + 1) * 128],
                                          q_.t[0:70, 512 * J + c0:512 * J + 512], start=True, stop=(r < 0)),
                 reads=[k_.b, q_.b], writes=[z.b])
            if r >= 0:
                i = r % 8
                S.op("pe", lambda e: e.matmul(z.t[:, c0:c0 + 128], idt.t[:], nm.t[:, i * 128:(i + 1) * 128],
                                              start=False, stop=True), reads=[idt.b, nm.b], writes=[z.b])
            S.op("act", lambda e: e.activation(p_.t[:, c0:512], z.t[:, c0:512], AF.Exp), reads=[z.b], writes=[p_.b])
            st[it] = (c0, p_)

        def s2(it):
            J, kb, nkb = it
            c0, p_ = st.pop(it)
            o_ = op_[J % 2]
            if odd:
                S.op("pe", lambda e: e.matmul(o_.t[:, c0:512], v_.t[:, kb * 130 + 1:kb * 130 + 129], p_.t[:, c0:512],
                                              start=(kb == nkb - 1), stop=(kb == 0), skip_group_check=(kb != nkb - 1 and kb != 0)),
                     reads=[v_.b, p_.b], writes=[o_.b])
            else:
                S.op("pe", lambda e: e.matmul(o_.t[0:65, c0:512], v_.t[:, kb * 130:kb * 130 + 65], p_.t[:, c0:512],
                                              start=(kb == nkb - 1), stop=(kb == 0), skip_group_check=(kb != nkb - 1 and kb != 0)),
                     reads=[v_.b, p_.b], writes=[o_.b])
            if kb == 0:
                ob = cx.nxt(osb)
                rb = cx.nxt(rbs)
                bc = bcp[0]
                if odd:
                    S.op("act", lambda e: e.copy(ob.t[32:64, :], o_.t[32:64, :]), reads=[o_.b], writes=[ob.b])
                    S.op("act", lambda e: e.copy(ob.t[64:128, :], o_.t[64:128, :]), reads=[o_.b], writes=[ob.b])
                    S.op("pe", lambda e: e.matmul(bc.t[:, :], selt.t[32:64, 128:256], ob.t[32:64, :], start=True, stop=True),
                         reads=[selt.b, ob.b], writes=[bc.b])
                    S.op("dve", lambda e: e.reciprocal(rb.t[64:128, :], bc.t[64:128, :]), reads=[bc.b], writes=[rb.b])
                    S.op("dve", lambda e: e.tensor_tensor(m_.t[64:128, 512 * J:512 * J + 512], ob.t[64:128, :], rb.t[64:128, :], ALU.mult),
                         reads=[ob.b, rb.b], writes=[m_.b])
                else:
                    S.op("act", lambda e: e.copy(ob.t[0:65, :], o_.t[0:65, :]), reads=[o_.b], writes=[ob.b])
                    S.op("pe", lambda e: e.matmul(bc.t[0:64, :], selt.t[64:65, 0:64], ob.t[64:65, :], start=True, stop=True),
                         reads=[selt.b, ob.b], writes=[bc.b])
                    S.op("dve", lambda e: e.reciprocal(rb.t[0:64, :], bc.t[0:64, :]), reads=[bc.b], writes=[rb.b])
                    S.op("dve", lambda e: e.tensor_tensor(m_.t[0:64, 512 * J:512 * J + 512], ob.t[0:64, :], rb.t[0:64, :], ALU.mult),
                         reads=[ob.b, rb.b], writes=[m_.b])

        n = len(items)
        for tau in range(-1, n + 1):
            if 0 <= tau + 1 < n:
                s1(items[tau + 1])
            if 0 <= tau - 1 < n:
                s2(items[tau - 1])
        if odd:
            cx.store("o_mix", mixT[hp * 128:(hp + 1) * 128, :], m_, m_.t[:])

    for h in range(n_heads):
        do_head(h)
    return cx.finish()


_PROGS = {}


def _prog(name, fn):
    if name not in _PROGS:
        _PROGS[name] = fn()
    return _PROGS[name]


def _shard_tok(a):
    F_ = a.shape[-1]
    r = a.reshape(16, 8, 128, F_)
    return [np.ascontiguousarray(r[:, c].reshape(TL, F_)) for c in range(NC_)]


def _gather_T(parts):
    R = parts[0].shape[0]
    out = np.zeros((R, 16, 8, 128), dtype=parts[0].dtype)
    for c in range(NC_):
        out[:, :, c, :] = parts[c].reshape(R, 16, 128)
    return out.reshape(R, S_ALL)


def _gather_v(parts):
    va = np.zeros((16, 8, 128, D), dtype=parts[0].dtype)
    for c in range(NC_):
        va[:, c] = parts[c].reshape(16, 128, D)
    va = va.reshape(128, 128, 8, 128)
    return np.ascontiguousarray(va.transpose(2, 1, 0, 3)).reshape(8, 128, 128 * 128)


def _consts():
    ar = np.arange(128)
    c = {}
    c["ident"] = np.eye(128, dtype=np.float32).astype(NPBF)
    c["tri"] = (ar[:, None] >= ar[None, :]).astype(np.float32).astype(NPBF)
    c["omt"] = (ar[:, None] < ar[None, :]).astype(np.float32).astype(NPBF)
    masks, negm, oh = [], [], []
    for cc in range(NC_):
        m = np.zeros((128, 8, 128), np.float32)
        n = np.full((128, 8, 128), -30000.0, np.float32)
        for i in range(8):
            if i < cc:
                m[:, i, :] = 1.0
                n[:, i, :] = 0.0
            elif i == cc:
                m[:, i, :] = (ar[:, None] < ar[None, :])
                n[:, i, :] = np.where(ar[:, None] <= ar[None, :], 0.0, -30000.0)
        masks.append(m.reshape(128, 1024).astype(NPBF))
        negm.append(n.reshape(128, 1024).astype(NPBF))
        o = np.zeros((NH, 8), np.float32)
        o[:, cc] = 1.0
        oh.append(o)
    c["masks"], c["negm"], c["onehot"] = masks, negm, oh
    sel = np.zeros((128, 256), np.float32)
    sel[64, 0:64] = 1.0
    sel[63, 192:256] = 1.0
    c["sel"] = sel
    return c


def _run(nc, in_maps):
    return run_bass_kernel_spmd(nc, in_maps, core_ids=list(range(NC_))).results


def kernel(x, p, attn_norm_g, sb_w_qkv, sb_w_o, shared_norm_g, shared_w_kvf, shared_b_f, shared_k_norm_g,
           fox_w_q, fox_q_norm_g, fox_w_o, ffn_norm_g, ffn_w_gu, ffn_w_d, ple_norm_g, ple_w_gate, ple_w_proj):
    f32 = lambda a: np.ascontiguousarray(np.asarray(a, dtype=np.float32))
    x = f32(x)
    p = f32(p)
    C = _consts()
    xs = _shard_tok(x[0])
    p0 = _shard_tok(p[0, 0])
    p1 = _shard_tok(p[1, 0])
    r1 = _run(_prog("pre0", build_pre0),
              [{"x": xs[c], "g": f32(attn_norm_g[0]), "w": f32(sb_w_qkv[0]), "ident": C["ident"]} for c in range(NC_)])
    kT_all = _gather_T([r["kT"] for r in r1])
    vr = _gather_v([r["v"] for r in r1])
    r2 = _run(_prog("attn0", build_attn0),
              [{"qT": r1[c]["qT"], "kT": kT_all, "vr": vr, "masks": C["masks"][c], "tri": C["tri"], "omt": C["omt"]}
               for c in range(NC_)])

    def post_in(c, h, mix, pl, w_o, li):
        return {"h": h, "mixT": mix, "p": pl, "w_o": f32(w_o), "ffn_g": f32(ffn_norm_g[li]), "w_gu": f32(ffn_w_gu[li]),
                "w_d": f32(ffn_w_d[li]), "ple_g": f32(ple_norm_g[li]), "w_gate": f32(ple_w_gate[li]),
                "w_proj": f32(ple_w_proj[li]), "ident": C["ident"]}

    in3 = []
    for c in range(NC_):
        d = post_in(c, xs[c], r2[c]["mixT"], p0[c], sb_w_o[0], 0)
        d.update({"a_g": f32(attn_norm_g[1]), "w_q": f32(fox_w_q[0]), "qn_g": f32(fox_q_norm_g[0]),
                  "sh_g": f32(shared_norm_g), "w_kvf": f32(shared_w_kvf), "b_f": f32(shared_b_f),
                  "kn_g": f32(shared_k_norm_g)})
        in3.append(d)
    r3 = _run(_prog("post_n", lambda: build_post(True)), in3)
    k1_all = _gather_T([r["k1T"] for r in r3])
    vr1 = _gather_v([r["v1"] for r in r3])
    fl_all = _gather_T([r["flogT"] for r in r3])
    r4 = _run(_prog("attn1", build_attn1),
              [{"qT": r3[c]["q1T"], "kT": k1_all, "vr": vr1, "flog": fl_all, "onehot": C["onehot"][c],
                "negm": C["negm"][c], "ident": C["ident"], "sel": C["sel"]} for c in range(NC_)])
    r5 = _run(_prog("post_l", lambda: build_post(False)),
              [post_in(c, r3[c]["hout"], r4[c]["mixT"], p1[c], fox_w_o[0], 1) for c in range(NC_)])
    out = np.zeros((16, 8, 128, D), np.float32)
    for c in range(NC_):
        out[:, c] = r5[c]["hout"].reshape(16, 128, D)
    return out.reshape(1, S_ALL, D)
```

```python
import numpy as np
import ml_dtypes
from contextlib import ExitStack
import concourse.bass as bass
import concourse.mybir as mybir
from concourse.bass_utils import run_bass_kernel_spmd

F32 = mybir.dt.float32
BF16 = mybir.dt.bfloat16
AF = mybir.ActivationFunctionType
ALU = mybir.AluOpType
NPBF = ml_dtypes.bfloat16

NC_ = 8
D = 1024
S_ALL = 16384
TL = 2048
NH = 16
DH = 64
DFF = 2816
PLE = 256
EPS = 1e-6
EPOCH = 4096


class Buf:
    __slots__ = ("name", "w", "r", "wm")

    def __init__(self, name, multi=False):
        self.name = name
        self.w = None
        self.r = []
        self.wm = {} if multi else None


class Tile:
    __slots__ = ("t", "b")

    def __init__(self, t, name):
        self.t = t
        self.b = Buf(name)


class Sched:
    ENG = ("pe", "act", "dve", "pool", "sp")

    def __init__(self, nc, es):
        self.nc = nc
        self.es = es
        self.lists = {e: [] for e in self.ENG}
        self.sems = {}
        self.cnt = {}
        self.alias = {}
        self.waited = {e: {} for e in self.ENG}
        self.ecount = {e: 0 for e in self.ENG}
        self.nsem = 0

    def _mksem(self, key):
        self.nsem += 1
        self.sems[key] = self.es.enter_context(self.nc.semaphore("s%d" % self.nsem))
        self.cnt[key] = 0

    def _deps(self, eng, reads, writes):
        deps = {}

        def add(tok):
            if tok is None:
                return
            k, v = tok
            if deps.get(k, 0) < v:
                deps[k] = v

        for b in reads:
            add(b.w)
            if b.wm is not None:
                for t in b.wm.items():
                    add(t)
        for b in writes:
            add(b.w)
            for t in b.r:
                add(t)
        out = []
        w = self.waited[eng]
        for k, v in deps.items():
            if eng == "pe" and k.startswith("E_pe#"):
                continue
            if w.get(k, 0) >= v:
                continue
            w[k] = v
            out.append((k, v))
        return out

    @staticmethod
    def _commit(tok, reads, writes):
        for b in writes:
            if b.wm is not None:
                if b.wm.get(tok[0], 0) < tok[1]:
                    b.wm[tok[0]] = tok[1]
                continue
            b.w = tok
            b.r = []
        for b in reads:
            b.r.append(tok)

    def op(self, eng, fn, reads=(), writes=()):
        deps = self._deps(eng, reads, writes)
        ep = self.ecount[eng] // EPOCH
        key = "E_%s#%d" % (eng, ep)
        if key not in self.sems:
            self._mksem(key)
        self.ecount[eng] += 1
        self.cnt[key] += 1
        tok = (key, self.cnt[key])
        sems = self.sems

        def thunk(e, deps=deps, fn=fn, key=key):
            for k, v in deps:
                e.wait_ge(sems[k], v)
            fn(e).then_inc(sems[key], 1)

        self.lists[eng].append(thunk)
        self._commit(tok, reads, writes)
        return tok

    def dma(self, eng, semname, fn, reads=(), writes=()):
        deps = self._deps(eng, reads, writes)
        key = self.alias.get(semname)
        if key is None or self.cnt[key] >= 16 * 240:
            n = 0 if key is None else int(key.split("#")[1]) + 1
            key = "D_%s#%d" % (semname, n)
            self.alias[semname] = key
            self._mksem(key)
        self.cnt[key] += 16
        tok = (key, self.cnt[key])
        sems = self.sems

        def thunk(e, deps=deps, fn=fn, key=key):
            for k, v in deps:
                e.wait_ge(sems[k], v)
            fn(e).then_inc(sems[key], 16)

        self.lists[eng].append(thunk)
        self._commit(tok, reads, writes)
        return tok

    def wait_all(self, eng, toks):
        sems = self.sems
        toks = list(toks)

        def thunk(e):
            for k, v in toks:
                e.wait_ge(sems[k], v)

        self.lists[eng].append(thunk)

    def emit(self):
        L = self.lists
        with self.nc.Block() as block:
            @block.tensor
            def _(e):
                for t in L["pe"]:
                    t(e)

            @block.scalar
            def _(e):
                for t in L["act"]:
                    t(e)

            @block.vector
            def _(e):
                for t in L["dve"]:
                    t(e)

            @block.gpsimd
            def _(e):
                for t in L["pool"]:
                    t(e)

            @block.sync
            def _(e):
                for t in L["sp"]:
                    t(e)


class Cx:
    def __init__(self):
        self.nc = bass.Bass("TRN2", target_bir_lowering=False)
        self.es = ExitStack()
        self.S = Sched(self.nc, self.es)
        self.out_toks = {}
        self.rr = {}

    def din(self, name, shape, dt):
        return self.nc.dram_tensor(name, list(shape), dt, kind="ExternalInput").ap()

    def dout(self, name, shape, dt):
        return self.nc.dram_tensor(name, list(shape), dt, kind="ExternalOutput").ap()

    def dint(self, name, shape, dt):
        return self.nc.dram_tensor(name, list(shape), dt, kind="Internal").ap()

    def sb(self, name, shape, dt, n=None):
        if n is None:
            return Tile(self.es.enter_context(self.nc.sbuf_tensor("sb_" + name, list(shape), dt)), name)
        return [Tile(self.es.enter_context(self.nc.sbuf_tensor("sb_%s%d" % (name, i), list(shape), dt)),
                     "%s%d" % (name, i)) for i in range(n)]

    def ps(self, name, shape, dt, n=None):
        if n is None:
            return Tile(self.es.enter_context(self.nc.psum_tensor("ps_" + name, list(shape), dt)), name)
        return [Tile(self.es.enter_context(self.nc.psum_tensor("ps_%s%d" % (name, i), list(shape), dt)),
                     "%s%d" % (name, i)) for i in range(n)]

    def nxt(self, lst, key=None):
        key = key or id(lst)
        i = self.rr.get(key, 0)
        self.rr[key] = i + 1
        return lst[i % len(lst)]

    def store(self, semname, out_ap, tile, in_ap, eng="act"):
        tok = self.S.dma(eng, "st_" + tile.b.name, lambda e: e.dma_start(out=out_ap, in_=in_ap), reads=[tile.b])
        self.out_toks[tok[0]] = max(self.out_toks.get(tok[0], 0), tok[1])

    def finish(self):
        self.S.wait_all("sp", list(self.out_toks.items()))
        self.S.emit()
        self.es.close()
        return self.nc


class Dense:
    def __init__(self, cx, ident_ap):
        self.cx = cx
        S = cx.S
        self.ident = cx.sb("ident", [128, 128], BF16)
        S.dma("sp", "const1", lambda e: e.dma_start(out=self.ident.t[:], in_=ident_ap), writes=[self.ident.b])
        self.junk = cx.sb("junk", [128, 1024], BF16, n=2)
        self.ss = cx.sb("ss", [128, 1], F32, n=4)
        self.lnv = cx.sb("lnv", [128, 1], F32, n=4)
        self.rstd = cx.sb("rstd", [128, 1], F32, n=4)
        self.hn = cx.sb("hn", [128, 1024], BF16, n=2)
        self.psT = cx.ps("psT", [128, 1024], BF16, n=2)
        self.wst = cx.sb("wst", [128, 512], F32, n=4)
        self.gv = {}
        self.evq = 0
        self.conv_engs = ("pool", "dve")

    def load_gain(self, name, g_ap):
        cx = self.cx
        t = cx.sb("g_" + name, [128, 8], F32)
        cx.S.dma("sp", "cg_" + name, lambda e: e.dma_start(out=t.t[:], in_=g_ap.rearrange("(k p) -> p k", p=128),
                                                      allow_slow_non_contiguous=True), writes=[t.b])
        self.gv[name] = t
        return t

    def prep_weight(self, w_ap, K, N, dst_fn, gain=None, post=None, c0=0, c1=None):
        cx = self.cx
        S = cx.S
        c1 = N if c1 is None else c1
        for kc in range(K // 128):
            for n0 in range(c0, c1, 512):
                wd = min(512, c1 - n0)
                st = cx.nxt(self.wst)
                S.dma("sp", "wst_" + st.b.name,
                      lambda e, st=st, kc=kc, n0=n0, wd=wd: e.dma_start(
                          out=st.t[:, 0:wd], in_=w_ap[kc * 128:(kc + 1) * 128, n0:n0 + wd]),
                      writes=[st.b])
                dt_, dap = dst_fn(kc, n0, wd)
                eng = self.conv_engs[self.evq % 2]
                self.evq += 1
                pv = post(n0) if post is not None else None
                if eng == "act":
                    if gain is not None:
                        g = gain
                        S.op("act", lambda e, st=st, dap=dap, kc=kc, wd=wd, g=g: e.activation(
                            dap, st.t[:, 0:wd], AF.Copy, scale=g.t[:, kc:kc + 1]), reads=[st.b, g.b], writes=[dt_.b])
                    else:
                        S.op("act", lambda e, st=st, dap=dap, wd=wd: e.copy(dap, st.t[:, 0:wd]),
                             reads=[st.b], writes=[dt_.b])
                elif gain is not None:
                    g = gain
                    if pv is not None:
                        fn = lambda e, st=st, dap=dap, kc=kc, wd=wd, g=g, pv=pv: e.tensor_scalar(
                            dap, st.t[:, 0:wd], g.t[:, kc:kc + 1], pv, ALU.mult, ALU.mult)
                    else:
                        fn = lambda e, st=st, dap=dap, kc=kc, wd=wd, g=g: e.tensor_scalar(
                            dap, st.t[:, 0:wd], g.t[:, kc:kc + 1], None, ALU.mult)
                    S.op(eng, fn, reads=[st.b, g.b], writes=[dt_.b])
                else:
                    S.op(eng, lambda e, st=st, dap=dap, wd=wd: e.tensor_copy(dap, st.t[:, 0:wd]),
                         reads=[st.b], writes=[dt_.b])

    def norm_T(self, h, hnT, col0):
        cx = self.cx
        S = cx.S
        junk = cx.nxt(self.junk)
        ss = cx.nxt(self.ss)
        lnv = cx.nxt(self.lnv)
        rstd = cx.nxt(self.rstd)
        hn = cx.nxt(self.hn)
        pT = cx.nxt(self.psT)
        S.op("act", lambda e: e.activation(junk.t[:], h.t[:], AF.Square, accum_out=ss.t[:]),
             reads=[h.b], writes=[junk.b, ss.b])
        S.op("act", lambda e: e.activation(lnv.t[:], ss.t[:], AF.Ln, scale=1.0 / D, bias=self.eps.t[:, 0:1]),
             reads=[ss.b, self.eps.b], writes=[lnv.b])
        S.op("act", lambda e: e.activation(rstd.t[:], lnv.t[:], AF.Exp, scale=-0.5),
             reads=[lnv.b], writes=[rstd.b])
        S.op("dve", lambda e: e.tensor_scalar(hn.t[:], h.t[:], rstd.t[:, 0:1], None, ALU.mult),
             reads=[h.b, rstd.b], writes=[hn.b])
        for kc in range(8):
            S.op("pe", lambda e, kc=kc: e.transpose(pT.t[:, kc * 128:(kc + 1) * 128],
                                                    hn.t[:, kc * 128:(kc + 1) * 128], self.ident.t[:]),
                 reads=[hn.b, self.ident.b], writes=[pT.b])
        S.op("dve", lambda e: e.tensor_copy(hnT.t[:, :, col0:col0 + 128],
                                            pT.t[:, :].rearrange("p (k t) -> p k t", k=8)),
             reads=[pT.b], writes=[hnT.b])

    def consts(self):
        cx = self.cx
        self.eps = cx.sb("epsc", [128, 1], F32)
        cx.S.op("pool", lambda e: e.memset(self.eps.t[:], EPS), writes=[self.eps.b])


def build_pre0():
    cx = Cx()
    S = cx.S
    x = cx.din("x", [TL, D], F32)
    g = cx.din("g", [D], F32)
    w = cx.din("w", [D, 3 * D], F32)
    ident = cx.din("ident", [128, 128], BF16)
    qT = cx.dout("qT", [D, TL], BF16)
    kT = cx.dout("kT", [D, TL], BF16)
    v = cx.dout("v", [TL, D], BF16)
    dn = Dense(cx, ident)
    dn.consts()
    gt = dn.load_gain("a", g)
    Wb = cx.sb("Wb", [128, 8, 3 * D], BF16)
    dn.prep_weight(w, D, 3 * D, lambda kc, n0, wd: (Wb, Wb.t[:, kc, n0:n0 + wd]), gain=gt,
                   post=lambda n0: (0.125 if n0 < D else 1.0))
    hblk = cx.sb("hblk", [128, D], F32, n=3)
    hnT = cx.sb("hnT", [128, 8, 512], BF16, n=2)
    pm = cx.ps("pm", [128, 512], F32, n=4)
    ost = cx.sb("ost", [128, 512], BF16, n=4)
    ev = 0
    for gi in range(4):
        hT = cx.nxt(hnT)
        for b in range(4):
            h = cx.nxt(hblk)
            r0 = (gi * 4 + b) * 128
            S.dma("sp", "ld_" + h.b.name, lambda e, h=h, r0=r0: e.dma_start(out=h.t[:], in_=x[r0:r0 + 128, :]),
                  writes=[h.b])
            dn.norm_T(h, hT, b * 128)
        for n in range(16):
            p = cx.nxt(pm)
            for kc in range(8):
                S.op("pe", lambda e, p=p, kc=kc, n=n, hT=hT: e.matmul(
                    p.t[:], Wb.t[:, kc, n * 128:(n + 1) * 128], hT.t[:, kc, :], start=(kc == 0), stop=(kc == 7)),
                    reads=[Wb.b, hT.b], writes=[p.b])
            o = cx.nxt(ost)
            if ev % 2 == 0:
                S.op("act", lambda e, o=o, p=p: e.copy(o.t[:], p.t[:]), reads=[p.b], writes=[o.b])
            else:
                S.op("dve", lambda e, o=o, p=p: e.tensor_copy(o.t[:], p.t[:]), reads=[p.b], writes=[o.b])
            ev += 1
            dst = qT if n < 8 else kT
            rr = (n % 8) * 128
            cx.store("o_qk", dst[rr:rr + 128, gi * 512:(gi + 1) * 512], o, o.t[:])
        for b in range(4):
            for hf in range(2):
                p = cx.nxt(pm)
                for kc in range(8):
                    S.op("pe", lambda e, p=p, kc=kc, b=b, hf=hf, hT=hT: e.matmul(
                        p.t[:], hT.t[:, kc, b * 128:(b + 1) * 128],
                        Wb.t[:, kc, 2 * D + hf * 512:2 * D + (hf + 1) * 512], start=(kc == 0), stop=(kc == 7)),
                        reads=[Wb.b, hT.b], writes=[p.b])
                o = cx.nxt(ost)
                if ev % 2 == 0:
                    S.op("act", lambda e, o=o, p=p: e.copy(o.t[:], p.t[:]), reads=[p.b], writes=[o.b])
                else:
                    S.op("dve", lambda e, o=o, p=p: e.tensor_copy(o.t[:], p.t[:]), reads=[p.b], writes=[o.b])
                ev += 1
                r0 = (gi * 4 + b) * 128
                cx.store("o_v", v[r0:r0 + 128, hf * 512:(hf + 1) * 512], o, o.t[:])
    return cx.finish()


def build_attn0(n_pairs=8):
    cx = Cx()
    S = cx.S
    qT = cx.din("qT", [D, TL], BF16)
    kT = cx.din("kT", [D, S_ALL], BF16)
    vr = cx.din("vr", [8, 128, 128 * 128], BF16)
    masks = cx.din("masks", [128, 8 * 128], BF16)
    tri = cx.din("tri", [128, 128], BF16)
    omt = cx.din("omt", [128, 128], BF16)
    mixT = cx.dout("mixT", [D, TL], BF16)

    mk = cx.sb("mk", [128, 8 * 128], BF16)
    trt = cx.sb("trt", [128, 128], BF16)
    omtt = cx.sb("omtt", [128, 128], BF16)
    S.dma("sp", "const3", lambda e: e.dma_start(out=mk.t[:], in_=masks), writes=[mk.b])
    S.dma("sp", "const4", lambda e: e.dma_start(out=trt.t[:], in_=tri), writes=[trt.b])
    S.dma("sp", "const5", lambda e: e.dma_start(out=omtt.t[:], in_=omt), writes=[omtt.b])

    one = cx.sb("one", [128, 1], F32)
    S.op("pool", lambda e: e.memset(one.t[:], 1.0), writes=[one.b])
    kTs = cx.sb("kTs", [128, S_ALL], BF16, n=2)
    vs = cx.sb("vs", [128, 128 * 128], BF16, n=2)
    qs = cx.sb("qs", [128, TL], BF16, n=2)
    mx = cx.sb("mx", [128, TL], BF16, n=2)
    z2 = cx.ps("z2", [128, 1024], F32, n=2)
    C2 = cx.ps("C2", [128, 1024], F32)
    op_ = cx.ps("op", [128, 512], F32, n=2)
    e2 = cx.sb("e2", [128, 1024], F32, n=5)
    sp2 = cx.sb("sp2", [128, 1024], BF16, n=5)
    x2 = cx.sb("x2", [128, 1024], BF16, n=3)
    w2 = cx.sb("w2", [128, 1024], BF16, n=3)

    def V(t, c0, w=None):
        v = t.t[:, :].rearrange("p (h c) -> p h c", h=2)
        return v[:, :, c0:512] if w is None else v[:, :, c0:c0 + w]

    def load_pair(hp):
        sl = hp % 2
        k_, v_, q_ = kTs[sl], vs[sl], qs[sl]
        S.dma("sp", "ldq%d" % sl, lambda e: e.dma_start(out=q_.t[:], in_=qT[hp * 128:(hp + 1) * 128, :]),
              writes=[q_.b])
        for part in range(4):
            c0 = part * 4096
            S.dma("sp", "ldk%d" % sl,
                  lambda e, c0=c0: e.dma_start(out=k_.t[:, c0:c0 + 4096], in_=kT[hp * 128:(hp + 1) * 128, c0:c0 + 4096]),
                  writes=[k_.b])
            S.dma("sp", "ldv%d" % sl,
                  lambda e, c0=c0: e.dma_start(out=v_.t[:, c0:c0 + 4096], in_=vr[hp, :, c0:c0 + 4096]),
                  writes=[v_.b])

    load_pair(0)

    def do_pair(hp):
        sl = hp % 2
        k_, v_, q_, m_ = kTs[sl], vs[sl], qs[sl], mx[sl]
        if hp + 1 < n_pairs:
            load_pair(hp + 1)
        items = []
        for J in range(4):
            nkb = 32 * J + 32
            for kb in range(nkb - 1, -1, -1):
                items.append((J, kb, nkb))
        st = {}

        def s1(it):
            J, kb, nkb = it
            r = kb - 32 * J
            c0 = 128 * (r // 8) if r >= 0 else 0
            z = cx.nxt(z2)
            e_ = cx.nxt(e2)
            sp_ = cx.nxt(sp2)
            for hh in range(2):
                pb = 64 * hh
                S.op("pe", lambda e, hh=hh, pb=pb: e.matmul(
                    z.t[:, 512 * hh + c0:512 * hh + 512], k_.t[pb:pb + 64, kb * 128:(kb + 1) * 128],
                    q_.t[pb:pb + 64, 512 * J + c0:512 * J + 512], start=True, stop=True),
                    reads=[k_.b, q_.b], writes=[z.b])
            S.op("act", lambda e: e.activation(V(e_, c0), V(z, c0), AF.Exp), reads=[z.b], writes=[e_.b])
            if r >= 0:
                i = r % 8
                S.op("pool", lambda e: e.tensor_tensor(
                    V(e_, c0, 128), V(e_, c0, 128),
                    mk.t[:, i * 128:(i + 1) * 128].unsqueeze(1).to_broadcast([128, 2, 128]), ALU.mult),
                    reads=[e_.b, mk.b], writes=[e_.b])
            S.op("act", lambda e: e.activation(V(sp_, c0), V(e_, c0), AF.Ln, bias=one.t[:, 0:1]),
                 reads=[e_.b, one.b], writes=[sp_.b])
            st[it] = [c0, e_, sp_, None, None]

        def s2(it):
            J, kb, nkb = it
            c0, e_, sp_, _, _ = st[it]
            x_ = cx.nxt(x2)
            for hh in range(2):
                S.op("pe", lambda e, hh=hh: e.matmul(C2.t[:, 512 * hh + c0:512 * hh + 512], trt.t[:],
                                                     sp_.t[:, 512 * hh + c0:512 * hh + 512],
                                                     start=(kb == nkb - 1), stop=True,
                                                     skip_group_check=(kb != nkb - 1)),
                     reads=[trt.b, sp_.b], writes=[C2.b])
            S.op("act", lambda e: e.activation(V(x_, c0), V(C2, c0), AF.Exp, scale=-1.0), reads=[C2.b], writes=[x_.b])
            st[it][3] = x_

        def s3(it):
            J, kb, nkb = it
            c0, e_, sp_, x_, _ = st[it]
            w_ = cx.nxt(w2)
            if kb > 0:
                for hh in range(2):
                    S.op("pe", lambda e, hh=hh: e.matmul(C2.t[:, 512 * hh + c0:512 * hh + 512], omtt.t[:],
                                                         sp_.t[:, 512 * hh + c0:512 * hh + 512], start=False, stop=True,
                                                         skip_group_check=True),
                         reads=[omtt.b, sp_.b], writes=[C2.b])
            S.op("dve", lambda e: e.tensor_tensor(V(w_, c0), V(e_, c0), V(x_, c0), ALU.mult),
                 reads=[e_.b, x_.b], writes=[w_.b])
            st[it][4] = w_

        def s4(it):
            J, kb, nkb = it
            c0, e_, sp_, x_, w_ = st.pop(it)
            o_ = op_[J % 2]
            for hh in range(2):
                pb = 64 * hh
                S.op("pe", lambda e, hh=hh, pb=pb: e.matmul(
                    o_.t[pb:pb + 64, c0:512], v_.t[:, kb * 128 + pb:kb * 128 + pb + 64],
                    w_.t[:, 512 * hh + c0:512 * hh + 512], start=(kb == nkb - 1), stop=(kb == 0),
                    skip_group_check=(kb != nkb - 1 and kb != 0)),
                    reads=[v_.b, w_.b], writes=[o_.b])
            if kb == 0:
                S.op("act", lambda e: e.copy(m_.t[:, 512 * J:512 * J + 512], o_.t[:, :]), reads=[o_.b], writes=[m_.b])

        n = len(items)
        for tau in range(-1, n + 3):
            if 0 <= tau - 2 < n:
                s3(items[tau - 2])
            if 0 <= tau - 1 < n:
                s2(items[tau - 1])
            if 0 <= tau + 1 < n:
                s1(items[tau + 1])
            if 0 <= tau - 3 < n:
                s4(items[tau - 3])
        cx.store("o_mix", mixT[hp * 128:(hp + 1) * 128, :], m_, m_.t[:])

    for hp in range(n_pairs):
        do_pair(hp)
    return cx.finish()


def build_post(nxt):
    cx = Cx()
    S = cx.S
    h_in = cx.din("h", [TL, D], F32)
    mixT = cx.din("mixT", [D, TL], BF16)
    p_in = cx.din("p", [TL, PLE], F32)
    w_o = cx.din("w_o", [D, D], F32)
    ffn_g = cx.din("ffn_g", [D], F32)
    w_gu = cx.din("w_gu", [D, 2 * DFF], F32)
    w_d = cx.din("w_d", [DFF, D], F32)
    ple_g = cx.din("ple_g", [D], F32)
    w_gate = cx.din("w_gate", [D, D], F32)
    w_proj = cx.din("w_proj", [PLE, D], F32)
    ident = cx.din("ident", [128, 128], BF16)
    hout = cx.dout("hout", [TL, D], F32)
    if nxt:
        a_g = cx.din("a_g", [D], F32)
        w_q = cx.din("w_q", [D, D], F32)
        qn_g = cx.din("qn_g", [DH], F32)
        sh_g = cx.din("sh_g", [D], F32)
        w_kvf = cx.din("w_kvf", [D, 2 * D + NH], F32)
        b_f = cx.din("b_f", [NH], F32)
        kn_g = cx.din("kn_g", [DH], F32)
        q1T = cx.dout("q1T", [D, TL], BF16)
        k1T = cx.dout("k1T", [D, TL], BF16)
        v1 = cx.dout("v1", [TL, D], BF16)
        flogT = cx.dout("flogT", [NH, TL], F32)

    dn = Dense(cx, ident)
    dn.consts()
    dn.conv_engs = ("act", "dve")
    g_ffn = dn.load_gain("ffn", ffn_g)
    g_ple = dn.load_gain("ple", ple_g)

    wbf = cx.sb("wbf", [128, 512], BF16, n=4)

    def to_scratch(name, dst_ap_fn):
        buf = Buf("scr_" + name)

        def dst_fn(kc, n0, wd):
            t = cx.nxt(wbf)
            return t, t.t[:, 0:wd]
        return buf, dst_fn

    def prep_dram(name, w_ap, K, N, dst_ap_fn, gain=None, c0=0, c1=None):
        buf = Buf("scr_" + name, multi=True)
        c1_ = N if c1 is None else c1
        for kc in range(K // 128):
            for n0 in range(c0, c1_, 512):
                wd = min(512, c1_ - n0)
                holder = {}

                def dst_fn(kc_, n0_, wd_, holder=holder):
                    t = cx.nxt(wbf)
                    holder["t"] = t
                    return t, t.t[:, 0:wd_]
                dn.prep_weight(w_ap[kc * 128:(kc + 1) * 128, :], 128, N, dst_fn, gain=None if gain is None else _GainCol(gain, kc),
                               c0=n0, c1=n0 + wd)
                t = holder["t"]
                dap = dst_ap_fn(kc, n0 - c0, wd)
                S.dma("act", "wp_" + t.b.name, lambda e, t=t, dap=dap, wd=wd: e.dma_start(out=dap, in_=t.t[:, 0:wd]),
                      reads=[t.b], writes=[buf])
        return buf

    class _GainCol:
        def __init__(self, g, kc):
            self.b = g.b
            self.t = _Shift(g.t, kc)

    class _Shift:
        def __init__(self, t, kc):
            self._t = t
            self._kc = kc

        def __getitem__(self, idx):
            return self._t[idx[0], self._kc:self._kc + 1]

    def scr8(name, N):
        return cx.dint("scr_" + name, [N // 512, 128, 8, 512], BF16)

    WB = {}
    LAZY = {}

    def ensure(name):
        f = LAZY.pop(name, None)
        if f is not None:
            f()

    WoB = scr8("wo", D)
    LAZY["wo"] = lambda: WB.__setitem__("wo", prep_dram("wo", w_o, D, D, lambda kc, n0, wd: WoB[n0 // 512, :, kc, :]))
    WguB = cx.dint("scr_wgu", [22, 128, 8, 256], BF16)

    def gu_dst(off):
        def f(kc, n0, wd):
            j0 = n0 // 128
            return WguB[j0:j0 + wd // 128, :, kc, off:off + 128].rearrange("j p i -> p j i")
        return f
    def _p_wgu():
        WB["wg"] = prep_dram("wg", w_gu, D, 2 * DFF, gu_dst(0), gain=g_ffn, c0=0, c1=DFF)
        WB["wu"] = prep_dram("wu", w_gu, D, 2 * DFF, gu_dst(128), gain=g_ffn, c0=DFF, c1=2 * DFF)
    LAZY["wgu"] = _p_wgu
    WdB = cx.dint("scr_wd", [4, 128, 22, 256], BF16)
    LAZY["wd"] = lambda: WB.__setitem__("wd", prep_dram(
        "wd", w_d, DFF, D, lambda kc, n0, wd: WdB[n0 // 256:n0 // 256 + 2, :, kc, :].rearrange("q p i -> p q i")))
    WgateB = scr8("wgate", D)
    Wproj = cx.sb("Wproj", [128, 2, D], BF16)

    def _p_wgate():
        WB["wgate"] = prep_dram("wgate", w_gate, D, D, lambda kc, n0, wd: WgateB[n0 // 512, :, kc, :], gain=g_ple)
        dn.prep_weight(w_proj, PLE, D, lambda kc, n0, wd: (Wproj, Wproj.t[:, kc, n0:n0 + wd]))
    LAZY["wgate"] = _p_wgate
    if nxt:
        g_a = dn.load_gain("a1", a_g)
        g_sh = dn.load_gain("sh", sh_g)
        WqB = scr8("wq", D)
        LAZY["wq"] = lambda: WB.__setitem__("wq", prep_dram(
            "wq", w_q, D, D, lambda kc, n0, wd: WqB[n0 // 512, :, kc, :], gain=g_a))
        WkvB = scr8("wkv", 2 * D)
        Wf = cx.sb("Wf", [128, 8, NH], BF16)

        def _p_wkv():
            WB["wkv"] = prep_dram("wkv", w_kvf, D, 2 * D + NH, lambda kc, n0, wd: WkvB[n0 // 512, :, kc, :], gain=g_sh,
                                  c0=0, c1=2 * D)
            dn.prep_weight(w_kvf, D, 2 * D + NH, lambda kc, n0, wd: (Wf, Wf.t[:, kc, 0:wd]), gain=g_sh,
                           c0=2 * D, c1=2 * D + NH)
        LAZY["wkv"] = _p_wkv
        qg = cx.sb("qg", [128, 8, DH], F32)
        kg = cx.sb("kg", [128, 8, DH], F32)
        S.dma("sp", "const6", lambda e: e.dma_start(out=qg.t[:], in_=qn_g.unsqueeze(0).unsqueeze(0).to_broadcast([128, 8, DH])),
              writes=[qg.b])
        S.dma("sp", "const7", lambda e: e.dma_start(out=kg.t[:], in_=kn_g.unsqueeze(0).unsqueeze(0).to_broadcast([128, 8, DH])),
              writes=[kg.b])
        S.op("dve", lambda e: e.tensor_scalar(qg.t[:], qg.t[:], 0.125, None, ALU.mult), reads=[qg.b], writes=[qg.b])
        nbf = cx.sb("nbf", [NH, 1], F32)
        S.dma("sp", "const8", lambda e: e.dma_start(out=nbf.t[:], in_=b_f.rearrange("(h o) -> h o", o=1)), writes=[nbf.b])
        S.op("dve", lambda e: e.tensor_scalar(nbf.t[:], nbf.t[:], -1.0, None, ALU.mult), reads=[nbf.b], writes=[nbf.b])
        one = cx.sb("one", [128, 1], F32)
        S.op("pool", lambda e: e.memset(one.t[:], 1.0), writes=[one.b])

    hres = cx.sb("hres", [128, D], F32, n=8)
    mTs = cx.sb("mT", [128, 8, 512], BF16, n=1)
    wt8 = cx.sb("wt8", [128, 8, 512], BF16, n=3)
    wgut = cx.sb("wgut", [128, 8, 256], BF16, n=4)
    wdt = cx.sb("wdt", [128, 22, 256], BF16, n=2)
    hnTs = cx.sb("hnT", [128, 8, 512], BF16, n=2)
    aT = cx.sb("aT", [128, 22, 512], BF16)
    sgs = cx.sb("sg", [128, 512], F32, n=2)
    tmps = cx.sb("tmp", [128, 512], F32, n=2)
    pblk = cx.sb("pblk", [128, PLE], F32, n=2)
    pbf = cx.sb("pbf", [128, PLE], BF16, n=2)
    pTs = cx.sb("pT", [128, 2, 128], BF16, n=2)
    pm = cx.ps("pm", [128, 512], F32, n=4)
    if nxt:
        hd8 = cx.sb("hd8", [128, 8], F32, n=4)
        qnb = cx.sb("qnb", [128, 512], BF16, n=2)
        oT = cx.sb("oT", [128, 512], BF16, n=2)
        fl = cx.sb("fl", [NH, 512], F32, n=2)

    def load_w8(scr, buf, hf, name):
        t = cx.nxt(wt8)
        S.dma("sp", "ld_" + t.b.name, lambda e: e.dma_start(out=t.t[:], in_=scr[hf]), reads=[buf], writes=[t.b])
        return t

    def add_res(h, c0, wd, src_tile, src_ap, flip=[0]):
        S.op("dve", lambda e: e.tensor_tensor(h.t[:, c0:c0 + wd], h.t[:, c0:c0 + wd], src_ap, ALU.add),
             reads=[h.b, src_tile.b], writes=[h.b])

    ensure("wo")
    ensure("wgu")
    for gi in range(4):
        t0 = gi * 512
        hb = []
        for b in range(4):
            h = cx.nxt(hres)
            r0 = t0 + b * 128
            S.dma("act", "ld_" + h.b.name, lambda e, h=h, r0=r0: e.dma_start(out=h.t[:], in_=h_in[r0:r0 + 128, :]),
                  writes=[h.b])
            hb.append(h)
        mT = cx.nxt(mTs)
        S.dma("act", "ld_mT", lambda e, mT=mT, t0=t0: e.dma_start(
            out=mT.t[:], in_=mixT[:, t0:t0 + 512].rearrange("(c p) t -> p c t", p=128)), writes=[mT.b])
        for hf in range(2):
            wt = load_w8(WoB, WB["wo"], hf, "wo")
            for b in range(4):
                p = cx.nxt(pm)
                for kc in range(8):
                    S.op("pe", lambda e, p=p, kc=kc, b=b, wt=wt, mT=mT: e.matmul(
                        p.t[:], mT.t[:, kc, b * 128:(b + 1) * 128], wt.t[:, kc, :], start=(kc == 0), stop=(kc == 7)),
                        reads=[mT.b, wt.b], writes=[p.b])
                add_res(hb[b], hf * 512, 512, p, p.t[:])
        ensure("wd")
        hT = cx.nxt(hnTs)
        for b in range(4):
            dn.norm_T(hb[b], hT, b * 128)
        for j in range(22):
            wg = cx.nxt(wgut)
            S.dma("sp", "ld_" + wg.b.name, lambda e, wg=wg, j=j: e.dma_start(out=wg.t[:], in_=WguB[j]),
                  reads=[WB["wg"], WB["wu"]], writes=[wg.b])
            pg = cx.nxt(pm)
            pu = cx.nxt(pm)
            for kc in range(8):
                S.op("pe", lambda e, pg=pg, kc=kc, wg=wg, hT=hT: e.matmul(
                    pg.t[:], wg.t[:, kc, 0:128], hT.t[:, kc, :], start=(kc == 0), stop=(kc == 7)),
                    reads=[wg.b, hT.b], writes=[pg.b])
            for kc in range(8):
                S.op("pe", lambda e, pu=pu, kc=kc, wg=wg, hT=hT: e.matmul(
                    pu.t[:], wg.t[:, kc, 128:256], hT.t[:, kc, :], start=(kc == 0), stop=(kc == 7)),
                    reads=[wg.b, hT.b], writes=[pu.b])
            sg = cx.nxt(sgs)
            S.op("act", lambda e, sg=sg, pg=pg: e.activation(sg.t[:], pg.t[:], AF.Silu), reads=[pg.b], writes=[sg.b])
            S.op("dve", lambda e, sg=sg, pu=pu, j=j: e.tensor_tensor(aT.t[:, j, :], sg.t[:], pu.t[:], ALU.mult),
                 reads=[sg.b, pu.b], writes=[aT.b])
        ensure("wgate")
        for qd in range(4):
            wd_ = cx.nxt(wdt)
            S.dma("sp", "ld_" + wd_.b.name, lambda e, wd_=wd_, qd=qd: e.dma_start(out=wd_.t[:], in_=WdB[qd]),
                  reads=[WB["wd"]], writes=[wd_.b])
            for b in range(4):
                p = cx.nxt(pm)
                for j in range(22):
                    S.op("pe", lambda e, p=p, j=j, b=b, wd_=wd_: e.matmul(
                        p.t[:, 0:256], aT.t[:, j, b * 128:(b + 1) * 128], wd_.t[:, j, :], start=(j == 0), stop=(j == 21)),
                        reads=[aT.b, wd_.b], writes=[p.b])
                add_res(hb[b], qd * 256, 256, p, p.t[:, 0:256])
        if nxt:
            ensure("wq")
        hT = cx.nxt(hnTs)
        for b in range(4):
            dn.norm_T(hb[b], hT, b * 128)
        for hf in range(2):
            wt = load_w8(WgateB, WB["wgate"], hf, "wgate")
            for b in range(4):
                pb_ = cx.nxt(pblk)
                r0 = t0 + b * 128
                S.dma("act", "ld_" + pb_.b.name, lambda e, pb_=pb_, r0=r0: e.dma_start(out=pb_.t[:], in_=p_in[r0:r0 + 128, :]),
                      writes=[pb_.b])
                pf = cx.nxt(pbf)
                S.op("act", lambda e, pf=pf, pb_=pb_: e.copy(pf.t[:], pb_.t[:]), reads=[pb_.b], writes=[pf.b])
                pT_ps = cx.nxt(dn.psT)
                for k2 in range(2):
                    S.op("pe", lambda e, k2=k2, pT_ps=pT_ps, pf=pf: e.transpose(
                        pT_ps.t[:, k2 * 128:(k2 + 1) * 128], pf.t[:, k2 * 128:(k2 + 1) * 128], dn.ident.t[:]),
                        reads=[pf.b, dn.ident.b], writes=[pT_ps.b])
                pT = cx.nxt(pTs)
                S.op("act", lambda e, pT=pT, pT_ps=pT_ps: e.copy(pT.t[:, :, :], pT_ps.t[:, 0:256].rearrange("p (k t) -> p k t", k=2)),
                     reads=[pT_ps.b], writes=[pT.b])
                pgate = cx.nxt(pm)
                for kc in range(8):
                    S.op("pe", lambda e, pgate=pgate, kc=kc, b=b, wt=wt, hT=hT: e.matmul(
                        pgate.t[:], hT.t[:, kc, b * 128:(b + 1) * 128], wt.t[:, kc, :], start=(kc == 0), stop=(kc == 7)),
                        reads=[hT.b, wt.b], writes=[pgate.b])
                pproj = cx.nxt(pm)
                for k2 in range(2):
                    S.op("pe", lambda e, pproj=pproj, k2=k2, pT=pT, hf=hf: e.matmul(
                        pproj.t[:], pT.t[:, k2, :], Wproj.t[:, k2, hf * 512:(hf + 1) * 512], start=(k2 == 0), stop=(k2 == 1)),
                        reads=[pT.b, Wproj.b], writes=[pproj.b])
                sg = cx.nxt(sgs)
                S.op("act", lambda e, sg=sg, pgate=pgate: e.activation(sg.t[:], pgate.t[:], AF.Sigmoid),
                     reads=[pgate.b], writes=[sg.b])
                tmp = cx.nxt(tmps)
                S.op("dve", lambda e, tmp=tmp, sg=sg, pproj=pproj: e.tensor_tensor(tmp.t[:], sg.t[:], pproj.t[:], ALU.mult),
                     reads=[sg.b, pproj.b], writes=[tmp.b])
                hh_ = hb[b]
                S.op("dve", lambda e, hh_=hh_, tmp=tmp, hf=hf: e.tensor_tensor(
                    hh_.t[:, hf * 512:(hf + 1) * 512], hh_.t[:, hf * 512:(hf + 1) * 512], tmp.t[:], ALU.add),
                    reads=[hh_.b, tmp.b], writes=[hh_.b])
        for b in range(4):
            r0 = t0 + b * 128
            cx.store("o_h", hout[r0:r0 + 128, :], hb[b], hb[b].t[:])
        if not nxt:
            continue
        ensure("wkv")
        hT = cx.nxt(hnTs)
        for b in range(4):
            dn.norm_T(hb[b], hT, b * 128)

        def head_norm_store(p, gtile, dstT, b, hf):
            sq = cx.nxt(tmps)
            S.op("act", lambda e: e.activation(sq.t[:], p.t[:], AF.Square), reads=[p.b], writes=[sq.b])
            s8 = cx.nxt(hd8)
            S.op("dve", lambda e: e.tensor_reduce(s8.t[:], sq.t[:, :].rearrange("p (h d) -> p h d", d=DH),
                                                  mybir.AxisListType.X, ALU.add), reads=[sq.b], writes=[s8.b])
            l8 = cx.nxt(hd8)
            S.op("act", lambda e: e.activation(l8.t[:], s8.t[:], AF.Ln, scale=1.0 / DH, bias=dn.eps.t[:, 0:1]),
                 reads=[s8.b, dn.eps.b], writes=[l8.b])
            r8 = cx.nxt(hd8)
            S.op("act", lambda e: e.activation(r8.t[:], l8.t[:], AF.Exp, scale=-0.5), reads=[l8.b], writes=[r8.b])
            qf = cx.nxt(sgs)
            S.op("dve", lambda e: e.tensor_tensor(qf.t[:, :].rearrange("p (h d) -> p h d", d=DH),
                                                  p.t[:, :].rearrange("p (h d) -> p h d", d=DH),
                                                  r8.t[:, :].unsqueeze(2).to_broadcast([128, 8, DH]), ALU.mult),
                 reads=[p.b, r8.b], writes=[qf.b])
            qn = cx.nxt(qnb)
            S.op("pool", lambda e: e.tensor_tensor(qn.t[:, :], qf.t[:, :], gtile.t[:, :, :].rearrange("p h d -> p (h d)"), ALU.mult),
                 reads=[qf.b, gtile.b], writes=[qn.b])
            tp = cx.nxt(dn.psT)
            for c4 in range(4):
                S.op("pe", lambda e, c4=c4: e.transpose(tp.t[:, c4 * 128:(c4 + 1) * 128], qn.t[:, c4 * 128:(c4 + 1) * 128],
                                                        dn.ident.t[:]), reads=[qn.b, dn.ident.b], writes=[tp.b])
            o = cx.nxt(oT)
            S.op("act", lambda e: e.copy(o.t[:], tp.t[:, 0:512]), reads=[tp.b], writes=[o.b])
            r0 = t0 + b * 128
            cx.store("o_qk1", dstT[hf * 512:(hf + 1) * 512, r0:r0 + 128].rearrange("(c p) t -> p c t", p=128), o,
                     o.t[:, :].rearrange("p (c t) -> p c t", c=4))

        for hf in range(2):
            wt = load_w8(WqB, WB["wq"], hf, "wq")
            for b in range(4):
                p = cx.nxt(pm)
                for kc in range(8):
                    S.op("pe", lambda e, p=p, kc=kc, b=b, wt=wt, hT=hT: e.matmul(
                        p.t[:], hT.t[:, kc, b * 128:(b + 1) * 128], wt.t[:, kc, :], start=(kc == 0), stop=(kc == 7)),
                        reads=[hT.b, wt.b], writes=[p.b])
                head_norm_store(p, qg, q1T, b, hf)
        for hf in range(4):
            wt = load_w8(WkvB, WB["wkv"], hf, "wkv")
            for b in range(4):
                p = cx.nxt(pm)
                for kc in range(8):
                    S.op("pe", lambda e, p=p, kc=kc, b=b, wt=wt, hT=hT: e.matmul(
                        p.t[:], hT.t[:, kc, b * 128:(b + 1) * 128], wt.t[:, kc, :], start=(kc == 0), stop=(kc == 7)),
                        reads=[hT.b, wt.b], writes=[p.b])
                if hf < 2:
                    head_norm_store(p, kg, k1T, b, hf)
                else:
                    o = cx.nxt(oT)
                    S.op("act", lambda e, o=o, p=p: e.copy(o.t[:], p.t[:]), reads=[p.b], writes=[o.b])
                    r0 = t0 + b * 128
                    cx.store("o_v1", v1[r0:r0 + 128, (hf - 2) * 512:(hf - 1) * 512], o, o.t[:])
        p = cx.nxt(pm)
        for kc in range(8):
            S.op("pe", lambda e, p=p, kc=kc, hT=hT: e.matmul(p.t[0:NH, :], Wf.t[:, kc, :], hT.t[:, kc, :],
                                                              start=(kc == 0), stop=(kc == 7)),
                 reads=[Wf.b, hT.b], writes=[p.b])
        f1 = cx.nxt(fl)
        S.op("act", lambda e, f1=f1, p=p: e.activation(f1.t[:], p.t[0:NH, :], AF.Exp, scale=-1.0, bias=nbf.t[:, 0:1]),
             reads=[p.b, nbf.b], writes=[f1.b])
        f2 = cx.nxt(fl)
        S.op("act", lambda e, f1=f1, f2=f2: e.activation(f2.t[:], f1.t[:], AF.Ln, bias=one.t[0:NH, 0:1]),
             reads=[f1.b, one.b], writes=[f2.b])
        S.op("dve", lambda e, f2=f2: e.tensor_scalar(f2.t[:], f2.t[:], -1.0, None, ALU.mult), reads=[f2.b], writes=[f2.b])
        cx.store("o_fl", flogT[:, t0:t0 + 512], f2, f2.t[:])
    return cx.finish()


def build_attn1(n_heads=NH):
    cx = Cx()
    S = cx.S
    qT = cx.din("qT", [D, TL], BF16)
    kT = cx.din("kT", [D, S_ALL], BF16)
    vr = cx.din("vr", [8, 128, 128 * 128], BF16)
    flog = cx.din("flog", [NH, S_ALL], F32)
    onehot = cx.din("onehot", [NH, 8], F32)
    negm = cx.din("negm", [128, 8 * 128], BF16)
    ident = cx.din("ident", [128, 128], BF16)
    sel = cx.din("sel", [128, 256], F32)
    mixT = cx.dout("mixT", [D, TL], BF16)
    kaug = cx.dint("kaug", [NH, 6, S_ALL], BF16)
    qaug = cx.dint("qaug", [NH, 6, TL], BF16)
    b_kaug = Buf("kaug")
    b_qaug = Buf("qaug")

    nm = cx.sb("nm", [128, 8 * 128], BF16)
    idt = cx.sb("idt", [128, 128], BF16)
    selt = cx.sb("selt", [128, 256], F32)
    oh = cx.sb("oh", [NH, 8], F32)
    for t_, src in ((nm, negm), (idt, ident), (selt, sel), (oh, onehot)):
        S.dma("sp", "cc_" + t_.b.name, lambda e, t_=t_, src=src: e.dma_start(out=t_.t[:], in_=src), writes=[t_.b])

    CH = 1024
    Fc = cx.sb("Fc", [NH, CH], F32, n=2)
    Fs = cx.sb("Fs", [NH, CH], F32, n=2)
    r1 = cx.sb("r1", [NH, CH], F32)
    onesf = cx.sb("onesf", [NH, CH], F32)
    S.op("pool", lambda e: e.memset(onesf.t[:], 1.0), writes=[onesf.b])
    ka = cx.sb("ka", [NH, 6, CH], BF16)
    Fq = cx.sb("Fq", [NH, TL], F32)
    qa = cx.sb("qa", [NH, 6, 512], BF16)
    carry = cx.sb("carry", [NH, 1], F32, n=2)
    S.op("pool", lambda e: e.memset(carry[1].t[:], 0.0), writes=[carry[1].b])
    S.op("pool", lambda e: e.memset(ka.t[:, 0:3, :], 1.0), writes=[ka.b])
    S.op("pool", lambda e: e.memset(qa.t[:, 3:6, :], 1.0), writes=[qa.b])
    for ci in range(S_ALL // CH):
        fc = Fc[ci % 2]
        fs = Fs[ci % 2]
        S.dma("sp", "ld_" + fc.b.name, lambda e, fc=fc, ci=ci: e.dma_start(out=fc.t[:], in_=flog[:, ci * CH:(ci + 1) * CH]),
              writes=[fc.b])
        cprev = carry[(ci + 1) % 2]
        ccur = carry[ci % 2]
        S.op("dve", lambda e, fs=fs, fc=fc, cprev=cprev: e.tensor_tensor_scan(
            fs.t[:], onesf.t[:], fc.t[:], cprev.t[:, 0:1], ALU.mult, ALU.add),
            reads=[onesf.b, fc.b, cprev.b], writes=[fs.b])
        S.op("dve", lambda e, fs=fs, ccur=ccur: e.tensor_copy(ccur.t[:], fs.t[:, CH - 1:CH]), reads=[fs.b], writes=[ccur.b])
        fview = fs.t[:, :].rearrange("h (c i) -> h c i", c=8)
        fqv = Fq.t[:, ci * 128:(ci + 1) * 128]
        for c in range(8):
            if c == 0:
                S.op("dve", lambda e, fview=fview, fqv=fqv, c=c: e.tensor_scalar(
                    fqv, fview[:, c, :], oh.t[:, c:c + 1], None, ALU.mult), reads=[fs.b, oh.b], writes=[Fq.b])
            else:
                S.op("dve", lambda e, fview=fview, fqv=fqv, c=c: e.scalar_tensor_tensor(
                    fqv, fview[:, c, :], oh.t[:, c:c + 1], fqv, ALU.mult, ALU.add), reads=[fs.b, oh.b, Fq.b], writes=[Fq.b])
        S.op("dve", lambda e, fs=fs: e.tensor_scalar(r1.t[:], fs.t[:], -1.0, None, ALU.mult), reads=[fs.b], writes=[r1.b])
        for part in range(3):
            S.op("dve", lambda e, part=part: e.tensor_copy(ka.t[:, 3 + part, :], r1.t[:]), reads=[r1.b], writes=[ka.b])
            if part < 2:
                S.op("dve", lambda e, part=part: e.tensor_tensor(r1.t[:], r1.t[:], ka.t[:, 3 + part, :], ALU.subtract),
                     reads=[r1.b, ka.b], writes=[r1.b])
        S.dma("sp", "st_kaug", lambda e, ci=ci: e.dma_start(out=kaug[:, :, ci * CH:(ci + 1) * CH], in_=ka.t[:]),
              reads=[ka.b], writes=[b_kaug])
    for qi in range(4):
        S.op("dve", lambda e, qi=qi: e.tensor_copy(r1.t[:, 0:512], Fq.t[:, qi * 512:(qi + 1) * 512]), reads=[Fq.b], writes=[r1.b])
        for part in range(3):
            S.op("dve", lambda e, part=part: e.tensor_copy(qa.t[:, part, :], r1.t[:, 0:512]), reads=[r1.b], writes=[qa.b])
            if part < 2:
                S.op("dve", lambda e, part=part: e.tensor_tensor(r1.t[:, 0:512], r1.t[:, 0:512], qa.t[:, part, :], ALU.subtract),
                     reads=[r1.b, qa.b], writes=[r1.b])
        S.dma("sp", "st_qaug", lambda e, qi=qi: e.dma_start(out=qaug[:, :, qi * 512:(qi + 1) * 512], in_=qa.t[:]),
              reads=[qa.b], writes=[b_qaug])

    kTs = cx.sb("kTs", [128, S_ALL], BF16, n=2)
    qs = cx.sb("qs", [128, TL], BF16, n=2)
    vs = cx.sb("vs", [128, 128 * 130], BF16, n=2)
    mx = cx.sb("mx", [128, TL], BF16, n=2)
    for sl in range(2):
        v3 = vs[sl].t[:, :].rearrange("p (k c) -> p k c", c=130)
        S.op("pool", lambda e, v3=v3: e.memset(v3[:, :, 64:65], 1.0), writes=[vs[sl].b])
        S.op("pool", lambda e, v3=v3: e.memset(v3[:, :, 129:130], 1.0), writes=[vs[sl].b])
    zp = cx.ps("zp", [128, 512], F32, n=3)
    op_ = cx.ps("op", [128, 512], F32, n=2)
    bcp = cx.ps("bcp", [128, 512], F32, n=1)
    pb_ = cx.sb("pb", [128, 512], BF16, n=4)
    osb = cx.sb("osb", [128, 512], F32, n=1)
    rbs = cx.sb("rbs", [128, 512], F32, n=1)

    def load_head(h):
        sl = h % 2
        k_, q_ = kTs[sl], qs[sl]
        S.dma("sp", "ldq%d" % sl, lambda e: e.dma_start(out=q_.t[0:64, :], in_=qT[h * 64:(h + 1) * 64, :]), writes=[q_.b])
        S.dma("sp", "ldq%d" % sl, lambda e: e.dma_start(out=q_.t[64:70, :], in_=qaug[h]), reads=[b_qaug], writes=[q_.b])
        S.dma("sp", "ldk%d" % sl, lambda e: e.dma_start(out=k_.t[64:70, :], in_=kaug[h]), reads=[b_kaug], writes=[k_.b])
        for part in range(4):
            c0 = part * 4096
            S.dma("sp", "ldk%d" % sl,
                  lambda e, c0=c0: e.dma_start(out=k_.t[0:64, c0:c0 + 4096], in_=kT[h * 64:(h + 1) * 64, c0:c0 + 4096]),
                  writes=[k_.b])

    def load_v(hp):
        v_ = vs[hp % 2]
        v3 = v_.t[:, :].rearrange("p (k c) -> p k c", c=130)
        src = vr[hp].rearrange("p (k c) -> p k c", c=128)
        for part in range(8):
            k0 = part * 16
            for hh in range(2):
                S.dma("sp", "ldv%d" % (hp % 2),
                      lambda e, k0=k0, hh=hh: e.dma_start(out=v3[:, k0:k0 + 16, 65 * hh:65 * hh + 64],
                                                          in_=src[:, k0:k0 + 16, 64 * hh:64 * hh + 64]),
                      writes=[v_.b])

    load_v(0)
    load_head(0)

    def do_head(h):
        sl = h % 2
        hp = h // 2
        odd = h % 2
        k_, q_ = kTs[sl], qs[sl]
        v_ = vs[hp % 2]
        m_ = mx[hp % 2]
        if h + 1 < n_heads:
            if (h + 1) % 2 == 0:
                load_v((h + 1) // 2)
            load_head(h + 1)
        items = []
        for J in range(4):
            nkb = 32 * J + 32
            for kb in range(nkb - 1, -1, -1):
                items.append((J, kb, nkb))
        st = {}

        def s1(it):
            J, kb, nkb = it
            r = kb - 32 * J
            c0 = 128 * (r // 8) if r >= 0 else 0
            z = cx.nxt(zp)
            p_ = cx.nxt(pb_)
            S.op("pe", lambda e: e.matmul(z.t[:, c0:512], k_.t[0:70, kb * 128:(kb + 1) * 128],
                                          q_.t[0:70, 512 * J + c0:512 * J + 512], start=True, stop=(r < 0)),
                 reads=[k_.b, q_.b], writes=[z.b])
            if r >= 0:
                i = r % 8
                S.op("pe", lambda e: e.matmul(z.t[:, c0:c0 + 128], idt.t[:], nm.t[:, i * 128:(i + 1) * 128],
                                              start=False, stop=True), reads=[idt.b, nm.b], writes=[z.b])
            S.op("act", lambda e: e.activation(p_.t[:, c0:512], z.t[:, c0:512], AF.Exp), reads=[z.b], writes=[p_.b])
            st[it] = (c0, p_)

        def s2(it):
            J, kb, nkb = it
            c0, p_ = st.pop(it)
            o_ = op_[J % 2]
            if odd:
                S.op("pe", lambda e: e.matmul(o_.t[:, c0:512], v_.t[:, kb * 130 + 1:kb * 130 + 129], p_.t[:, c0:512],
                                              start=(kb == nkb - 1), stop=(kb == 0), skip_group_check=(kb != nkb - 1 and kb != 0)),
                     reads=[v_.b, p_.b], writes=[o_.b])
            else:
                S.op("pe", lambda e: e.matmul(o_.t[0:65, c0:512], v_.t[:, kb * 130:kb * 130 + 65], p_.t[:, c0:512],
                                              start=(kb == nkb - 1), stop=(kb == 0), skip_group_check=(kb != nkb - 1 and kb != 0)),
                     reads=[v_.b, p_.b], writes=[o_.b])
            if kb == 0:
                ob = cx.nxt(osb)
                rb = cx.nxt(rbs)
                bc = bcp[0]
                if odd:
                    S.op("act", lambda e: e.copy(ob.t[32:64, :], o_.t[32:64, :]), reads=[o_.b], writes=[ob.b])
                    S.op("act", lambda e: e.copy(ob.t[64:128, :], o_.t[64:128, :]), reads=[o_.b], writes=[ob.b])
                    S.op("pe", lambda e: e.matmul(bc.t[:, :], selt.t[32:64, 128:256], ob.t[32:64, :], start=True, stop=True),
                         reads=[selt.b, ob.b], writes=[bc.b])
                    S.op("dve", lambda e: e.reciprocal(rb.t[64:128, :], bc.t[64:128, :]), reads=[bc.b], writes=[rb.b])
                    S.op("dve", lambda e: e.tensor_tensor(m_.t[64:128, 512 * J:512 * J + 512], ob.t[64:128, :], rb.t[64:128, :], ALU.mult),
                         reads=[ob.b, rb.b], writes=[m_.b])
                else:
                    S.op("act", lambda e: e.copy(ob.t[0:65, :], o_.t[0:65, :]), reads=[o_.b], writes=[ob.b])
                    S.op("pe", lambda e: e.matmul(bc.t[0:64, :], selt.t[64:65, 0:64], ob.t[64:65, :], start=True, stop=True),
                         reads=[selt.b, ob.b], writes=[bc.b])
                    S.op("dve", lambda e: e.reciprocal(rb.t[0:64, :], bc.t[0:64, :]), reads=[bc.b], writes=[rb.b])
                    S.op("dve", lambda e: e.tensor_tensor(m_.t[0:64, 512 * J:512 * J + 512], ob.t[0:64, :], rb.t[0:64, :], ALU.mult),
                         reads=[ob.b, rb.b], writes=[m_.b])

        n = len(items)
        for tau in range(-1, n + 1):
            if 0 <= tau + 1 < n:
                s1(items[tau + 1])
            if 0 <= tau - 1 < n:
                s2(items[tau - 1])
        if odd:
            cx.store("o_mix", mixT[hp * 128:(hp + 1) * 128, :], m_, m_.t[:])

    for h in range(n_heads):
        do_head(h)
    return cx.finish()


_PROGS = {}


def _prog(name, fn):
    if name not in _PROGS:
        _PROGS[name] = fn()
    return _PROGS[name]


def _shard_tok(a):
    F_ = a.shape[-1]
    r = a.reshape(16, 8, 128, F_)
    return [np.ascontiguousarray(r[:, c].reshape(TL, F_)) for c in range(NC_)]


def _gather_T(parts):
    R = parts[0].shape[0]
    out = np.zeros((R, 16, 8, 128), dtype=parts[0].dtype)
    for c in range(NC_):
        out[:, :, c, :] = parts[c].reshape(R, 16, 128)
    return out.reshape(R, S_ALL)


def _gather_v(parts):
    va = np.zeros((16, 8, 128, D), dtype=parts[0].dtype)
    for c in range(NC_):
        va[:, c] = parts[c].reshape(16, 128, D)
    va = va.reshape(128, 128, 8, 128)
    return np.ascontiguousarray(va.transpose(2, 1, 0, 3)).reshape(8, 128, 128 * 128)


def _consts():
    ar = np.arange(128)
    c = {}
    c["ident"] = np.eye(128, dtype=np.float32).astype(NPBF)
    c["tri"] = (ar[:, None] >= ar[None, :]).astype(np.float32).astype(NPBF)
    c["omt"] = (ar[:, None] < ar[None, :]).astype(np.float32).astype(NPBF)
    masks, negm, oh = [], [], []
    for cc in range(NC_):
        m = np.zeros((128, 8, 128), np.float32)
        n = np.full((128, 8, 128), -30000.0, np.float32)
        for i in range(8):
            if i < cc:
                m[:, i, :] = 1.0
                n[:, i, :] = 0.0
            elif i == cc:
                m[:, i, :] = (ar[:, None] < ar[None, :])
                n[:, i, :] = np.where(ar[:, None] <= ar[None, :], 0.0, -30000.0)
        masks.append(m.reshape(128, 1024).astype(NPBF))
        negm.append(n.reshape(128, 1024).astype(NPBF))
        o = np.zeros((NH, 8), np.float32)
        o[:, cc] = 1.0
        oh.append(o)
    c["masks"], c["negm"], c["onehot"] = masks, negm, oh
    sel = np.zeros((128, 256), np.float32)
    sel[64, 0:64] = 1.0
    sel[63, 192:256] = 1.0
    c["sel"] = sel
    return c


def _run(nc, in_maps):
    return run_bass_kernel_spmd(nc, in_maps, core_ids=list(range(NC_))).results


def kernel(x, p, attn_norm_g, sb_w_qkv, sb_w_o, shared_norm_g, shared_w_kvf, shared_b_f, shared_k_norm_g,
           fox_w_q, fox_q_norm_g, fox_w_o, ffn_norm_g, ffn_w_gu, ffn_w_d, ple_norm_g, ple_w_gate, ple_w_proj):
    f32 = lambda a: np.ascontiguousarray(np.asarray(a, dtype=np.float32))
    x = f32(x)
    p = f32(p)
    C = _consts()
    xs = _shard_tok(x[0])
    p0 = _shard_tok(p[0, 0])
    p1 = _shard_tok(p[1, 0])
    r1 = _run(_prog("pre0", build_pre0),
              [{"x": xs[c], "g": f32(attn_norm_g[0]), "w": f32(sb_w_qkv[0]), "ident": C["ident"]} for c in range(NC_)])
    kT_all = _gather_T([r["kT"] for r in r1])
    vr = _gather_v([r["v"] for r in r1])
    r2 = _run(_prog("attn0", build_attn0),
              [{"qT": r1[c]["qT"], "kT": kT_all, "vr": vr, "masks": C["masks"][c], "tri": C["tri"], "omt": C["omt"]}
               for c in range(NC_)])

    def post_in(c, h, mix, pl, w_o, li):
        return {"h": h, "mixT": mix, "p": pl, "w_o": f32(w_o), "ffn_g": f32(ffn_norm_g[li]), "w_gu": f32(ffn_w_gu[li]),
                "w_d": f32(ffn_w_d[li]), "ple_g": f32(ple_norm_g[li]), "w_gate": f32(ple_w_gate[li]),
                "w_proj": f32(ple_w_proj[li]), "ident": C["ident"]}

    in3 = []
    for c in range(NC_):
        d = post_in(c, xs[c], r2[c]["mixT"], p0[c], sb_w_o[0], 0)
        d.update({"a_g": f32(attn_norm_g[1]), "w_q": f32(fox_w_q[0]), "qn_g": f32(fox_q_norm_g[0]),
                  "sh_g": f32(shared_norm_g), "w_kvf": f32(shared_w_kvf), "b_f": f32(shared_b_f),
                  "kn_g": f32(shared_k_norm_g)})
        in3.append(d)
    r3 = _run(_prog("post_n", lambda: build_post(True)), in3)
    k1_all = _gather_T([r["k1T"] for r in r3])
    vr1 = _gather_v([r["v1"] for r in r3])
    fl_all = _gather_T([r["flogT"] for r in r3])
    r4 = _run(_prog("attn1", build_attn1),
              [{"qT": r3[c]["q1T"], "kT": k1_all, "vr": vr1, "flog": fl_all, "onehot": C["onehot"][c],
                "negm": C["negm"][c], "ident": C["ident"], "sel": C["sel"]} for c in range(NC_)])
    r5 = _run(_prog("post_l", lambda: build_post(False)),
              [post_in(c, r3[c]["hout"], r4[c]["mixT"], p1[c], fox_w_o[0], 1) for c in range(NC_)])
    out = np.zeros((16, 8, 128, D), np.float32)
    for c in range(NC_):
        out[:, c] = r5[c]["hout"].reshape(16, 128, D)
    return out.reshape(1, S_ALL, D)
```

```python
import numpy as np
import ml_dtypes
from contextlib import ExitStack
import concourse.bass as bass
import concourse.mybir as mybir
from concourse.bass_utils import run_bass_kernel_spmd

F32 = mybir.dt.float32
BF16 = mybir.dt.bfloat16
AF = mybir.ActivationFunctionType
ALU = mybir.AluOpType
NPBF = ml_dtypes.bfloat16

NC_ = 8
D = 1024
S_ALL = 16384
TL = 2048
NH = 16
DH = 64
DFF = 2816
PLE = 256
EPS = 1e-6
EPOCH = 4096


class Buf:
    __slots__ = ("name", "w", "r", "wm")

    def __init__(self, name, multi=False):
        self.name = name
        self.w = None
        self.r = []
        self.wm = {} if multi else None


class Tile:
    __slots__ = ("t", "b")

    def __init__(self, t, name):
        self.t = t
        self.b = Buf(name)


class Sched:
    ENG = ("pe", "act", "dve", "pool", "sp")

    def __init__(self, nc, es):
        self.nc = nc
        self.es = es
        self.lists = {e: [] for e in self.ENG}
        self.sems = {}
        self.cnt = {}
        self.alias = {}
        self.waited = {e: {} for e in self.ENG}
        self.ecount = {e: 0 for e in self.ENG}
        self.nsem = 0

    def _mksem(self, key):
        self.nsem += 1
        self.sems[key] = self.es.enter_context(self.nc.semaphore("s%d" % self.nsem))
        self.cnt[key] = 0

    def _deps(self, eng, reads, writes):
        deps = {}

        def add(tok):
            if tok is None:
                return
            k, v = tok
            if deps.get(k, 0) < v:
                deps[k] = v

        for b in reads:
            add(b.w)
            if b.wm is not None:
                for t in b.wm.items():
                    add(t)
        for b in writes:
            add(b.w)
            for t in b.r:
                add(t)
        out = []
        w = self.waited[eng]
        for k, v in deps.items():
            if eng == "pe" and k.startswith("E_pe#"):
                continue
            if w.get(k, 0) >= v:
                continue
            w[k] = v
            out.append((k, v))
        return out

    @staticmethod
    def _commit(tok, reads, writes):
        for b in writes:
            if b.wm is not None:
                if b.wm.get(tok[0], 0) < tok[1]:
                    b.wm[tok[0]] = tok[1]
                continue
            b.w = tok
            b.r = []
        for b in reads:
            b.r.append(tok)

    def op(self, eng, fn, reads=(), writes=()):
        deps = self._deps(eng, reads, writes)
        ep = self.ecount[eng] // EPOCH
        key = "E_%s#%d" % (eng, ep)
        if key not in self.sems:
            self._mksem(key)
        self.ecount[eng] += 1
        self.cnt[key] += 1
        tok = (key, self.cnt[key])
        sems = self.sems

        def thunk(e, deps=deps, fn=fn, key=key):
            for k, v in deps:
                e.wait_ge(sems[k], v)
            fn(e).then_inc(sems[key], 1)

        self.lists[eng].append(thunk)
        self._commit(tok, reads, writes)
        return tok

    def dma(self, eng, semname, fn, reads=(), writes=()):
        deps = self._deps(eng, reads, writes)
        key = self.alias.get(semname)
        if key is None or self.cnt[key] >= 16 * 240:
            n = 0 if key is None else int(key.split("#")[1]) + 1
            key = "D_%s#%d" % (semname, n)
            self.alias[semname] = key
            self._mksem(key)
        self.cnt[key] += 16
        tok = (key, self.cnt[key])
        sems = self.sems

        def thunk(e, deps=deps, fn=fn, key=key):
            for k, v in deps:
                e.wait_ge(sems[k], v)
            fn(e).then_inc(sems[key], 16)

        self.lists[eng].append(thunk)
        self._commit(tok, reads, writes)
        return tok

    def wait_all(self, eng, toks):
        sems = self.sems
        toks = list(toks)

        def thunk(e):
            for k, v in toks:
                e.wait_ge(sems[k], v)

        self.lists[eng].append(thunk)

    def barrier(self):
        sems = self.sems
        for eng in self.ENG:
            w = self.waited[eng]
            todo = [(k, v) for k, v in self.cnt.items() if v > 0 and w.get(k, 0) < v]
            for k, v in todo:
                w[k] = v

            def thunk(e, todo=todo):
                for k, v in todo:
                    e.wait_ge(sems[k], v)

            self.lists[eng].append(thunk)

    def emit(self):
        L = self.lists
        self.lists = {e: [] for e in self.ENG}
        with self.nc.Block() as block:
            @block.tensor
            def _(e):
                for t in L["pe"]:
                    t(e)

            @block.scalar
            def _(e):
                for t in L["act"]:
                    t(e)

            @block.vector
            def _(e):
                for t in L["dve"]:
                    t(e)

            @block.gpsimd
            def _(e):
                for t in L["pool"]:
                    t(e)

            @block.sync
            def _(e):
                for t in L["sp"]:
                    t(e)


class Cx:
    def __init__(self):
        self.nc = bass.Bass("TRN2", target_bir_lowering=False)
        self.es = ExitStack()
        self.S = Sched(self.nc, self.es)
        self.out_toks = {}
        self.rr = {}
        self.scopes = []
        self.prefix = ""
        self._din = {}

    def begin_phase(self, prefix):
        self.prefix = prefix
        self.scopes.append(ExitStack())

    def end_phase(self):
        self.S.barrier()
        self.S.emit()
        self.scopes.pop().close()
        self.prefix = ""

    def _cur(self):
        return self.scopes[-1] if self.scopes else self.es

    def din(self, name, shape, dt):
        if name not in self._din:
            self._din[name] = self.nc.dram_tensor(name, list(shape), dt, kind="ExternalInput").ap()
        return self._din[name]

    def dout(self, name, shape, dt):
        return self.nc.dram_tensor(name, list(shape), dt, kind="ExternalOutput").ap()

    def dint(self, name, shape, dt):
        return self.nc.dram_tensor(name, list(shape), dt, kind="Internal").ap()

    def sb(self, name, shape, dt, n=None):
        if n is None:
            return Tile(self._cur().enter_context(self.nc.sbuf_tensor("sb_" + self.prefix + name, list(shape), dt)), name)
        return [Tile(self._cur().enter_context(self.nc.sbuf_tensor("sb_%s%s%d" % (self.prefix, name, i), list(shape), dt)),
                     "%s%d" % (name, i)) for i in range(n)]

    def ps(self, name, shape, dt, n=None):
        if n is None:
            return Tile(self._cur().enter_context(self.nc.psum_tensor("ps_" + self.prefix + name, list(shape), dt)), name)
        return [Tile(self._cur().enter_context(self.nc.psum_tensor("ps_%s%s%d" % (self.prefix, name, i), list(shape), dt)),
                     "%s%d" % (name, i)) for i in range(n)]

    def nxt(self, lst, key=None):
        key = key or id(lst)
        i = self.rr.get(key, 0)
        self.rr[key] = i + 1
        return lst[i % len(lst)]

    def store(self, semname, out_ap, tile, in_ap, eng="act"):
        tok = self.S.dma(eng, "st_" + self.prefix + tile.b.name, lambda e: e.dma_start(out=out_ap, in_=in_ap), reads=[tile.b])
        self.out_toks[tok[0]] = max(self.out_toks.get(tok[0], 0), tok[1])

    def finish(self):
        self.S.wait_all("sp", list(self.out_toks.items()))
        self.S.emit()
        while self.scopes:
            self.scopes.pop().close()
        self.es.close()
        return self.nc


class Dense:
    def __init__(self, cx, ident_ap):
        self.cx = cx
        S = cx.S
        self.ident = cx.sb("ident", [128, 128], BF16)
        S.dma("sp", "const1", lambda e: e.dma_start(out=self.ident.t[:], in_=ident_ap), writes=[self.ident.b])
        self.junk = cx.sb("junk", [128, 1024], BF16, n=2)
        self.ss = cx.sb("ss", [128, 1], F32, n=4)
        self.lnv = cx.sb("lnv", [128, 1], F32, n=4)
        self.rstd = cx.sb("rstd", [128, 1], F32, n=4)
        self.hn = cx.sb("hn", [128, 1024], BF16, n=2)
        self.psT = cx.ps("psT", [128, 1024], BF16, n=2)
        self.wst = cx.sb("wst", [128, 512], F32, n=4)
        self.gv = {}
        self.evq = 0
        self.conv_engs = ("pool", "dve")

    def load_gain(self, name, g_ap):
        cx = self.cx
        t = cx.sb("g_" + name, [128, 8], F32)
        cx.S.dma("sp", "cg_" + name, lambda e: e.dma_start(out=t.t[:], in_=g_ap.rearrange("(k p) -> p k", p=128),
                                                      allow_slow_non_contiguous=True), writes=[t.b])
        self.gv[name] = t
        return t

    def prep_weight(self, w_ap, K, N, dst_fn, gain=None, post=None, c0=0, c1=None):
        cx = self.cx
        S = cx.S
        c1 = N if c1 is None else c1
        for kc in range(K // 128):
            for n0 in range(c0, c1, 512):
                wd = min(512, c1 - n0)
                st = cx.nxt(self.wst)
                S.dma("sp", "wst_" + st.b.name,
                      lambda e, st=st, kc=kc, n0=n0, wd=wd: e.dma_start(
                          out=st.t[:, 0:wd], in_=w_ap[kc * 128:(kc + 1) * 128, n0:n0 + wd]),
                      writes=[st.b])
                dt_, dap = dst_fn(kc, n0, wd)
                eng = self.conv_engs[self.evq % 2]
                self.evq += 1
                pv = post(n0) if post is not None else None
                if eng == "act":
                    if gain is not None:
                        g = gain
                        S.op("act", lambda e, st=st, dap=dap, kc=kc, wd=wd, g=g: e.activation(
                            dap, st.t[:, 0:wd], AF.Copy, scale=g.t[:, kc:kc + 1]), reads=[st.b, g.b], writes=[dt_.b])
                    else:
                        S.op("act", lambda e, st=st, dap=dap, wd=wd: e.copy(dap, st.t[:, 0:wd]),
                             reads=[st.b], writes=[dt_.b])
                elif gain is not None:
                    g = gain
                    if pv is not None:
                        fn = lambda e, st=st, dap=dap, kc=kc, wd=wd, g=g, pv=pv: e.tensor_scalar(
                            dap, st.t[:, 0:wd], g.t[:, kc:kc + 1], pv, ALU.mult, ALU.mult)
                    else:
                        fn = lambda e, st=st, dap=dap, kc=kc, wd=wd, g=g: e.tensor_scalar(
                            dap, st.t[:, 0:wd], g.t[:, kc:kc + 1], None, ALU.mult)
                    S.op(eng, fn, reads=[st.b, g.b], writes=[dt_.b])
                else:
                    S.op(eng, lambda e, st=st, dap=dap, wd=wd: e.tensor_copy(dap, st.t[:, 0:wd]),
                         reads=[st.b], writes=[dt_.b])

    def norm_T(self, h, hnT, col0):
        cx = self.cx
        S = cx.S
        junk = cx.nxt(self.junk)
        ss = cx.nxt(self.ss)
        lnv = cx.nxt(self.lnv)
        rstd = cx.nxt(self.rstd)
        hn = cx.nxt(self.hn)
        pT = cx.nxt(self.psT)
        S.op("act", lambda e: e.activation(junk.t[:], h.t[:], AF.Square, accum_out=ss.t[:]),
             reads=[h.b], writes=[junk.b, ss.b])
        S.op("act", lambda e: e.activation(lnv.t[:], ss.t[:], AF.Ln, scale=1.0 / D, bias=self.eps.t[:, 0:1]),
             reads=[ss.b, self.eps.b], writes=[lnv.b])
        S.op("act", lambda e: e.activation(rstd.t[:], lnv.t[:], AF.Exp, scale=-0.5),
             reads=[lnv.b], writes=[rstd.b])
        S.op("dve", lambda e: e.tensor_scalar(hn.t[:], h.t[:], rstd.t[:, 0:1], None, ALU.mult),
             reads=[h.b, rstd.b], writes=[hn.b])
        for kc in range(8):
            S.op("pe", lambda e, kc=kc: e.transpose(pT.t[:, kc * 128:(kc + 1) * 128],
                                                    hn.t[:, kc * 128:(kc + 1) * 128], self.ident.t[:]),
                 reads=[hn.b, self.ident.b], writes=[pT.b])
        S.op("dve", lambda e: e.tensor_copy(hnT.t[:, :, col0:col0 + 128],
                                            pT.t[:, :].rearrange("p (k t) -> p k t", k=8)),
             reads=[pT.b], writes=[hnT.b])

    def consts(self):
        cx = self.cx
        self.eps = cx.sb("epsc", [128, 1], F32)
        cx.S.op("pool", lambda e: e.memset(self.eps.t[:], EPS), writes=[self.eps.b])


def build_pre0():
    cx = Cx()
    S = cx.S
    x = cx.din("x", [TL, D], F32)
    g = cx.din("g", [D], F32)
    w = cx.din("w", [D, 3 * D], F32)
    ident = cx.din("ident", [128, 128], BF16)
    qT = cx.dout("qT", [D, TL], BF16)
    kT = cx.dout("kT", [D, TL], BF16)
    v = cx.dout("v", [TL, D], BF16)
    dn = Dense(cx, ident)
    dn.consts()
    gt = dn.load_gain("a", g)
    Wb = cx.sb("Wb", [128, 8, 3 * D], BF16)
    dn.prep_weight(w, D, 3 * D, lambda kc, n0, wd: (Wb, Wb.t[:, kc, n0:n0 + wd]), gain=gt,
                   post=lambda n0: (0.125 if n0 < D else 1.0))
    hblk = cx.sb("hblk", [128, D], F32, n=3)
    hnT = cx.sb("hnT", [128, 8, 512], BF16, n=2)
    pm = cx.ps("pm", [128, 512], F32, n=4)
    ost = cx.sb("ost", [128, 512], BF16, n=4)
    ev = 0
    for gi in range(4):
        hT = cx.nxt(hnT)
        for b in range(4):
            h = cx.nxt(hblk)
            r0 = (gi * 4 + b) * 128
            S.dma("sp", "ld_" + h.b.name, lambda e, h=h, r0=r0: e.dma_start(out=h.t[:], in_=x[r0:r0 + 128, :]),
                  writes=[h.b])
            dn.norm_T(h, hT, b * 128)
        for n in range(16):
            p = cx.nxt(pm)
            for kc in range(8):
                S.op("pe", lambda e, p=p, kc=kc, n=n, hT=hT: e.matmul(
                    p.t[:], Wb.t[:, kc, n * 128:(n + 1) * 128], hT.t[:, kc, :], start=(kc == 0), stop=(kc == 7)),
                    reads=[Wb.b, hT.b], writes=[p.b])
            o = cx.nxt(ost)
            if ev % 2 == 0:
                S.op("act", lambda e, o=o, p=p: e.copy(o.t[:], p.t[:]), reads=[p.b], writes=[o.b])
            else:
                S.op("dve", lambda e, o=o, p=p: e.tensor_copy(o.t[:], p.t[:]), reads=[p.b], writes=[o.b])
            ev += 1
            dst = qT if n < 8 else kT
            rr = (n % 8) * 128
            cx.store("o_qk", dst[rr:rr + 128, gi * 512:(gi + 1) * 512], o, o.t[:])
        for b in range(4):
            for hf in range(2):
                p = cx.nxt(pm)
                for kc in range(8):
                    S.op("pe", lambda e, p=p, kc=kc, b=b, hf=hf, hT=hT: e.matmul(
                        p.t[:], hT.t[:, kc, b * 128:(b + 1) * 128],
                        Wb.t[:, kc, 2 * D + hf * 512:2 * D + (hf + 1) * 512], start=(kc == 0), stop=(kc == 7)),
                        reads=[Wb.b, hT.b], writes=[p.b])
                o = cx.nxt(ost)
                if ev % 2 == 0:
                    S.op("act", lambda e, o=o, p=p: e.copy(o.t[:], p.t[:]), reads=[p.b], writes=[o.b])
                else:
                    S.op("dve", lambda e, o=o, p=p: e.tensor_copy(o.t[:], p.t[:]), reads=[p.b], writes=[o.b])
                ev += 1
                r0 = (gi * 4 + b) * 128
                cx.store("o_v", v[r0:r0 + 128, hf * 512:(hf + 1) * 512], o, o.t[:])
    return cx.finish()


def build_attn0(n_pairs=8, cx=None, mix_dst=None):
    own = cx is None
    if own:
        cx = Cx()
    S = cx.S
    qT = cx.din("qT", [D, TL], BF16)
    kT = cx.din("kT", [D, S_ALL], BF16)
    vr = cx.din("vr", [8, 128, 128 * 128], BF16)
    masks = cx.din("masks", [128, 8 * 128], BF16)
    tri = cx.din("tri", [128, 128], BF16)
    omt = cx.din("omt", [128, 128], BF16)
    mixT = cx.dout("mixT", [D, TL], BF16) if mix_dst is None else mix_dst

    mk = cx.sb("mk", [128, 8 * 128], BF16)
    trt = cx.sb("trt", [128, 128], BF16)
    omtt = cx.sb("omtt", [128, 128], BF16)
    S.dma("sp", "const3", lambda e: e.dma_start(out=mk.t[:], in_=masks), writes=[mk.b])
    S.dma("sp", "const4", lambda e: e.dma_start(out=trt.t[:], in_=tri), writes=[trt.b])
    S.dma("sp", "const5", lambda e: e.dma_start(out=omtt.t[:], in_=omt), writes=[omtt.b])

    one = cx.sb("one", [128, 1], F32)
    S.op("pool", lambda e: e.memset(one.t[:], 1.0), writes=[one.b])
    kTs = cx.sb("kTs", [128, S_ALL], BF16, n=2)
    vs = cx.sb("vs", [128, 128 * 128], BF16, n=2)
    qs = cx.sb("qs", [128, TL], BF16, n=2)
    mx = cx.sb("mx", [128, TL], BF16, n=2)
    z2 = cx.ps("z2", [128, 1024], F32, n=2)
    C2 = cx.ps("C2", [128, 1024], F32)
    op_ = cx.ps("op", [128, 512], F32, n=2)
    e2 = cx.sb("e2", [128, 1024], F32, n=5)
    sp2 = cx.sb("sp2", [128, 1024], BF16, n=5)
    x2 = cx.sb("x2", [128, 1024], BF16, n=3)
    w2 = cx.sb("w2", [128, 1024], BF16, n=3)

    def V(t, c0, w=None):
        v = t.t[:, :].rearrange("p (h c) -> p h c", h=2)
        return v[:, :, c0:512] if w is None else v[:, :, c0:c0 + w]

    def load_pair(hp):
        sl = hp % 2
        k_, v_, q_ = kTs[sl], vs[sl], qs[sl]
        S.dma("sp", "ldq%d" % sl, lambda e: e.dma_start(out=q_.t[:], in_=qT[hp * 128:(hp + 1) * 128, :]),
              writes=[q_.b])
        for part in range(4):
            c0 = part * 4096
            S.dma("sp", "ldk%d" % sl,
                  lambda e, c0=c0: e.dma_start(out=k_.t[:, c0:c0 + 4096], in_=kT[hp * 128:(hp + 1) * 128, c0:c0 + 4096]),
                  writes=[k_.b])
            S.dma("sp", "ldv%d" % sl,
                  lambda e, c0=c0: e.dma_start(out=v_.t[:, c0:c0 + 4096], in_=vr[hp, :, c0:c0 + 4096]),
                  writes=[v_.b])

    load_pair(0)

    def do_pair(hp):
        sl = hp % 2
        k_, v_, q_, m_ = kTs[sl], vs[sl], qs[sl], mx[sl]
        if hp + 1 < n_pairs:
            load_pair(hp + 1)
        items = []
        for J in range(4):
            nkb = 32 * J + 32
            for kb in range(nkb - 1, -1, -1):
                items.append((J, kb, nkb))
        st = {}

        def s1(it):
            J, kb, nkb = it
            r = kb - 32 * J
            c0 = 128 * (r // 8) if r >= 0 else 0
            z = cx.nxt(z2)
            e_ = cx.nxt(e2)
            sp_ = cx.nxt(sp2)
            for hh in range(2):
                pb = 64 * hh
                S.op("pe", lambda e, hh=hh, pb=pb: e.matmul(
                    z.t[:, 512 * hh + c0:512 * hh + 512], k_.t[pb:pb + 64, kb * 128:(kb + 1) * 128],
                    q_.t[pb:pb + 64, 512 * J + c0:512 * J + 512], start=True, stop=True),
                    reads=[k_.b, q_.b], writes=[z.b])
            S.op("act", lambda e: e.activation(V(e_, c0), V(z, c0), AF.Exp), reads=[z.b], writes=[e_.b])
            if r >= 0:
                i = r % 8
                S.op("pool", lambda e: e.tensor_tensor(
                    V(e_, c0, 128), V(e_, c0, 128),
                    mk.t[:, i * 128:(i + 1) * 128].unsqueeze(1).to_broadcast([128, 2, 128]), ALU.mult),
                    reads=[e_.b, mk.b], writes=[e_.b])
            S.op("act", lambda e: e.activation(V(sp_, c0), V(e_, c0), AF.Ln, bias=one.t[:, 0:1]),
                 reads=[e_.b, one.b], writes=[sp_.b])
            st[it] = [c0, e_, sp_, None, None]

        def s2(it):
            J, kb, nkb = it
            c0, e_, sp_, _, _ = st[it]
            x_ = cx.nxt(x2)
            for hh in range(2):
                S.op("pe", lambda e, hh=hh: e.matmul(C2.t[:, 512 * hh + c0:512 * hh + 512], trt.t[:],
                                                     sp_.t[:, 512 * hh + c0:512 * hh + 512],
                                                     start=(kb == nkb - 1), stop=True,
                                                     skip_group_check=(kb != nkb - 1)),
                     reads=[trt.b, sp_.b], writes=[C2.b])
            S.op("act", lambda e: e.activation(V(x_, c0), V(C2, c0), AF.Exp, scale=-1.0), reads=[C2.b], writes=[x_.b])
            st[it][3] = x_

        def s3(it):
            J, kb, nkb = it
            c0, e_, sp_, x_, _ = st[it]
            w_ = cx.nxt(w2)
            if kb > 0:
                for hh in range(2):
                    S.op("pe", lambda e, hh=hh: e.matmul(C2.t[:, 512 * hh + c0:512 * hh + 512], omtt.t[:],
                                                         sp_.t[:, 512 * hh + c0:512 * hh + 512], start=False, stop=True,
                                                         skip_group_check=True),
                         reads=[omtt.b, sp_.b], writes=[C2.b])
            S.op("dve", lambda e: e.tensor_tensor(V(w_, c0), V(e_, c0), V(x_, c0), ALU.mult),
                 reads=[e_.b, x_.b], writes=[w_.b])
            st[it][4] = w_

        def s4(it):
            J, kb, nkb = it
            c0, e_, sp_, x_, w_ = st.pop(it)
            o_ = op_[J % 2]
            for hh in range(2):
                pb = 64 * hh
                S.op("pe", lambda e, hh=hh, pb=pb: e.matmul(
                    o_.t[pb:pb + 64, c0:512], v_.t[:, kb * 128 + pb:kb * 128 + pb + 64],
                    w_.t[:, 512 * hh + c0:512 * hh + 512], start=(kb == nkb - 1), stop=(kb == 0),
                    skip_group_check=(kb != nkb - 1 and kb != 0)),
                    reads=[v_.b, w_.b], writes=[o_.b])
            if kb == 0:
                S.op("act", lambda e: e.copy(m_.t[:, 512 * J:512 * J + 512], o_.t[:, :]), reads=[o_.b], writes=[m_.b])

        n = len(items)
        for tau in range(-1, n + 3):
            if 0 <= tau - 2 < n:
                s3(items[tau - 2])
            if 0 <= tau - 1 < n:
                s2(items[tau - 1])
            if 0 <= tau + 1 < n:
                s1(items[tau + 1])
            if 0 <= tau - 3 < n:
                s4(items[tau - 3])
        cx.store("o_mix", mixT[hp * 128:(hp + 1) * 128, :], m_, m_.t[:])

    for hp in range(n_pairs):
        do_pair(hp)
    return cx.finish() if own else None


def build_post(nxt, cx=None, mix_src=None):
    own = cx is None
    if own:
        cx = Cx()
    S = cx.S
    h_in = cx.din("h", [TL, D], F32)
    mixT = cx.din("mixT", [D, TL], BF16) if mix_src is None else mix_src
    p_in = cx.din("p", [TL, PLE], F32)
    w_o = cx.din("w_o", [D, D], F32)
    ffn_g = cx.din("ffn_g", [D], F32)
    w_gu = cx.din("w_gu", [D, 2 * DFF], F32)
    w_d = cx.din("w_d", [DFF, D], F32)
    ple_g = cx.din("ple_g", [D], F32)
    w_gate = cx.din("w_gate", [D, D], F32)
    w_proj = cx.din("w_proj", [PLE, D], F32)
    ident = cx.din("ident", [128, 128], BF16)
    hout = cx.dout("hout", [TL, D], F32)
    if nxt:
        a_g = cx.din("a_g", [D], F32)
        w_q = cx.din("w_q", [D, D], F32)
        qn_g = cx.din("qn_g", [DH], F32)
        sh_g = cx.din("sh_g", [D], F32)
        w_kvf = cx.din("w_kvf", [D, 2 * D + NH], F32)
        b_f = cx.din("b_f", [NH], F32)
        kn_g = cx.din("kn_g", [DH], F32)
        q1T = cx.dout("q1T", [D, TL], BF16)
        k1T = cx.dout("k1T", [D, TL], BF16)
        v1 = cx.dout("v1", [TL, D], BF16)
        flogT = cx.dout("flogT", [NH, TL], F32)

    dn = Dense(cx, ident)
    dn.consts()
    dn.conv_engs = ("act", "dve")
    g_ffn = dn.load_gain("ffn", ffn_g)
    g_ple = dn.load_gain("ple", ple_g)

    wbf = cx.sb("wbf", [128, 512], BF16, n=4)

    def to_scratch(name, dst_ap_fn):
        buf = Buf("scr_" + name)

        def dst_fn(kc, n0, wd):
            t = cx.nxt(wbf)
            return t, t.t[:, 0:wd]
        return buf, dst_fn

    def prep_dram(name, w_ap, K, N, dst_ap_fn, gain=None, c0=0, c1=None):
        buf = Buf("scr_" + name, multi=True)
        c1_ = N if c1 is None else c1
        for kc in range(K // 128):
            for n0 in range(c0, c1_, 512):
                wd = min(512, c1_ - n0)
                holder = {}

                def dst_fn(kc_, n0_, wd_, holder=holder):
                    t = cx.nxt(wbf)
                    holder["t"] = t
                    return t, t.t[:, 0:wd_]
                dn.prep_weight(w_ap[kc * 128:(kc + 1) * 128, :], 128, N, dst_fn, gain=None if gain is None else _GainCol(gain, kc),
                               c0=n0, c1=n0 + wd)
                t = holder["t"]
                dap = dst_ap_fn(kc, n0 - c0, wd)
                S.dma("act", "wp_" + t.b.name, lambda e, t=t, dap=dap, wd=wd: e.dma_start(out=dap, in_=t.t[:, 0:wd]),
                      reads=[t.b], writes=[buf])
        return buf

    class _GainCol:
        def __init__(self, g, kc):
            self.b = g.b
            self.t = _Shift(g.t, kc)

    class _Shift:
        def __init__(self, t, kc):
            self._t = t
            self._kc = kc

        def __getitem__(self, idx):
            return self._t[idx[0], self._kc:self._kc + 1]

    def scr8(name, N):
        return cx.dint("scr_" + name, [N // 512, 128, 8, 512], BF16)

    WB = {}
    LAZY = {}

    def ensure(name):
        f = LAZY.pop(name, None)
        if f is not None:
            f()

    WoB = scr8("wo", D)
    LAZY["wo"] = lambda: WB.__setitem__("wo", prep_dram("wo", w_o, D, D, lambda kc, n0, wd: WoB[n0 // 512, :, kc, :]))
    WguB = cx.dint("scr_wgu", [22, 128, 8, 256], BF16)

    def gu_dst(off):
        def f(kc, n0, wd):
            j0 = n0 // 128
            return WguB[j0:j0 + wd // 128, :, kc, off:off + 128].rearrange("j p i -> p j i")
        return f
    def _p_wgu():
        WB["wg"] = prep_dram("wg", w_gu, D, 2 * DFF, gu_dst(0), gain=g_ffn, c0=0, c1=DFF)
        WB["wu"] = prep_dram("wu", w_gu, D, 2 * DFF, gu_dst(128), gain=g_ffn, c0=DFF, c1=2 * DFF)
    LAZY["wgu"] = _p_wgu
    WdB = cx.dint("scr_wd", [4, 128, 22, 256], BF16)
    LAZY["wd"] = lambda: WB.__setitem__("wd", prep_dram(
        "wd", w_d, DFF, D, lambda kc, n0, wd: WdB[n0 // 256:n0 // 256 + 2, :, kc, :].rearrange("q p i -> p q i")))
    WgateB = scr8("wgate", D)
    Wproj = cx.sb("Wproj", [128, 2, D], BF16)

    def _p_wgate():
        WB["wgate"] = prep_dram("wgate", w_gate, D, D, lambda kc, n0, wd: WgateB[n0 // 512, :, kc, :], gain=g_ple)
        dn.prep_weight(w_proj, PLE, D, lambda kc, n0, wd: (Wproj, Wproj.t[:, kc, n0:n0 + wd]))
    LAZY["wgate"] = _p_wgate
    if nxt:
        g_a = dn.load_gain("a1", a_g)
        g_sh = dn.load_gain("sh", sh_g)
        WqB = scr8("wq", D)
        LAZY["wq"] = lambda: WB.__setitem__("wq", prep_dram(
            "wq", w_q, D, D, lambda kc, n0, wd: WqB[n0 // 512, :, kc, :], gain=g_a))
        WkvB = scr8("wkv", 2 * D)
        Wf = cx.sb("Wf", [128, 8, NH], BF16)

        def _p_wkv():
            WB["wkv"] = prep_dram("wkv", w_kvf, D, 2 * D + NH, lambda kc, n0, wd: WkvB[n0 // 512, :, kc, :], gain=g_sh,
                                  c0=0, c1=2 * D)
            dn.prep_weight(w_kvf, D, 2 * D + NH, lambda kc, n0, wd: (Wf, Wf.t[:, kc, 0:wd]), gain=g_sh,
                           c0=2 * D, c1=2 * D + NH)
        LAZY["wkv"] = _p_wkv
        qg = cx.sb("qg", [128, 8, DH], F32)
        kg = cx.sb("kg", [128, 8, DH], F32)
        S.dma("sp", "const6", lambda e: e.dma_start(out=qg.t[:], in_=qn_g.unsqueeze(0).unsqueeze(0).to_broadcast([128, 8, DH])),
              writes=[qg.b])
        S.dma("sp", "const7", lambda e: e.dma_start(out=kg.t[:], in_=kn_g.unsqueeze(0).unsqueeze(0).to_broadcast([128, 8, DH])),
              writes=[kg.b])
        S.op("dve", lambda e: e.tensor_scalar(qg.t[:], qg.t[:], 0.125, None, ALU.mult), reads=[qg.b], writes=[qg.b])
        nbf = cx.sb("nbf", [NH, 1], F32)
        S.dma("sp", "const8", lambda e: e.dma_start(out=nbf.t[:], in_=b_f.rearrange("(h o) -> h o", o=1)), writes=[nbf.b])
        S.op("dve", lambda e: e.tensor_scalar(nbf.t[:], nbf.t[:], -1.0, None, ALU.mult), reads=[nbf.b], writes=[nbf.b])
        one = cx.sb("one", [128, 1], F32)
        S.op("pool", lambda e: e.memset(one.t[:], 1.0), writes=[one.b])

    hres = cx.sb("hres", [128, D], F32, n=8)
    mTs = cx.sb("mT", [128, 8, 512], BF16, n=1)
    wt8 = cx.sb("wt8", [128, 8, 512], BF16, n=3)
    wgut = cx.sb("wgut", [128, 8, 256], BF16, n=4)
    wdt = cx.sb("wdt", [128, 22, 256], BF16, n=2)
    hnTs = cx.sb("hnT", [128, 8, 512], BF16, n=2)
    aT = cx.sb("aT", [128, 22, 512], BF16)
    sgs = cx.sb("sg", [128, 512], F32, n=2)
    tmps = cx.sb("tmp", [128, 512], F32, n=2)
    pblk = cx.sb("pblk", [128, PLE], F32, n=2)
    pbf = cx.sb("pbf", [128, PLE], BF16, n=2)
    pTs = cx.sb("pT", [128, 2, 128], BF16, n=2)
    pm = cx.ps("pm", [128, 512], F32, n=4)
    if nxt:
        hd8 = cx.sb("hd8", [128, 8], F32, n=4)
        qnb = cx.sb("qnb", [128, 512], BF16, n=2)
        oT = cx.sb("oT", [128, 512], BF16, n=2)
        fl = cx.sb("fl", [NH, 512], F32, n=2)

    def load_w8(scr, buf, hf, name):
        t = cx.nxt(wt8)
        S.dma("sp", "ld_" + t.b.name, lambda e: e.dma_start(out=t.t[:], in_=scr[hf]), reads=[buf], writes=[t.b])
        return t

    def add_res(h, c0, wd, src_tile, src_ap, flip=[0]):
        S.op("dve", lambda e: e.tensor_tensor(h.t[:, c0:c0 + wd], h.t[:, c0:c0 + wd], src_ap, ALU.add),
             reads=[h.b, src_tile.b], writes=[h.b])

    ensure("wo")
    ensure("wgu")
    for gi in range(4):
        t0 = gi * 512
        hb = []
        for b in range(4):
            h = cx.nxt(hres)
            r0 = t0 + b * 128
            S.dma("act", "ld_" + h.b.name, lambda e, h=h, r0=r0: e.dma_start(out=h.t[:], in_=h_in[r0:r0 + 128, :]),
                  writes=[h.b])
            hb.append(h)
        mT = cx.nxt(mTs)
        S.dma("act", "ld_mT", lambda e, mT=mT, t0=t0: e.dma_start(
            out=mT.t[:], in_=mixT[:, t0:t0 + 512].rearrange("(c p) t -> p c t", p=128)), writes=[mT.b])
        for hf in range(2):
            wt = load_w8(WoB, WB["wo"], hf, "wo")
            for b in range(4):
                p = cx.nxt(pm)
                for kc in range(8):
                    S.op("pe", lambda e, p=p, kc=kc, b=b, wt=wt, mT=mT: e.matmul(
                        p.t[:], mT.t[:, kc, b * 128:(b + 1) * 128], wt.t[:, kc, :], start=(kc == 0), stop=(kc == 7)),
                        reads=[mT.b, wt.b], writes=[p.b])
                add_res(hb[b], hf * 512, 512, p, p.t[:])
        ensure("wd")
        hT = cx.nxt(hnTs)
        for b in range(4):
            dn.norm_T(hb[b], hT, b * 128)
        for j in range(22):
            wg = cx.nxt(wgut)
            S.dma("sp", "ld_" + wg.b.name, lambda e, wg=wg, j=j: e.dma_start(out=wg.t[:], in_=WguB[j]),
                  reads=[WB["wg"], WB["wu"]], writes=[wg.b])
            pg = cx.nxt(pm)
            pu = cx.nxt(pm)
            for kc in range(8):
                S.op("pe", lambda e, pg=pg, kc=kc, wg=wg, hT=hT: e.matmul(
                    pg.t[:], wg.t[:, kc, 0:128], hT.t[:, kc, :], start=(kc == 0), stop=(kc == 7)),
                    reads=[wg.b, hT.b], writes=[pg.b])
            for kc in range(8):
                S.op("pe", lambda e, pu=pu, kc=kc, wg=wg, hT=hT: e.matmul(
                    pu.t[:], wg.t[:, kc, 128:256], hT.t[:, kc, :], start=(kc == 0), stop=(kc == 7)),
                    reads=[wg.b, hT.b], writes=[pu.b])
            sg = cx.nxt(sgs)
            S.op("act", lambda e, sg=sg, pg=pg: e.activation(sg.t[:], pg.t[:], AF.Silu), reads=[pg.b], writes=[sg.b])
            S.op("dve", lambda e, sg=sg, pu=pu, j=j: e.tensor_tensor(aT.t[:, j, :], sg.t[:], pu.t[:], ALU.mult),
                 reads=[sg.b, pu.b], writes=[aT.b])
        ensure("wgate")
        for qd in range(4):
            wd_ = cx.nxt(wdt)
            S.dma("sp", "ld_" + wd_.b.name, lambda e, wd_=wd_, qd=qd: e.dma_start(out=wd_.t[:], in_=WdB[qd]),
                  reads=[WB["wd"]], writes=[wd_.b])
            for b in range(4):
                p = cx.nxt(pm)
                for j in range(22):
                    S.op("pe", lambda e, p=p, j=j, b=b, wd_=wd_: e.matmul(
                        p.t[:, 0:256], aT.t[:, j, b * 128:(b + 1) * 128], wd_.t[:, j, :], start=(j == 0), stop=(j == 21)),
                        reads=[aT.b, wd_.b], writes=[p.b])
                add_res(hb[b], qd * 256, 256, p, p.t[:, 0:256])
        if nxt:
            ensure("wq")
        hT = cx.nxt(hnTs)
        for b in range(4):
            dn.norm_T(hb[b], hT, b * 128)
        for hf in range(2):
            wt = load_w8(WgateB, WB["wgate"], hf, "wgate")
            for b in range(4):
                pb_ = cx.nxt(pblk)
                r0 = t0 + b * 128
                S.dma("act", "ld_" + pb_.b.name, lambda e, pb_=pb_, r0=r0: e.dma_start(out=pb_.t[:], in_=p_in[r0:r0 + 128, :]),
                      writes=[pb_.b])
                pf = cx.nxt(pbf)
                S.op("act", lambda e, pf=pf, pb_=pb_: e.copy(pf.t[:], pb_.t[:]), reads=[pb_.b], writes=[pf.b])
                pT_ps = cx.nxt(dn.psT)
                for k2 in range(2):
                    S.op("pe", lambda e, k2=k2, pT_ps=pT_ps, pf=pf: e.transpose(
                        pT_ps.t[:, k2 * 128:(k2 + 1) * 128], pf.t[:, k2 * 128:(k2 + 1) * 128], dn.ident.t[:]),
                        reads=[pf.b, dn.ident.b], writes=[pT_ps.b])
                pT = cx.nxt(pTs)
                S.op("act", lambda e, pT=pT, pT_ps=pT_ps: e.copy(pT.t[:, :, :], pT_ps.t[:, 0:256].rearrange("p (k t) -> p k t", k=2)),
                     reads=[pT_ps.b], writes=[pT.b])
                pgate = cx.nxt(pm)
                for kc in range(8):
                    S.op("pe", lambda e, pgate=pgate, kc=kc, b=b, wt=wt, hT=hT: e.matmul(
                        pgate.t[:], hT.t[:, kc, b * 128:(b + 1) * 128], wt.t[:, kc, :], start=(kc == 0), stop=(kc == 7)),
                        reads=[hT.b, wt.b], writes=[pgate.b])
                pproj = cx.nxt(pm)
                for k2 in range(2):
                    S.op("pe", lambda e, pproj=pproj, k2=k2, pT=pT, hf=hf: e.matmul(
                        pproj.t[:], pT.t[:, k2, :], Wproj.t[:, k2, hf * 512:(hf + 1) * 512], start=(k2 == 0), stop=(k2 == 1)),
                        reads=[pT.b, Wproj.b], writes=[pproj.b])
                sg = cx.nxt(sgs)
                S.op("act", lambda e, sg=sg, pgate=pgate: e.activation(sg.t[:], pgate.t[:], AF.Sigmoid),
                     reads=[pgate.b], writes=[sg.b])
                tmp = cx.nxt(tmps)
                S.op("dve", lambda e, tmp=tmp, sg=sg, pproj=pproj: e.tensor_tensor(tmp.t[:], sg.t[:], pproj.t[:], ALU.mult),
                     reads=[sg.b, pproj.b], writes=[tmp.b])
                hh_ = hb[b]
                S.op("dve", lambda e, hh_=hh_, tmp=tmp, hf=hf: e.tensor_tensor(
                    hh_.t[:, hf * 512:(hf + 1) * 512], hh_.t[:, hf * 512:(hf + 1) * 512], tmp.t[:], ALU.add),
                    reads=[hh_.b, tmp.b], writes=[hh_.b])
        for b in range(4):
            r0 = t0 + b * 128
            cx.store("o_h", hout[r0:r0 + 128, :], hb[b], hb[b].t[:])
        if not nxt:
            continue
        ensure("wkv")
        hT = cx.nxt(hnTs)
        for b in range(4):
            dn.norm_T(hb[b], hT, b * 128)

        def head_norm_store(p, gtile, dstT, b, hf):
            sq = cx.nxt(tmps)
            S.op("act", lambda e: e.activation(sq.t[:], p.t[:], AF.Square), reads=[p.b], writes=[sq.b])
            s8 = cx.nxt(hd8)
            S.op("dve", lambda e: e.tensor_reduce(s8.t[:], sq.t[:, :].rearrange("p (h d) -> p h d", d=DH),
                                                  mybir.AxisListType.X, ALU.add), reads=[sq.b], writes=[s8.b])
            l8 = cx.nxt(hd8)
            S.op("act", lambda e: e.activation(l8.t[:], s8.t[:], AF.Ln, scale=1.0 / DH, bias=dn.eps.t[:, 0:1]),
                 reads=[s8.b, dn.eps.b], writes=[l8.b])
            r8 = cx.nxt(hd8)
            S.op("act", lambda e: e.activation(r8.t[:], l8.t[:], AF.Exp, scale=-0.5), reads=[l8.b], writes=[r8.b])
            qf = cx.nxt(sgs)
            S.op("dve", lambda e: e.tensor_tensor(qf.t[:, :].rearrange("p (h d) -> p h d", d=DH),
                                                  p.t[:, :].rearrange("p (h d) -> p h d", d=DH),
                                                  r8.t[:, :].unsqueeze(2).to_broadcast([128, 8, DH]), ALU.mult),
                 reads=[p.b, r8.b], writes=[qf.b])
            qn = cx.nxt(qnb)
            S.op("pool", lambda e: e.tensor_tensor(qn.t[:, :], qf.t[:, :], gtile.t[:, :, :].rearrange("p h d -> p (h d)"), ALU.mult),
                 reads=[qf.b, gtile.b], writes=[qn.b])
            tp = cx.nxt(dn.psT)
            for c4 in range(4):
                S.op("pe", lambda e, c4=c4: e.transpose(tp.t[:, c4 * 128:(c4 + 1) * 128], qn.t[:, c4 * 128:(c4 + 1) * 128],
                                                        dn.ident.t[:]), reads=[qn.b, dn.ident.b], writes=[tp.b])
            o = cx.nxt(oT)
            S.op("act", lambda e: e.copy(o.t[:], tp.t[:, 0:512]), reads=[tp.b], writes=[o.b])
            r0 = t0 + b * 128
            cx.store("o_qk1", dstT[hf * 512:(hf + 1) * 512, r0:r0 + 128].rearrange("(c p) t -> p c t", p=128), o,
                     o.t[:, :].rearrange("p (c t) -> p c t", c=4))

        for hf in range(2):
            wt = load_w8(WqB, WB["wq"], hf, "wq")
            for b in range(4):
                p = cx.nxt(pm)
                for kc in range(8):
                    S.op("pe", lambda e, p=p, kc=kc, b=b, wt=wt, hT=hT: e.matmul(
                        p.t[:], hT.t[:, kc, b * 128:(b + 1) * 128], wt.t[:, kc, :], start=(kc == 0), stop=(kc == 7)),
                        reads=[hT.b, wt.b], writes=[p.b])
                head_norm_store(p, qg, q1T, b, hf)
        for hf in range(4):
            wt = load_w8(WkvB, WB["wkv"], hf, "wkv")
            for b in range(4):
                p = cx.nxt(pm)
                for kc in range(8):
                    S.op("pe", lambda e, p=p, kc=kc, b=b, wt=wt, hT=hT: e.matmul(
                        p.t[:], hT.t[:, kc, b * 128:(b + 1) * 128], wt.t[:, kc, :], start=(kc == 0), stop=(kc == 7)),
                        reads=[hT.b, wt.b], writes=[p.b])
                if hf < 2:
                    head_norm_store(p, kg, k1T, b, hf)
                else:
                    o = cx.nxt(oT)
                    S.op("act", lambda e, o=o, p=p: e.copy(o.t[:], p.t[:]), reads=[p.b], writes=[o.b])
                    r0 = t0 + b * 128
                    cx.store("o_v1", v1[r0:r0 + 128, (hf - 2) * 512:(hf - 1) * 512], o, o.t[:])
        p = cx.nxt(pm)
        for kc in range(8):
            S.op("pe", lambda e, p=p, kc=kc, hT=hT: e.matmul(p.t[0:NH, :], Wf.t[:, kc, :], hT.t[:, kc, :],
                                                              start=(kc == 0), stop=(kc == 7)),
                 reads=[Wf.b, hT.b], writes=[p.b])
        f1 = cx.nxt(fl)
        S.op("act", lambda e, f1=f1, p=p: e.activation(f1.t[:], p.t[0:NH, :], AF.Exp, scale=-1.0, bias=nbf.t[:, 0:1]),
             reads=[p.b, nbf.b], writes=[f1.b])
        f2 = cx.nxt(fl)
        S.op("act", lambda e, f1=f1, f2=f2: e.activation(f2.t[:], f1.t[:], AF.Ln, bias=one.t[0:NH, 0:1]),
             reads=[f1.b, one.b], writes=[f2.b])
        S.op("dve", lambda e, f2=f2: e.tensor_scalar(f2.t[:], f2.t[:], -1.0, None, ALU.mult), reads=[f2.b], writes=[f2.b])
        cx.store("o_fl", flogT[:, t0:t0 + 512], f2, f2.t[:])
    return cx.finish() if own else None


def build_attn1(n_heads=NH, cx=None, mix_dst=None):
    own = cx is None
    if own:
        cx = Cx()
    S = cx.S
    qT = cx.din("qT", [D, TL], BF16)
    kT = cx.din("kT", [D, S_ALL], BF16)
    vr = cx.din("vr", [8, 128, 128 * 128], BF16)
    flog = cx.din("flog", [NH, S_ALL], F32)
    onehot = cx.din("onehot", [NH, 8], F32)
    negm = cx.din("negm", [128, 8 * 128], BF16)
    ident = cx.din("ident", [128, 128], BF16)
    sel = cx.din("sel", [128, 256], F32)
    mixT = cx.dout("mixT", [D, TL], BF16) if mix_dst is None else mix_dst
    kaug = cx.dint("kaug", [NH, 6, S_ALL], BF16)
    qaug = cx.dint("qaug", [NH, 6, TL], BF16)
    b_kaug = Buf("kaug")
    b_qaug = Buf("qaug")

    nm = cx.sb("nm", [128, 8 * 128], BF16)
    idt = cx.sb("idt", [128, 128], BF16)
    selt = cx.sb("selt", [128, 256], F32)
    oh = cx.sb("oh", [NH, 8], F32)
    for t_, src in ((nm, negm), (idt, ident), (selt, sel), (oh, onehot)):
        S.dma("sp", "cc_" + t_.b.name, lambda e, t_=t_, src=src: e.dma_start(out=t_.t[:], in_=src), writes=[t_.b])

    CH = 1024
    Fc = cx.sb("Fc", [NH, CH], F32, n=2)
    Fs = cx.sb("Fs", [NH, CH], F32, n=2)
    r1 = cx.sb("r1", [NH, CH], F32)
    onesf = cx.sb("onesf", [NH, CH], F32)
    S.op("pool", lambda e: e.memset(onesf.t[:], 1.0), writes=[onesf.b])
    ka = cx.sb("ka", [NH, 6, CH], BF16)
    Fq = cx.sb("Fq", [NH, TL], F32)
    qa = cx.sb("qa", [NH, 6, 512], BF16)
    carry = cx.sb("carry", [NH, 1], F32, n=2)
    S.op("pool", lambda e: e.memset(carry[1].t[:], 0.0), writes=[carry[1].b])
    S.op("pool", lambda e: e.memset(ka.t[:, 0:3, :], 1.0), writes=[ka.b])
    S.op("pool", lambda e: e.memset(qa.t[:, 3:6, :], 1.0), writes=[qa.b])
    for ci in range(S_ALL // CH):
        fc = Fc[ci % 2]
        fs = Fs[ci % 2]
        S.dma("sp", "ld_" + fc.b.name, lambda e, fc=fc, ci=ci: e.dma_start(out=fc.t[:], in_=flog[:, ci * CH:(ci + 1) * CH]),
              writes=[fc.b])
        cprev = carry[(ci + 1) % 2]
        ccur = carry[ci % 2]
        S.op("dve", lambda e, fs=fs, fc=fc, cprev=cprev: e.tensor_tensor_scan(
            fs.t[:], onesf.t[:], fc.t[:], cprev.t[:, 0:1], ALU.mult, ALU.add),
            reads=[onesf.b, fc.b, cprev.b], writes=[fs.b])
        S.op("dve", lambda e, fs=fs, ccur=ccur: e.tensor_copy(ccur.t[:], fs.t[:, CH - 1:CH]), reads=[fs.b], writes=[ccur.b])
        fview = fs.t[:, :].rearrange("h (c i) -> h c i", c=8)
        fqv = Fq.t[:, ci * 128:(ci + 1) * 128]
        for c in range(8):
            if c == 0:
                S.op("dve", lambda e, fview=fview, fqv=fqv, c=c: e.tensor_scalar(
                    fqv, fview[:, c, :], oh.t[:, c:c + 1], None, ALU.mult), reads=[fs.b, oh.b], writes=[Fq.b])
            else:
                S.op("dve", lambda e, fview=fview, fqv=fqv, c=c: e.scalar_tensor_tensor(
                    fqv, fview[:, c, :], oh.t[:, c:c + 1], fqv, ALU.mult, ALU.add), reads=[fs.b, oh.b, Fq.b], writes=[Fq.b])
        S.op("dve", lambda e, fs=fs: e.tensor_scalar(r1.t[:], fs.t[:], -1.0, None, ALU.mult), reads=[fs.b], writes=[r1.b])
        for part in range(3):
            S.op("dve", lambda e, part=part: e.tensor_copy(ka.t[:, 3 + part, :], r1.t[:]), reads=[r1.b], writes=[ka.b])
            if part < 2:
                S.op("dve", lambda e, part=part: e.tensor_tensor(r1.t[:], r1.t[:], ka.t[:, 3 + part, :], ALU.subtract),
                     reads=[r1.b, ka.b], writes=[r1.b])
        S.dma("sp", "st_kaug", lambda e, ci=ci: e.dma_start(out=kaug[:, :, ci * CH:(ci + 1) * CH], in_=ka.t[:]),
              reads=[ka.b], writes=[b_kaug])
    for qi in range(4):
        S.op("dve", lambda e, qi=qi: e.tensor_copy(r1.t[:, 0:512], Fq.t[:, qi * 512:(qi + 1) * 512]), reads=[Fq.b], writes=[r1.b])
        for part in range(3):
            S.op("dve", lambda e, part=part: e.tensor_copy(qa.t[:, part, :], r1.t[:, 0:512]), reads=[r1.b], writes=[qa.b])
            if part < 2:
                S.op("dve", lambda e, part=part: e.tensor_tensor(r1.t[:, 0:512], r1.t[:, 0:512], qa.t[:, part, :], ALU.subtract),
                     reads=[r1.b, qa.b], writes=[r1.b])
        S.dma("sp", "st_qaug", lambda e, qi=qi: e.dma_start(out=qaug[:, :, qi * 512:(qi + 1) * 512], in_=qa.t[:]),
              reads=[qa.b], writes=[b_qaug])

    kTs = cx.sb("kTs", [128, S_ALL], BF16, n=2)
    qs = cx.sb("qs", [128, TL], BF16, n=2)
    vs = cx.sb("vs", [128, 128 * 130], BF16, n=2)
    mx = cx.sb("mx", [128, TL], BF16, n=2)
    for sl in range(2):
        v3 = vs[sl].t[:, :].rearrange("p (k c) -> p k c", c=130)
        S.op("pool", lambda e, v3=v3: e.memset(v3[:, :, 64:65], 1.0), writes=[vs[sl].b])
        S.op("pool", lambda e, v3=v3: e.memset(v3[:, :, 129:130], 1.0), writes=[vs[sl].b])
    zp = cx.ps("zp", [128, 512], F32, n=3)
    op_ = cx.ps("op", [128, 512], F32, n=2)
    bcp = cx.ps("bcp", [128, 512], F32, n=1)
    pb_ = cx.sb("pb", [128, 512], BF16, n=4)
    osb = cx.sb("osb", [128, 512], F32, n=1)
    rbs = cx.sb("rbs", [128, 512], F32, n=1)

    def load_head(h):
        sl = h % 2
        k_, q_ = kTs[sl], qs[sl]
        S.dma("sp", "ldq%d" % sl, lambda e: e.dma_start(out=q_.t[0:64, :], in_=qT[h * 64:(h + 1) * 64, :]), writes=[q_.b])
        S.dma("sp", "ldq%d" % sl, lambda e: e.dma_start(out=q_.t[64:70, :], in_=qaug[h]), reads=[b_qaug], writes=[q_.b])
        S.dma("sp", "ldk%d" % sl, lambda e: e.dma_start(out=k_.t[64:70, :], in_=kaug[h]), reads=[b_kaug], writes=[k_.b])
        for part in range(4):
            c0 = part * 4096
            S.dma("sp", "ldk%d" % sl,
                  lambda e, c0=c0: e.dma_start(out=k_.t[0:64, c0:c0 + 4096], in_=kT[h * 64:(h + 1) * 64, c0:c0 + 4096]),
                  writes=[k_.b])

    def load_v(hp):
        v_ = vs[hp % 2]
        v3 = v_.t[:, :].rearrange("p (k c) -> p k c", c=130)
        src = vr[hp].rearrange("p (k c) -> p k c", c=128)
        for part in range(8):
            k0 = part * 16
            for hh in range(2):
                S.dma("sp", "ldv%d" % (hp % 2),
                      lambda e, k0=k0, hh=hh: e.dma_start(out=v3[:, k0:k0 + 16, 65 * hh:65 * hh + 64],
                                                          in_=src[:, k0:k0 + 16, 64 * hh:64 * hh + 64]),
                      writes=[v_.b])

    load_v(0)
    load_head(0)

    def do_head(h):
        sl = h % 2
        hp = h // 2
        odd = h % 2
        k_, q_ = kTs[sl], qs[sl]
        v_ = vs[hp % 2]
        m_ = mx[hp % 2]
        if h + 1 < n_heads:
            if (h + 1) % 2 == 0:
                load_v((h + 1) // 2)
            load_head(h + 1)
        items = []
        for J in range(4):
            nkb = 32 * J + 32
            for kb in range(nkb - 1, -1, -1):
                items.append((J, kb, nkb))
        st = {}

        def s1(it):
            J, kb, nkb = it
            r = kb - 32 * J
            c0 = 128 * (r // 8) if r >= 0 else 0
            z = cx.nxt(zp)
            p_ = cx.nxt(pb_)
            S.op("pe", lambda e: e.matmul(z.t[:, c0:512], k_.t[0:70, kb * 128:(kb + 1) * 128],
                                          q_.t[0:70, 512 * J + c0:512 * J + 512], start=True, stop=(r < 0)),
                 reads=[k_.b, q_.b], writes=[z.b])
            if r >= 0:
                i = r % 8
                S.op("pe", lambda e: e.matmul(z.t[:, c0:c0 + 128], idt.t[:], nm.t[:, i * 128:(i + 1) * 128],
                                              start=False, stop=True), reads=[idt.b, nm.b], writes=[z.b])
            S.op("act", lambda e: e.activation(p_.t[:, c0:512], z.t[:, c0:512], AF.Exp), reads=[z.b], writes=[p_.b])
            st[it] = (c0, p_)

        def s2(it):
            J, kb, nkb = it
            c0, p_ = st.pop(it)
            o_ = op_[J % 2]
            if odd:
                S.op("pe", lambda e: e.matmul(o_.t[:, c0:512], v_.t[:, kb * 130 + 1:kb * 130 + 129], p_.t[:, c0:512],
                                              start=(kb == nkb - 1), stop=(kb == 0), skip_group_check=(kb != nkb - 1 and kb != 0)),
                     reads=[v_.b, p_.b], writes=[o_.b])
            else:
                S.op("pe", lambda e: e.matmul(o_.t[0:65, c0:512], v_.t[:, kb * 130:kb * 130 + 65], p_.t[:, c0:512],
                                              start=(kb == nkb - 1), stop=(kb == 0), skip_group_check=(kb != nkb - 1 and kb != 0)),
                     reads=[v_.b, p_.b], writes=[o_.b])
            if kb == 0:
                ob = cx.nxt(osb)
                rb = cx.nxt(rbs)
                bc = bcp[0]
                if odd:
                    S.op("act", lambda e: e.copy(ob.t[32:64, :], o_.t[32:64, :]), reads=[o_.b], writes=[ob.b])
                    S.op("act", lambda e: e.copy(ob.t[64:128, :], o_.t[64:128, :]), reads=[o_.b], writes=[ob.b])
                    S.op("pe", lambda e: e.matmul(bc.t[:, :], selt.t[32:64, 128:256], ob.t[32:64, :], start=True, stop=True),
                         reads=[selt.b, ob.b], writes=[bc.b])
                    S.op("dve", lambda e: e.reciprocal(rb.t[64:128, :], bc.t[64:128, :]), reads=[bc.b], writes=[rb.b])
                    S.op("dve", lambda e: e.tensor_tensor(m_.t[64:128, 512 * J:512 * J + 512], ob.t[64:128, :], rb.t[64:128, :], ALU.mult),
                         reads=[ob.b, rb.b], writes=[m_.b])
                else:
                    S.op("act", lambda e: e.copy(ob.t[0:65, :], o_.t[0:65, :]), reads=[o_.b], writes=[ob.b])
                    S.op("pe", lambda e: e.matmul(bc.t[0:64, :], selt.t[64:65, 0:64], ob.t[64:65, :], start=True, stop=True),
                         reads=[selt.b, ob.b], writes=[bc.b])
                    S.op("dve", lambda e: e.reciprocal(rb.t[0:64, :], bc.t[0:64, :]), reads=[bc.b], writes=[rb.b])
                    S.op("dve", lambda e: e.tensor_tensor(m_.t[0:64, 512 * J:512 * J + 512], ob.t[0:64, :], rb.t[0:64, :], ALU.mult),
                         reads=[ob.b, rb.b], writes=[m_.b])

        n = len(items)
        for tau in range(-1, n + 1):
            if 0 <= tau + 1 < n:
                s1(items[tau + 1])
            if 0 <= tau - 1 < n:
                s2(items[tau - 1])
        if odd:
            cx.store("o_mix", mixT[hp * 128:(hp + 1) * 128, :], m_, m_.t[:])

    for h in range(n_heads):
        do_head(h)
    return cx.finish() if own else None


def build_layer(first):
    cx = Cx()
    mix = cx.dint("mix_scr", [D, TL], BF16)
    cx.begin_phase("a_")
    if first:
        build_attn0(cx=cx, mix_dst=mix)
    else:
        build_attn1(cx=cx, mix_dst=mix)
    cx.end_phase()
    cx.begin_phase("p_")
    build_post(first, cx=cx, mix_src=mix)
    return cx.finish()


_PROGS = {}


def _prog(name, fn):
    if name not in _PROGS:
        _PROGS[name] = fn()
    return _PROGS[name]


def _shard_tok(a):
    F_ = a.shape[-1]
    r = a.reshape(16, 8, 128, F_)
    return [np.ascontiguousarray(r[:, c].reshape(TL, F_)) for c in range(NC_)]


def _gather_T(parts):
    R = parts[0].shape[0]
    out = np.zeros((R, 16, 8, 128), dtype=parts[0].dtype)
    for c in range(NC_):
        out[:, :, c, :] = parts[c].reshape(R, 16, 128)
    return out.reshape(R, S_ALL)


def _gather_v(parts):
    va = np.zeros((16, 8, 128, D), dtype=parts[0].dtype)
    for c in range(NC_):
        va[:, c] = parts[c].reshape(16, 128, D)
    va = va.reshape(128, 128, 8, 128)
    return np.ascontiguousarray(va.transpose(2, 1, 0, 3)).reshape(8, 128, 128 * 128)


def _consts():
    ar = np.arange(128)
    c = {}
    c["ident"] = np.eye(128, dtype=np.float32).astype(NPBF)
    c["tri"] = (ar[:, None] >= ar[None, :]).astype(np.float32).astype(NPBF)
    c["omt"] = (ar[:, None] < ar[None, :]).astype(np.float32).astype(NPBF)
    masks, negm, oh = [], [], []
    for cc in range(NC_):
        m = np.zeros((128, 8, 128), np.float32)
        n = np.full((128, 8, 128), -30000.0, np.float32)
        for i in range(8):
            if i < cc:
                m[:, i, :] = 1.0
                n[:, i, :] = 0.0
            elif i == cc:
                m[:, i, :] = (ar[:, None] < ar[None, :])
                n[:, i, :] = np.where(ar[:, None] <= ar[None, :], 0.0, -30000.0)
        masks.append(m.reshape(128, 1024).astype(NPBF))
        negm.append(n.reshape(128, 1024).astype(NPBF))
        o = np.zeros((NH, 8), np.float32)
        o[:, cc] = 1.0
        oh.append(o)
    c["masks"], c["negm"], c["onehot"] = masks, negm, oh
    sel = np.zeros((128, 256), np.float32)
    sel[64, 0:64] = 1.0
    sel[63, 192:256] = 1.0
    c["sel"] = sel
    return c


def _run(nc, in_maps):
    return run_bass_kernel_spmd(nc, in_maps, core_ids=list(range(NC_))).results


def kernel(x, p, attn_norm_g, sb_w_qkv, sb_w_o, shared_norm_g, shared_w_kvf, shared_b_f, shared_k_norm_g,
           fox_w_q, fox_q_norm_g, fox_w_o, ffn_norm_g, ffn_w_gu, ffn_w_d, ple_norm_g, ple_w_gate, ple_w_proj):
    f32 = lambda a: np.ascontiguousarray(np.asarray(a, dtype=np.float32))
    x = f32(x)
    p = f32(p)
    C = _consts()
    xs = _shard_tok(x[0])
    p0 = _shard_tok(p[0, 0])
    p1 = _shard_tok(p[1, 0])
    r1 = _run(_prog("pre0", build_pre0),
              [{"x": xs[c], "g": f32(attn_norm_g[0]), "w": f32(sb_w_qkv[0]), "ident": C["ident"]} for c in range(NC_)])
    kT_all = _gather_T([r["kT"] for r in r1])
    vr = _gather_v([r["v"] for r in r1])

    def post_in(c, h, pl, w_o, li):
        return {"h": h, "p": pl, "w_o": f32(w_o), "ffn_g": f32(ffn_norm_g[li]), "w_gu": f32(ffn_w_gu[li]),
                "w_d": f32(ffn_w_d[li]), "ple_g": f32(ple_norm_g[li]), "w_gate": f32(ple_w_gate[li]),
                "w_proj": f32(ple_w_proj[li]), "ident": C["ident"]}

    in3 = []
    for c in range(NC_):
        d = post_in(c, xs[c], p0[c], sb_w_o[0], 0)
        d.update({"qT": r1[c]["qT"], "kT": kT_all, "vr": vr, "masks": C["masks"][c], "tri": C["tri"], "omt": C["omt"]})
        d.update({"a_g": f32(attn_norm_g[1]), "w_q": f32(fox_w_q[0]), "qn_g": f32(fox_q_norm_g[0]),
                  "sh_g": f32(shared_norm_g), "w_kvf": f32(shared_w_kvf), "b_f": f32(shared_b_f),
                  "kn_g": f32(shared_k_norm_g)})
        in3.append(d)
    r3 = _run(_prog("layer0", lambda: build_layer(True)), in3)
    k1_all = _gather_T([r["k1T"] for r in r3])
    vr1 = _gather_v([r["v1"] for r in r3])
    fl_all = _gather_T([r["flogT"] for r in r3])
    in5 = []
    for c in range(NC_):
        d = post_in(c, r3[c]["hout"], p1[c], fox_w_o[0], 1)
        d.update({"qT": r3[c]["q1T"], "kT": k1_all, "vr": vr1, "flog": fl_all, "onehot": C["onehot"][c],
                  "negm": C["negm"][c], "sel": C["sel"]})
        in5.append(d)
    r5 = _run(_prog("layer1", lambda: build_layer(False)), in5)
    out = np.zeros((16, 8, 128, D), np.float32)
    for c in range(NC_):
        out[:, c] = r5[c]["hout"].reshape(16, 128, D)
    return out.reshape(1, S_ALL, D)
```
